# Optimizing a Trainium2 kernel written in Bass

```python
import jax
import jax.numpy as jnp
from jax import lax
import numpy as np

D_MODEL = 1024
BATCH = 8
SEQ = 8192
DEPTH = 1
DEC_BATCH = 16
DEC_SEQ = 64
PAST_LEN = 1024

CHUNK = 64
EPS = 1e-6
ROPE_THETA = 10000.0
N_HEADS_A = 8
N_KV_A = 2
GROUP_A = N_HEADS_A // N_KV_A
HD_A = 64
D_ATT_A = N_HEADS_A * HD_A
D_KV_A = N_KV_A * HD_A
N_IDX_HEADS = 8
D_IDX = 32
D_QIDX = N_IDX_HEADS * D_IDX
IDX_SCALE = (N_IDX_HEADS * D_IDX) ** -0.5
TOPK_MAX = 256
Q_BLOCK = 128
POOL_WINDOWS = (2, 4, 8, 16)
N_POOL_GROUPS = 4
POOL_GROUP = 128
D_POOL = N_POOL_GROUPS * POOL_GROUP
POOL_HIST = 15
N_MEM = 256
N_HEADS_M = 4
HD_M = 128
D_MEM_ATT = N_HEADS_M * HD_M
N_BRANCH = 3
D_FF = -(-(8 * D_MODEL) // (3 * 256)) * 256
IN_WIDTHS = (D_ATT_A, D_KV_A, D_KV_A, D_QIDX, D_IDX, N_IDX_HEADS, D_POOL, D_MEM_ATT, N_BRANCH * D_MODEL)
D_IN = D_ATT_A + 2 * D_KV_A + D_QIDX + D_IDX + N_IDX_HEADS + D_POOL + D_MEM_ATT + N_BRANCH * D_MODEL

kernel_name = 'hybrid_dsa_pool_memory_streaming_step'


def rmsnorm(x, g):
    x32 = x.astype(jnp.float32)
    y = x32 * lax.rsqrt(jnp.mean(x32 * x32, axis=-1, keepdims=True) + EPS)
    return (y * g.astype(jnp.float32)).astype(x.dtype)


def rope(x, pos):
    half = x.shape[-1] // 2
    inv = ROPE_THETA ** (-jnp.arange(half, dtype=jnp.float32) / half)
    ang = pos.astype(jnp.float32)[:, None] * inv[None, :]
    cos = jnp.cos(ang)[None, :, None, :]
    sin = jnp.sin(ang)[None, :, None, :]
    x32 = x.astype(jnp.float32)
    x1, x2 = x32[..., :half], x32[..., half:]
    return jnp.concatenate([x1 * cos - x2 * sin, x2 * cos + x1 * sin], axis=-1).astype(x.dtype)


def project_inputs(x, pos, g_mix, w_in, g_qa, g_ka, g_kidx, g_qm):
    B, T, _ = x.shape
    h = rmsnorm(x, g_mix)
    p = h @ w_in
    cuts = [int(c) for c in np.cumsum(IN_WIDTHS)[:-1]]
    qa, ka, va, qi, ki, wi, ub, qm, gates = jnp.split(p, cuts, axis=-1)
    qa = rope(rmsnorm(qa.reshape(B, T, N_HEADS_A, HD_A), g_qa), pos)
    ka = rope(rmsnorm(ka.reshape(B, T, N_KV_A, HD_A), g_ka), pos)
    va = va.reshape(B, T, N_KV_A, HD_A)
    qi = rope(qi.reshape(B, T, N_IDX_HEADS, D_IDX), pos)
    ki = rope(rmsnorm(ki, g_kidx)[:, :, None, :], pos)[:, :, 0, :]
    wi = wi * IDX_SCALE
    qm = rmsnorm(qm.reshape(B, T, N_HEADS_M, HD_M), g_qm)
    return qa, ka, va, qi, ki, wi, ub, qm, gates


def dsa_attend(qa, qi, wi, qpos, ka, va, ki, topk):
    B, Tq = qa.shape[:2]
    n_keys = ka.shape[1]
    s = jax.nn.relu(jnp.einsum('bthd,bsd->bths', qi, ki, preferred_element_type=jnp.float32))
    score = jnp.einsum('bths,bth->bts', s, wi.astype(jnp.float32))
    q_chunk = qpos // CHUNK
    adm = (jnp.arange(n_keys) // CHUNK)[None, :] <= q_chunk[:, None]
    score = jnp.where(adm[None], score, -jnp.inf)
    _, sel = lax.top_k(score, topk)
    gather_rows = jax.vmap(lambda rows, i: rows[i])
    k_sel = gather_rows(ka, sel)
    v_sel = gather_rows(va, sel)
    valid = (sel // CHUNK) <= q_chunk[None, :, None]
    q = qa.reshape(B, Tq, N_KV_A, GROUP_A, HD_A)
    logits = jnp.einsum('bthgd,btjhd->bthgj', q, k_sel, preferred_element_type=jnp.float32) * (HD_A ** -0.5)
    logits = jnp.where(valid[:, :, None, None, :], logits, -jnp.inf)
    probs = jax.nn.softmax(logits, axis=-1).astype(va.dtype)
    out = jnp.einsum('bthgj,btjhd->bthgd', probs, v_sel)
    return out.reshape(B, Tq, D_ATT_A)


def dsa_prompt(qa, qi, wi, pos, ka, va, ki, topk):
    B, T = qa.shape[:2]
    nblk = T // Q_BLOCK

    def to_blocks(a):
        return jnp.moveaxis(a.reshape((B, nblk, Q_BLOCK) + a.shape[2:]), 1, 0)

    def one_block(args):
        qa_b, qi_b, wi_b, pos_b = args
        return dsa_attend(qa_b, qi_b, wi_b, pos_b, ka, va, ki, topk)

    out = lax.map(one_block, (to_blocks(qa), to_blocks(qi), to_blocks(wi), pos.reshape(nblk, Q_BLOCK)))
    return jnp.moveaxis(out, 0, 1).reshape(B, T, D_ATT_A)


def pool_mix(u_ext, pos0, w_pool, s_pool):
    B, n, _ = u_ext.shape
    T = n - POOL_HIST
    u = u_ext[:, POOL_HIST:]
    cs = jnp.cumsum(u_ext.astype(jnp.float32), axis=1)
    cs = jnp.concatenate([jnp.zeros_like(cs[:, :1]), cs], axis=1)
    pos = pos0 + jnp.arange(T, dtype=jnp.int32)
    means = []
    for g, w in enumerate(POOL_WINDOWS):
        c = slice(g * POOL_GROUP, (g + 1) * POOL_GROUP)
        win = cs[:, POOL_HIST + 1:, c] - cs[:, POOL_HIST + 1 - w:POOL_HIST + 1 - w + T, c]
        cnt = jnp.minimum(w, pos + 1).astype(jnp.float32)[None, :, None]
        means.append(win / cnt)
    mean = jnp.concatenate(means, axis=-1)
    p = (mean - u.astype(jnp.float32)).astype(u.dtype).reshape(B, T, N_POOL_GROUPS, POOL_GROUP)
    y = jnp.einsum('btgc,gce->btge', p, w_pool).reshape(B, T, D_POOL)
    return y * s_pool


def memory_kv(mem, g_mem, w_mem_kv, g_km):
    B, M, _ = mem.shape
    kv = rmsnorm(mem, g_mem) @ w_mem_kv
    k, v = jnp.split(kv, 2, axis=-1)
    k = rmsnorm(k.reshape(B, M, N_HEADS_M, HD_M), g_km)
    return k, v.reshape(B, M, N_HEADS_M, HD_M)


def memory_attend(qm, mk, mv):
    B, T = qm.shape[:2]
    logits = jnp.einsum('bthd,bmhd->bhtm', qm, mk, preferred_element_type=jnp.float32) * (HD_M ** -0.5)
    probs = jax.nn.softmax(logits, axis=-1).astype(mv.dtype)
    return jnp.einsum('bhtm,bmhd->bthd', probs, mv).reshape(B, T, D_MEM_ATT)


def merge_and_ffn(x, a, b, m, gates, w_oa, w_ob, w_om, w_out, g_ffn, w_gate, w_up, w_down):
    ga, gb, gm = jnp.split(jax.nn.sigmoid(gates.astype(jnp.float32)).astype(x.dtype), N_BRANCH, axis=-1)
    mixed = ga * (a @ w_oa) + gb * (b @ w_ob) + gm * (m @ w_om)
    x = x + mixed @ w_out
    h = rmsnorm(x, g_ffn)
    return x + (jax.nn.silu(h @ w_gate) * (h @ w_up)) @ w_down


def setup_inputs(seed: int = 0) -> dict:
    key = jax.random.key(seed)
    ks = jax.random.split(key, 32)
    f32 = jnp.float32
    L = DEPTH

    def nrm(k, shape, scale):
        return jax.random.normal(k, shape, f32) * scale

    def gain(k, shape):
        return 1.0 + 0.1 * jax.random.normal(k, shape, f32)

    return {
        'x_prompt': nrm(ks[0], (BATCH, SEQ, D_MODEL), 1.0),
        'x_sample': nrm(ks[1], (DEC_BATCH, DEC_SEQ, D_MODEL), 1.0),
        'mem_prompt': nrm(ks[2], (BATCH, N_MEM, D_MODEL), 1.0),
        'cache_a_k': nrm(ks[3], (L, DEC_BATCH, PAST_LEN, N_KV_A, HD_A), 1.0),
        'cache_a_v': nrm(ks[4], (L, DEC_BATCH, PAST_LEN, N_KV_A, HD_A), 1.0),
        'cache_idx_k': nrm(ks[5], (L, DEC_BATCH, PAST_LEN, D_IDX), 1.0),
        'cache_pool': nrm(ks[6], (L, DEC_BATCH, POOL_HIST, D_POOL), 1.0),
        'cache_mem_k': nrm(ks[7], (L, DEC_BATCH, N_MEM, N_HEADS_M, HD_M), 1.0),
        'cache_mem_v': nrm(ks[8], (L, DEC_BATCH, N_MEM, N_HEADS_M, HD_M), 1.0),
        'g_mix': gain(ks[9], (L, D_MODEL)),
        'w_in': nrm(ks[10], (L, D_MODEL, D_IN), D_MODEL ** -0.5),
        'g_qa': gain(ks[11], (L, HD_A)),
        'g_ka': gain(ks[12], (L, HD_A)),
        'g_kidx': gain(ks[13], (L, D_IDX)),
        'g_qm': gain(ks[14], (L, HD_M)),
        'g_mem': gain(ks[15], (L, D_MODEL)),
        'w_mem_kv': nrm(ks[16], (L, D_MODEL, 2 * D_MEM_ATT), D_MODEL ** -0.5),
        'g_km': gain(ks[17], (L, HD_M)),
        'w_pool': nrm(ks[18], (L, N_POOL_GROUPS, POOL_GROUP, POOL_GROUP), POOL_GROUP ** -0.5),
        's_pool': gain(ks[19], (L, D_POOL)),
        'w_oa': nrm(ks[20], (L, D_ATT_A, D_MODEL), D_ATT_A ** -0.5),
        'w_ob': nrm(ks[21], (L, D_POOL, D_MODEL), D_POOL ** -0.5),
        'w_om': nrm(ks[22], (L, D_MEM_ATT, D_MODEL), D_MEM_ATT ** -0.5),
        'w_out': nrm(ks[23], (L, D_MODEL, D_MODEL), D_MODEL ** -0.5),
        'g_ffn': gain(ks[24], (L, D_MODEL)),
        'w_gate': nrm(ks[25], (L, D_MODEL, D_FF), D_MODEL ** -0.5),
        'w_up': nrm(ks[26], (L, D_MODEL, D_FF), D_MODEL ** -0.5),
        'w_down': nrm(ks[27], (L, D_FF, D_MODEL), D_FF ** -0.5),
    }


def reference(x_prompt, x_sample, mem_prompt, cache_a_k, cache_a_v, cache_idx_k, cache_pool, cache_mem_k,
              cache_mem_v, g_mix, w_in, g_qa, g_ka, g_kidx, g_qm, g_mem, w_mem_kv, g_km, w_pool, s_pool,
              w_oa, w_ob, w_om, w_out, g_ffn, w_gate, w_up, w_down):
    T = x_prompt.shape[1]
    TS = x_sample.shape[1]
    P = cache_a_k.shape[2]
    pos_p = jnp.arange(T, dtype=jnp.int32)
    pos_s = P + jnp.arange(TS, dtype=jnp.int32)
    topk_p = min(TOPK_MAX, T // 4)
    topk_s = min(TOPK_MAX, (P + TS) // 4)
    xp, xs = x_prompt, x_sample
    ak_p, av_p, ik_p, pool_p, mk_p, mv_p = [], [], [], [], [], []
    ak_s, av_s, ik_s, pool_s = [], [], [], []
    for l in range(DEPTH):
        qa, ka, va, qi, ki, wi, ub, qm, gates = project_inputs(xp, pos_p, g_mix[l], w_in[l], g_qa[l], g_ka[l], g_kidx[l], g_qm[l])
        a = dsa_prompt(qa, qi, wi, pos_p, ka, va, ki, topk_p)
        ub_ext = jnp.pad(ub, ((0, 0), (POOL_HIST, 0), (0, 0)))
        b = pool_mix(ub_ext, 0, w_pool[l], s_pool[l])
        mk, mv = memory_kv(mem_prompt, g_mem[l], w_mem_kv[l], g_km[l])
        m = memory_attend(qm, mk, mv)
        xp = merge_and_ffn(xp, a, b, m, gates, w_oa[l], w_ob[l], w_om[l], w_out[l], g_ffn[l], w_gate[l], w_up[l], w_down[l])
        ak_p.append(ka)
        av_p.append(va)
        ik_p.append(ki)
        pool_p.append(ub[:, -POOL_HIST:])
        mk_p.append(mk)
        mv_p.append(mv)
        qa, ka, va, qi, ki, wi, ub, qm, gates = project_inputs(xs, pos_s, g_mix[l], w_in[l], g_qa[l], g_ka[l], g_kidx[l], g_qm[l])
        k_all = jnp.concatenate([cache_a_k[l], ka], axis=1)
        v_all = jnp.concatenate([cache_a_v[l], va], axis=1)
        ki_all = jnp.concatenate([cache_idx_k[l], ki], axis=1)
        a = dsa_attend(qa, qi, wi, pos_s, k_all, v_all, ki_all, topk_s)
        ub_ext = jnp.concatenate([cache_pool[l], ub], axis=1)
        b = pool_mix(ub_ext, P, w_pool[l], s_pool[l])
        m = memory_attend(qm, cache_mem_k[l], cache_mem_v[l])
        xs = merge_and_ffn(xs, a, b, m, gates, w_oa[l], w_ob[l], w_om[l], w_out[l], g_ffn[l], w_gate[l], w_up[l], w_down[l])
        ak_s.append(ka)
        av_s.append(va)
        ik_s.append(ki)
        pool_s.append(ub_ext[:, -POOL_HIST:])
    y_prompt = xp
    y_sample = xs
    new_a_k_prompt = jnp.stack(ak_p)
    new_a_v_prompt = jnp.stack(av_p)
    new_idx_k_prompt = jnp.stack(ik_p)
    new_pool_prompt = jnp.stack(pool_p)
    new_mem_k_prompt = jnp.stack(mk_p)
    new_mem_v_prompt = jnp.stack(mv_p)
    new_a_k_sample = jnp.stack(ak_s)
    new_a_v_sample = jnp.stack(av_s)
    new_idx_k_sample = jnp.stack(ik_s)
    new_pool_sample = jnp.stack(pool_s)
    return (y_prompt, y_sample, new_a_k_prompt, new_a_v_prompt, new_idx_k_prompt, new_pool_prompt, new_mem_k_prompt, new_mem_v_prompt, new_a_k_sample, new_a_v_sample, new_idx_k_sample, new_pool_sample)
```

```python
from contextlib import ExitStack
import numpy as np
import concourse.bass as bass
import concourse.mybir as mybir
from concourse.bass_utils import run_bass_kernel_spmd

F32 = mybir.dt.float32
BF16 = mybir.dt.bfloat16
AF = mybir.ActivationFunctionType
ALU = mybir.AluOpType
AX = mybir.AxisListType


class Ctr:
    def __init__(self, name, sem):
        self.name = name
        self.sem = sem
        self.count = 0


class Buf:
    def __init__(self, name, ap, ctr=None):
        self.name = name
        self.ap = ap
        self.ctr = ctr
        self.w = None
        self.r = {}

    def __getitem__(self, k):
        return self.ap[k]


class Eng:
    def __init__(self, name, be, ctr):
        self.name = name
        self.be = be
        self.ctr = ctr
        self.ops = []
        self.seen = {}


class Prog:
    def __init__(self, nc, stack):
        self.nc = nc
        self.gstack = stack
        self.stack = stack
        self.engs = {}
        self.nsem = 0
        for nm, be in (("pe", nc.tensor), ("act", nc.scalar), ("dve", nc.vector),
                       ("pool", nc.gpsimd), ("sp", nc.sync)):
            self.engs[nm] = Eng(nm, be, self.new_ctr("e_" + nm))
        self.dma_ctrs = []

    def new_ctr(self, name):
        sem = self.gstack.enter_context(self.nc.semaphore(name))
        self.nsem += 1
        return Ctr(name, sem)

    def sbuf(self, name, shape, dtype, dma=False):
        t = self.stack.enter_context(self.nc.sbuf_tensor(name, list(shape), dtype))
        return self.buf(name, t, dma)

    def psum(self, name, shape, dtype):
        t = self.stack.enter_context(self.nc.psum_tensor(name, list(shape), dtype))
        return Buf(name, t)

    def buf(self, name, ap, dma=False):
        c = None
        if dma:
            c = self.new_ctr("d_" + name)
            self.dma_ctrs.append(c)
        return Buf(name, ap, c)

    def _deps(self, eng, reads, writes, skip_same_pe=False):
        deps = {}

        def add(cv):
            if cv is None:
                return
            c, v = cv
            if skip_same_pe and c is eng.ctr:
                return
            if deps.get(c, 0) < v:
                deps[c] = v

        for b in reads:
            add(b.w)
        for b in writes:
            add(b.w)
            for c, v in b.r.items():
                add((c, v))
        waits = []
        for c, v in deps.items():
            if eng.seen.get(c, 0) < v:
                eng.seen[c] = v
                waits.append((c.sem, v))
        return waits

    def _cut(self):
        import os
        self.nrec = getattr(self, "nrec", 0) + 1
        return self.nrec > int(os.environ.get("OPCUT", "1000000000"))

    def op(self, engname, fn, reads=(), writes=()):
        if self._cut():
            return 0
        eng = self.engs[engname]
        waits = self._deps(eng, reads, writes, skip_same_pe=(engname == "pe"))
        eng.ctr.count += 1
        val = eng.ctr.count
        sem = eng.ctr.sem

        def emit(be, waits=waits, fn=fn, sem=sem):
            for s, v in waits:
                be.wait_ge(s, v)
            fn(be).then_inc(sem, 1)

        eng.ops.append(emit)
        for b in writes:
            b.w = (eng.ctr, val)
            b.r = {}
        for b in reads:
            if b not in writes:
                b.r[eng.ctr] = val
        return val

    def dma(self, engname, fns, ctrbuf, reads=(), writes=()):
        if self._cut():
            return 0
        eng = self.engs[engname]
        ctr = ctrbuf.ctr
        assert ctr is not None, ctrbuf.name
        waits = self._deps(eng, reads, writes)
        ctr.count += 16 * len(fns)
        val = ctr.count
        sem = ctr.sem

        def emit(be, waits=waits, fns=fns, sem=sem):
            for s, v in waits:
                be.wait_ge(s, v)
            for f in fns:
                f(be).then_inc(sem, 16)

        eng.ops.append(emit)
        for b in writes:
            b.w = (ctr, val)
            b.r = {}
        for b in reads:
            if b not in writes:
                b.r[ctr] = val
        return val

    def barrier(self):
        targets = [(e.ctr, e.ctr.count) for e in self.engs.values()]
        targets += [(c, c.count) for c in self.dma_ctrs]
        for eng in self.engs.values():
            waits = []
            for c, v in targets:
                if v > 0 and c is not eng.ctr and eng.seen.get(c, 0) < v:
                    eng.seen[c] = v
                    waits.append((c.sem, v))

            def emit(be, waits=waits):
                for s, v in waits:
                    be.wait_ge(s, v)

            eng.ops.append(emit)

    def emit_block(self):
        nc = self.nc
        ops = {k: e.ops for k, e in self.engs.items()}
        for e in self.engs.values():
            e.ops = []
        with nc.Block() as block:
            @block.tensor
            def _(be):
                for f in ops["pe"]:
                    f(be)

            @block.scalar
            def _(be):
                for f in ops["act"]:
                    f(be)

            @block.vector
            def _(be):
                for f in ops["dve"]:
                    f(be)

            @block.gpsimd
            def _(be):
                for f in ops["pool"]:
                    f(be)

            @block.sync
            def _(be):
                for f in ops["sp"]:
                    f(be)

D = 1024
NTP = 64
DIN = 5160
DFF = 2816
EPS = 1e-6
NEG = -1.0e30
NBIS = 14
IDX_SCALE = 256.0 ** -0.5


class Ring:
    def __init__(self, bufs):
        self.bufs = bufs
        self.i = 0

    def next(self):
        b = self.bufs[self.i % len(self.bufs)]
        self.i += 1
        return b


def build_program(dbg=False):
    import os as _os
    KSTOP = int(_os.environ.get('KSTOP', '99'))
    NT = NTP + 2
    SP = NTP * 128
    SCW = max(SP, 1152)
    nc = bass.Bass("TRN2", target_bir_lowering=False)

    def din(name, shape, dt=F32):
        return nc.dram_tensor(name, list(shape), dt, kind="ExternalInput").ap()

    def dout(name, shape, dt=F32):
        return nc.dram_tensor(name, list(shape), dt, kind="ExternalOutput").ap()

    def dscr(name, shape, dt):
        return nc.dram_tensor(name, list(shape), dt, kind=("ExternalOutput" if dbg else "Internal")).ap()

    x_p = din("x_p", [SP, D]); x_s = din("x_s", [2, 128, D]); mem = din("mem", [256, D])
    cak = din("cak", [2, 1024, 128]); cav = din("cav", [2, 1024, 128]); cik = din("cik", [2, 1024, 32])
    cpool = din("cpool", [2, 15, 512]); cmk = din("cmk", [2, 256, 512]); cmv = din("cmv", [2, 256, 512])
    w_in = din("w_in", [D, DIN]); w_mem = din("w_mem", [D, 1024]); w_pool = din("w_pool", [4, 128, 128])
    w_o = [din("w_oa", [512, D]), din("w_ob", [512, D]), din("w_om", [512, D])]
    w_out = din("w_out", [D, D]); w_gate = din("w_gate", [D, DFF]); w_up = din("w_up", [D, DFF]); w_down = din("w_down", [DFF, D])
    g_mix = din("g_mix", [1, D]); g_ffn = din("g_ffn", [1, D]); g_mem = din("g_mem", [1, D])
    g_qa = din("g_qa", [1, 64]); g_ka = din("g_ka", [1, 64]); g_kidx = din("g_kidx", [1, 32])
    g_qm = din("g_qm", [1, 128]); g_km = din("g_km", [1, 128]); s_pool = din("s_pool", [128, 4])
    rope = din("rope", [NT, 128, 96]); bands = din("bands", [128, 3, 4, 128]); ident = din("ident", [128, 128])

    y_p = dout("y_p", [SP, D]); y_s = dout("y_s", [2, 64, D])
    nak_p = dout("nak_p", [SP, 128]); nav_p = dout("nav_p", [SP, 128]); nik_p = dout("nik_p", [SP, 32])
    npool_p = dout("npool_p", [15, 512]); nmk_p = dout("nmk_p", [256, 512]); nmv_p = dout("nmv_p", [256, 512])
    nak_s = dout("nak_s", [2, 64, 128]); nav_s = dout("nav_s", [2, 64, 128]); nik_s = dout("nik_s", [2, 64, 32])
    npool_s = dout("npool_s", [2, 15, 512])

    s_qa = dscr("s_qa", [NT, 128, 512], BF16); s_qi = dscr("s_qi", [NT, 128, 384], BF16)
    s_qm = dscr("s_qm", [NT, 128, 512], BF16); s_ub = dscr("s_ub", [NT, 128, 512], BF16)
    s_wi = dscr("s_wi", [NT, 128, 8], F32); s_gate = dscr("s_gate", [NT, 128, 3072], BF16)
    s_KT = dscr("s_KT", [NT, 128, 128], BF16); s_V = dscr("s_V", [NT, 128, 128], BF16); s_KI = dscr("s_KI", [NT, 128, 128], BF16)
    s_br = dscr("s_br", [NT, 128, 1536], BF16); s_x1 = dscr("s_x1", [NT, 128, D], F32)

    def xsrc(t):
        return x_p[t * 128:(t + 1) * 128, :] if t < NTP else x_s[t - NTP]

    with ExitStack() as gst:
        P = Prog(nc, gst)

        def bc(ap2, shape):
            return ap2.unsqueeze(1).to_broadcast(shape)

        def bl(ap2, shape):
            return ap2.unsqueeze(2).to_broadcast(shape)

        idb = P.sbuf("idb", [128, 128], BF16)
        gq = P.sbuf("gq", [128, 64 + 64 + 32 + 128 + 128], F32, dma=True)
        spl = P.sbuf("spl", [128, 4], F32, dma=True)
        with ExitStack() as st0:
            P.stack = st0
            idf = P.sbuf("idf", [128, 128], F32, dma=True)
            P.dma("sp", [lambda be: be.dma_start(out=idf[:], in_=ident)], idf, writes=[idf])
            P.op("dve", lambda be: be.tensor_copy(out=idb[:], in_=idf[:]), reads=[idf], writes=[idb])
            offs = {}
            o = 0
            fl = []
            for nm, ap_, n in (("qa", g_qa, 64), ("ka", g_ka, 64), ("kidx", g_kidx, 32), ("qm", g_qm, 128), ("km", g_km, 128)):
                offs[nm] = (o, n)
                fl.append(lambda be, ap_=ap_, o=o, n=n: be.dma_start(out=gq[:, o:o + n], in_=ap_.partition_broadcast(128)))
                o += n
            P.dma("sp", fl, gq, writes=[gq])
            P.dma("sp", [lambda be: be.dma_start(out=spl[:], in_=s_pool)], spl, writes=[spl])
            P.barrier()
            P.emit_block()
            if KSTOP == 0:
                return nc
        P.stack = gst

        def gvec(nm):
            o, n = offs[nm]
            return gq[:, o:o + n]

        def rms_rows(xb, xap, gb, hb, n, junk, ss, rs):
            P.op("act", lambda be: be.activation(out=junk[:, 0:n], in_=xap, func=AF.Square, accum_out=ss[:]), reads=[xb], writes=[junk, ss])
            P.op("dve", lambda be: be.tensor_scalar(out=rs[:], in0=ss[:], scalar1=1.0 / n, scalar2=EPS, op0=ALU.mult, op1=ALU.add), reads=[ss], writes=[rs])
            P.op("act", lambda be: be.activation(out=rs[:], in_=rs[:], func=AF.Sqrt), reads=[rs], writes=[rs])
            P.op("dve", lambda be: be.reciprocal(out=rs[:], in_=rs[:]), reads=[rs], writes=[rs])
            P.op("dve", lambda be: be.scalar_tensor_tensor(out=hb[:], in0=xap, scalar=rs[:, 0:1], in1=gb[:], op0=ALU.mult, op1=ALU.mult), reads=[xb, rs, gb], writes=[hb])

        def head_norm(srcb, src3, gap, outb, out3, H, hd, sq, ssq, eng="dve"):
            sh = [128, H, hd]
            sqv = sq[:, 0:H * hd].rearrange("p (h d) -> p h d", h=H)
            P.op(eng, lambda be: be.tensor_tensor(out=sqv, in0=src3, in1=src3, op=ALU.mult), reads=[srcb], writes=[sq])
            P.op("dve", lambda be: be.tensor_reduce(out=ssq[:, 0:H], in_=sqv, axis=AX.X, op=ALU.add), reads=[sq], writes=[ssq])
            P.op("dve", lambda be: be.tensor_scalar(out=ssq[:, 0:H], in0=ssq[:, 0:H], scalar1=1.0 / hd, scalar2=EPS, op0=ALU.mult, op1=ALU.add), reads=[ssq], writes=[ssq])
            P.op("act", lambda be: be.activation(out=ssq[:, 0:H], in_=ssq[:, 0:H], func=AF.Sqrt), reads=[ssq], writes=[ssq])
            P.op("dve", lambda be: be.reciprocal(out=ssq[:, 0:H], in_=ssq[:, 0:H]), reads=[ssq], writes=[ssq])
            P.op(eng, lambda be: be.tensor_tensor(out=sqv, in0=src3, in1=bl(ssq[:, 0:H], sh), op=ALU.mult), reads=[srcb, ssq], writes=[sq])
            P.op(eng, lambda be: be.tensor_tensor(out=out3, in0=sqv, in1=bc(gap, sh), op=ALU.mult), reads=[sq, gq], writes=[outb])

        def rope_ops(srcb, x1, x2, cs, sn, outb, o1, o2, tb, t1, t2, eng="dve"):
            P.op(eng, lambda be: be.tensor_tensor(out=t1, in0=x1, in1=cs, op=ALU.mult), reads=[srcb, rp_cur[0]], writes=[tb])
            P.op(eng, lambda be: be.tensor_tensor(out=t2, in0=x2, in1=sn, op=ALU.mult), reads=[srcb, rp_cur[0]], writes=[tb])
            P.op(eng, lambda be: be.tensor_tensor(out=o1, in0=t1, in1=t2, op=ALU.subtract), reads=[tb], writes=[outb])
            P.op(eng, lambda be: be.tensor_tensor(out=t1, in0=x2, in1=cs, op=ALU.mult), reads=[srcb, rp_cur[0]], writes=[tb])
            P.op(eng, lambda be: be.tensor_tensor(out=t2, in0=x1, in1=sn, op=ALU.mult), reads=[srcb, rp_cur[0]], writes=[tb])
            P.op(eng, lambda be: be.tensor_tensor(out=o2, in0=t1, in1=t2, op=ALU.add), reads=[tb], writes=[outb])

        rp_cur = [None]

        def load_w_cast(dst, dst_view_fn, src, rows, cols, kcs):
            fl = []
            for kc in range(kcs):
                c0 = 0
                while c0 < cols:
                    cw = min(2048, cols - c0)
                    fl.append(lambda be, kc=kc, c0=c0, cw=cw: be.dma_start(out=dst_view_fn(kc, c0, cw), in_=src[kc * 128:(kc + 1) * 128, c0:c0 + cw]))
                    c0 += cw
            P.dma("pool", fl, dst, writes=[dst])

        def transpose_to(psb, ps3, srcb, src2, n, dstb, dst3, evac="act"):
            for k in range(n):
                P.op("pe", lambda be, k=k: be.transpose(out=ps3[:, k, :], in_=src2[:, k * 128:(k + 1) * 128], identity=idb[:]), reads=[srcb, idb], writes=[psb])
            if evac == "act":
                P.op("act", lambda be: be.copy(out=dst3, in_=ps3[:, 0:n, :]), reads=[psb], writes=[dstb])
            else:
                P.op("dve", lambda be: be.tensor_copy(out=dst3, in_=ps3[:, 0:n, :]), reads=[psb], writes=[dstb])

        with ExitStack() as st:
            P.stack = st
            win = P.sbuf("win", [128, 8, DIN], BF16, dma=True)
            load_w_cast(win, lambda kc, c0, cw: win[:, kc, c0:c0 + cw], w_in, 1024, DIN, 8)
            gmix = P.sbuf("gmix", [128, D], F32, dma=True)
            P.dma("sp", [lambda be: be.dma_start(out=gmix[:], in_=g_mix.partition_broadcast(128))], gmix, writes=[gmix])
            xr = Ring([P.sbuf("xa%d" % i, [128, D], F32, dma=True) for i in range(3)])
            rpr = Ring([P.sbuf("rp%d" % i, [128, 96], F32, dma=True) for i in range(3)])
            junk = P.sbuf("junk1", [128, D], BF16)
            ss = P.sbuf("ss1", [128, 1], F32); rs = P.sbuf("rs1", [128, 1], F32)
            hb = P.sbuf("hb1", [128, D], BF16); hT = P.sbuf("hT1", [128, 8, 128], BF16)
            c0f = P.sbuf("c0f", [128, 512], F32); c1f = Ring([P.sbuf("c1f%d" % i, [128, 512], F32, dma=True) for i in range(2)])
            c2f = P.sbuf("c2f", [128, 40], F32); c4f = P.sbuf("c4f", [128, 512], F32)
            sq = P.sbuf("sq1", [128, 512], F32); ssq = P.sbuf("ssq1", [128, 8], F32)
            qn = P.sbuf("qn1", [128, 512], F32)
            tb = P.sbuf("tb1", [128, 512], F32)
            qab = P.sbuf("qab", [128, 512], BF16)
            kof = Ring([P.sbuf("kof%d" % i, [128, 128], F32, dma=True) for i in range(2)])
            kbb = P.sbuf("kbb", [128, 128], BF16)
            qib = P.sbuf("qib", [128, 256], BF16)
            kif = Ring([P.sbuf("kif%d" % i, [128, 32], F32, dma=True) for i in range(2)])
            ki4 = P.sbuf("ki4", [128, 128], BF16)
            wif = Ring([P.sbuf("wif%d" % i, [128, 8], F32, dma=True) for i in range(2)])
            ubb = Ring([P.sbuf("ubb%d" % i, [128, 512], BF16, dma=True) for i in range(2)])
            ubf = P.sbuf("ubf", [128, 512], F32, dma=True)
            qmb = P.sbuf("qmb", [128, 512], BF16)
            qaT = Ring([P.sbuf("qaT%d" % i, [128, 4, 128], BF16, dma=True) for i in range(2)])
            qiT = Ring([P.sbuf("qiT%d" % i, [128, 3, 128], BF16, dma=True) for i in range(2)])
            qmT = Ring([P.sbuf("qmT%d" % i, [128, 4, 128], BF16, dma=True) for i in range(2)])
            gsb = Ring([P.sbuf("gsb%d" % i, [128, 6, 512], BF16, dma=True) for i in range(2)])
            ktT = Ring([P.sbuf("ktT%d" % i, [128, 128], BF16, dma=True) for i in range(2)])
            vbT = Ring([P.sbuf("vbT%d" % i, [128, 128], BF16, dma=True) for i in range(2)])
            kiT = Ring([P.sbuf("kiT%d" % i, [128, 128], BF16, dma=True) for i in range(2)])
            pT = P.psum("pT1", [128, 8, 128], BF16)
            pP = Ring([P.psum("pP1_%d" % i, [128, 512], F32) for i in range(4)])
            pR = Ring([P.psum("pR1_%d" % i, [128, 8, 128], BF16) for i in range(2)])

            xbufs = {}
            rbufs = {}

            def a1_load(t):
                xb = xr.next(); rb = rpr.next()
                xbufs[t] = xb; rbufs[t] = rb
                P.dma("sp", [lambda be: be.dma_start(out=xb[:], in_=xsrc(t))], xb, writes=[xb])
                P.dma("sp", [lambda be: be.dma_start(out=rb[:], in_=rope[t])], rb, writes=[rb])

            def a1_tile(t):
                xb = xbufs.pop(t); rb = rbufs.pop(t)
                rp_cur[0] = rb
                samp = t >= NTP
                b = t - NTP
                nrow = 64 if samp else 128
                rms_rows(xb, xb[:], gmix, hb, D, junk, ss, rs)
                transpose_to(pT, pT, hb, hb, 8, hT, hT[:])

                def proj(ps, c0, cw):
                    for kc in range(8):
                        P.op("pe", lambda be, kc=kc: be.matmul(ps[:, 0:cw], lhsT=hT[:, kc, :], rhs=win[:, kc, c0:c0 + cw], start=(kc == 0), stop=(kc == 7)), reads=[hT, win], writes=[ps])

                cos64 = rb[:, 0:32]; sin64 = rb[:, 32:64]; cos32 = rb[:, 64:80]; sin32 = rb[:, 80:96]
                ps = pP.next(); proj(ps, 0, 512)
                P.op("act", lambda be, ps=ps: be.copy(out=c0f[:], in_=ps[:]), reads=[ps], writes=[c0f])
                head_norm(c0f, c0f[:].rearrange("p (h d) -> p h d", h=8), gvec("qa"), qn, qn[:].rearrange("p (h d) -> p h d", h=8), 8, 64, sq, ssq)
                qn4 = qn[:].rearrange("p (k g d) -> p k g d", k=2, g=4)
                qo4 = qab[:].rearrange("p (g k d) -> p k g d", g=4, k=2)
                tb4a = tb[:, 0:256].rearrange("p (k g d) -> p k g d", k=2, g=4)
                tb4b = tb[:, 256:512].rearrange("p (k g d) -> p k g d", k=2, g=4)
                sh4 = [128, 2, 4, 32]
                cs4 = cos64.unsqueeze(1).unsqueeze(1).to_broadcast(sh4)
                sn4 = sin64.unsqueeze(1).unsqueeze(1).to_broadcast(sh4)
                rope_ops(qn, qn4[:, :, :, 0:32], qn4[:, :, :, 32:64], cs4, sn4, qab, qo4[:, :, :, 0:32], qo4[:, :, :, 32:64], tb, tb4a, tb4b)
                qa_t = qaT.next(); pr = pR.next()
                transpose_to(pr, pr, qab, qab, 4, qa_t, qa_t[:])
                P.dma("sp", [lambda be, qa_t=qa_t: be.dma_start(out=s_qa[t], in_=qa_t[:].rearrange("p a b -> p (a b)"))], qa_t, reads=[qa_t])
                ps = pP.next(); proj(ps, 512, 512)
                c1 = c1f.next()
                P.op("act", lambda be, ps=ps, c1=c1: be.copy(out=c1[:], in_=ps[:]), reads=[ps], writes=[c1])
                head_norm(c1, c1[:, 0:128].rearrange("p (h d) -> p h d", h=2), gvec("ka"), qn, qn[:, 0:128].rearrange("p (h d) -> p h d", h=2), 2, 64, sq, ssq)
                ko = kof.next()
                k3 = qn[:, 0:128].rearrange("p (h d) -> p h d", h=2)
                ko3 = ko[:].rearrange("p (h d) -> p h d", h=2)
                sh3 = [128, 2, 32]
                rope_ops(qn, k3[:, :, 0:32], k3[:, :, 32:64], bc(cos64, sh3), bc(sin64, sh3), ko, ko3[:, :, 0:32], ko3[:, :, 32:64], tb,
                         tb[:, 0:64].rearrange("p (h d) -> p h d", h=2), tb[:, 64:128].rearrange("p (h d) -> p h d", h=2))
                if samp:
                    P.dma("sp", [lambda be, ko=ko: be.dma_start(out=nak_s[b], in_=ko[0:64, :]),
                                 lambda be, c1=c1: be.dma_start(out=nav_s[b], in_=c1[0:64, 128:256])], ko, reads=[ko, c1])
                else:
                    P.dma("sp", [lambda be, ko=ko: be.dma_start(out=nak_p[t * 128:(t + 1) * 128, :], in_=ko[:]),
                                 lambda be, c1=c1: be.dma_start(out=nav_p[t * 128:(t + 1) * 128, :], in_=c1[:, 128:256])], ko, reads=[ko, c1])
                P.op("pool", lambda be, ko=ko: be.tensor_copy(out=kbb[:], in_=ko[:]), reads=[ko], writes=[kbb])
                pr = pR.next(); kt_o = ktT.next()
                transpose_to(pr, pr, kbb, kbb, 1, kt_o, kt_o[:].unsqueeze(1))
                P.dma("sp", [lambda be, kt_o=kt_o: be.dma_start(out=s_KT[t], in_=kt_o[:])], kt_o, reads=[kt_o])
                vb_o = vbT.next()
                P.op("pool", lambda be, c1=c1, vb_o=vb_o: be.tensor_copy(out=vb_o[:], in_=c1[:, 128:256]), reads=[c1], writes=[vb_o])
                P.dma("sp", [lambda be, vb_o=vb_o: be.dma_start(out=s_V[t], in_=vb_o[:])], vb_o, reads=[vb_o])
                qi3 = c1[:, 256:512].rearrange("p (h d) -> p h d", h=8)
                qo3 = qib[:].rearrange("p (h d) -> p h d", h=8)
                sh3 = [128, 8, 16]
                rope_ops(c1, qi3[:, :, 0:16], qi3[:, :, 16:32], bc(cos32, sh3), bc(sin32, sh3), qib, qo3[:, :, 0:16], qo3[:, :, 16:32], tb,
                         tb[:, 0:128].rearrange("p (h d) -> p h d", h=8), tb[:, 128:256].rearrange("p (h d) -> p h d", h=8))
                qi_t = qiT.next(); pr = pR.next()
                for k in range(3):
                    nr = 96 if k < 2 else 64
                    P.op("pe", lambda be, k=k, nr=nr, pr=pr: be.transpose(out=pr[0:nr, k, :], in_=qib[:, 96 * k:96 * k + nr], identity=idb[:]), reads=[qib, idb], writes=[pr])
                P.op("act", lambda be, pr=pr, qi_t=qi_t: be.copy(out=qi_t[0:96, 0:2, :], in_=pr[0:96, 0:2, :]), reads=[pr], writes=[qi_t])
                P.op("act", lambda be, pr=pr, qi_t=qi_t: be.copy(out=qi_t[0:64, 2, :], in_=pr[0:64, 2, :]), reads=[pr], writes=[qi_t])
                P.dma("sp", [lambda be, qi_t=qi_t: be.dma_start(out=s_qi[t, 0:96, 0:256], in_=qi_t[0:96, 0:2, :].rearrange("p a b -> p (a b)")),
                             lambda be, qi_t=qi_t: be.dma_start(out=s_qi[t, 0:64, 256:384], in_=qi_t[0:64, 2, :])], qi_t, reads=[qi_t])
                ps = pP.next(); proj(ps, 1024, 40)
                P.op("act", lambda be, ps=ps: be.copy(out=c2f[:], in_=ps[:, 0:40]), reads=[ps], writes=[c2f])
                head_norm(c2f, c2f[:, 0:32].unsqueeze(1), gvec("kidx"), qn, qn[:, 0:32].unsqueeze(1), 1, 32, sq, ssq)
                kio = kif.next()
                rope_ops(qn, qn[:, 0:16], qn[:, 16:32], cos32, sin32, kio, kio[:, 0:16], kio[:, 16:32], tb, tb[:, 0:16], tb[:, 16:32])
                wo = wif.next()
                P.op("dve", lambda be, wo=wo: be.tensor_scalar(out=wo[:], in0=c2f[:, 32:40], scalar1=IDX_SCALE, scalar2=None, op0=ALU.mult), reads=[c2f], writes=[wo])
                P.dma("sp", [lambda be, wo=wo: be.dma_start(out=s_wi[t], in_=wo[:])], wo, reads=[wo])
                if samp:
                    P.dma("sp", [lambda be, kio=kio: be.dma_start(out=nik_s[b], in_=kio[0:64, :])], kio, reads=[kio])
                else:
                    P.dma("sp", [lambda be, kio=kio: be.dma_start(out=nik_p[t * 128:(t + 1) * 128, :], in_=kio[:])], kio, reads=[kio])
                P.op("pool", lambda be, kio=kio: be.tensor_copy(out=ki4[:].rearrange("p (r c) -> p r c", r=4), in_=kio[:].unsqueeze(1).to_broadcast([128, 4, 32])), reads=[kio], writes=[ki4])
                pr = pR.next(); ki_o = kiT.next()
                transpose_to(pr, pr, ki4, ki4, 1, ki_o, ki_o[:].unsqueeze(1))
                P.dma("sp", [lambda be, ki_o=ki_o: be.dma_start(out=s_KI[t], in_=ki_o[:])], ki_o, reads=[ki_o])
                ps = pP.next(); proj(ps, 1064, 512)
                ub_ = ubb.next()
                P.op("act", lambda be, ps=ps, ub_=ub_: be.copy(out=ub_[:], in_=ps[:]), reads=[ps], writes=[ub_])
                P.dma("sp", [lambda be, ub_=ub_: be.dma_start(out=s_ub[t], in_=ub_[:])], ub_, reads=[ub_])
                if samp or t == NTP - 1:
                    P.op("act", lambda be, ps=ps: be.copy(out=ubf[:], in_=ps[:]), reads=[ps], writes=[ubf])
                    if samp:
                        P.dma("sp", [lambda be: be.dma_start(out=npool_s[b], in_=ubf[49:64, :])], ubf, reads=[ubf])
                    else:
                        P.dma("sp", [lambda be: be.dma_start(out=npool_p, in_=ubf[113:128, :])], ubf, reads=[ubf])
                ps = pP.next(); proj(ps, 1576, 512)
                P.op("act", lambda be, ps=ps: be.copy(out=c4f[:], in_=ps[:]), reads=[ps], writes=[c4f])
                head_norm(c4f, c4f[:].rearrange("p (h d) -> p h d", h=4), gvec("qm"), qmb, qmb[:].rearrange("p (h d) -> p h d", h=4), 4, 128, sq, ssq)
                qm_t = qmT.next(); pr = pR.next()
                transpose_to(pr, pr, qmb, qmb, 4, qm_t, qm_t[:])
                P.dma("sp", [lambda be, qm_t=qm_t: be.dma_start(out=s_qm[t], in_=qm_t[:].rearrange("p a b -> p (a b)"))], qm_t, reads=[qm_t])
                gs = gsb.next()
                for j in range(6):
                    ps = pP.next(); proj(ps, 2088 + 512 * j, 512)
                    P.op("act", lambda be, ps=ps, j=j, gs=gs: be.activation(out=gs[:, j, :], in_=ps[:], func=AF.Sigmoid), reads=[ps], writes=[gs])
                P.dma("sp", [lambda be, gs=gs: be.dma_start(out=s_gate[t], in_=gs[:].rearrange("p a b -> p (a b)"))], gs, reads=[gs])

            a1_load(0); a1_load(1)
            for t in range(NT):
                if t + 2 < NT:
                    a1_load(t + 2)
                a1_tile(t)
            P.barrier()
            P.emit_block()
            if KSTOP == 1:
                return nc

        with ExitStack() as kvst:
            P.stack = kvst
            KTp = P.sbuf("KTp", [128, SP], BF16)
            V1p = P.sbuf("V1p", [128, NTP, 2, 65], BF16)
            KIp = P.sbuf("KIp", [128, SP], BF16)
            KTs = [P.sbuf("KTs%d" % b, [128, 1152], BF16) for b in range(2)]
            V1s = [P.sbuf("V1s%d" % b, [128, 9, 2, 65], BF16) for b in range(2)]
            KIs = [P.sbuf("KIs%d" % b, [128, 1152], BF16) for b in range(2)]
            mkT = [P.sbuf("mkT%d" % i, [128, 4, 256], BF16) for i in range(3)]
            mv1 = [P.sbuf("mv1%d" % i, [128, 2, 4, 129], BF16) for i in range(3)]
            ubh = [P.sbuf("ubh%d" % b, [128, 512], BF16, dma=True) for b in range(2)]

            P.op("pool", lambda be: be.memset(V1p[:, :, :, 64:65], 1.0), writes=[V1p])
            for b in range(2):
                P.op("pool", lambda be, b=b: be.memset(V1s[b][:, :, :, 64:65], 1.0), writes=[V1s[b]])
                P.op("pool", lambda be, b=b: be.memset(ubh[b][:], 0.0), writes=[ubh[b]])
            for i in range(3):
                P.op("pool", lambda be, i=i: be.memset(mv1[i][:, :, :, 128:129], 1.0), writes=[mv1[i]])

            with ExitStack() as st:
                P.stack = st
                wm = P.sbuf("wm", [128, 8, 1024], BF16, dma=True)
                load_w_cast(wm, lambda kc, c0, cw: wm[:, kc, c0:c0 + cw], w_mem, 1024, 1024, 8)
                gmem = P.sbuf("gmem", [128, D], F32, dma=True)
                P.dma("sp", [lambda be: be.dma_start(out=gmem[:], in_=g_mem.partition_broadcast(128))], gmem, writes=[gmem])
                xm = Ring([P.sbuf("xm%d" % i, [128, D], F32, dma=True) for i in range(2)])
                st32 = Ring([P.sbuf("st32_%d" % i, [128, 1024], F32, dma=True) for i in range(3)])
                stb = Ring([P.sbuf("stb%d" % i, [128, 1024], BF16) for i in range(2)])
                junk = P.sbuf("junk0", [128, D], BF16)
                ss = P.sbuf("ss0", [128, 1], F32); rs = P.sbuf("rs0", [128, 1], F32)
                hb = P.sbuf("hb0", [128, D], BF16); hT = P.sbuf("hT0", [128, 8, 128], BF16)
                sq = P.sbuf("sq0", [128, 512], F32); ssq = P.sbuf("ssq0", [128, 8], F32)
                kout = Ring([P.sbuf("kout%d" % i, [128, 512], F32, dma=True) for i in range(2)])
                vout = Ring([P.sbuf("vout%d" % i, [128, 512], F32, dma=True) for i in range(2)])
                kb = P.sbuf("kb0", [128, 512], BF16)
                pT = P.psum("pT0", [128, 8, 128], BF16)
                pK = P.psum("pK0", [128, 512], F32); pV = P.psum("pV0", [128, 512], F32)
                pR = Ring([P.psum("pR0_%d" % i, [128, 8, 128], BF16) for i in range(2)])

                for mt in range(2):
                    xb = xm.next()
                    P.dma("sp", [lambda be, xb=xb, mt=mt: be.dma_start(out=xb[:], in_=mem[mt * 128:(mt + 1) * 128, :])], xb, writes=[xb])
                    rms_rows(xb, xb[:], gmem, hb, D, junk, ss, rs)
                    transpose_to(pT, pT, hb, hb, 8, hT, hT[:])
                    for kc in range(8):
                        P.op("pe", lambda be, kc=kc: be.matmul(pK[:], lhsT=hT[:, kc, :], rhs=wm[:, kc, 0:512], start=(kc == 0), stop=(kc == 7)), reads=[hT, wm], writes=[pK])
                    for kc in range(8):
                        P.op("pe", lambda be, kc=kc: be.matmul(pV[:], lhsT=hT[:, kc, :], rhs=wm[:, kc, 512:1024], start=(kc == 0), stop=(kc == 7)), reads=[hT, wm], writes=[pV])
                    kf = st32.next()
                    P.op("act", lambda be, kf=kf: be.copy(out=kf[:, 0:512], in_=pK[:]), reads=[pK], writes=[kf])
                    ko = kout.next()
                    head_norm(kf, kf[:, 0:512].rearrange("p (h d) -> p h d", h=4), gvec("km"), ko, ko[:].rearrange("p (h d) -> p h d", h=4), 4, 128, sq, ssq)
                    P.dma("sp", [lambda be, ko=ko, mt=mt: be.dma_start(out=nmk_p[mt * 128:(mt + 1) * 128, :], in_=ko[:])], ko, reads=[ko])
                    P.op("pool", lambda be, ko=ko: be.tensor_copy(out=kb[:], in_=ko[:]), reads=[ko], writes=[kb])
                    pr = pR.next()
                    transpose_to(pr, pr, kb, kb, 4, mkT[0], mkT[0][:, :, mt * 128:(mt + 1) * 128])
                    vo = vout.next()
                    P.op("act", lambda be, vo=vo: be.copy(out=vo[:], in_=pV[:]), reads=[pV], writes=[vo])
                    P.dma("sp", [lambda be, vo=vo, mt=mt: be.dma_start(out=nmv_p[mt * 128:(mt + 1) * 128, :], in_=vo[:])], vo, reads=[vo])
                    P.op("pool", lambda be, vo=vo, mt=mt: be.tensor_copy(out=mv1[0][:, mt, :, 0:128], in_=vo[:].rearrange("p (h d) -> p h d", h=4)), reads=[vo], writes=[mv1[0]])
                for b in range(2):
                    for mt in range(2):
                        kf = st32.next()
                        P.dma("sp", [lambda be, kf=kf, b=b, mt=mt: be.dma_start(out=kf[:, 0:512], in_=cmk[b, mt * 128:(mt + 1) * 128, :])], kf, writes=[kf])
                        sb_ = stb.next()
                        P.op("dve", lambda be, kf=kf, sb_=sb_: be.tensor_copy(out=sb_[:, 0:512], in_=kf[:, 0:512]), reads=[kf], writes=[sb_])
                        pr = pR.next()
                        transpose_to(pr, pr, sb_, sb_, 4, mkT[1 + b], mkT[1 + b][:, :, mt * 128:(mt + 1) * 128])
                        vf = st32.next()
                        P.dma("sp", [lambda be, vf=vf, b=b, mt=mt: be.dma_start(out=vf[:, 0:512], in_=cmv[b, mt * 128:(mt + 1) * 128, :])], vf, writes=[vf])
                        P.op("pool", lambda be, vf=vf, b=b, mt=mt: be.tensor_copy(out=mv1[1 + b][:, mt, :, 0:128], in_=vf[:, 0:512].rearrange("p (h d) -> p h d", h=4)), reads=[vf], writes=[mv1[1 + b]])
                for b in range(2):
                    ck = st32.next()
                    P.dma("sp", [lambda be, ck=ck, b=b: be.dma_start(out=ck[:].rearrange("p (k c) -> p k c", k=8), in_=cak[b].rearrange("(k p) c -> p k c", p=128))], ck, writes=[ck])
                    cb = stb.next()
                    P.op("dve", lambda be, ck=ck, cb=cb: be.tensor_copy(out=cb[:], in_=ck[:]), reads=[ck], writes=[cb])
                    pr = pR.next()
                    transpose_to(pr, pr, cb, cb, 8, KTs[b], KTs[b][:, 0:1024].rearrange("p (k c) -> p k c", k=8))
                    cv = st32.next()
                    P.dma("sp", [lambda be, cv=cv, b=b: be.dma_start(out=cv[:].rearrange("p (k c) -> p k c", k=8), in_=cav[b].rearrange("(k p) c -> p k c", p=128))], cv, writes=[cv])
                    P.op("pool", lambda be, cv=cv, b=b: be.tensor_copy(out=V1s[b][:, 0:8, :, 0:64], in_=cv[:].rearrange("p (k h d) -> p k h d", k=8, h=2)), reads=[cv], writes=[V1s[b]])
                    ci = st32.next()
                    P.dma("sp", [lambda be, ci=ci, b=b: be.dma_start(out=ci[:, 0:256].rearrange("p (k c) -> p k c", k=8), in_=cik[b].rearrange("(k p) c -> p k c", p=128))], ci, writes=[ci])
                    c4 = stb.next()
                    P.op("dve", lambda be, ci=ci, c4=c4: be.tensor_copy(out=c4[:].rearrange("p (k r c) -> p k r c", k=8, r=4), in_=ci[:, 0:256].rearrange("p (k c) -> p k c", k=8).unsqueeze(2).to_broadcast([128, 8, 4, 32])), reads=[ci], writes=[c4])
                    pr = pR.next()
                    transpose_to(pr, pr, c4, c4, 8, KIs[b], KIs[b][:, 0:1024].rearrange("p (k c) -> p k c", k=8))
                    P.dma("pool", [lambda be, b=b: be.dma_start(out=ubh[b][113:128, :], in_=cpool[b])], ubh[b], writes=[ubh[b]])
                kvl = P.buf("kvl", None, dma=True)
                fl = []
                for q in range((NTP + 15) // 16):
                    t0_ = q * 16; t1_ = min(NTP, t0_ + 16)
                    fl.append(lambda be, t0_=t0_, t1_=t1_: be.dma_start(out=KTp[:, t0_ * 128:t1_ * 128].rearrange("p (t k) -> p t k", k=128), in_=s_KT[t0_:t1_].rearrange("t p k -> p t k")))
                    fl.append(lambda be, t0_=t0_, t1_=t1_: be.dma_start(out=KIp[:, t0_ * 128:t1_ * 128].rearrange("p (t k) -> p t k", k=128), in_=s_KI[t0_:t1_].rearrange("t p k -> p t k")))
                    for hh in range(2):
                        fl.append(lambda be, t0_=t0_, t1_=t1_, hh=hh: be.dma_start(out=V1p[:, t0_:t1_, hh, 0:64], in_=s_V[t0_:t1_, :, hh * 64:(hh + 1) * 64].rearrange("t p d -> p t d")))
                for b in range(2):
                    fl.append(lambda be, b=b: be.dma_start(out=KTs[b][:, 1024:1152], in_=s_KT[NTP + b]))
                    fl.append(lambda be, b=b: be.dma_start(out=KIs[b][:, 1024:1152], in_=s_KI[NTP + b]))
                    fl.append(lambda be, b=b: be.dma_start(out=V1s[b][:, 8, :, 0:64], in_=s_V[NTP + b].rearrange("p (h d) -> p h d", h=2)))
                P.dma("sp", fl, kvl, writes=[KTp, KIp, V1p, KTs[0], KTs[1], KIs[0], KIs[1], V1s[0], V1s[1]])
                P.barrier()
                P.emit_block()
                if KSTOP == 2:
                    return nc

            with ExitStack() as st:
                P.stack = st
                wpl = P.sbuf("wpl", [128, 4, 128], BF16, dma=True)
                P.dma("pool", [lambda be: be.dma_start(out=wpl[:], in_=w_pool.rearrange("g c e -> c g e"))], wpl, writes=[wpl])
                bnd = P.sbuf("bnd", [128, 3, 4, 128], BF16, dma=True)
                P.dma("pool", [lambda be: be.dma_start(out=bnd[:].rearrange("p a g t -> p (a g t)"), in_=bands.rearrange("p a g t -> p (a g t)"))], bnd, writes=[bnd])
                sc = P.sbuf("sc", [128, SCW], F32)
                Mq = P.sbuf("Mq", [128, SCW], BF16)
                MT = P.sbuf("MT", [128, SCW // 128, 128], BF16)
                rl = Ring([P.sbuf("rl%d" % i, [128, 512], F32) for i in range(3)])
                er = Ring([P.sbuf("er%d" % i, [128, 4, 128], BF16) for i in range(3)])
                pr_ = Ring([P.sbuf("pp%d" % i, [128, 4, 128], BF16) for i in range(3)])
                ld = {}
                qaL = Ring([P.sbuf("qaL%d" % i, [128, 4, 128], BF16, dma=True) for i in range(3)])
                qiL = Ring([P.sbuf("qiL%d" % i, [128, 3, 128], BF16, dma=True) for i in range(3)])
                qmL = Ring([P.sbuf("qmL%d" % i, [128, 4, 128], BF16, dma=True) for i in range(3)])
                wiL = Ring([P.sbuf("wiL%d" % i, [128, 8], F32, dma=True) for i in range(3)])
                ubL = Ring([P.sbuf("ubL%d" % i, [128, 512], BF16, dma=True) for i in range(4)])
                lo = P.sbuf("lo", [128, 1], F32); w0 = P.sbuf("w0", [128, 1], F32); mx = P.sbuf("mx", [128, 1], F32)
                mid = P.sbuf("mid", [128, 1], F32); cnt = P.sbuf("cnt", [128, 1], F32); tt_ = P.sbuf("tt", [128, 1], F32)
                thr = Ring([P.sbuf("thr%d" % i, [128, 1], F32) for i in range(2)])
                rec = P.sbuf("rec", [128, 8], F32)
                a_sb = P.sbuf("a_sb", [128, 512], BF16); m_sb = P.sbuf("m_sb", [128, 512], BF16)
                em = [P.sbuf("em%d" % i, [128, 4, 128], BF16) for i in range(2)]
                pTs = P.sbuf("pTs", [128, 4, 128], BF16)
                br = Ring([P.sbuf("br%d" % i, [128, 3, 4, 128], BF16, dma=True) for i in range(2)])
                pI = Ring([P.psum("pI%d" % i, [128, 512], F32) for i in range(2)])
                pM = Ring([P.psum("pM%d" % i, [128, 8, 128], BF16) for i in range(2)])
                pS = Ring([P.psum("pS%d" % i, [128, 4, 128], F32) for i in range(2)])
                pA = [P.psum("pA%d" % i, [128, 512], F32) for i in range(2)]

                seqs = []
                for t in range(NTP):
                    seqs.append(dict(t=t, KT=KTp, V1=V1p, KI=KIp, nkt=t + 1, samp=False, mi=0))
                for b in range(2):
                    seqs.append(dict(t=NTP + b, KT=KTs[b], V1=V1s[b], KI=KIs[b], nkt=9, samp=True, mi=1 + b, b=b))

                def a2_load(i):
                    s = seqs[i]; t = s["t"]
                    d = dict(qa=qaL.next(), qi=qiL.next(), qm=qmL.next(), wi=wiL.next(), ub=ubL.next())
                    ld[i] = d
                    P.dma("sp", [lambda be: be.dma_start(out=d["qa"][:].rearrange("p a b -> p (a b)"), in_=s_qa[t])], d["qa"], writes=[d["qa"]])
                    P.dma("sp", [lambda be: be.dma_start(out=d["qi"][0:96, 0:2, :].rearrange("p a b -> p (a b)"), in_=s_qi[t, 0:96, 0:256]),
                             lambda be: be.dma_start(out=d["qi"][0:64, 2, :], in_=s_qi[t, 0:64, 256:384])], d["qi"], writes=[d["qi"]])
                    P.dma("sp", [lambda be: be.dma_start(out=d["qm"][:].rearrange("p a b -> p (a b)"), in_=s_qm[t])], d["qm"], writes=[d["qm"]])
                    P.dma("sp", [lambda be: be.dma_start(out=d["wi"][:], in_=s_wi[t])], d["wi"], writes=[d["wi"]])
                    P.dma("sp", [lambda be: be.dma_start(out=d["ub"][:], in_=s_ub[t])], d["ub"], writes=[d["ub"]])

                def stageA(i):
                    s = seqs[i]; d = ld[i]; S = s["nkt"] * 128
                    KI = s["KI"]; qi = d["qi"]; wi = d["wi"]
                    for c in range((S + 511) // 512):
                        c0 = c * 512; cw = min(512, S - c0)
                        for h in range(8):
                            r0 = (h % 3) * 32
                            ps = pI.next()
                            P.op("pe", lambda be, ps=ps, r0=r0, h=h, c0=c0, cw=cw: be.matmul(ps[:, 0:cw], lhsT=qi[r0:r0 + 32, h // 3, :], rhs=KI[r0:r0 + 32, c0:c0 + cw], start=True, stop=True), reads=[qi, KI], writes=[ps])
                            r = rl.next()
                            P.op("act", lambda be, ps=ps, r=r, cw=cw: be.activation(out=r[:, 0:cw], in_=ps[:, 0:cw], func=AF.Relu), reads=[ps], writes=[r])
                            if h == 0:
                                P.op("dve", lambda be, r=r, c0=c0, cw=cw: be.tensor_scalar(out=sc[:, c0:c0 + cw], in0=r[:, 0:cw], scalar1=wi[:, 0:1], scalar2=None, op0=ALU.mult), reads=[r, wi], writes=[sc])
                            else:
                                P.op("dve", lambda be, r=r, h=h, c0=c0, cw=cw: be.scalar_tensor_tensor(out=sc[:, c0:c0 + cw], in0=r[:, 0:cw], scalar=wi[:, h:h + 1], in1=sc[:, c0:c0 + cw], op0=ALU.mult, op1=ALU.add), reads=[r, wi, sc], writes=[sc])
                    if s["samp"]:
                        P.op("dve", lambda be: be.memset(sc[:, S - 64:S], NEG), writes=[sc])
                    else:
                        P.op("dve", lambda be: be.memset(sc[0:64, S - 64:S], NEG), writes=[sc])

                def stageC1(i):
                    s = seqs[i]; S = s["nkt"] * 128
                    th = thr.next()
                    s["thr"] = th
                    if S <= 256:
                        P.op("dve", lambda be: be.tensor_scalar(out=Mq[:, 0:S], in0=sc[:, 0:S], scalar1=-1.0e29, scalar2=None, op0=ALU.is_ge), reads=[sc], writes=[Mq])
                        return
                    P.op("dve", lambda be: be.tensor_reduce(out=lo[:], in_=sc[:, 0:S - 64], axis=AX.X, op=ALU.min), reads=[sc], writes=[lo])
                    P.op("dve", lambda be: be.tensor_reduce(out=mx[:], in_=sc[:, 0:S], axis=AX.X, op=ALU.max), reads=[sc], writes=[mx])
                    P.op("dve", lambda be: be.tensor_tensor(out=w0[:], in0=mx[:], in1=lo[:], op=ALU.subtract), reads=[mx, lo], writes=[w0])
                    P.op("dve", lambda be: be.tensor_scalar(out=w0[:], in0=w0[:], scalar1=1.0 + 2.0 ** -10, scalar2=1e-12, op0=ALU.mult, op1=ALU.add), reads=[w0], writes=[w0])
                    for k in range(1, NBIS + 1):
                        f = 2.0 ** -k
                        P.op("dve", lambda be, f=f: be.scalar_tensor_tensor(out=mid[:], in0=w0[:], scalar=f, in1=lo[:], op0=ALU.mult, op1=ALU.add), reads=[w0, lo], writes=[mid])
                        P.op("dve", lambda be: be.tensor_scalar(out=Mq[:, 0:S], in0=sc[:, 0:S], scalar1=mid[:, 0:1], scalar2=None, op0=ALU.is_ge, op1=ALU.add, accum_out=cnt[:]), reads=[sc, mid], writes=[Mq, cnt])
                        P.op("dve", lambda be, f=f: be.tensor_scalar(out=tt_[:], in0=cnt[:], scalar1=255.5, scalar2=f, op0=ALU.is_ge, op1=ALU.mult), reads=[cnt], writes=[tt_])
                        P.op("dve", lambda be: be.scalar_tensor_tensor(out=lo[:], in0=tt_[:], scalar=w0[:, 0:1], in1=lo[:], op0=ALU.mult, op1=ALU.add), reads=[tt_, w0, lo], writes=[lo])
                    P.op("dve", lambda be: be.tensor_scalar(out=Mq[:, 0:S], in0=sc[:, 0:S], scalar1=lo[:, 0:1], scalar2=None, op0=ALU.is_ge), reads=[sc, lo], writes=[Mq])

                def stageC2(i):
                    s = seqs[i]; nkt = s["nkt"]
                    for j in range((nkt + 7) // 8):
                        n = min(8, nkt - 8 * j)
                        pm = pM.next()
                        transpose_to(pm, pm, Mq, Mq[:, j * 1024:j * 1024 + n * 128], n, MT, MT[:, 8 * j:8 * j + n, :])

                def stageB(i):
                    s = seqs[i]; d = ld.pop(i); nkt = s["nkt"]; t = s["t"]
                    KT = s["KT"]; V1 = s["V1"]; qa = d["qa"]
                    nch = (nkt + 3) // 4
                    for h in range(8):
                        kv = h // 4; g = h % 4; r0 = kv * 64
                        acc = pA[kv][:, 0:260].rearrange("p (h d) -> p h d", h=4)
                        for c in range(nch):
                            n = min(4, nkt - 4 * c)
                            ps = pS.next()
                            for k in range(n):
                                kt = 4 * c + k
                                P.op("pe", lambda be, ps=ps, k=k, kt=kt, r0=r0, g=g: be.matmul(ps[:, k, :], lhsT=KT[r0:r0 + 64, kt * 128:(kt + 1) * 128], rhs=qa[r0:r0 + 64, g, :], start=True, stop=True), reads=[KT, qa], writes=[ps])
                            e = er.next()
                            P.op("act", lambda be, ps=ps, e=e, n=n: be.activation(out=e[:, 0:n, :], in_=ps[:, 0:n, :], func=AF.Exp, scale=0.125), reads=[ps], writes=[e])
                            p_ = pr_.next()
                            P.op("pool", lambda be, e=e, p_=p_, n=n, c=c: be.tensor_tensor(out=p_[:, 0:n, :], in0=e[:, 0:n, :], in1=MT[:, 4 * c:4 * c + n, :], op=ALU.mult), reads=[e, MT], writes=[p_])
                            for k in range(n):
                                kt = 4 * c + k
                                P.op("pe", lambda be, p_=p_, k=k, kt=kt, acc=acc, g=g, kv=kv: be.matmul(acc[:, g, :], lhsT=p_[:, k, :], rhs=V1[:, kt, kv, :], start=(kt == 0), stop=(kt == nkt - 1)), reads=[p_, V1], writes=[pA[kv]])
                    mi = s["mi"]; qm = d["qm"]
                    for mt in range(2):
                        ps = pS.next()
                        for h in range(4):
                            P.op("pe", lambda be, ps=ps, h=h, mt=mt: be.matmul(ps[:, h, :], lhsT=mkT[mi][:, h, mt * 128:(mt + 1) * 128], rhs=qm[:, h, :], start=True, stop=True), reads=[mkT[mi], qm], writes=[ps])
                        P.op("act", lambda be, ps=ps, mt=mt: be.activation(out=em[mt][:], in_=ps[:], func=AF.Exp, scale=128.0 ** -0.5), reads=[ps], writes=[em[mt]])
                    for kv in range(2):
                        acc = pA[kv][:, 0:260].rearrange("p (h d) -> p h d", h=4)
                        P.op("dve", lambda be, acc=acc, kv=kv: be.reciprocal(out=rec[:, 4 * kv:4 * kv + 4], in_=acc[:, :, 64]), reads=[pA[kv]], writes=[rec])
                        P.op("dve", lambda be, acc=acc, kv=kv: be.tensor_tensor(out=a_sb[:, 256 * kv:256 * kv + 256].rearrange("p (h d) -> p h d", h=4), in0=acc[:, :, 0:64], in1=bl(rec[:, 4 * kv:4 * kv + 4], [128, 4, 64]), op=ALU.mult), reads=[pA[kv], rec], writes=[a_sb])
                    bo = br.next()
                    pm = pM.next()
                    transpose_to(pm, pm, a_sb, a_sb, 4, bo, bo[:, 0, :, :])
                    for h in range(4):
                        accm = pA[h // 2][:, 0:258].rearrange("p (h d) -> p h d", h=2)
                        for mt in range(2):
                            P.op("pe", lambda be, accm=accm, h=h, mt=mt: be.matmul(accm[:, h % 2, :], lhsT=em[mt][:, h, :], rhs=mv1[mi][:, mt, h, :], start=(mt == 0), stop=(mt == 1)), reads=[em[mt], mv1[mi]], writes=[pA[h // 2]])
                    for hh in range(2):
                        accm = pA[hh][:, 0:258].rearrange("p (h d) -> p h d", h=2)
                        P.op("dve", lambda be, accm=accm, hh=hh: be.reciprocal(out=rec[:, 2 * hh:2 * hh + 2], in_=accm[:, :, 128]), reads=[pA[hh]], writes=[rec])
                        P.op("dve", lambda be, accm=accm, hh=hh: be.tensor_tensor(out=m_sb[:, 256 * hh:256 * hh + 256].rearrange("p (h d) -> p h d", h=2), in0=accm[:, :, 0:128], in1=bl(rec[:, 2 * hh:2 * hh + 2], [128, 2, 128]), op=ALU.mult), reads=[pA[hh], rec], writes=[m_sb])
                    pm = pM.next()
                    transpose_to(pm, pm, m_sb, m_sb, 4, bo, bo[:, 2, :, :])
                    ub = d["ub"]
                    if s["samp"]:
                        prev = ubh[s["b"]]; ai = 0
                    elif t == 0:
                        prev = None; ai = 2
                    else:
                        prev = s_prev_ub[0]; ai = 0
                    ps = pS.next()
                    for g in range(4):
                        P.op("pe", lambda be, ps=ps, g=g: be.matmul(ps[:, g, :], lhsT=ub[:, g * 128:(g + 1) * 128], rhs=bnd[:, ai, g, :], start=True, stop=(prev is None)), reads=[ub, bnd], writes=[ps])
                        if prev is not None:
                            P.op("pe", lambda be, ps=ps, g=g: be.matmul(ps[:, g, :], lhsT=prev[:, g * 128:(g + 1) * 128], rhs=bnd[:, 1, g, :], start=False, stop=True), reads=[prev, bnd], writes=[ps])
                    P.op("act", lambda be, ps=ps: be.copy(out=pTs[:], in_=ps[:]), reads=[ps], writes=[pTs])
                    ps2 = pS.next()
                    for g in range(4):
                        P.op("pe", lambda be, ps2=ps2, g=g: be.matmul(ps2[:, g, :], lhsT=wpl[:, g, :], rhs=pTs[:, g, :], start=True, stop=True), reads=[wpl, pTs], writes=[ps2])
                    P.op("dve", lambda be, ps2=ps2: be.tensor_tensor(out=bo[:, 1, :, :], in0=ps2[:], in1=bl(spl[:], [128, 4, 128]), op=ALU.mult), reads=[ps2, spl], writes=[bo])
                    P.dma("sp", [lambda be: be.dma_start(out=s_br[t], in_=bo[:].rearrange("p a b c -> p (a b c)"))], bo, reads=[bo])
                    s_prev_ub[0] = ub

                s_prev_ub = [None]
                nseq = len(seqs)
                a2_load(0); a2_load(1)
                stageA(0); stageC1(0); stageC2(0)
                for i in range(nseq):
                    if i + 2 < nseq:
                        a2_load(i + 2)
                    if i + 1 < nseq:
                        stageA(i + 1); stageC1(i + 1)
                    stageB(i)
                    if i + 1 < nseq:
                        stageC2(i + 1)
                P.barrier()
                P.emit_block()
                if KSTOP == 3:
                    return nc

        with ExitStack() as st:
            P.stack = st
            wo = P.sbuf("wo", [128, 3, 4, D], BF16, dma=True)
            fl = []
            for b in range(3):
                for kc in range(4):
                    fl.append(lambda be, b=b, kc=kc: be.dma_start(out=wo[:, b, kc, :], in_=w_o[b][kc * 128:(kc + 1) * 128, :]))
            P.dma("pool", fl, wo, writes=[wo])
            wout = P.sbuf("wout", [128, 8, D], BF16, dma=True)
            load_w_cast(wout, lambda kc, c0, cw: wout[:, kc, c0:c0 + cw], w_out, 1024, D, 8)
            brL = Ring([P.sbuf("brL%d" % i, [128, 3, 4, 128], BF16, dma=True) for i in range(3)])
            gtL = Ring([P.sbuf("gtL%d" % i, [128, 3, D], BF16, dma=True) for i in range(3)])
            xL = Ring([P.sbuf("xL%d" % i, [128, D], F32, dma=True) for i in range(3)])
            mixed = P.sbuf("mixed", [128, D], F32); tmpm = P.sbuf("tmpm", [128, D], F32)
            mxb = P.sbuf("mxb", [128, D], BF16); mxT = P.sbuf("mxT", [128, 8, 128], BF16)
            x1o = Ring([P.sbuf("x1o%d" % i, [128, D], F32, dma=True) for i in range(2)])
            pt = Ring([P.psum("pt3_%d" % i, [128, D], F32) for i in range(2)])
            pT = P.psum("pT3", [128, 8, 128], BF16)
            pO = P.psum("pO3", [128, D], F32)
            l3 = {}

            def a3_load(t):
                d = dict(br=brL.next(), gt=gtL.next(), x=xL.next())
                l3[t] = d
                P.dma("sp", [lambda be: be.dma_start(out=d["br"][:].rearrange("p a b c -> p (a b c)"), in_=s_br[t])], d["br"], writes=[d["br"]])
                P.dma("sp", [lambda be: be.dma_start(out=d["gt"][:].rearrange("p a b -> p (a b)"), in_=s_gate[t])], d["gt"], writes=[d["gt"]])
                P.dma("sp", [lambda be: be.dma_start(out=d["x"][:], in_=xsrc(t))], d["x"], writes=[d["x"]])

            def a3_tile(t):
                d = l3.pop(t)
                brt = d["br"]; gt = d["gt"]; xb = d["x"]
                for b in range(3):
                    ps = pt.next()
                    for half in range(2):
                        for kc in range(4):
                            P.op("pe", lambda be, ps=ps, b=b, half=half, kc=kc: be.matmul(ps[:, half * 512:(half + 1) * 512], lhsT=brt[:, b, kc, :], rhs=wo[:, b, kc, half * 512:(half + 1) * 512], start=(kc == 0), stop=(kc == 3)), reads=[brt, wo], writes=[ps])
                    if b == 0:
                        P.op("dve", lambda be, ps=ps: be.tensor_tensor(out=mixed[:], in0=ps[:], in1=gt[:, 0, :], op=ALU.mult), reads=[ps, gt], writes=[mixed])
                    else:
                        P.op("dve", lambda be, ps=ps, b=b: be.tensor_tensor(out=tmpm[:], in0=ps[:], in1=gt[:, b, :], op=ALU.mult), reads=[ps, gt], writes=[tmpm])
                        if b == 1:
                            P.op("pool", lambda be: be.tensor_tensor(out=mixed[:], in0=mixed[:], in1=tmpm[:], op=ALU.add), reads=[mixed, tmpm], writes=[mixed])
                        else:
                            P.op("pool", lambda be: be.tensor_tensor(out=mxb[:], in0=mixed[:], in1=tmpm[:], op=ALU.add), reads=[mixed, tmpm], writes=[mxb])
                transpose_to(pT, pT, mxb, mxb, 8, mxT, mxT[:])
                for half in range(2):
                    for kc in range(8):
                        P.op("pe", lambda be, half=half, kc=kc: be.matmul(pO[:, half * 512:(half + 1) * 512], lhsT=mxT[:, kc, :], rhs=wout[:, kc, half * 512:(half + 1) * 512], start=(kc == 0), stop=(kc == 7)), reads=[mxT, wout], writes=[pO])
                xo = x1o.next()
                P.op("dve", lambda be, xo=xo: be.tensor_tensor(out=xo[:], in0=pO[:], in1=xb[:], op=ALU.add), reads=[pO, xb], writes=[xo])
                P.dma("sp", [lambda be, xo=xo: be.dma_start(out=s_x1[t], in_=xo[:])], xo, reads=[xo])

            a3_load(0); a3_load(1)
            for t in range(NT):
                if t + 2 < NT:
                    a3_load(t + 2)
                a3_tile(t)
            P.barrier()
            P.emit_block()
            if KSTOP == 4:
                return nc

        with ExitStack() as st:
            P.stack = st
            wg = P.sbuf("wg", [128, 8, DFF], BF16, dma=True)
            load_w_cast(wg, lambda kc, c0, cw: wg[:, kc, c0:c0 + cw], w_gate, 1024, DFF, 8)
            wu = P.sbuf("wu", [128, 8, DFF], BF16, dma=True)
            load_w_cast(wu, lambda kc, c0, cw: wu[:, kc, c0:c0 + cw], w_up, 1024, DFF, 8)
            wd = P.sbuf("wd", [128, 22, D], BF16, dma=True)
            load_w_cast(wd, lambda kc, c0, cw: wd[:, kc, c0:c0 + cw], w_down, DFF, D, 22)
            gffn = P.sbuf("gffn", [128, D], F32, dma=True)
            P.dma("sp", [lambda be: be.dma_start(out=gffn[:], in_=g_ffn.partition_broadcast(128))], gffn, writes=[gffn])
            x1L = Ring([P.sbuf("x1L%d" % i, [128, D], F32, dma=True) for i in range(3)])
            junk = P.sbuf("junkb", [128, D], BF16)
            ss = P.sbuf("ssb", [128, 1], F32); rs = P.sbuf("rsb", [128, 1], F32)
            hb = P.sbuf("hbb", [128, D], BF16); hT = P.sbuf("hTb", [128, 8, 128], BF16)
            sg = Ring([P.sbuf("sg%d" % i, [128, 4, 128], F32) for i in range(2)])
            gT = P.sbuf("gT", [128, 22, 128], BF16)
            yo = Ring([P.sbuf("yo%d" % i, [128, D], F32, dma=True) for i in range(2)])
            pT = P.psum("pTb", [128, 8, 128], BF16)
            pG = Ring([P.psum("pG%d" % i, [128, 4, 128], F32) for i in range(2)])
            pU = Ring([P.psum("pU%d" % i, [128, 4, 128], F32) for i in range(2)])
            pO = P.psum("pOb", [128, D], F32)
            lb = {}

            def b_load(t):
                xb = x1L.next()
                lb[t] = xb
                P.dma("sp", [lambda be: be.dma_start(out=xb[:], in_=s_x1[t])], xb, writes=[xb])

            def b_tile(t):
                xb = lb.pop(t)
                rms_rows(xb, xb[:], gffn, hb, D, junk, ss, rs)
                transpose_to(pT, pT, hb, hb, 8, hT, hT[:])
                for fg in range(6):
                    nf = min(4, 22 - 4 * fg)
                    pg = pG.next(); pu = pU.next()
                    for j in range(nf):
                        fc = 4 * fg + j
                        for kc in range(8):
                            P.op("pe", lambda be, pg=pg, j=j, fc=fc, kc=kc: be.matmul(pg[:, j, :], lhsT=wg[:, kc, fc * 128:(fc + 1) * 128], rhs=hT[:, kc, :], start=(kc == 0), stop=(kc == 7)), reads=[wg, hT], writes=[pg])
                    for j in range(nf):
                        fc = 4 * fg + j
                        for kc in range(8):
                            P.op("pe", lambda be, pu=pu, j=j, fc=fc, kc=kc: be.matmul(pu[:, j, :], lhsT=wu[:, kc, fc * 128:(fc + 1) * 128], rhs=hT[:, kc, :], start=(kc == 0), stop=(kc == 7)), reads=[wu, hT], writes=[pu])
                    s_ = sg.next()
                    P.op("act", lambda be, pg=pg, s_=s_, nf=nf: be.activation(out=s_[:, 0:nf, :], in_=pg[:, 0:nf, :], func=AF.Silu), reads=[pg], writes=[s_])
                    P.op("dve", lambda be, pu=pu, s_=s_, nf=nf, fg=fg: be.tensor_tensor(out=gT[:, 4 * fg:4 * fg + nf, :], in0=pu[:, 0:nf, :], in1=s_[:, 0:nf, :], op=ALU.mult), reads=[pu, s_], writes=[gT])
                for half in range(2):
                    for fc in range(22):
                        P.op("pe", lambda be, half=half, fc=fc: be.matmul(pO[:, half * 512:(half + 1) * 512], lhsT=gT[:, fc, :], rhs=wd[:, fc, half * 512:(half + 1) * 512], start=(fc == 0), stop=(fc == 21)), reads=[gT, wd], writes=[pO])
                yb = yo.next()
                P.op("dve", lambda be, yb=yb: be.tensor_tensor(out=yb[:], in0=pO[:], in1=xb[:], op=ALU.add), reads=[pO, xb], writes=[yb])
                if t < NTP:
                    P.dma("sp", [lambda be, yb=yb: be.dma_start(out=y_p[t * 128:(t + 1) * 128, :], in_=yb[:])], yb, reads=[yb])
                else:
                    P.dma("sp", [lambda be, yb=yb: be.dma_start(out=y_s[t - NTP], in_=yb[0:64, :])], yb, reads=[yb])

            b_load(0); b_load(1)
            for t in range(NT):
                if t + 2 < NT:
                    b_load(t + 2)
                b_tile(t)
            P.barrier()
            P.emit_block()
            if KSTOP == 5:
                return nc
    return nc


def _consts():
    NT = NTP + 2
    theta = np.float32(10000.0)
    tab = np.zeros((NT, 128, 96), np.float32)
    inv64 = (theta ** (-np.arange(32, dtype=np.float32) / np.float32(32))).astype(np.float32)
    inv32 = (theta ** (-np.arange(16, dtype=np.float32) / np.float32(16))).astype(np.float32)
    for t in range(NT):
        pos = (np.arange(128) + (t * 128 if t < NTP else 1024)).astype(np.float32)
        a64 = (pos[:, None] * inv64[None, :]).astype(np.float32)
        a32 = (pos[:, None] * inv32[None, :]).astype(np.float32)
        tab[t, :, 0:32] = np.cos(a64.astype(np.float64)); tab[t, :, 32:64] = np.sin(a64.astype(np.float64))
        tab[t, :, 64:80] = np.cos(a32.astype(np.float64)); tab[t, :, 80:96] = np.sin(a32.astype(np.float64))
    bands = np.zeros((128, 3, 4, 128), np.float32)
    tp = np.arange(128)[:, None]; tq = np.arange(128)[None, :]
    for g, w in enumerate((2, 4, 8, 16)):
        inwin = (tp <= tq) & (tp > tq - w)
        bands[:, 0, g, :] = inwin / w - (tp == tq)
        bands[:, 1, g, :] = ((tp - 128) > (tq - w)) / w
        cntf = np.minimum(w, tq + 1).astype(np.float64)
        bands[:, 2, g, :] = inwin / cntf - (tp == tq)
    return tab, bands.astype(np.float32), np.eye(128, dtype=np.float32)


_CACHE = {}


def kernel(x_prompt, x_sample, mem_prompt, cache_a_k, cache_a_v, cache_idx_k, cache_pool, cache_mem_k,
           cache_mem_v, g_mix, w_in, g_qa, g_ka, g_kidx, g_qm, g_mem, w_mem_kv, g_km, w_pool, s_pool,
           w_oa, w_ob, w_om, w_out, g_ffn, w_gate, w_up, w_down):
    f = lambda a: np.ascontiguousarray(np.asarray(a, dtype=np.float32))
    if "nc" not in _CACHE:
        _CACHE["nc"] = build_program()
        _CACHE["consts"] = _consts()
    nc = _CACHE["nc"]
    tab, bands, ident = _CACHE["consts"]
    xs_pad = np.zeros((16, 128, D), np.float32)
    xs_pad[:, 0:64, :] = f(x_sample)
    shared = {
        "w_in": f(w_in[0]), "w_mem": f(w_mem_kv[0]), "w_pool": f(w_pool[0]), "w_oa": f(w_oa[0]), "w_ob": f(w_ob[0]),
        "w_om": f(w_om[0]), "w_out": f(w_out[0]), "w_gate": f(w_gate[0]), "w_up": f(w_up[0]), "w_down": f(w_down[0]),
        "g_mix": f(g_mix), "g_ffn": f(g_ffn), "g_mem": f(g_mem), "g_qa": f(g_qa), "g_ka": f(g_ka), "g_kidx": f(g_kidx),
        "g_qm": f(g_qm), "g_km": f(g_km), "s_pool": f(np.asarray(s_pool[0]).reshape(4, 128).T),
        "rope": tab, "bands": bands, "ident": ident,
    }
    in_maps = []
    for c in range(8):
        m = dict(shared)
        m["x_p"] = f(x_prompt[c]); m["x_s"] = np.ascontiguousarray(xs_pad[2 * c:2 * c + 2]); m["mem"] = f(mem_prompt[c])
        m["cak"] = f(np.asarray(cache_a_k[0, 2 * c:2 * c + 2]).reshape(2, 1024, 128))
        m["cav"] = f(np.asarray(cache_a_v[0, 2 * c:2 * c + 2]).reshape(2, 1024, 128))
        m["cik"] = f(cache_idx_k[0, 2 * c:2 * c + 2]); m["cpool"] = f(cache_pool[0, 2 * c:2 * c + 2])
        m["cmk"] = f(np.asarray(cache_mem_k[0, 2 * c:2 * c + 2]).reshape(2, 256, 512))
        m["cmv"] = f(np.asarray(cache_mem_v[0, 2 * c:2 * c + 2]).reshape(2, 256, 512))
        in_maps.append(m)
    res = run_bass_kernel_spmd(nc, in_maps, core_ids=list(range(8)))
    R = res.results
    cat = lambda k: np.stack([np.asarray(r[k], dtype=np.float32) for r in R], 0)
    cat2 = lambda k: np.concatenate([np.asarray(r[k], dtype=np.float32) for r in R], 0)
    y_prompt = cat("y_p")
    y_sample = cat2("y_s")
    return (
        y_prompt, y_sample,
        cat("nak_p").reshape(1, 8, 8192, 2, 64), cat("nav_p").reshape(1, 8, 8192, 2, 64), cat("nik_p").reshape(1, 8, 8192, 32),
        cat("npool_p").reshape(1, 8, 15, 512), cat("nmk_p").reshape(1, 8, 256, 4, 128), cat("nmv_p").reshape(1, 8, 256, 4, 128),
        cat2("nak_s").reshape(1, 16, 64, 2, 64), cat2("nav_s").reshape(1, 16, 64, 2, 64), cat2("nik_s").reshape(1, 16, 64, 32),
        cat2("npool_s").reshape(1, 16, 15, 512),
    )
```

```python
from contextlib import ExitStack
import numpy as np
import concourse.bass as bass
import concourse.mybir as mybir
from concourse.bass_utils import run_bass_kernel_spmd

F32 = mybir.dt.float32
BF16 = mybir.dt.bfloat16
AF = mybir.ActivationFunctionType
ALU = mybir.AluOpType
AX = mybir.AxisListType


class Ctr:
    def __init__(self, name, sem):
        self.name = name
        self.sem = sem
        self.count = 0


class Buf:
    def __init__(self, name, ap, ctr=None):
        self.name = name
        self.ap = ap
        self.ctr = ctr
        self.w = None
        self.r = {}

    def __getitem__(self, k):
        return self.ap[k]


class Eng:
    def __init__(self, name, be, ctr):
        self.name = name
        self.be = be
        self.ctr = ctr
        self.ops = []
        self.seen = {}


class Prog:
    def __init__(self, nc, stack):
        self.nc = nc
        self.gstack = stack
        self.stack = stack
        self.engs = {}
        self.nsem = 0
        for nm, be in (("pe", nc.tensor), ("act", nc.scalar), ("dve", nc.vector),
                       ("pool", nc.gpsimd), ("sp", nc.sync)):
            self.engs[nm] = Eng(nm, be, self.new_ctr("e_" + nm))
        self.dma_ctrs = []

    def new_ctr(self, name):
        sem = self.gstack.enter_context(self.nc.semaphore(name))
        self.nsem += 1
        return Ctr(name, sem)

    def sbuf(self, name, shape, dtype, dma=False):
        t = self.stack.enter_context(self.nc.sbuf_tensor(name, list(shape), dtype))
        return self.buf(name, t, dma)

    def psum(self, name, shape, dtype):
        t = self.stack.enter_context(self.nc.psum_tensor(name, list(shape), dtype))
        return Buf(name, t)

    def buf(self, name, ap, dma=False):
        c = None
        if dma:
            c = self.new_ctr("d_" + name)
            self.dma_ctrs.append(c)
        return Buf(name, ap, c)

    def _deps(self, eng, reads, writes, skip_same_pe=False):
        deps = {}

        def add(cv):
            if cv is None:
                return
            c, v = cv
            if skip_same_pe and c is eng.ctr:
                return
            if deps.get(c, 0) < v:
                deps[c] = v

        for b in reads:
            add(b.w)
        for b in writes:
            add(b.w)
            for c, v in b.r.items():
                add((c, v))
        waits = []
        for c, v in deps.items():
            if eng.seen.get(c, 0) < v:
                eng.seen[c] = v
                waits.append((c.sem, v))
        return waits

    def _cut(self):
        import os
        self.nrec = getattr(self, "nrec", 0) + 1
        return self.nrec > int(os.environ.get("OPCUT", "1000000000"))

    def op(self, engname, fn, reads=(), writes=()):
        if self._cut():
            return 0
        eng = self.engs[engname]
        waits = self._deps(eng, reads, writes, skip_same_pe=(engname == "pe"))
        eng.ctr.count += 1
        val = eng.ctr.count
        sem = eng.ctr.sem

        def emit(be, waits=waits, fn=fn, sem=sem):
            for s, v in waits:
                be.wait_ge(s, v)
            fn(be).then_inc(sem, 1)

        eng.ops.append(emit)
        for b in writes:
            b.w = (eng.ctr, val)
            b.r = {}
        for b in reads:
            if b not in writes:
                b.r[eng.ctr] = val
        return val

    def dma(self, engname, fns, ctrbuf, reads=(), writes=()):
        if self._cut():
            return 0
        eng = self.engs[engname]
        ctr = ctrbuf.ctr
        assert ctr is not None, ctrbuf.name
        waits = self._deps(eng, reads, writes)
        ctr.count += 16 * len(fns)
        val = ctr.count
        sem = ctr.sem

        def emit(be, waits=waits, fns=fns, sem=sem):
            for s, v in waits:
                be.wait_ge(s, v)
            for f in fns:
                f(be).then_inc(sem, 16)

        eng.ops.append(emit)
        for b in writes:
            b.w = (ctr, val)
            b.r = {}
        for b in reads:
            if b not in writes:
                b.r[ctr] = val
        return val

    def barrier(self):
        targets = [(e.ctr, e.ctr.count) for e in self.engs.values()]
        targets += [(c, c.count) for c in self.dma_ctrs]
        for eng in self.engs.values():
            waits = []
            for c, v in targets:
                if v > 0 and c is not eng.ctr and eng.seen.get(c, 0) < v:
                    eng.seen[c] = v
                    waits.append((c.sem, v))

            def emit(be, waits=waits):
                for s, v in waits:
                    be.wait_ge(s, v)

            eng.ops.append(emit)

    def emit_block(self):
        nc = self.nc
        ops = {k: e.ops for k, e in self.engs.items()}
        for e in self.engs.values():
            e.ops = []
        with nc.Block() as block:
            @block.tensor
            def _(be):
                for f in ops["pe"]:
                    f(be)

            @block.scalar
            def _(be):
                for f in ops["act"]:
                    f(be)

            @block.vector
            def _(be):
                for f in ops["dve"]:
                    f(be)

            @block.gpsimd
            def _(be):
                for f in ops["pool"]:
                    f(be)

            @block.sync
            def _(be):
                for f in ops["sp"]:
                    f(be)

D = 1024
NTP = 64
DIN = 5160
DFF = 2816
EPS = 1e-6
NEG = -1.0e30
NBIS = 14
IDX_SCALE = 256.0 ** -0.5


class Ring:
    def __init__(self, bufs):
        self.bufs = bufs
        self.i = 0

    def next(self):
        b = self.bufs[self.i % len(self.bufs)]
        self.i += 1
        return b


def build_program(dbg=False):
    import os as _os
    KSTOP = int(_os.environ.get('KSTOP', '99'))
    NT = NTP + 2
    SP = NTP * 128
    SCW = max(SP, 1152)
    nc = bass.Bass("TRN2", target_bir_lowering=False)

    def din(name, shape, dt=F32):
        return nc.dram_tensor(name, list(shape), dt, kind="ExternalInput").ap()

    def dout(name, shape, dt=F32):
        return nc.dram_tensor(name, list(shape), dt, kind="ExternalOutput").ap()

    def dscr(name, shape, dt):
        return nc.dram_tensor(name, list(shape), dt, kind=("ExternalOutput" if dbg else "Internal")).ap()

    x_p = din("x_p", [SP, D]); x_s = din("x_s", [2, 128, D]); mem = din("mem", [256, D])
    cak = din("cak", [2, 1024, 128]); cav = din("cav", [2, 1024, 128]); cik = din("cik", [2, 1024, 32])
    cpool = din("cpool", [2, 15, 512]); cmk = din("cmk", [2, 256, 512]); cmv = din("cmv", [2, 256, 512])
    w_in = din("w_in", [D, DIN]); w_mem = din("w_mem", [D, 1024]); w_pool = din("w_pool", [4, 128, 128])
    w_o = [din("w_oa", [512, D]), din("w_ob", [512, D]), din("w_om", [512, D])]
    w_out = din("w_out", [D, D]); w_gate = din("w_gate", [D, DFF]); w_up = din("w_up", [D, DFF]); w_down = din("w_down", [DFF, D])
    g_mix = din("g_mix", [1, D]); g_ffn = din("g_ffn", [1, D]); g_mem = din("g_mem", [1, D])
    g_qa = din("g_qa", [1, 64]); g_ka = din("g_ka", [1, 64]); g_kidx = din("g_kidx", [1, 32])
    g_qm = din("g_qm", [1, 128]); g_km = din("g_km", [1, 128]); s_pool = din("s_pool", [128, 4])
    rope = din("rope", [NT, 128, 96]); bands = din("bands", [128, 3, 4, 128]); ident = din("ident", [128, 128])

    y_p = dout("y_p", [SP, D]); y_s = dout("y_s", [2, 64, D])
    nak_p = dout("nak_p", [SP, 128]); nav_p = dout("nav_p", [SP, 128]); nik_p = dout("nik_p", [SP, 32])
    npool_p = dout("npool_p", [15, 512]); nmk_p = dout("nmk_p", [256, 512]); nmv_p = dout("nmv_p", [256, 512])
    nak_s = dout("nak_s", [2, 64, 128]); nav_s = dout("nav_s", [2, 64, 128]); nik_s = dout("nik_s", [2, 64, 32])
    npool_s = dout("npool_s", [2, 15, 512])

    s_qa = dscr("s_qa", [NT, 128, 512], BF16); s_qi = dscr("s_qi", [NT, 128, 384], BF16)
    s_qm = dscr("s_qm", [NT, 128, 512], BF16); s_ub = dscr("s_ub", [NT, 128, 512], BF16)
    s_wi = dscr("s_wi", [NT, 128, 8], F32); s_gate = dscr("s_gate", [NT, 128, 3072], BF16)
    s_KT = dscr("s_KT", [NT, 128, 128], BF16); s_V = dscr("s_V", [NT, 128, 128], BF16); s_KI = dscr("s_KI", [NT, 128, 128], BF16)
    s_br = dscr("s_br", [NT, 128, 1536], BF16); s_x1 = dscr("s_x1", [NT, 128, D], F32)

    def xsrc(t):
        return x_p[t * 128:(t + 1) * 128, :] if t < NTP else x_s[t - NTP]

    with ExitStack() as gst:
        P = Prog(nc, gst)

        def bc(ap2, shape):
            return ap2.unsqueeze(1).to_broadcast(shape)

        def bl(ap2, shape):
            return ap2.unsqueeze(2).to_broadcast(shape)

        idb = P.sbuf("idb", [128, 128], BF16)
        gq = P.sbuf("gq", [128, 64 + 64 + 32 + 128 + 128], F32, dma=True)
        spl = P.sbuf("spl", [128, 4], F32, dma=True)
        with ExitStack() as st0:
            P.stack = st0
            idf = P.sbuf("idf", [128, 128], F32, dma=True)
            P.dma("sp", [lambda be: be.dma_start(out=idf[:], in_=ident)], idf, writes=[idf])
            P.op("dve", lambda be: be.tensor_copy(out=idb[:], in_=idf[:]), reads=[idf], writes=[idb])
            offs = {}
            o = 0
            fl = []
            for nm, ap_, n in (("qa", g_qa, 64), ("ka", g_ka, 64), ("kidx", g_kidx, 32), ("qm", g_qm, 128), ("km", g_km, 128)):
                offs[nm] = (o, n)
                fl.append(lambda be, ap_=ap_, o=o, n=n: be.dma_start(out=gq[:, o:o + n], in_=ap_.partition_broadcast(128)))
                o += n
            P.dma("sp", fl, gq, writes=[gq])
            P.dma("sp", [lambda be: be.dma_start(out=spl[:], in_=s_pool)], spl, writes=[spl])
            P.barrier()
            P.emit_block()
            if KSTOP == 0:
                return nc
        P.stack = gst

        def gvec(nm):
            o, n = offs[nm]
            return gq[:, o:o + n]

        def rms_rows(xb, xap, gb, hb, n, junk, ss, rs):
            P.op("act", lambda be: be.activation(out=junk[:, 0:n], in_=xap, func=AF.Square, accum_out=ss[:]), reads=[xb], writes=[junk, ss])
            P.op("dve", lambda be: be.tensor_scalar(out=rs[:], in0=ss[:], scalar1=1.0 / n, scalar2=EPS, op0=ALU.mult, op1=ALU.add), reads=[ss], writes=[rs])
            P.op("act", lambda be: be.activation(out=rs[:], in_=rs[:], func=AF.Sqrt), reads=[rs], writes=[rs])
            P.op("dve", lambda be: be.reciprocal(out=rs[:], in_=rs[:]), reads=[rs], writes=[rs])
            P.op("dve", lambda be: be.scalar_tensor_tensor(out=hb[:], in0=xap, scalar=rs[:, 0:1], in1=gb[:], op0=ALU.mult, op1=ALU.mult), reads=[xb, rs, gb], writes=[hb])

        def head_norm(srcb, src3, gap, outb, out3, H, hd, sq, ssq, eng="dve"):
            sh = [128, H, hd]
            sqv = sq[:, 0:H * hd].rearrange("p (h d) -> p h d", h=H)
            P.op(eng, lambda be: be.tensor_tensor(out=sqv, in0=src3, in1=src3, op=ALU.mult), reads=[srcb], writes=[sq])
            P.op("dve", lambda be: be.tensor_reduce(out=ssq[:, 0:H], in_=sqv, axis=AX.X, op=ALU.add), reads=[sq], writes=[ssq])
            P.op("dve", lambda be: be.tensor_scalar(out=ssq[:, 0:H], in0=ssq[:, 0:H], scalar1=1.0 / hd, scalar2=EPS, op0=ALU.mult, op1=ALU.add), reads=[ssq], writes=[ssq])
            P.op("act", lambda be: be.activation(out=ssq[:, 0:H], in_=ssq[:, 0:H], func=AF.Sqrt), reads=[ssq], writes=[ssq])
            P.op("dve", lambda be: be.reciprocal(out=ssq[:, 0:H], in_=ssq[:, 0:H]), reads=[ssq], writes=[ssq])
            P.op(eng, lambda be: be.tensor_tensor(out=sqv, in0=src3, in1=bl(ssq[:, 0:H], sh), op=ALU.mult), reads=[srcb, ssq], writes=[sq])
            P.op(eng, lambda be: be.tensor_tensor(out=out3, in0=sqv, in1=bc(gap, sh), op=ALU.mult), reads=[sq, gq], writes=[outb])

        def rope_ops(srcb, x1, x2, cs, sn, outb, o1, o2, tb, t1, t2, eng="dve"):
            P.op(eng, lambda be: be.tensor_tensor(out=t1, in0=x1, in1=cs, op=ALU.mult), reads=[srcb, rp_cur[0]], writes=[tb])
            P.op(eng, lambda be: be.tensor_tensor(out=t2, in0=x2, in1=sn, op=ALU.mult), reads=[srcb, rp_cur[0]], writes=[tb])
            P.op(eng, lambda be: be.tensor_tensor(out=o1, in0=t1, in1=t2, op=ALU.subtract), reads=[tb], writes=[outb])
            P.op(eng, lambda be: be.tensor_tensor(out=t1, in0=x2, in1=cs, op=ALU.mult), reads=[srcb, rp_cur[0]], writes=[tb])
            P.op(eng, lambda be: be.tensor_tensor(out=t2, in0=x1, in1=sn, op=ALU.mult), reads=[srcb, rp_cur[0]], writes=[tb])
            P.op(eng, lambda be: be.tensor_tensor(out=o2, in0=t1, in1=t2, op=ALU.add), reads=[tb], writes=[outb])

        rp_cur = [None]

        def load_w_cast(dst, dst_view_fn, src, rows, cols, kcs):
            fl = []
            for kc in range(kcs):
                c0 = 0
                while c0 < cols:
                    cw = min(2048, cols - c0)
                    fl.append(lambda be, kc=kc, c0=c0, cw=cw: be.dma_start(out=dst_view_fn(kc, c0, cw), in_=src[kc * 128:(kc + 1) * 128, c0:c0 + cw]))
                    c0 += cw
            P.dma("pool", fl, dst, writes=[dst])

        def transpose_to(psb, ps3, srcb, src2, n, dstb, dst3, evac="act"):
            for k in range(n):
                P.op("pe", lambda be, k=k: be.transpose(out=ps3[:, k, :], in_=src2[:, k * 128:(k + 1) * 128], identity=idb[:]), reads=[srcb, idb], writes=[psb])
            if evac == "act":
                P.op("act", lambda be: be.copy(out=dst3, in_=ps3[:, 0:n, :]), reads=[psb], writes=[dstb])
            else:
                P.op("dve", lambda be: be.tensor_copy(out=dst3, in_=ps3[:, 0:n, :]), reads=[psb], writes=[dstb])

        def interleave(gens):
            gens = list(gens)
            while gens:
                for g_ in list(gens):
                    try:
                        next(g_)
                    except StopIteration:
                        gens.remove(g_)

        def head_norm_g(srcb, src3, gap, outb, out3, H, hd, sq, ssq):
            sh = [128, H, hd]
            sqv = sq[:, 0:H * hd].rearrange("p (h d) -> p h d", h=H)
            P.op("dve", lambda be: be.tensor_tensor(out=sqv, in0=src3, in1=src3, op=ALU.mult), reads=[srcb], writes=[sq]); yield
            P.op("dve", lambda be: be.tensor_reduce(out=ssq[:, 0:H], in_=sqv, axis=AX.X, op=ALU.add), reads=[sq], writes=[ssq]); yield
            P.op("dve", lambda be: be.tensor_scalar(out=ssq[:, 0:H], in0=ssq[:, 0:H], scalar1=1.0 / hd, scalar2=EPS, op0=ALU.mult, op1=ALU.add), reads=[ssq], writes=[ssq]); yield
            P.op("act", lambda be: be.activation(out=ssq[:, 0:H], in_=ssq[:, 0:H], func=AF.Sqrt), reads=[ssq], writes=[ssq]); yield
            P.op("dve", lambda be: be.reciprocal(out=ssq[:, 0:H], in_=ssq[:, 0:H]), reads=[ssq], writes=[ssq]); yield
            P.op("dve", lambda be: be.tensor_tensor(out=sqv, in0=src3, in1=bl(ssq[:, 0:H], sh), op=ALU.mult), reads=[srcb, ssq], writes=[sq]); yield
            P.op("dve", lambda be: be.tensor_tensor(out=out3, in0=sqv, in1=bc(gap, sh), op=ALU.mult), reads=[sq, gq], writes=[outb]); yield

        def rope_g(srcb, x1, x2, cs, sn, rb, outb, o1, o2, tbs, tv):
            P.op("dve", lambda be: be.tensor_tensor(out=tv[0], in0=x1, in1=cs, op=ALU.mult), reads=[srcb, rb], writes=[tbs[0]]); yield
            P.op("dve", lambda be: be.tensor_tensor(out=tv[1], in0=x2, in1=sn, op=ALU.mult), reads=[srcb, rb], writes=[tbs[1]]); yield
            P.op("dve", lambda be: be.tensor_tensor(out=tv[2], in0=x2, in1=cs, op=ALU.mult), reads=[srcb, rb], writes=[tbs[2]]); yield
            P.op("dve", lambda be: be.tensor_tensor(out=tv[3], in0=x1, in1=sn, op=ALU.mult), reads=[srcb, rb], writes=[tbs[3]]); yield
            P.op("dve", lambda be: be.tensor_tensor(out=o1, in0=tv[0], in1=tv[1], op=ALU.subtract), reads=[tbs[0], tbs[1]], writes=[outb]); yield
            P.op("dve", lambda be: be.tensor_tensor(out=o2, in0=tv[2], in1=tv[3], op=ALU.add), reads=[tbs[2], tbs[3]], writes=[outb]); yield

        with ExitStack() as st:
            P.stack = st
            win = P.sbuf("win", [128, 8, DIN], BF16, dma=True)
            load_w_cast(win, lambda kc, c0, cw: win[:, kc, c0:c0 + cw], w_in, 1024, DIN, 8)
            gmix = P.sbuf("gmix", [128, D], F32, dma=True)
            P.dma("sp", [lambda be: be.dma_start(out=gmix[:], in_=g_mix.partition_broadcast(128))], gmix, writes=[gmix])
            xr = Ring([P.sbuf("xa%d" % i, [128, D], F32, dma=True) for i in range(3)])
            rpr = Ring([P.sbuf("rp%d" % i, [128, 96], F32, dma=True) for i in range(3)])
            junk = P.sbuf("junk1", [128, D], BF16)
            ssL = [P.sbuf("ss1_%d" % i, [128, 1], F32) for i in range(2)]; rsL = [P.sbuf("rs1_%d" % i, [128, 1], F32) for i in range(2)]
            hbL = [P.sbuf("hb1_%d" % i, [128, D], BF16) for i in range(2)]; hTL = [P.sbuf("hT1_%d" % i, [128, 8, 128], BF16) for i in range(2)]
            c0L = [P.sbuf("c0f%d" % i, [128, 512], F32) for i in range(2)]
            c1L = [P.sbuf("c1f%d" % i, [128, 512], F32, dma=True) for i in range(2)]
            c2L = [P.sbuf("c2f%d" % i, [128, 40], F32) for i in range(2)]
            c4L = [P.sbuf("c4f%d" % i, [128, 512], F32) for i in range(2)]
            sq_qa = P.sbuf("sq_qa", [128, 512], F32); ssq_qa = P.sbuf("ssq_qa", [128, 8], F32); qn_qa = P.sbuf("qn_qa", [128, 512], F32)
            tb_qa = [P.sbuf("tb_qa%d" % i, [128, 256], F32) for i in range(4)]
            sq_ka = P.sbuf("sq_ka", [128, 128], F32); ssq_ka = P.sbuf("ssq_ka", [128, 8], F32); qn_ka = P.sbuf("qn_ka", [128, 128], F32)
            tb_ka = [P.sbuf("tb_ka%d" % i, [128, 64], F32) for i in range(4)]
            tb_qi = [P.sbuf("tb_qi%d" % i, [128, 128], F32) for i in range(4)]
            sq_ki = P.sbuf("sq_ki", [128, 32], F32); ssq_ki = P.sbuf("ssq_ki", [128, 8], F32); qn_ki = P.sbuf("qn_ki", [128, 32], F32)
            tb_ki = [P.sbuf("tb_ki%d" % i, [128, 16], F32) for i in range(4)]
            sq_qm = P.sbuf("sq_qm", [128, 512], F32); ssq_qm = P.sbuf("ssq_qm", [128, 8], F32)
            qab = P.sbuf("qab", [128, 512], BF16)
            kof = Ring([P.sbuf("kof%d" % i, [128, 128], F32, dma=True) for i in range(2)])
            kbb = P.sbuf("kbb", [128, 128], BF16)
            qib = P.sbuf("qib", [128, 256], BF16)
            kif = Ring([P.sbuf("kif%d" % i, [128, 32], F32, dma=True) for i in range(2)])
            ki4 = P.sbuf("ki4", [128, 128], BF16)
            wif = Ring([P.sbuf("wif%d" % i, [128, 8], F32, dma=True) for i in range(2)])
            ubb = Ring([P.sbuf("ubb%d" % i, [128, 512], BF16, dma=True) for i in range(2)])
            ubf = P.sbuf("ubf", [128, 512], F32, dma=True)
            qmb = P.sbuf("qmb", [128, 512], BF16)
            qaT = Ring([P.sbuf("qaT%d" % i, [128, 4, 128], BF16, dma=True) for i in range(2)])
            qiT = Ring([P.sbuf("qiT%d" % i, [128, 3, 128], BF16, dma=True) for i in range(2)])
            qmT = Ring([P.sbuf("qmT%d" % i, [128, 4, 128], BF16, dma=True) for i in range(2)])
            gsb = Ring([P.sbuf("gsb%d" % i, [128, 6, 512], BF16, dma=True) for i in range(2)])
            ktT = Ring([P.sbuf("ktT%d" % i, [128, 128], BF16, dma=True) for i in range(2)])
            vbT = Ring([P.sbuf("vbT%d" % i, [128, 128], BF16, dma=True) for i in range(2)])
            kiT = Ring([P.sbuf("kiT%d" % i, [128, 128], BF16, dma=True) for i in range(2)])
            pT = P.psum("pT1", [128, 8, 128], BF16)
            pP = Ring([P.psum("pP1_%d" % i, [128, 512], F32) for i in range(4)])
            pR = Ring([P.psum("pR1_%d" % i, [128, 8, 128], BF16) for i in range(2)])

            xbufs = {}
            rbufs = {}
            tst = {}

            def a1_load(t):
                xb = xr.next(); rb = rpr.next()
                xbufs[t] = xb; rbufs[t] = rb
                P.dma("sp", [lambda be: be.dma_start(out=xb[:], in_=xsrc(t))], xb, writes=[xb])
                P.dma("sp", [lambda be: be.dma_start(out=rb[:], in_=rope[t])], rb, writes=[rb])

            def a1_R(t):
                par = t % 2
                xb = xbufs.pop(t)
                rms_rows(xb, xb[:], gmix, hbL[par], D, junk, ssL[par], rsL[par])
                transpose_to(pT, pT, hbL[par], hbL[par], 8, hTL[par], hTL[par][:])

            def a1_M(t):
                par = t % 2
                hT = hTL[par]
                samp = t >= NTP
                b = t - NTP

                def proj(ps, c0, cw):
                    for kc in range(8):
                        P.op("pe", lambda be, kc=kc: be.matmul(ps[:, 0:cw], lhsT=hT[:, kc, :], rhs=win[:, kc, c0:c0 + cw], start=(kc == 0), stop=(kc == 7)), reads=[hT, win], writes=[ps])

                c0f, c1, c2f, c4f = c0L[par], c1L[par], c2L[par], c4L[par]
                ps0 = pP.next(); proj(ps0, 0, 512)
                P.op("act", lambda be: be.copy(out=c0f[:], in_=ps0[:]), reads=[ps0], writes=[c0f])
                ps1 = pP.next(); proj(ps1, 512, 512)
                P.op("act", lambda be: be.copy(out=c1[:], in_=ps1[:]), reads=[ps1], writes=[c1])
                ps2 = pP.next(); proj(ps2, 1024, 40)
                P.op("act", lambda be: be.copy(out=c2f[:], in_=ps2[:, 0:40]), reads=[ps2], writes=[c2f])
                ps3 = pP.next(); proj(ps3, 1064, 512)
                ub_ = ubb.next()
                P.op("act", lambda be: be.copy(out=ub_[:], in_=ps3[:]), reads=[ps3], writes=[ub_])
                P.dma("sp", [lambda be: be.dma_start(out=s_ub[t], in_=ub_[:])], ub_, reads=[ub_])
                if samp or t == NTP - 1:
                    P.op("act", lambda be: be.copy(out=ubf[:], in_=ps3[:]), reads=[ps3], writes=[ubf])
                    if samp:
                        P.dma("sp", [lambda be: be.dma_start(out=npool_s[b], in_=ubf[49:64, :])], ubf, reads=[ubf])
                    else:
                        P.dma("sp", [lambda be: be.dma_start(out=npool_p, in_=ubf[113:128, :])], ubf, reads=[ubf])
                ps4 = pP.next(); proj(ps4, 1576, 512)
                P.op("act", lambda be: be.copy(out=c4f[:], in_=ps4[:]), reads=[ps4], writes=[c4f])
                gs = gsb.next()
                for j in range(6):
                    ps = pP.next(); proj(ps, 2088 + 512 * j, 512)
                    P.op("act", lambda be, ps=ps, j=j: be.activation(out=gs[:, j, :], in_=ps[:], func=AF.Sigmoid), reads=[ps], writes=[gs])
                P.dma("sp", [lambda be: be.dma_start(out=s_gate[t], in_=gs[:].rearrange("p a b -> p (a b)"))], gs, reads=[gs])

            def a1_C(t):
                par = t % 2
                rb = rbufs.pop(t)
                c0f, c1, c2f, c4f = c0L[par], c1L[par], c2L[par], c4L[par]
                cos64 = rb[:, 0:32]; sin64 = rb[:, 32:64]; cos32 = rb[:, 64:80]; sin32 = rb[:, 80:96]
                ko = kof.next(); kio = kif.next(); wo = wif.next()
                tst[t] = dict(ko=ko, kio=kio, wo=wo, c1=c1)

                def ch_qa():
                    yield from head_norm_g(c0f, c0f[:].rearrange("p (h d) -> p h d", h=8), gvec("qa"), qn_qa, qn_qa[:].rearrange("p (h d) -> p h d", h=8), 8, 64, sq_qa, ssq_qa)
                    qn4 = qn_qa[:].rearrange("p (k g d) -> p k g d", k=2, g=4)
                    qo4 = qab[:].rearrange("p (g k d) -> p k g d", g=4, k=2)
                    sh4 = [128, 2, 4, 32]
                    cs4 = cos64.unsqueeze(1).unsqueeze(1).to_broadcast(sh4)
                    sn4 = sin64.unsqueeze(1).unsqueeze(1).to_broadcast(sh4)
                    tv = [tb_[:].rearrange("p (k g d) -> p k g d", k=2, g=4) for tb_ in tb_qa]
                    yield from rope_g(qn_qa, qn4[:, :, :, 0:32], qn4[:, :, :, 32:64], cs4, sn4, rb, qab, qo4[:, :, :, 0:32], qo4[:, :, :, 32:64], tb_qa, tv)

                def ch_ka():
                    yield from head_norm_g(c1, c1[:, 0:128].rearrange("p (h d) -> p h d", h=2), gvec("ka"), qn_ka, qn_ka[:].rearrange("p (h d) -> p h d", h=2), 2, 64, sq_ka, ssq_ka)
                    k3 = qn_ka[:].rearrange("p (h d) -> p h d", h=2)
                    ko3 = ko[:].rearrange("p (h d) -> p h d", h=2)
                    sh3 = [128, 2, 32]
                    tv = [tb_[:].rearrange("p (h d) -> p h d", h=2) for tb_ in tb_ka]
                    yield from rope_g(qn_ka, k3[:, :, 0:32], k3[:, :, 32:64], bc(cos64, sh3), bc(sin64, sh3), rb, ko, ko3[:, :, 0:32], ko3[:, :, 32:64], tb_ka, tv)
                    P.op("pool", lambda be: be.tensor_copy(out=kbb[:], in_=ko[:]), reads=[ko], writes=[kbb]); yield

                def ch_qi():
                    qi3 = c1[:, 256:512].rearrange("p (h d) -> p h d", h=8)
                    qo3 = qib[:].rearrange("p (h d) -> p h d", h=8)
                    sh3 = [128, 8, 16]
                    tv = [tb_[:].rearrange("p (h d) -> p h d", h=8) for tb_ in tb_qi]
                    yield from rope_g(c1, qi3[:, :, 0:16], qi3[:, :, 16:32], bc(cos32, sh3), bc(sin32, sh3), rb, qib, qo3[:, :, 0:16], qo3[:, :, 16:32], tb_qi, tv)

                def ch_ki():
                    yield from head_norm_g(c2f, c2f[:, 0:32].unsqueeze(1), gvec("kidx"), qn_ki, qn_ki[:, 0:32].unsqueeze(1), 1, 32, sq_ki, ssq_ki)
                    tv = [tb_[:] for tb_ in tb_ki]
                    yield from rope_g(qn_ki, qn_ki[:, 0:16], qn_ki[:, 16:32], cos32, sin32, rb, kio, kio[:, 0:16], kio[:, 16:32], tb_ki, tv)
                    P.op("pool", lambda be: be.tensor_copy(out=ki4[:].rearrange("p (r c) -> p r c", r=4), in_=kio[:].unsqueeze(1).to_broadcast([128, 4, 32])), reads=[kio], writes=[ki4]); yield
                    P.op("dve", lambda be: be.tensor_scalar(out=wo[:], in0=c2f[:, 32:40], scalar1=IDX_SCALE, scalar2=None, op0=ALU.mult), reads=[c2f], writes=[wo]); yield

                def ch_qm():
                    yield from head_norm_g(c4f, c4f[:].rearrange("p (h d) -> p h d", h=4), gvec("qm"), qmb, qmb[:].rearrange("p (h d) -> p h d", h=4), 4, 128, sq_qm, ssq_qm)

                interleave([ch_qa(), ch_ka(), ch_qi(), ch_ki(), ch_qm()])

            def a1_T(t):
                samp = t >= NTP
                b = t - NTP
                d_ = tst.pop(t)
                ko, kio, wo, c1 = d_["ko"], d_["kio"], d_["wo"], d_["c1"]
                qa_t = qaT.next(); pr = pR.next()
                transpose_to(pr, pr, qab, qab, 4, qa_t, qa_t[:])
                P.dma("sp", [lambda be: be.dma_start(out=s_qa[t], in_=qa_t[:].rearrange("p a b -> p (a b)"))], qa_t, reads=[qa_t])
                if samp:
                    P.dma("sp", [lambda be: be.dma_start(out=nak_s[b], in_=ko[0:64, :]),
                                 lambda be: be.dma_start(out=nav_s[b], in_=c1[0:64, 128:256])], ko, reads=[ko, c1])
                else:
                    P.dma("sp", [lambda be: be.dma_start(out=nak_p[t * 128:(t + 1) * 128, :], in_=ko[:]),
                                 lambda be: be.dma_start(out=nav_p[t * 128:(t + 1) * 128, :], in_=c1[:, 128:256])], ko, reads=[ko, c1])
                pr = pR.next(); kt_o = ktT.next()
                transpose_to(pr, pr, kbb, kbb, 1, kt_o, kt_o[:].unsqueeze(1))
                P.dma("sp", [lambda be: be.dma_start(out=s_KT[t], in_=kt_o[:])], kt_o, reads=[kt_o])
                vb_o = vbT.next()
                P.op("pool", lambda be: be.tensor_copy(out=vb_o[:], in_=c1[:, 128:256]), reads=[c1], writes=[vb_o])
                P.dma("sp", [lambda be: be.dma_start(out=s_V[t], in_=vb_o[:])], vb_o, reads=[vb_o])
                qi_t = qiT.next(); pr = pR.next()
                for k in range(3):
                    nr = 96 if k < 2 else 64
                    P.op("pe", lambda be, k=k, nr=nr, pr=pr: be.transpose(out=pr[0:nr, k, :], in_=qib[:, 96 * k:96 * k + nr], identity=idb[:]), reads=[qib, idb], writes=[pr])
                P.op("act", lambda be, pr=pr: be.copy(out=qi_t[0:96, 0:2, :], in_=pr[0:96, 0:2, :]), reads=[pr], writes=[qi_t])
                P.op("act", lambda be, pr=pr: be.copy(out=qi_t[0:64, 2, :], in_=pr[0:64, 2, :]), reads=[pr], writes=[qi_t])
                P.dma("sp", [lambda be: be.dma_start(out=s_qi[t, 0:96, 0:256], in_=qi_t[0:96, 0:2, :].rearrange("p a b -> p (a b)")),
                             lambda be: be.dma_start(out=s_qi[t, 0:64, 256:384], in_=qi_t[0:64, 2, :])], qi_t, reads=[qi_t])
                P.dma("sp", [lambda be: be.dma_start(out=s_wi[t], in_=wo[:])], wo, reads=[wo])
                if samp:
                    P.dma("sp", [lambda be: be.dma_start(out=nik_s[b], in_=kio[0:64, :])], kio, reads=[kio])
                else:
                    P.dma("sp", [lambda be: be.dma_start(out=nik_p[t * 128:(t + 1) * 128, :], in_=kio[:])], kio, reads=[kio])
                pr = pR.next(); ki_o = kiT.next()
                transpose_to(pr, pr, ki4, ki4, 1, ki_o, ki_o[:].unsqueeze(1))
                P.dma("sp", [lambda be: be.dma_start(out=s_KI[t], in_=ki_o[:])], ki_o, reads=[ki_o])
                qm_t = qmT.next(); pr = pR.next()
                transpose_to(pr, pr, qmb, qmb, 4, qm_t, qm_t[:])
                P.dma("sp", [lambda be: be.dma_start(out=s_qm[t], in_=qm_t[:].rearrange("p a b -> p (a b)"))], qm_t, reads=[qm_t])

            a1_load(0); a1_load(1)
            a1_R(0)
            for t in range(NT):
                if t + 2 < NT:
                    a1_load(t + 2)
                a1_M(t)
                if t >= 1:
                    a1_T(t - 1)
                if t + 1 < NT:
                    a1_R(t + 1)
                a1_C(t)
            a1_T(NT - 1)
            P.barrier()
            P.emit_block()
            if KSTOP == 1:
                return nc

        with ExitStack() as kvst:
            P.stack = kvst
            KTp = P.sbuf("KTp", [128, SP], BF16)
            V1p = P.sbuf("V1p", [128, NTP, 2, 65], BF16)
            KIp = P.sbuf("KIp", [128, SP], BF16)
            KTs = [P.sbuf("KTs%d" % b, [128, 1152], BF16) for b in range(2)]
            V1s = [P.sbuf("V1s%d" % b, [128, 9, 2, 65], BF16) for b in range(2)]
            KIs = [P.sbuf("KIs%d" % b, [128, 1152], BF16) for b in range(2)]
            mkT = [P.sbuf("mkT%d" % i, [128, 4, 256], BF16) for i in range(3)]
            mv1 = [P.sbuf("mv1%d" % i, [128, 2, 4, 129], BF16) for i in range(3)]
            ubh = [P.sbuf("ubh%d" % b, [128, 512], BF16, dma=True) for b in range(2)]

            P.op("pool", lambda be: be.memset(V1p[:, :, :, 64:65], 1.0), writes=[V1p])
            for b in range(2):
                P.op("pool", lambda be, b=b: be.memset(V1s[b][:, :, :, 64:65], 1.0), writes=[V1s[b]])
                P.op("pool", lambda be, b=b: be.memset(ubh[b][:], 0.0), writes=[ubh[b]])
            for i in range(3):
                P.op("pool", lambda be, i=i: be.memset(mv1[i][:, :, :, 128:129], 1.0), writes=[mv1[i]])

            with ExitStack() as st:
                P.stack = st
                wm = P.sbuf("wm", [128, 8, 1024], BF16, dma=True)
                load_w_cast(wm, lambda kc, c0, cw: wm[:, kc, c0:c0 + cw], w_mem, 1024, 1024, 8)
                gmem = P.sbuf("gmem", [128, D], F32, dma=True)
                P.dma("sp", [lambda be: be.dma_start(out=gmem[:], in_=g_mem.partition_broadcast(128))], gmem, writes=[gmem])
                xm = Ring([P.sbuf("xm%d" % i, [128, D], F32, dma=True) for i in range(2)])
                st32 = Ring([P.sbuf("st32_%d" % i, [128, 1024], F32, dma=True) for i in range(3)])
                stb = Ring([P.sbuf("stb%d" % i, [128, 1024], BF16) for i in range(2)])
                junk = P.sbuf("junk0", [128, D], BF16)
                ss = P.sbuf("ss0", [128, 1], F32); rs = P.sbuf("rs0", [128, 1], F32)
                hb = P.sbuf("hb0", [128, D], BF16); hT = P.sbuf("hT0", [128, 8, 128], BF16)
                sq = P.sbuf("sq0", [128, 512], F32); ssq = P.sbuf("ssq0", [128, 8], F32)
                kout = Ring([P.sbuf("kout%d" % i, [128, 512], F32, dma=True) for i in range(2)])
                vout = Ring([P.sbuf("vout%d" % i, [128, 512], F32, dma=True) for i in range(2)])
                kb = P.sbuf("kb0", [128, 512], BF16)
                pT = P.psum("pT0", [128, 8, 128], BF16)
                pK = P.psum("pK0", [128, 512], F32); pV = P.psum("pV0", [128, 512], F32)
                pR = Ring([P.psum("pR0_%d" % i, [128, 8, 128], BF16) for i in range(2)])

                for mt in range(2):
                    xb = xm.next()
                    P.dma("sp", [lambda be, xb=xb, mt=mt: be.dma_start(out=xb[:], in_=mem[mt * 128:(mt + 1) * 128, :])], xb, writes=[xb])
                    rms_rows(xb, xb[:], gmem, hb, D, junk, ss, rs)
                    transpose_to(pT, pT, hb, hb, 8, hT, hT[:])
                    for kc in range(8):
                        P.op("pe", lambda be, kc=kc: be.matmul(pK[:], lhsT=hT[:, kc, :], rhs=wm[:, kc, 0:512], start=(kc == 0), stop=(kc == 7)), reads=[hT, wm], writes=[pK])
                    for kc in range(8):
                        P.op("pe", lambda be, kc=kc: be.matmul(pV[:], lhsT=hT[:, kc, :], rhs=wm[:, kc, 512:1024], start=(kc == 0), stop=(kc == 7)), reads=[hT, wm], writes=[pV])
                    kf = st32.next()
                    P.op("act", lambda be, kf=kf: be.copy(out=kf[:, 0:512], in_=pK[:]), reads=[pK], writes=[kf])
                    ko = kout.next()
                    head_norm(kf, kf[:, 0:512].rearrange("p (h d) -> p h d", h=4), gvec("km"), ko, ko[:].rearrange("p (h d) -> p h d", h=4), 4, 128, sq, ssq)
                    P.dma("sp", [lambda be, ko=ko, mt=mt: be.dma_start(out=nmk_p[mt * 128:(mt + 1) * 128, :], in_=ko[:])], ko, reads=[ko])
                    P.op("pool", lambda be, ko=ko: be.tensor_copy(out=kb[:], in_=ko[:]), reads=[ko], writes=[kb])
                    pr = pR.next()
                    transpose_to(pr, pr, kb, kb, 4, mkT[0], mkT[0][:, :, mt * 128:(mt + 1) * 128])
                    vo = vout.next()
                    P.op("act", lambda be, vo=vo: be.copy(out=vo[:], in_=pV[:]), reads=[pV], writes=[vo])
                    P.dma("sp", [lambda be, vo=vo, mt=mt: be.dma_start(out=nmv_p[mt * 128:(mt + 1) * 128, :], in_=vo[:])], vo, reads=[vo])
                    P.op("pool", lambda be, vo=vo, mt=mt: be.tensor_copy(out=mv1[0][:, mt, :, 0:128], in_=vo[:].rearrange("p (h d) -> p h d", h=4)), reads=[vo], writes=[mv1[0]])
                for b in range(2):
                    for mt in range(2):
                        kf = st32.next()
                        P.dma("sp", [lambda be, kf=kf, b=b, mt=mt: be.dma_start(out=kf[:, 0:512], in_=cmk[b, mt * 128:(mt + 1) * 128, :])], kf, writes=[kf])
                        sb_ = stb.next()
                        P.op("dve", lambda be, kf=kf, sb_=sb_: be.tensor_copy(out=sb_[:, 0:512], in_=kf[:, 0:512]), reads=[kf], writes=[sb_])
                        pr = pR.next()
                        transpose_to(pr, pr, sb_, sb_, 4, mkT[1 + b], mkT[1 + b][:, :, mt * 128:(mt + 1) * 128])
                        vf = st32.next()
                        P.dma("sp", [lambda be, vf=vf, b=b, mt=mt: be.dma_start(out=vf[:, 0:512], in_=cmv[b, mt * 128:(mt + 1) * 128, :])], vf, writes=[vf])
                        P.op("pool", lambda be, vf=vf, b=b, mt=mt: be.tensor_copy(out=mv1[1 + b][:, mt, :, 0:128], in_=vf[:, 0:512].rearrange("p (h d) -> p h d", h=4)), reads=[vf], writes=[mv1[1 + b]])
                for b in range(2):
                    ck = st32.next()
                    P.dma("sp", [lambda be, ck=ck, b=b: be.dma_start(out=ck[:].rearrange("p (k c) -> p k c", k=8), in_=cak[b].rearrange("(k p) c -> p k c", p=128))], ck, writes=[ck])
                    cb = stb.next()
                    P.op("dve", lambda be, ck=ck, cb=cb: be.tensor_copy(out=cb[:], in_=ck[:]), reads=[ck], writes=[cb])
                    pr = pR.next()
                    transpose_to(pr, pr, cb, cb, 8, KTs[b], KTs[b][:, 0:1024].rearrange("p (k c) -> p k c", k=8))
                    cv = st32.next()
                    P.dma("sp", [lambda be, cv=cv, b=b: be.dma_start(out=cv[:].rearrange("p (k c) -> p k c", k=8), in_=cav[b].rearrange("(k p) c -> p k c", p=128))], cv, writes=[cv])
                    P.op("pool", lambda be, cv=cv, b=b: be.tensor_copy(out=V1s[b][:, 0:8, :, 0:64], in_=cv[:].rearrange("p (k h d) -> p k h d", k=8, h=2)), reads=[cv], writes=[V1s[b]])
                    ci = st32.next()
                    P.dma("sp", [lambda be, ci=ci, b=b: be.dma_start(out=ci[:, 0:256].rearrange("p (k c) -> p k c", k=8), in_=cik[b].rearrange("(k p) c -> p k c", p=128))], ci, writes=[ci])
                    c4 = stb.next()
                    P.op("dve", lambda be, ci=ci, c4=c4: be.tensor_copy(out=c4[:].rearrange("p (k r c) -> p k r c", k=8, r=4), in_=ci[:, 0:256].rearrange("p (k c) -> p k c", k=8).unsqueeze(2).to_broadcast([128, 8, 4, 32])), reads=[ci], writes=[c4])
                    pr = pR.next()
                    transpose_to(pr, pr, c4, c4, 8, KIs[b], KIs[b][:, 0:1024].rearrange("p (k c) -> p k c", k=8))
                    P.dma("pool", [lambda be, b=b: be.dma_start(out=ubh[b][113:128, :], in_=cpool[b])], ubh[b], writes=[ubh[b]])
                kvl = P.buf("kvl", None, dma=True)
                fl = []
                for q in range((NTP + 15) // 16):
                    t0_ = q * 16; t1_ = min(NTP, t0_ + 16)
                    fl.append(lambda be, t0_=t0_, t1_=t1_: be.dma_start(out=KTp[:, t0_ * 128:t1_ * 128].rearrange("p (t k) -> p t k", k=128), in_=s_KT[t0_:t1_].rearrange("t p k -> p t k")))
                    fl.append(lambda be, t0_=t0_, t1_=t1_: be.dma_start(out=KIp[:, t0_ * 128:t1_ * 128].rearrange("p (t k) -> p t k", k=128), in_=s_KI[t0_:t1_].rearrange("t p k -> p t k")))
                    for hh in range(2):
                        fl.append(lambda be, t0_=t0_, t1_=t1_, hh=hh: be.dma_start(out=V1p[:, t0_:t1_, hh, 0:64], in_=s_V[t0_:t1_, :, hh * 64:(hh + 1) * 64].rearrange("t p d -> p t d")))
                for b in range(2):
                    fl.append(lambda be, b=b: be.dma_start(out=KTs[b][:, 1024:1152], in_=s_KT[NTP + b]))
                    fl.append(lambda be, b=b: be.dma_start(out=KIs[b][:, 1024:1152], in_=s_KI[NTP + b]))
                    fl.append(lambda be, b=b: be.dma_start(out=V1s[b][:, 8, :, 0:64], in_=s_V[NTP + b].rearrange("p (h d) -> p h d", h=2)))
                P.dma("sp", fl, kvl, writes=[KTp, KIp, V1p, KTs[0], KTs[1], KIs[0], KIs[1], V1s[0], V1s[1]])
                P.barrier()
                P.emit_block()
                if KSTOP == 2:
                    return nc

            with ExitStack() as st:
                P.stack = st
                wpl = P.sbuf("wpl", [128, 4, 128], BF16, dma=True)
                P.dma("pool", [lambda be: be.dma_start(out=wpl[:], in_=w_pool.rearrange("g c e -> c g e"))], wpl, writes=[wpl])
                bnd = P.sbuf("bnd", [128, 3, 4, 128], BF16, dma=True)
                P.dma("pool", [lambda be: be.dma_start(out=bnd[:].rearrange("p a g t -> p (a g t)"), in_=bands.rearrange("p a g t -> p (a g t)"))], bnd, writes=[bnd])
                sc = P.sbuf("sc", [128, SCW], F32)
                scC = [P.buf("scC%d" % c, None) for c in range((SCW + 511) // 512)]
                pw = P.sbuf("pw", [128, NBIS + 1], F32); wk = P.sbuf("wk", [128, NBIS + 1], F32)
                for k in range(NBIS + 1):
                    P.op("pool", lambda be, k=k: be.memset(pw[:, k:k + 1], 2.0 ** -(k + 1)), writes=[pw])
                Mq = P.sbuf("Mq", [128, SCW], BF16)
                MT = P.sbuf("MT", [128, SCW // 128, 128], BF16)
                rl = Ring([P.sbuf("rl%d" % i, [128, 512], F32) for i in range(3)])
                er = Ring([P.sbuf("er%d" % i, [128, 4, 128], BF16) for i in range(3)])
                pr_ = Ring([P.sbuf("pp%d" % i, [128, 4, 128], BF16) for i in range(3)])
                ld = {}
                qaL = Ring([P.sbuf("qaL%d" % i, [128, 4, 128], BF16, dma=True) for i in range(3)])
                qiL = Ring([P.sbuf("qiL%d" % i, [128, 3, 128], BF16, dma=True) for i in range(3)])
                qmL = Ring([P.sbuf("qmL%d" % i, [128, 4, 128], BF16, dma=True) for i in range(3)])
                wiL = Ring([P.sbuf("wiL%d" % i, [128, 8], F32, dma=True) for i in range(3)])
                ubL = Ring([P.sbuf("ubL%d" % i, [128, 512], BF16, dma=True) for i in range(4)])
                lo = P.sbuf("lo", [128, 1], F32); w0 = P.sbuf("w0", [128, 1], F32); mx = P.sbuf("mx", [128, 1], F32)
                mid = P.sbuf("mid", [128, 1], F32); cnt = P.sbuf("cnt", [128, 1], F32); tt_ = P.sbuf("tt", [128, 1], F32)
                thr = Ring([P.sbuf("thr%d" % i, [128, 1], F32) for i in range(2)])
                rec = P.sbuf("rec", [128, 8], F32)
                a_sb = P.sbuf("a_sb", [128, 512], BF16); m_sb = P.sbuf("m_sb", [128, 512], BF16)
                em = [P.sbuf("em%d" % i, [128, 4, 128], BF16) for i in range(2)]
                pTs = P.sbuf("pTs", [128, 4, 128], BF16)
                br = Ring([P.sbuf("br%d" % i, [128, 3, 4, 128], BF16, dma=True) for i in range(2)])
                pI = Ring([P.psum("pI%d" % i, [128, 512], F32) for i in range(2)])
                pM = Ring([P.psum("pM%d" % i, [128, 8, 128], BF16) for i in range(2)])
                pS = Ring([P.psum("pS%d" % i, [128, 4, 128], F32) for i in range(2)])
                pA = [P.psum("pA%d" % i, [128, 512], F32) for i in range(2)]

                seqs = []
                for t in range(NTP):
                    seqs.append(dict(t=t, KT=KTp, V1=V1p, KI=KIp, nkt=t + 1, samp=False, mi=0))
                for b in range(2):
                    seqs.append(dict(t=NTP + b, KT=KTs[b], V1=V1s[b], KI=KIs[b], nkt=9, samp=True, mi=1 + b, b=b))

                def a2_load(i):
                    s = seqs[i]; t = s["t"]
                    d = dict(qa=qaL.next(), qi=qiL.next(), qm=qmL.next(), wi=wiL.next(), ub=ubL.next())
                    ld[i] = d
                    P.dma("sp", [lambda be: be.dma_start(out=d["qa"][:].rearrange("p a b -> p (a b)"), in_=s_qa[t])], d["qa"], writes=[d["qa"]])
                    P.dma("sp", [lambda be: be.dma_start(out=d["qi"][0:96, 0:2, :].rearrange("p a b -> p (a b)"), in_=s_qi[t, 0:96, 0:256]),
                             lambda be: be.dma_start(out=d["qi"][0:64, 2, :], in_=s_qi[t, 0:64, 256:384])], d["qi"], writes=[d["qi"]])
                    P.dma("sp", [lambda be: be.dma_start(out=d["qm"][:].rearrange("p a b -> p (a b)"), in_=s_qm[t])], d["qm"], writes=[d["qm"]])
                    P.dma("sp", [lambda be: be.dma_start(out=d["wi"][:], in_=s_wi[t])], d["wi"], writes=[d["wi"]])
                    P.dma("sp", [lambda be: be.dma_start(out=d["ub"][:], in_=s_ub[t])], d["ub"], writes=[d["ub"]])

                def stageA(i):
                    s = seqs[i]; d = ld[i]; S = s["nkt"] * 128
                    KI = s["KI"]; qi = d["qi"]; wi = d["wi"]
                    nch = (S + 511) // 512
                    for h in range(8):
                        r0 = (h % 3) * 32
                        for c in range(nch):
                            c0 = c * 512; cw = min(512, S - c0)
                            ps = pI.next()
                            P.op("pe", lambda be, ps=ps, r0=r0, h=h, c0=c0, cw=cw: be.matmul(ps[:, 0:cw], lhsT=qi[r0:r0 + 32, h // 3, :], rhs=KI[r0:r0 + 32, c0:c0 + cw], start=True, stop=True), reads=[qi, KI], writes=[ps])
                            r = rl.next()
                            P.op("act", lambda be, ps=ps, r=r, cw=cw: be.activation(out=r[:, 0:cw], in_=ps[:, 0:cw], func=AF.Relu), reads=[ps], writes=[r])
                            if h == 0:
                                P.op("dve", lambda be, r=r, c0=c0, cw=cw: be.tensor_scalar(out=sc[:, c0:c0 + cw], in0=r[:, 0:cw], scalar1=wi[:, 0:1], scalar2=None, op0=ALU.mult), reads=[r, wi], writes=[scC[c]])
                            else:
                                P.op("dve", lambda be, r=r, h=h, c0=c0, cw=cw: be.scalar_tensor_tensor(out=sc[:, c0:c0 + cw], in0=r[:, 0:cw], scalar=wi[:, h:h + 1], in1=sc[:, c0:c0 + cw], op0=ALU.mult, op1=ALU.add), reads=[r, wi, scC[c]], writes=[scC[c]])
                    if s["samp"]:
                        P.op("dve", lambda be: be.memset(sc[:, S - 64:S], NEG), writes=[scC[nch - 1]])
                    else:
                        P.op("dve", lambda be: be.memset(sc[0:64, S - 64:S], NEG), writes=[scC[nch - 1]])

                def stageC1(i):
                    s = seqs[i]; S = s["nkt"] * 128
                    nch = (S + 511) // 512
                    scs = scC[0:nch]
                    if S <= 256:
                        P.op("dve", lambda be: be.tensor_scalar(out=Mq[:, 0:S], in0=sc[:, 0:S], scalar1=-1.0e29, scalar2=None, op0=ALU.is_ge), reads=scs, writes=[Mq])
                        return
                    P.op("dve", lambda be: be.tensor_reduce(out=lo[:], in_=sc[:, 0:S - 64], axis=AX.X, op=ALU.min), reads=scs, writes=[lo])
                    P.op("dve", lambda be: be.tensor_reduce(out=mx[:], in_=sc[:, 0:S], axis=AX.X, op=ALU.max), reads=scs, writes=[mx])
                    P.op("dve", lambda be: be.tensor_tensor(out=w0[:], in0=mx[:], in1=lo[:], op=ALU.subtract), reads=[mx, lo], writes=[w0])
                    P.op("dve", lambda be: be.tensor_scalar(out=w0[:], in0=w0[:], scalar1=1.0 + 2.0 ** -10, scalar2=1e-12, op0=ALU.mult, op1=ALU.add), reads=[w0], writes=[w0])
                    P.op("dve", lambda be: be.tensor_scalar(out=wk[:], in0=pw[:], scalar1=w0[:, 0:1], scalar2=None, op0=ALU.mult), reads=[pw, w0], writes=[wk])
                    P.op("dve", lambda be: be.tensor_tensor(out=mid[:], in0=lo[:], in1=wk[:, 0:1], op=ALU.add), reads=[lo, wk], writes=[mid])
                    for k in range(NBIS):
                        P.op("dve", lambda be: be.tensor_scalar(out=Mq[:, 0:S], in0=sc[:, 0:S], scalar1=mid[:, 0:1], scalar2=None, op0=ALU.is_ge, op1=ALU.add, accum_out=cnt[:]), reads=scs + [mid], writes=[Mq, cnt])
                        P.op("dve", lambda be: be.tensor_scalar(out=tt_[:], in0=cnt[:], scalar1=255.5, scalar2=0.5, op0=ALU.is_ge, op1=ALU.subtract), reads=[cnt], writes=[tt_])
                        P.op("dve", lambda be, k=k: be.scalar_tensor_tensor(out=mid[:], in0=tt_[:], scalar=wk[:, k:k + 1], in1=mid[:], op0=ALU.mult, op1=ALU.add), reads=[tt_, wk, mid], writes=[mid])
                    P.op("dve", lambda be: be.tensor_tensor(out=lo[:], in0=mid[:], in1=wk[:, NBIS:NBIS + 1], op=ALU.subtract), reads=[mid, wk], writes=[lo])
                    P.op("dve", lambda be: be.tensor_scalar(out=Mq[:, 0:S], in0=sc[:, 0:S], scalar1=lo[:, 0:1], scalar2=None, op0=ALU.is_ge), reads=scs + [lo], writes=[Mq])

                def stageC2(i):
                    s = seqs[i]; nkt = s["nkt"]
                    for j in range((nkt + 7) // 8):
                        n = min(8, nkt - 8 * j)
                        pm = pM.next()
                        transpose_to(pm, pm, Mq, Mq[:, j * 1024:j * 1024 + n * 128], n, MT, MT[:, 8 * j:8 * j + n, :])

                def stageB(i):
                    s = seqs[i]; d = ld.pop(i); nkt = s["nkt"]; t = s["t"]
                    KT = s["KT"]; V1 = s["V1"]; qa = d["qa"]
                    nch = (nkt + 3) // 4
                    for h in range(8):
                        kv = h // 4; g = h % 4; r0 = kv * 64
                        acc = pA[kv][:, 0:260].rearrange("p (h d) -> p h d", h=4)
                        for c in range(nch):
                            n = min(4, nkt - 4 * c)
                            ps = pS.next()
                            for k in range(n):
                                kt = 4 * c + k
                                P.op("pe", lambda be, ps=ps, k=k, kt=kt, r0=r0, g=g: be.matmul(ps[:, k, :], lhsT=KT[r0:r0 + 64, kt * 128:(kt + 1) * 128], rhs=qa[r0:r0 + 64, g, :], start=True, stop=True), reads=[KT, qa], writes=[ps])
                            e = er.next()
                            P.op("act", lambda be, ps=ps, e=e, n=n: be.activation(out=e[:, 0:n, :], in_=ps[:, 0:n, :], func=AF.Exp, scale=0.125), reads=[ps], writes=[e])
                            p_ = pr_.next()
                            P.op("pool", lambda be, e=e, p_=p_, n=n, c=c: be.tensor_tensor(out=p_[:, 0:n, :], in0=e[:, 0:n, :], in1=MT[:, 4 * c:4 * c + n, :], op=ALU.mult), reads=[e, MT], writes=[p_])
                            for k in range(n):
                                kt = 4 * c + k
                                P.op("pe", lambda be, p_=p_, k=k, kt=kt, acc=acc, g=g, kv=kv: be.matmul(acc[:, g, :], lhsT=p_[:, k, :], rhs=V1[:, kt, kv, :], start=(kt == 0), stop=(kt == nkt - 1)), reads=[p_, V1], writes=[pA[kv]])
                    mi = s["mi"]; qm = d["qm"]
                    for mt in range(2):
                        ps = pS.next()
                        for h in range(4):
                            P.op("pe", lambda be, ps=ps, h=h, mt=mt: be.matmul(ps[:, h, :], lhsT=mkT[mi][:, h, mt * 128:(mt + 1) * 128], rhs=qm[:, h, :], start=True, stop=True), reads=[mkT[mi], qm], writes=[ps])
                        P.op("act", lambda be, ps=ps, mt=mt: be.activation(out=em[mt][:], in_=ps[:], func=AF.Exp, scale=128.0 ** -0.5), reads=[ps], writes=[em[mt]])
                    for kv in range(2):
                        acc = pA[kv][:, 0:260].rearrange("p (h d) -> p h d", h=4)
                        P.op("dve", lambda be, acc=acc, kv=kv: be.reciprocal(out=rec[:, 4 * kv:4 * kv + 4], in_=acc[:, :, 64]), reads=[pA[kv]], writes=[rec])
                        P.op("dve", lambda be, acc=acc, kv=kv: be.tensor_tensor(out=a_sb[:, 256 * kv:256 * kv + 256].rearrange("p (h d) -> p h d", h=4), in0=acc[:, :, 0:64], in1=bl(rec[:, 4 * kv:4 * kv + 4], [128, 4, 64]), op=ALU.mult), reads=[pA[kv], rec], writes=[a_sb])
                    bo = br.next()
                    pm = pM.next()
                    transpose_to(pm, pm, a_sb, a_sb, 4, bo, bo[:, 0, :, :])
                    for h in range(4):
                        accm = pA[h // 2][:, 0:258].rearrange("p (h d) -> p h d", h=2)
                        for mt in range(2):
                            P.op("pe", lambda be, accm=accm, h=h, mt=mt: be.matmul(accm[:, h % 2, :], lhsT=em[mt][:, h, :], rhs=mv1[mi][:, mt, h, :], start=(mt == 0), stop=(mt == 1)), reads=[em[mt], mv1[mi]], writes=[pA[h // 2]])
                    for hh in range(2):
                        accm = pA[hh][:, 0:258].rearrange("p (h d) -> p h d", h=2)
                        P.op("dve", lambda be, accm=accm, hh=hh: be.reciprocal(out=rec[:, 2 * hh:2 * hh + 2], in_=accm[:, :, 128]), reads=[pA[hh]], writes=[rec])
                        P.op("dve", lambda be, accm=accm, hh=hh: be.tensor_tensor(out=m_sb[:, 256 * hh:256 * hh + 256].rearrange("p (h d) -> p h d", h=2), in0=accm[:, :, 0:128], in1=bl(rec[:, 2 * hh:2 * hh + 2], [128, 2, 128]), op=ALU.mult), reads=[pA[hh], rec], writes=[m_sb])
                    pm = pM.next()
                    transpose_to(pm, pm, m_sb, m_sb, 4, bo, bo[:, 2, :, :])
                    ub = d["ub"]
                    if s["samp"]:
                        prev = ubh[s["b"]]; ai = 0
                    elif t == 0:
                        prev = None; ai = 2
                    else:
                        prev = s_prev_ub[0]; ai = 0
                    ps = pS.next()
                    for g in range(4):
                        P.op("pe", lambda be, ps=ps, g=g: be.matmul(ps[:, g, :], lhsT=ub[:, g * 128:(g + 1) * 128], rhs=bnd[:, ai, g, :], start=True, stop=(prev is None)), reads=[ub, bnd], writes=[ps])
                        if prev is not None:
                            P.op("pe", lambda be, ps=ps, g=g: be.matmul(ps[:, g, :], lhsT=prev[:, g * 128:(g + 1) * 128], rhs=bnd[:, 1, g, :], start=False, stop=True), reads=[prev, bnd], writes=[ps])
                    P.op("act", lambda be, ps=ps: be.copy(out=pTs[:], in_=ps[:]), reads=[ps], writes=[pTs])
                    ps2 = pS.next()
                    for g in range(4):
                        P.op("pe", lambda be, ps2=ps2, g=g: be.matmul(ps2[:, g, :], lhsT=wpl[:, g, :], rhs=pTs[:, g, :], start=True, stop=True), reads=[wpl, pTs], writes=[ps2])
                    P.op("dve", lambda be, ps2=ps2: be.tensor_tensor(out=bo[:, 1, :, :], in0=ps2[:], in1=bl(spl[:], [128, 4, 128]), op=ALU.mult), reads=[ps2, spl], writes=[bo])
                    P.dma("sp", [lambda be: be.dma_start(out=s_br[t], in_=bo[:].rearrange("p a b c -> p (a b c)"))], bo, reads=[bo])
                    s_prev_ub[0] = ub

                s_prev_ub = [None]
                nseq = len(seqs)
                a2_load(0); a2_load(1)
                stageA(0); stageC1(0); stageC2(0)
                for i in range(nseq):
                    if i + 2 < nseq:
                        a2_load(i + 2)
                    if i + 1 < nseq:
                        stageA(i + 1); stageC1(i + 1)
                    stageB(i)
                    if i + 1 < nseq:
                        stageC2(i + 1)
                P.barrier()
                P.emit_block()
                if KSTOP == 3:
                    return nc

        with ExitStack() as st:
            P.stack = st
            wo = P.sbuf("wo", [128, 3, 4, D], BF16, dma=True)
            fl = []
            for b in range(3):
                for kc in range(4):
                    fl.append(lambda be, b=b, kc=kc: be.dma_start(out=wo[:, b, kc, :], in_=w_o[b][kc * 128:(kc + 1) * 128, :]))
            P.dma("pool", fl, wo, writes=[wo])
            wout = P.sbuf("wout", [128, 8, D], BF16, dma=True)
            load_w_cast(wout, lambda kc, c0, cw: wout[:, kc, c0:c0 + cw], w_out, 1024, D, 8)
            brL = Ring([P.sbuf("brL%d" % i, [128, 3, 4, 128], BF16, dma=True) for i in range(3)])
            gtL = Ring([P.sbuf("gtL%d" % i, [128, 3, D], BF16, dma=True) for i in range(3)])
            xL = Ring([P.sbuf("xL%d" % i, [128, D], F32, dma=True) for i in range(3)])
            mixed = P.sbuf("mixed", [128, D], F32); tmpm = P.sbuf("tmpm", [128, D], F32)
            mxb = P.sbuf("mxb", [128, D], BF16); mxT = P.sbuf("mxT", [128, 8, 128], BF16)
            x1o = Ring([P.sbuf("x1o%d" % i, [128, D], F32, dma=True) for i in range(2)])
            pt = Ring([P.psum("pt3_%d" % i, [128, D], F32) for i in range(2)])
            pT = P.psum("pT3", [128, 8, 128], BF16)
            pO = P.psum("pO3", [128, D], F32)
            l3 = {}

            def a3_load(t):
                d = dict(br=brL.next(), gt=gtL.next(), x=xL.next())
                l3[t] = d
                P.dma("sp", [lambda be: be.dma_start(out=d["br"][:].rearrange("p a b c -> p (a b c)"), in_=s_br[t])], d["br"], writes=[d["br"]])
                P.dma("sp", [lambda be: be.dma_start(out=d["gt"][:].rearrange("p a b -> p (a b)"), in_=s_gate[t])], d["gt"], writes=[d["gt"]])
                P.dma("sp", [lambda be: be.dma_start(out=d["x"][:], in_=xsrc(t))], d["x"], writes=[d["x"]])

            def a3_tile(t):
                d = l3.pop(t)
                brt = d["br"]; gt = d["gt"]; xb = d["x"]
                for b in range(3):
                    ps = pt.next()
                    for half in range(2):
                        for kc in range(4):
                            P.op("pe", lambda be, ps=ps, b=b, half=half, kc=kc: be.matmul(ps[:, half * 512:(half + 1) * 512], lhsT=brt[:, b, kc, :], rhs=wo[:, b, kc, half * 512:(half + 1) * 512], start=(kc == 0), stop=(kc == 3)), reads=[brt, wo], writes=[ps])
                    if b == 0:
                        P.op("dve", lambda be, ps=ps: be.tensor_tensor(out=mixed[:], in0=ps[:], in1=gt[:, 0, :], op=ALU.mult), reads=[ps, gt], writes=[mixed])
                    else:
                        P.op("dve", lambda be, ps=ps, b=b: be.tensor_tensor(out=tmpm[:], in0=ps[:], in1=gt[:, b, :], op=ALU.mult), reads=[ps, gt], writes=[tmpm])
                        if b == 1:
                            P.op("pool", lambda be: be.tensor_tensor(out=mixed[:], in0=mixed[:], in1=tmpm[:], op=ALU.add), reads=[mixed, tmpm], writes=[mixed])
                        else:
                            P.op("pool", lambda be: be.tensor_tensor(out=mxb[:], in0=mixed[:], in1=tmpm[:], op=ALU.add), reads=[mixed, tmpm], writes=[mxb])
                transpose_to(pT, pT, mxb, mxb, 8, mxT, mxT[:])
                for half in range(2):
                    for kc in range(8):
                        P.op("pe", lambda be, half=half, kc=kc: be.matmul(pO[:, half * 512:(half + 1) * 512], lhsT=mxT[:, kc, :], rhs=wout[:, kc, half * 512:(half + 1) * 512], start=(kc == 0), stop=(kc == 7)), reads=[mxT, wout], writes=[pO])
                xo = x1o.next()
                P.op("dve", lambda be, xo=xo: be.tensor_tensor(out=xo[:], in0=pO[:], in1=xb[:], op=ALU.add), reads=[pO, xb], writes=[xo])
                P.dma("sp", [lambda be, xo=xo: be.dma_start(out=s_x1[t], in_=xo[:])], xo, reads=[xo])

            a3_load(0); a3_load(1)
            for t in range(NT):
                if t + 2 < NT:
                    a3_load(t + 2)
                a3_tile(t)
            P.barrier()
            P.emit_block()
            if KSTOP == 4:
                return nc

        with ExitStack() as st:
            P.stack = st
            wg = P.sbuf("wg", [128, 8, DFF], BF16, dma=True)
            load_w_cast(wg, lambda kc, c0, cw: wg[:, kc, c0:c0 + cw], w_gate, 1024, DFF, 8)
            wu = P.sbuf("wu", [128, 8, DFF], BF16, dma=True)
            load_w_cast(wu, lambda kc, c0, cw: wu[:, kc, c0:c0 + cw], w_up, 1024, DFF, 8)
            wd = P.sbuf("wd", [128, 22, D], BF16, dma=True)
            load_w_cast(wd, lambda kc, c0, cw: wd[:, kc, c0:c0 + cw], w_down, DFF, D, 22)
            gffn = P.sbuf("gffn", [128, D], F32, dma=True)
            P.dma("sp", [lambda be: be.dma_start(out=gffn[:], in_=g_ffn.partition_broadcast(128))], gffn, writes=[gffn])
            x1L = Ring([P.sbuf("x1L%d" % i, [128, D], F32, dma=True) for i in range(3)])
            junk = P.sbuf("junkb", [128, D], BF16)
            ss = P.sbuf("ssb", [128, 1], F32); rs = P.sbuf("rsb", [128, 1], F32)
            hb = P.sbuf("hbb", [128, D], BF16); hT = P.sbuf("hTb", [128, 8, 128], BF16)
            sg = Ring([P.sbuf("sg%d" % i, [128, 4, 128], F32) for i in range(2)])
            gT = P.sbuf("gT", [128, 22, 128], BF16)
            yo = Ring([P.sbuf("yo%d" % i, [128, D], F32, dma=True) for i in range(2)])
            pT = P.psum("pTb", [128, 8, 128], BF16)
            pG = Ring([P.psum("pG%d" % i, [128, 4, 128], F32) for i in range(2)])
            pU = Ring([P.psum("pU%d" % i, [128, 4, 128], F32) for i in range(2)])
            pO = P.psum("pOb", [128, D], F32)
            lb = {}

            def b_load(t):
                xb = x1L.next()
                lb[t] = xb
                P.dma("sp", [lambda be: be.dma_start(out=xb[:], in_=s_x1[t])], xb, writes=[xb])

            def b_tile(t):
                xb = lb.pop(t)
                rms_rows(xb, xb[:], gffn, hb, D, junk, ss, rs)
                transpose_to(pT, pT, hb, hb, 8, hT, hT[:])
                for fg in range(6):
                    nf = min(4, 22 - 4 * fg)
                    pg = pG.next(); pu = pU.next()
                    for j in range(nf):
                        fc = 4 * fg + j
                        for kc in range(8):
                            P.op("pe", lambda be, pg=pg, j=j, fc=fc, kc=kc: be.matmul(pg[:, j, :], lhsT=wg[:, kc, fc * 128:(fc + 1) * 128], rhs=hT[:, kc, :], start=(kc == 0), stop=(kc == 7)), reads=[wg, hT], writes=[pg])
                    for j in range(nf):
                        fc = 4 * fg + j
                        for kc in range(8):
                            P.op("pe", lambda be, pu=pu, j=j, fc=fc, kc=kc: be.matmul(pu[:, j, :], lhsT=wu[:, kc, fc * 128:(fc + 1) * 128], rhs=hT[:, kc, :], start=(kc == 0), stop=(kc == 7)), reads=[wu, hT], writes=[pu])
                    s_ = sg.next()
                    P.op("act", lambda be, pg=pg, s_=s_, nf=nf: be.activation(out=s_[:, 0:nf, :], in_=pg[:, 0:nf, :], func=AF.Silu), reads=[pg], writes=[s_])
                    P.op("dve", lambda be, pu=pu, s_=s_, nf=nf, fg=fg: be.tensor_tensor(out=gT[:, 4 * fg:4 * fg + nf, :], in0=pu[:, 0:nf, :], in1=s_[:, 0:nf, :], op=ALU.mult), reads=[pu, s_], writes=[gT])
                for half in range(2):
                    for fc in range(22):
                        P.op("pe", lambda be, half=half, fc=fc: be.matmul(pO[:, half * 512:(half + 1) * 512], lhsT=gT[:, fc, :], rhs=wd[:, fc, half * 512:(half + 1) * 512], start=(fc == 0), stop=(fc == 21)), reads=[gT, wd], writes=[pO])
                yb = yo.next()
                P.op("dve", lambda be, yb=yb: be.tensor_tensor(out=yb[:], in0=pO[:], in1=xb[:], op=ALU.add), reads=[pO, xb], writes=[yb])
                if t < NTP:
                    P.dma("sp", [lambda be, yb=yb: be.dma_start(out=y_p[t * 128:(t + 1) * 128, :], in_=yb[:])], yb, reads=[yb])
                else:
                    P.dma("sp", [lambda be, yb=yb: be.dma_start(out=y_s[t - NTP], in_=yb[0:64, :])], yb, reads=[yb])

            b_load(0); b_load(1)
            for t in range(NT):
                if t + 2 < NT:
                    b_load(t + 2)
                b_tile(t)
            P.barrier()
            P.emit_block()
            if KSTOP == 5:
                return nc
    return nc


def _consts():
    NT = NTP + 2
    theta = np.float32(10000.0)
    tab = np.zeros((NT, 128, 96), np.float32)
    inv64 = (theta ** (-np.arange(32, dtype=np.float32) / np.float32(32))).astype(np.float32)
    inv32 = (theta ** (-np.arange(16, dtype=np.float32) / np.float32(16))).astype(np.float32)
    for t in range(NT):
        pos = (np.arange(128) + (t * 128 if t < NTP else 1024)).astype(np.float32)
        a64 = (pos[:, None] * inv64[None, :]).astype(np.float32)
        a32 = (pos[:, None] * inv32[None, :]).astype(np.float32)
        tab[t, :, 0:32] = np.cos(a64.astype(np.float64)); tab[t, :, 32:64] = np.sin(a64.astype(np.float64))
        tab[t, :, 64:80] = np.cos(a32.astype(np.float64)); tab[t, :, 80:96] = np.sin(a32.astype(np.float64))
    bands = np.zeros((128, 3, 4, 128), np.float32)
    tp = np.arange(128)[:, None]; tq = np.arange(128)[None, :]
    for g, w in enumerate((2, 4, 8, 16)):
        inwin = (tp <= tq) & (tp > tq - w)
        bands[:, 0, g, :] = inwin / w - (tp == tq)
        bands[:, 1, g, :] = ((tp - 128) > (tq - w)) / w
        cntf = np.minimum(w, tq + 1).astype(np.float64)
        bands[:, 2, g, :] = inwin / cntf - (tp == tq)
    return tab, bands.astype(np.float32), np.eye(128, dtype=np.float32)


_CACHE = {}


def kernel(x_prompt, x_sample, mem_prompt, cache_a_k, cache_a_v, cache_idx_k, cache_pool, cache_mem_k,
           cache_mem_v, g_mix, w_in, g_qa, g_ka, g_kidx, g_qm, g_mem, w_mem_kv, g_km, w_pool, s_pool,
           w_oa, w_ob, w_om, w_out, g_ffn, w_gate, w_up, w_down):
    f = lambda a: np.ascontiguousarray(np.asarray(a, dtype=np.float32))
    if "nc" not in _CACHE:
        _CACHE["nc"] = build_program()
        _CACHE["consts"] = _consts()
    nc = _CACHE["nc"]
    tab, bands, ident = _CACHE["consts"]
    xs_pad = np.zeros((16, 128, D), np.float32)
    xs_pad[:, 0:64, :] = f(x_sample)
    shared = {
        "w_in": f(w_in[0]), "w_mem": f(w_mem_kv[0]), "w_pool": f(w_pool[0]), "w_oa": f(w_oa[0]), "w_ob": f(w_ob[0]),
        "w_om": f(w_om[0]), "w_out": f(w_out[0]), "w_gate": f(w_gate[0]), "w_up": f(w_up[0]), "w_down": f(w_down[0]),
        "g_mix": f(g_mix), "g_ffn": f(g_ffn), "g_mem": f(g_mem), "g_qa": f(g_qa), "g_ka": f(g_ka), "g_kidx": f(g_kidx),
        "g_qm": f(g_qm), "g_km": f(g_km), "s_pool": f(np.asarray(s_pool[0]).reshape(4, 128).T),
        "rope": tab, "bands": bands, "ident": ident,
    }
    in_maps = []
    for c in range(8):
        m = dict(shared)
        m["x_p"] = f(x_prompt[c]); m["x_s"] = np.ascontiguousarray(xs_pad[2 * c:2 * c + 2]); m["mem"] = f(mem_prompt[c])
        m["cak"] = f(np.asarray(cache_a_k[0, 2 * c:2 * c + 2]).reshape(2, 1024, 128))
        m["cav"] = f(np.asarray(cache_a_v[0, 2 * c:2 * c + 2]).reshape(2, 1024, 128))
        m["cik"] = f(cache_idx_k[0, 2 * c:2 * c + 2]); m["cpool"] = f(cache_pool[0, 2 * c:2 * c + 2])
        m["cmk"] = f(np.asarray(cache_mem_k[0, 2 * c:2 * c + 2]).reshape(2, 256, 512))
        m["cmv"] = f(np.asarray(cache_mem_v[0, 2 * c:2 * c + 2]).reshape(2, 256, 512))
        in_maps.append(m)
    res = run_bass_kernel_spmd(nc, in_maps, core_ids=list(range(8)))
    R = res.results
    cat = lambda k: np.stack([np.asarray(r[k], dtype=np.float32) for r in R], 0)
    cat2 = lambda k: np.concatenate([np.asarray(r[k], dtype=np.float32) for r in R], 0)
    y_prompt = cat("y_p")
    y_sample = cat2("y_s")
    return (
        y_prompt, y_sample,
        cat("nak_p").reshape(1, 8, 8192, 2, 64), cat("nav_p").reshape(1, 8, 8192, 2, 64), cat("nik_p").reshape(1, 8, 8192, 32),
        cat("npool_p").reshape(1, 8, 15, 512), cat("nmk_p").reshape(1, 8, 256, 4, 128), cat("nmv_p").reshape(1, 8, 256, 4, 128),
        cat2("nak_s").reshape(1, 16, 64, 2, 64), cat2("nav_s").reshape(1, 16, 64, 2, 64), cat2("nik_s").reshape(1, 16, 64, 32),
        cat2("npool_s").reshape(1, 16, 15, 512),
    )
```

```python
from contextlib import ExitStack
import numpy as np
import concourse.bass as bass
import concourse.mybir as mybir
from concourse.bass_utils import run_bass_kernel_spmd

F32 = mybir.dt.float32
BF16 = mybir.dt.bfloat16
AF = mybir.ActivationFunctionType
ALU = mybir.AluOpType
AX = mybir.AxisListType


class Ctr:
    def __init__(self, name, sem):
        self.name = name
        self.sem = sem
        self.count = 0


class Buf:
    def __init__(self, name, ap, ctr=None):
        self.name = name
        self.ap = ap
        self.ctr = ctr
        self.w = None
        self.r = {}

    def __getitem__(self, k):
        return self.ap[k]


class Eng:
    def __init__(self, name, be, ctr):
        self.name = name
        self.be = be
        self.ctr = ctr
        self.ops = []
        self.seen = {}


class Prog:
    def __init__(self, nc, stack):
        self.nc = nc
        self.gstack = stack
        self.stack = stack
        self.engs = {}
        self.nsem = 0
        for nm, be in (("pe", nc.tensor), ("act", nc.scalar), ("dve", nc.vector),
                       ("pool", nc.gpsimd), ("sp", nc.sync)):
            self.engs[nm] = Eng(nm, be, self.new_ctr("e_" + nm))
        self.dma_ctrs = []

    def new_ctr(self, name):
        sem = self.gstack.enter_context(self.nc.semaphore(name))
        self.nsem += 1
        return Ctr(name, sem)

    def sbuf(self, name, shape, dtype, dma=False):
        t = self.stack.enter_context(self.nc.sbuf_tensor(name, list(shape), dtype))
        return self.buf(name, t, dma)

    def psum(self, name, shape, dtype):
        t = self.stack.enter_context(self.nc.psum_tensor(name, list(shape), dtype))
        return Buf(name, t)

    def buf(self, name, ap, dma=False):
        c = None
        if dma:
            c = self.new_ctr("d_" + name)
            self.dma_ctrs.append(c)
        return Buf(name, ap, c)

    def _deps(self, eng, reads, writes, skip_same_pe=False):
        deps = {}

        def add(cv):
            if cv is None:
                return
            c, v = cv
            if skip_same_pe and c is eng.ctr:
                return
            if deps.get(c, 0) < v:
                deps[c] = v

        for b in reads:
            add(b.w)
        for b in writes:
            add(b.w)
            for c, v in b.r.items():
                add((c, v))
        waits = []
        for c, v in deps.items():
            if eng.seen.get(c, 0) < v:
                eng.seen[c] = v
                waits.append((c.sem, v))
        return waits

    def _cut(self):
        import os
        self.nrec = getattr(self, "nrec", 0) + 1
        return self.nrec > int(os.environ.get("OPCUT", "1000000000"))

    def op(self, engname, fn, reads=(), writes=()):
        if self._cut():
            return 0
        eng = self.engs[engname]
        waits = self._deps(eng, reads, writes, skip_same_pe=(engname == "pe"))
        eng.ctr.count += 1
        val = eng.ctr.count
        sem = eng.ctr.sem

        def emit(be, waits=waits, fn=fn, sem=sem):
            for s, v in waits:
                be.wait_ge(s, v)
            fn(be).then_inc(sem, 1)

        eng.ops.append(emit)
        for b in writes:
            b.w = (eng.ctr, val)
            b.r = {}
        for b in reads:
            if b not in writes:
                b.r[eng.ctr] = val
        return val

    def dma(self, engname, fns, ctrbuf, reads=(), writes=()):
        if self._cut():
            return 0
        eng = self.engs[engname]
        ctr = ctrbuf.ctr
        assert ctr is not None, ctrbuf.name
        waits = self._deps(eng, reads, writes)
        ctr.count += 16 * len(fns)
        val = ctr.count
        sem = ctr.sem

        def emit(be, waits=waits, fns=fns, sem=sem):
            for s, v in waits:
                be.wait_ge(s, v)
            for f in fns:
                f(be).then_inc(sem, 16)

        eng.ops.append(emit)
        for b in writes:
            b.w = (ctr, val)
            b.r = {}
        for b in reads:
            if b not in writes:
                b.r[ctr] = val
        return val

    def barrier(self):
        targets = [(e.ctr, e.ctr.count) for e in self.engs.values()]
        targets += [(c, c.count) for c in self.dma_ctrs]
        for eng in self.engs.values():
            waits = []
            for c, v in targets:
                if v > 0 and c is not eng.ctr and eng.seen.get(c, 0) < v:
                    eng.seen[c] = v
                    waits.append((c.sem, v))

            def emit(be, waits=waits):
                for s, v in waits:
                    be.wait_ge(s, v)

            eng.ops.append(emit)

    def emit_block(self):
        nc = self.nc
        ops = {k: e.ops for k, e in self.engs.items()}
        for e in self.engs.values():
            e.ops = []
        with nc.Block() as block:
            @block.tensor
            def _(be):
                for f in ops["pe"]:
                    f(be)

            @block.scalar
            def _(be):
                for f in ops["act"]:
                    f(be)

            @block.vector
            def _(be):
                for f in ops["dve"]:
                    f(be)

            @block.gpsimd
            def _(be):
                for f in ops["pool"]:
                    f(be)

            @block.sync
            def _(be):
                for f in ops["sp"]:
                    f(be)

D = 1024
NTP = 64
DIN = 5160
DFF = 2816
EPS = 1e-6
NEG = -1.0e30
NBIS = 12
IDX_SCALE = 256.0 ** -0.5


class Ring:
    def __init__(self, bufs):
        self.bufs = bufs
        self.i = 0

    def next(self):
        b = self.bufs[self.i % len(self.bufs)]
        self.i += 1
        return b


def build_program(dbg=False):
    import os as _os
    KSTOP = int(_os.environ.get('KSTOP', '99'))
    NT = NTP + 2
    SP = NTP * 128
    SCW = max(SP, 1152)
    nc = bass.Bass("TRN2", target_bir_lowering=False)

    def din(name, shape, dt=F32):
        return nc.dram_tensor(name, list(shape), dt, kind="ExternalInput").ap()

    def dout(name, shape, dt=F32):
        return nc.dram_tensor(name, list(shape), dt, kind="ExternalOutput").ap()

    def dscr(name, shape, dt):
        return nc.dram_tensor(name, list(shape), dt, kind=("ExternalOutput" if dbg else "Internal")).ap()

    x_p = din("x_p", [SP, D]); x_s = din("x_s", [2, 128, D]); mem = din("mem", [256, D])
    cak = din("cak", [2, 1024, 128]); cav = din("cav", [2, 1024, 128]); cik = din("cik", [2, 1024, 32])
    cpool = din("cpool", [2, 15, 512]); cmk = din("cmk", [2, 256, 512]); cmv = din("cmv", [2, 256, 512])
    w_in = din("w_in", [D, DIN]); w_mem = din("w_mem", [D, 1024]); w_pool = din("w_pool", [4, 128, 128])
    w_o = [din("w_oa", [512, D]), din("w_ob", [512, D]), din("w_om", [512, D])]
    w_out = din("w_out", [D, D]); w_gate = din("w_gate", [D, DFF]); w_up = din("w_up", [D, DFF]); w_down = din("w_down", [DFF, D])
    g_mix = din("g_mix", [1, D]); g_ffn = din("g_ffn", [1, D]); g_mem = din("g_mem", [1, D])
    g_qa = din("g_qa", [1, 64]); g_ka = din("g_ka", [1, 64]); g_kidx = din("g_kidx", [1, 32])
    g_qm = din("g_qm", [1, 128]); g_km = din("g_km", [1, 128]); s_pool = din("s_pool", [128, 4])
    rope = din("rope", [NT, 128, 96]); bands = din("bands", [128, 3, 4, 128]); ident = din("ident", [128, 128])

    y_p = dout("y_p", [SP, D]); y_s = dout("y_s", [2, 64, D])
    nak_p = dout("nak_p", [SP, 128]); nav_p = dout("nav_p", [SP, 128]); nik_p = dout("nik_p", [SP, 32])
    npool_p = dout("npool_p", [15, 512]); nmk_p = dout("nmk_p", [256, 512]); nmv_p = dout("nmv_p", [256, 512])
    nak_s = dout("nak_s", [2, 64, 128]); nav_s = dout("nav_s", [2, 64, 128]); nik_s = dout("nik_s", [2, 64, 32])
    npool_s = dout("npool_s", [2, 15, 512])

    s_qa = dscr("s_qa", [NT, 128, 512], BF16); s_qi = dscr("s_qi", [NT, 128, 384], BF16)
    s_qm = dscr("s_qm", [NT, 128, 512], BF16); s_ub = dscr("s_ub", [NT, 128, 512], BF16)
    s_wi = dscr("s_wi", [NT, 128, 8], F32); s_gate = dscr("s_gate", [NT, 128, 3072], BF16)
    s_KT = dscr("s_KT", [NT, 128, 128], BF16); s_V = dscr("s_V", [NT, 128, 128], BF16); s_KI = dscr("s_KI", [NT, 128, 128], BF16)
    s_br = dscr("s_br", [NT, 128, 1536], BF16); s_x1 = dscr("s_x1", [NT, 128, D], F32)

    def xsrc(t):
        return x_p[t * 128:(t + 1) * 128, :] if t < NTP else x_s[t - NTP]

    with ExitStack() as gst:
        P = Prog(nc, gst)

        def bc(ap2, shape):
            return ap2.unsqueeze(1).to_broadcast(shape)

        def bl(ap2, shape):
            return ap2.unsqueeze(2).to_broadcast(shape)

        idb = P.sbuf("idb", [128, 128], BF16)
        gq = P.sbuf("gq", [128, 64 + 64 + 32 + 128 + 128], F32, dma=True)
        spl = P.sbuf("spl", [128, 4], F32, dma=True)
        with ExitStack() as st0:
            P.stack = st0
            idf = P.sbuf("idf", [128, 128], F32, dma=True)
            P.dma("sp", [lambda be: be.dma_start(out=idf[:], in_=ident)], idf, writes=[idf])
            P.op("dve", lambda be: be.tensor_copy(out=idb[:], in_=idf[:]), reads=[idf], writes=[idb])
            offs = {}
            o = 0
            fl = []
            for nm, ap_, n in (("qa", g_qa, 64), ("ka", g_ka, 64), ("kidx", g_kidx, 32), ("qm", g_qm, 128), ("km", g_km, 128)):
                offs[nm] = (o, n)
                fl.append(lambda be, ap_=ap_, o=o, n=n: be.dma_start(out=gq[:, o:o + n], in_=ap_.partition_broadcast(128)))
                o += n
            P.dma("sp", fl, gq, writes=[gq])
            P.dma("sp", [lambda be: be.dma_start(out=spl[:], in_=s_pool)], spl, writes=[spl])
            P.barrier()
            P.emit_block()
            if KSTOP == 0:
                return nc
        P.stack = gst

        def gvec(nm):
            o, n = offs[nm]
            return gq[:, o:o + n]

        def rms_rows(xb, xap, gb, hb, n, junk, ss, rs):
            P.op("act", lambda be: be.activation(out=junk[:, 0:n], in_=xap, func=AF.Square, accum_out=ss[:]), reads=[xb], writes=[junk, ss])
            P.op("dve", lambda be: be.tensor_scalar(out=rs[:], in0=ss[:], scalar1=1.0 / n, scalar2=EPS, op0=ALU.mult, op1=ALU.add), reads=[ss], writes=[rs])
            P.op("act", lambda be: be.activation(out=rs[:], in_=rs[:], func=AF.Sqrt), reads=[rs], writes=[rs])
            P.op("dve", lambda be: be.reciprocal(out=rs[:], in_=rs[:]), reads=[rs], writes=[rs])
            P.op("dve", lambda be: be.scalar_tensor_tensor(out=hb[:], in0=xap, scalar=rs[:, 0:1], in1=gb[:], op0=ALU.mult, op1=ALU.mult), reads=[xb, rs, gb], writes=[hb])

        def head_norm(srcb, src3, gap, outb, out3, H, hd, sq, ssq, eng="dve"):
            sh = [128, H, hd]
            sqv = sq[:, 0:H * hd].rearrange("p (h d) -> p h d", h=H)
            P.op(eng, lambda be: be.tensor_tensor(out=sqv, in0=src3, in1=src3, op=ALU.mult), reads=[srcb], writes=[sq])
            P.op("dve", lambda be: be.tensor_reduce(out=ssq[:, 0:H], in_=sqv, axis=AX.X, op=ALU.add), reads=[sq], writes=[ssq])
            P.op("dve", lambda be: be.tensor_scalar(out=ssq[:, 0:H], in0=ssq[:, 0:H], scalar1=1.0 / hd, scalar2=EPS, op0=ALU.mult, op1=ALU.add), reads=[ssq], writes=[ssq])
            P.op("act", lambda be: be.activation(out=ssq[:, 0:H], in_=ssq[:, 0:H], func=AF.Sqrt), reads=[ssq], writes=[ssq])
            P.op("dve", lambda be: be.reciprocal(out=ssq[:, 0:H], in_=ssq[:, 0:H]), reads=[ssq], writes=[ssq])
            P.op(eng, lambda be: be.tensor_tensor(out=sqv, in0=src3, in1=bl(ssq[:, 0:H], sh), op=ALU.mult), reads=[srcb, ssq], writes=[sq])
            P.op(eng, lambda be: be.tensor_tensor(out=out3, in0=sqv, in1=bc(gap, sh), op=ALU.mult), reads=[sq, gq], writes=[outb])

        def rope_ops(srcb, x1, x2, cs, sn, outb, o1, o2, tb, t1, t2, eng="dve"):
            P.op(eng, lambda be: be.tensor_tensor(out=t1, in0=x1, in1=cs, op=ALU.mult), reads=[srcb, rp_cur[0]], writes=[tb])
            P.op(eng, lambda be: be.tensor_tensor(out=t2, in0=x2, in1=sn, op=ALU.mult), reads=[srcb, rp_cur[0]], writes=[tb])
            P.op(eng, lambda be: be.tensor_tensor(out=o1, in0=t1, in1=t2, op=ALU.subtract), reads=[tb], writes=[outb])
            P.op(eng, lambda be: be.tensor_tensor(out=t1, in0=x2, in1=cs, op=ALU.mult), reads=[srcb, rp_cur[0]], writes=[tb])
            P.op(eng, lambda be: be.tensor_tensor(out=t2, in0=x1, in1=sn, op=ALU.mult), reads=[srcb, rp_cur[0]], writes=[tb])
            P.op(eng, lambda be: be.tensor_tensor(out=o2, in0=t1, in1=t2, op=ALU.add), reads=[tb], writes=[outb])

        rp_cur = [None]

        def load_w_cast(dst, dst_view_fn, src, rows, cols, kcs):
            fl = []
            for kc in range(kcs):
                c0 = 0
                while c0 < cols:
                    cw = min(2048, cols - c0)
                    fl.append(lambda be, kc=kc, c0=c0, cw=cw: be.dma_start(out=dst_view_fn(kc, c0, cw), in_=src[kc * 128:(kc + 1) * 128, c0:c0 + cw]))
                    c0 += cw
            P.dma("pool", fl, dst, writes=[dst])

        def transpose_to(psb, ps3, srcb, src2, n, dstb, dst3, evac="act"):
            for k in range(n):
                P.op("pe", lambda be, k=k: be.transpose(out=ps3[:, k, :], in_=src2[:, k * 128:(k + 1) * 128], identity=idb[:]), reads=[srcb, idb], writes=[psb])
            if evac == "act":
                P.op("act", lambda be: be.copy(out=dst3, in_=ps3[:, 0:n, :]), reads=[psb], writes=[dstb])
            else:
                P.op("dve", lambda be: be.tensor_copy(out=dst3, in_=ps3[:, 0:n, :]), reads=[psb], writes=[dstb])

        def interleave(gens):
            gens = list(gens)
            while gens:
                for g_ in list(gens):
                    try:
                        next(g_)
                    except StopIteration:
                        gens.remove(g_)

        def head_norm_g(srcb, src3, gap, outb, out3, H, hd, sq, ssq):
            sh = [128, H, hd]
            sqv = sq[:, 0:H * hd].rearrange("p (h d) -> p h d", h=H)
            P.op("dve", lambda be: be.tensor_tensor(out=sqv, in0=src3, in1=src3, op=ALU.mult), reads=[srcb], writes=[sq]); yield
            P.op("dve", lambda be: be.tensor_reduce(out=ssq[:, 0:H], in_=sqv, axis=AX.X, op=ALU.add), reads=[sq], writes=[ssq]); yield
            P.op("dve", lambda be: be.tensor_scalar(out=ssq[:, 0:H], in0=ssq[:, 0:H], scalar1=1.0 / hd, scalar2=EPS, op0=ALU.mult, op1=ALU.add), reads=[ssq], writes=[ssq]); yield
            P.op("act", lambda be: be.activation(out=ssq[:, 0:H], in_=ssq[:, 0:H], func=AF.Sqrt), reads=[ssq], writes=[ssq]); yield
            P.op("dve", lambda be: be.reciprocal(out=ssq[:, 0:H], in_=ssq[:, 0:H]), reads=[ssq], writes=[ssq]); yield
            P.op("dve", lambda be: be.tensor_tensor(out=sqv, in0=src3, in1=bl(ssq[:, 0:H], sh), op=ALU.mult), reads=[srcb, ssq], writes=[sq]); yield
            P.op("dve", lambda be: be.tensor_tensor(out=out3, in0=sqv, in1=bc(gap, sh), op=ALU.mult), reads=[sq, gq], writes=[outb]); yield

        def rope_g(srcb, x1, x2, cs, sn, rb, outb, o1, o2, tbs, tv):
            P.op("dve", lambda be: be.tensor_tensor(out=tv[0], in0=x1, in1=cs, op=ALU.mult), reads=[srcb, rb], writes=[tbs[0]]); yield
            P.op("dve", lambda be: be.tensor_tensor(out=tv[1], in0=x2, in1=sn, op=ALU.mult), reads=[srcb, rb], writes=[tbs[1]]); yield
            P.op("dve", lambda be: be.tensor_tensor(out=tv[2], in0=x2, in1=cs, op=ALU.mult), reads=[srcb, rb], writes=[tbs[2]]); yield
            P.op("dve", lambda be: be.tensor_tensor(out=tv[3], in0=x1, in1=sn, op=ALU.mult), reads=[srcb, rb], writes=[tbs[3]]); yield
            P.op("dve", lambda be: be.tensor_tensor(out=o1, in0=tv[0], in1=tv[1], op=ALU.subtract), reads=[tbs[0], tbs[1]], writes=[outb]); yield
            P.op("dve", lambda be: be.tensor_tensor(out=o2, in0=tv[2], in1=tv[3], op=ALU.add), reads=[tbs[2], tbs[3]], writes=[outb]); yield

        with ExitStack() as st:
            P.stack = st
            win = P.sbuf("win", [128, 8, DIN], BF16, dma=True)
            load_w_cast(win, lambda kc, c0, cw: win[:, kc, c0:c0 + cw], w_in, 1024, DIN, 8)
            gmix = P.sbuf("gmix", [128, D], F32, dma=True)
            P.dma("sp", [lambda be: be.dma_start(out=gmix[:], in_=g_mix.partition_broadcast(128))], gmix, writes=[gmix])
            xr = Ring([P.sbuf("xa%d" % i, [128, D], F32, dma=True) for i in range(3)])
            rpr = Ring([P.sbuf("rp%d" % i, [128, 96], F32, dma=True) for i in range(3)])
            junk = P.sbuf("junk1", [128, D], BF16)
            ssL = [P.sbuf("ss1_%d" % i, [128, 1], F32) for i in range(2)]; rsL = [P.sbuf("rs1_%d" % i, [128, 1], F32) for i in range(2)]
            hbL = [P.sbuf("hb1_%d" % i, [128, D], BF16) for i in range(2)]; hTL = [P.sbuf("hT1_%d" % i, [128, 8, 128], BF16) for i in range(2)]
            c0L = [P.sbuf("c0f%d" % i, [128, 512], F32) for i in range(2)]
            c1L = [P.sbuf("c1f%d" % i, [128, 512], F32, dma=True) for i in range(2)]
            c2L = [P.sbuf("c2f%d" % i, [128, 40], F32) for i in range(2)]
            c4L = [P.sbuf("c4f%d" % i, [128, 512], F32) for i in range(2)]
            sq_qa = P.sbuf("sq_qa", [128, 512], F32); ssq_qa = P.sbuf("ssq_qa", [128, 8], F32); qn_qa = P.sbuf("qn_qa", [128, 512], F32)
            tb_qa = [P.sbuf("tb_qa%d" % i, [128, 256], F32) for i in range(4)]
            sq_ka = P.sbuf("sq_ka", [128, 128], F32); ssq_ka = P.sbuf("ssq_ka", [128, 8], F32); qn_ka = P.sbuf("qn_ka", [128, 128], F32)
            tb_ka = [P.sbuf("tb_ka%d" % i, [128, 64], F32) for i in range(4)]
            tb_qi = [P.sbuf("tb_qi%d" % i, [128, 128], F32) for i in range(4)]
            sq_ki = P.sbuf("sq_ki", [128, 32], F32); ssq_ki = P.sbuf("ssq_ki", [128, 8], F32); qn_ki = P.sbuf("qn_ki", [128, 32], F32)
            tb_ki = [P.sbuf("tb_ki%d" % i, [128, 16], F32) for i in range(4)]
            sq_qm = P.sbuf("sq_qm", [128, 512], F32); ssq_qm = P.sbuf("ssq_qm", [128, 8], F32)
            qab = P.sbuf("qab", [128, 512], BF16)
            kof = Ring([P.sbuf("kof%d" % i, [128, 128], F32, dma=True) for i in range(2)])
            kbb = P.sbuf("kbb", [128, 128], BF16)
            qib = P.sbuf("qib", [128, 256], BF16)
            kif = Ring([P.sbuf("kif%d" % i, [128, 32], F32, dma=True) for i in range(2)])
            ki4 = P.sbuf("ki4", [128, 128], BF16)
            wif = Ring([P.sbuf("wif%d" % i, [128, 8], F32, dma=True) for i in range(2)])
            ubb = Ring([P.sbuf("ubb%d" % i, [128, 512], BF16, dma=True) for i in range(2)])
            ubf = P.sbuf("ubf", [128, 512], F32, dma=True)
            qmb = P.sbuf("qmb", [128, 512], BF16)
            qaT = Ring([P.sbuf("qaT%d" % i, [128, 4, 128], BF16, dma=True) for i in range(2)])
            qiT = Ring([P.sbuf("qiT%d" % i, [128, 3, 128], BF16, dma=True) for i in range(2)])
            qmT = Ring([P.sbuf("qmT%d" % i, [128, 4, 128], BF16, dma=True) for i in range(2)])
            gsb = Ring([P.sbuf("gsb%d" % i, [128, 6, 512], BF16, dma=True) for i in range(2)])
            ktT = Ring([P.sbuf("ktT%d" % i, [128, 128], BF16, dma=True) for i in range(2)])
            vbT = Ring([P.sbuf("vbT%d" % i, [128, 128], BF16, dma=True) for i in range(2)])
            kiT = Ring([P.sbuf("kiT%d" % i, [128, 128], BF16, dma=True) for i in range(2)])
            pT = P.psum("pT1", [128, 8, 128], BF16)
            pP = Ring([P.psum("pP1_%d" % i, [128, 512], F32) for i in range(4)])
            pR = Ring([P.psum("pR1_%d" % i, [128, 8, 128], BF16) for i in range(2)])

            xbufs = {}
            rbufs = {}
            tst = {}

            def a1_load(t):
                xb = xr.next(); rb = rpr.next()
                xbufs[t] = xb; rbufs[t] = rb
                P.dma("sp", [lambda be: be.dma_start(out=xb[:], in_=xsrc(t))], xb, writes=[xb])
                P.dma("sp", [lambda be: be.dma_start(out=rb[:], in_=rope[t])], rb, writes=[rb])

            def a1_R(t):
                par = t % 2
                xb = xbufs.pop(t)
                rms_rows(xb, xb[:], gmix, hbL[par], D, junk, ssL[par], rsL[par])
                transpose_to(pT, pT, hbL[par], hbL[par], 8, hTL[par], hTL[par][:])

            def a1_M(t):
                par = t % 2
                hT = hTL[par]
                samp = t >= NTP
                b = t - NTP

                def proj(ps, c0, cw):
                    for kc in range(8):
                        P.op("pe", lambda be, kc=kc: be.matmul(ps[:, 0:cw], lhsT=hT[:, kc, :], rhs=win[:, kc, c0:c0 + cw], start=(kc == 0), stop=(kc == 7)), reads=[hT, win], writes=[ps])

                c0f, c1, c2f, c4f = c0L[par], c1L[par], c2L[par], c4L[par]
                ps0 = pP.next(); proj(ps0, 0, 512)
                P.op("act", lambda be: be.copy(out=c0f[:], in_=ps0[:]), reads=[ps0], writes=[c0f])
                ps1 = pP.next(); proj(ps1, 512, 512)
                P.op("act", lambda be: be.copy(out=c1[:], in_=ps1[:]), reads=[ps1], writes=[c1])
                ps2 = pP.next(); proj(ps2, 1024, 40)
                P.op("act", lambda be: be.copy(out=c2f[:], in_=ps2[:, 0:40]), reads=[ps2], writes=[c2f])
                ps3 = pP.next(); proj(ps3, 1064, 512)
                ub_ = ubb.next()
                P.op("act", lambda be: be.copy(out=ub_[:], in_=ps3[:]), reads=[ps3], writes=[ub_])
                P.dma("sp", [lambda be: be.dma_start(out=s_ub[t], in_=ub_[:])], ub_, reads=[ub_])
                if samp or t == NTP - 1:
                    P.op("act", lambda be: be.copy(out=ubf[:], in_=ps3[:]), reads=[ps3], writes=[ubf])
                    if samp:
                        P.dma("sp", [lambda be: be.dma_start(out=npool_s[b], in_=ubf[49:64, :])], ubf, reads=[ubf])
                    else:
                        P.dma("sp", [lambda be: be.dma_start(out=npool_p, in_=ubf[113:128, :])], ubf, reads=[ubf])
                ps4 = pP.next(); proj(ps4, 1576, 512)
                P.op("act", lambda be: be.copy(out=c4f[:], in_=ps4[:]), reads=[ps4], writes=[c4f])
                gs = gsb.next()
                for j in range(6):
                    ps = pP.next(); proj(ps, 2088 + 512 * j, 512)
                    P.op("act", lambda be, ps=ps, j=j: be.activation(out=gs[:, j, :], in_=ps[:], func=AF.Sigmoid), reads=[ps], writes=[gs])
                P.dma("sp", [lambda be: be.dma_start(out=s_gate[t], in_=gs[:].rearrange("p a b -> p (a b)"))], gs, reads=[gs])

            def a1_C(t):
                par = t % 2
                rb = rbufs.pop(t)
                c0f, c1, c2f, c4f = c0L[par], c1L[par], c2L[par], c4L[par]
                cos64 = rb[:, 0:32]; sin64 = rb[:, 32:64]; cos32 = rb[:, 64:80]; sin32 = rb[:, 80:96]
                ko = kof.next(); kio = kif.next(); wo = wif.next()
                tst[t] = dict(ko=ko, kio=kio, wo=wo, c1=c1)

                def ch_qa():
                    yield from head_norm_g(c0f, c0f[:].rearrange("p (h d) -> p h d", h=8), gvec("qa"), qn_qa, qn_qa[:].rearrange("p (h d) -> p h d", h=8), 8, 64, sq_qa, ssq_qa)
                    qn4 = qn_qa[:].rearrange("p (k g d) -> p k g d", k=2, g=4)
                    qo4 = qab[:].rearrange("p (g k d) -> p k g d", g=4, k=2)
                    sh4 = [128, 2, 4, 32]
                    cs4 = cos64.unsqueeze(1).unsqueeze(1).to_broadcast(sh4)
                    sn4 = sin64.unsqueeze(1).unsqueeze(1).to_broadcast(sh4)
                    tv = [tb_[:].rearrange("p (k g d) -> p k g d", k=2, g=4) for tb_ in tb_qa]
                    yield from rope_g(qn_qa, qn4[:, :, :, 0:32], qn4[:, :, :, 32:64], cs4, sn4, rb, qab, qo4[:, :, :, 0:32], qo4[:, :, :, 32:64], tb_qa, tv)

                def ch_ka():
                    yield from head_norm_g(c1, c1[:, 0:128].rearrange("p (h d) -> p h d", h=2), gvec("ka"), qn_ka, qn_ka[:].rearrange("p (h d) -> p h d", h=2), 2, 64, sq_ka, ssq_ka)
                    k3 = qn_ka[:].rearrange("p (h d) -> p h d", h=2)
                    ko3 = ko[:].rearrange("p (h d) -> p h d", h=2)
                    sh3 = [128, 2, 32]
                    tv = [tb_[:].rearrange("p (h d) -> p h d", h=2) for tb_ in tb_ka]
                    yield from rope_g(qn_ka, k3[:, :, 0:32], k3[:, :, 32:64], bc(cos64, sh3), bc(sin64, sh3), rb, ko, ko3[:, :, 0:32], ko3[:, :, 32:64], tb_ka, tv)
                    P.op("pool", lambda be: be.tensor_copy(out=kbb[:], in_=ko[:]), reads=[ko], writes=[kbb]); yield

                def ch_qi():
                    qi3 = c1[:, 256:512].rearrange("p (h d) -> p h d", h=8)
                    qo3 = qib[:].rearrange("p (h d) -> p h d", h=8)
                    sh3 = [128, 8, 16]
                    tv = [tb_[:].rearrange("p (h d) -> p h d", h=8) for tb_ in tb_qi]
                    yield from rope_g(c1, qi3[:, :, 0:16], qi3[:, :, 16:32], bc(cos32, sh3), bc(sin32, sh3), rb, qib, qo3[:, :, 0:16], qo3[:, :, 16:32], tb_qi, tv)

                def ch_ki():
                    yield from head_norm_g(c2f, c2f[:, 0:32].unsqueeze(1), gvec("kidx"), qn_ki, qn_ki[:, 0:32].unsqueeze(1), 1, 32, sq_ki, ssq_ki)
                    tv = [tb_[:] for tb_ in tb_ki]
                    yield from rope_g(qn_ki, qn_ki[:, 0:16], qn_ki[:, 16:32], cos32, sin32, rb, kio, kio[:, 0:16], kio[:, 16:32], tb_ki, tv)
                    P.op("pool", lambda be: be.tensor_copy(out=ki4[:].rearrange("p (r c) -> p r c", r=4), in_=kio[:].unsqueeze(1).to_broadcast([128, 4, 32])), reads=[kio], writes=[ki4]); yield
                    P.op("dve", lambda be: be.tensor_scalar(out=wo[:], in0=c2f[:, 32:40], scalar1=IDX_SCALE, scalar2=None, op0=ALU.mult), reads=[c2f], writes=[wo]); yield

                def ch_qm():
                    yield from head_norm_g(c4f, c4f[:].rearrange("p (h d) -> p h d", h=4), gvec("qm"), qmb, qmb[:].rearrange("p (h d) -> p h d", h=4), 4, 128, sq_qm, ssq_qm)

                interleave([ch_qa(), ch_ka(), ch_qi(), ch_ki(), ch_qm()])

            def a1_T(t):
                samp = t >= NTP
                b = t - NTP
                d_ = tst.pop(t)
                ko, kio, wo, c1 = d_["ko"], d_["kio"], d_["wo"], d_["c1"]
                qa_t = qaT.next(); pr = pR.next()
                transpose_to(pr, pr, qab, qab, 4, qa_t, qa_t[:])
                P.dma("sp", [lambda be: be.dma_start(out=s_qa[t], in_=qa_t[:].rearrange("p a b -> p (a b)"))], qa_t, reads=[qa_t])
                if samp:
                    P.dma("sp", [lambda be: be.dma_start(out=nak_s[b], in_=ko[0:64, :]),
                                 lambda be: be.dma_start(out=nav_s[b], in_=c1[0:64, 128:256])], ko, reads=[ko, c1])
                else:
                    P.dma("sp", [lambda be: be.dma_start(out=nak_p[t * 128:(t + 1) * 128, :], in_=ko[:]),
                                 lambda be: be.dma_start(out=nav_p[t * 128:(t + 1) * 128, :], in_=c1[:, 128:256])], ko, reads=[ko, c1])
                pr = pR.next(); kt_o = ktT.next()
                transpose_to(pr, pr, kbb, kbb, 1, kt_o, kt_o[:].unsqueeze(1))
                P.dma("sp", [lambda be: be.dma_start(out=s_KT[t], in_=kt_o[:])], kt_o, reads=[kt_o])
                vb_o = vbT.next()
                P.op("pool", lambda be: be.tensor_copy(out=vb_o[:], in_=c1[:, 128:256]), reads=[c1], writes=[vb_o])
                P.dma("sp", [lambda be: be.dma_start(out=s_V[t], in_=vb_o[:])], vb_o, reads=[vb_o])
                qi_t = qiT.next(); pr = pR.next()
                for k in range(3):
                    nr = 96 if k < 2 else 64
                    P.op("pe", lambda be, k=k, nr=nr, pr=pr: be.transpose(out=pr[0:nr, k, :], in_=qib[:, 96 * k:96 * k + nr], identity=idb[:]), reads=[qib, idb], writes=[pr])
                P.op("act", lambda be, pr=pr: be.copy(out=qi_t[0:96, 0:2, :], in_=pr[0:96, 0:2, :]), reads=[pr], writes=[qi_t])
                P.op("act", lambda be, pr=pr: be.copy(out=qi_t[0:64, 2, :], in_=pr[0:64, 2, :]), reads=[pr], writes=[qi_t])
                P.dma("sp", [lambda be: be.dma_start(out=s_qi[t, 0:96, 0:256], in_=qi_t[0:96, 0:2, :].rearrange("p a b -> p (a b)")),
                             lambda be: be.dma_start(out=s_qi[t, 0:64, 256:384], in_=qi_t[0:64, 2, :])], qi_t, reads=[qi_t])
                P.dma("sp", [lambda be: be.dma_start(out=s_wi[t], in_=wo[:])], wo, reads=[wo])
                if samp:
                    P.dma("sp", [lambda be: be.dma_start(out=nik_s[b], in_=kio[0:64, :])], kio, reads=[kio])
                else:
                    P.dma("sp", [lambda be: be.dma_start(out=nik_p[t * 128:(t + 1) * 128, :], in_=kio[:])], kio, reads=[kio])
                pr = pR.next(); ki_o = kiT.next()
                transpose_to(pr, pr, ki4, ki4, 1, ki_o, ki_o[:].unsqueeze(1))
                P.dma("sp", [lambda be: be.dma_start(out=s_KI[t], in_=ki_o[:])], ki_o, reads=[ki_o])
                qm_t = qmT.next(); pr = pR.next()
                transpose_to(pr, pr, qmb, qmb, 4, qm_t, qm_t[:])
                P.dma("sp", [lambda be: be.dma_start(out=s_qm[t], in_=qm_t[:].rearrange("p a b -> p (a b)"))], qm_t, reads=[qm_t])

            a1_load(0); a1_load(1)
            a1_R(0)
            for t in range(NT):
                if t + 2 < NT:
                    a1_load(t + 2)
                a1_M(t)
                if t >= 1:
                    a1_T(t - 1)
                if t + 1 < NT:
                    a1_R(t + 1)
                a1_C(t)
            a1_T(NT - 1)
            P.barrier()
            P.emit_block()
            if KSTOP == 1:
                return nc

        with ExitStack() as kvst:
            P.stack = kvst
            KTp = P.sbuf("KTp", [128, SP], BF16)
            V1p = P.sbuf("V1p", [128, NTP, 2, 65], BF16)
            KIp = P.sbuf("KIp", [128, SP], BF16)
            KTs = [P.sbuf("KTs%d" % b, [128, 1152], BF16) for b in range(2)]
            V1s = [P.sbuf("V1s%d" % b, [128, 9, 2, 65], BF16) for b in range(2)]
            KIs = [P.sbuf("KIs%d" % b, [128, 1152], BF16) for b in range(2)]
            mkT = [P.sbuf("mkT%d" % i, [128, 4, 256], BF16) for i in range(3)]
            mv1 = [P.sbuf("mv1%d" % i, [128, 2, 4, 129], BF16) for i in range(3)]
            ubh = [P.sbuf("ubh%d" % b, [128, 512], BF16, dma=True) for b in range(2)]

            P.op("pool", lambda be: be.memset(V1p[:, :, :, 64:65], 1.0), writes=[V1p])
            for b in range(2):
                P.op("pool", lambda be, b=b: be.memset(V1s[b][:, :, :, 64:65], 1.0), writes=[V1s[b]])
                P.op("pool", lambda be, b=b: be.memset(ubh[b][:], 0.0), writes=[ubh[b]])
            for i in range(3):
                P.op("pool", lambda be, i=i: be.memset(mv1[i][:, :, :, 128:129], 1.0), writes=[mv1[i]])

            with ExitStack() as st:
                P.stack = st
                wm = P.sbuf("wm", [128, 8, 1024], BF16, dma=True)
                load_w_cast(wm, lambda kc, c0, cw: wm[:, kc, c0:c0 + cw], w_mem, 1024, 1024, 8)
                gmem = P.sbuf("gmem", [128, D], F32, dma=True)
                P.dma("sp", [lambda be: be.dma_start(out=gmem[:], in_=g_mem.partition_broadcast(128))], gmem, writes=[gmem])
                xm = Ring([P.sbuf("xm%d" % i, [128, D], F32, dma=True) for i in range(2)])
                st32 = Ring([P.sbuf("st32_%d" % i, [128, 1024], F32, dma=True) for i in range(3)])
                stb = Ring([P.sbuf("stb%d" % i, [128, 1024], BF16) for i in range(2)])
                junk = P.sbuf("junk0", [128, D], BF16)
                ss = P.sbuf("ss0", [128, 1], F32); rs = P.sbuf("rs0", [128, 1], F32)
                hb = P.sbuf("hb0", [128, D], BF16); hT = P.sbuf("hT0", [128, 8, 128], BF16)
                sq = P.sbuf("sq0", [128, 512], F32); ssq = P.sbuf("ssq0", [128, 8], F32)
                kout = Ring([P.sbuf("kout%d" % i, [128, 512], F32, dma=True) for i in range(2)])
                vout = Ring([P.sbuf("vout%d" % i, [128, 512], F32, dma=True) for i in range(2)])
                kb = P.sbuf("kb0", [128, 512], BF16)
                pT = P.psum("pT0", [128, 8, 128], BF16)
                pK = P.psum("pK0", [128, 512], F32); pV = P.psum("pV0", [128, 512], F32)
                pR = Ring([P.psum("pR0_%d" % i, [128, 8, 128], BF16) for i in range(2)])

                for mt in range(2):
                    xb = xm.next()
                    P.dma("sp", [lambda be, xb=xb, mt=mt: be.dma_start(out=xb[:], in_=mem[mt * 128:(mt + 1) * 128, :])], xb, writes=[xb])
                    rms_rows(xb, xb[:], gmem, hb, D, junk, ss, rs)
                    transpose_to(pT, pT, hb, hb, 8, hT, hT[:])
                    for kc in range(8):
                        P.op("pe", lambda be, kc=kc: be.matmul(pK[:], lhsT=hT[:, kc, :], rhs=wm[:, kc, 0:512], start=(kc == 0), stop=(kc == 7)), reads=[hT, wm], writes=[pK])
                    for kc in range(8):
                        P.op("pe", lambda be, kc=kc: be.matmul(pV[:], lhsT=hT[:, kc, :], rhs=wm[:, kc, 512:1024], start=(kc == 0), stop=(kc == 7)), reads=[hT, wm], writes=[pV])
                    kf = st32.next()
                    P.op("act", lambda be, kf=kf: be.copy(out=kf[:, 0:512], in_=pK[:]), reads=[pK], writes=[kf])
                    ko = kout.next()
                    head_norm(kf, kf[:, 0:512].rearrange("p (h d) -> p h d", h=4), gvec("km"), ko, ko[:].rearrange("p (h d) -> p h d", h=4), 4, 128, sq, ssq)
                    P.dma("sp", [lambda be, ko=ko, mt=mt: be.dma_start(out=nmk_p[mt * 128:(mt + 1) * 128, :], in_=ko[:])], ko, reads=[ko])
                    P.op("pool", lambda be, ko=ko: be.tensor_copy(out=kb[:], in_=ko[:]), reads=[ko], writes=[kb])
                    pr = pR.next()
                    transpose_to(pr, pr, kb, kb, 4, mkT[0], mkT[0][:, :, mt * 128:(mt + 1) * 128])
                    vo = vout.next()
                    P.op("act", lambda be, vo=vo: be.copy(out=vo[:], in_=pV[:]), reads=[pV], writes=[vo])
                    P.dma("sp", [lambda be, vo=vo, mt=mt: be.dma_start(out=nmv_p[mt * 128:(mt + 1) * 128, :], in_=vo[:])], vo, reads=[vo])
                    P.op("pool", lambda be, vo=vo, mt=mt: be.tensor_copy(out=mv1[0][:, mt, :, 0:128], in_=vo[:].rearrange("p (h d) -> p h d", h=4)), reads=[vo], writes=[mv1[0]])
                for b in range(2):
                    for mt in range(2):
                        kf = st32.next()
                        P.dma("sp", [lambda be, kf=kf, b=b, mt=mt: be.dma_start(out=kf[:, 0:512], in_=cmk[b, mt * 128:(mt + 1) * 128, :])], kf, writes=[kf])
                        sb_ = stb.next()
                        P.op("dve", lambda be, kf=kf, sb_=sb_: be.tensor_copy(out=sb_[:, 0:512], in_=kf[:, 0:512]), reads=[kf], writes=[sb_])
                        pr = pR.next()
                        transpose_to(pr, pr, sb_, sb_, 4, mkT[1 + b], mkT[1 + b][:, :, mt * 128:(mt + 1) * 128])
                        vf = st32.next()
                        P.dma("sp", [lambda be, vf=vf, b=b, mt=mt: be.dma_start(out=vf[:, 0:512], in_=cmv[b, mt * 128:(mt + 1) * 128, :])], vf, writes=[vf])
                        P.op("pool", lambda be, vf=vf, b=b, mt=mt: be.tensor_copy(out=mv1[1 + b][:, mt, :, 0:128], in_=vf[:, 0:512].rearrange("p (h d) -> p h d", h=4)), reads=[vf], writes=[mv1[1 + b]])
                for b in range(2):
                    ck = st32.next()
                    P.dma("sp", [lambda be, ck=ck, b=b: be.dma_start(out=ck[:].rearrange("p (k c) -> p k c", k=8), in_=cak[b].rearrange("(k p) c -> p k c", p=128))], ck, writes=[ck])
                    cb = stb.next()
                    P.op("dve", lambda be, ck=ck, cb=cb: be.tensor_copy(out=cb[:], in_=ck[:]), reads=[ck], writes=[cb])
                    pr = pR.next()
                    transpose_to(pr, pr, cb, cb, 8, KTs[b], KTs[b][:, 0:1024].rearrange("p (k c) -> p k c", k=8))
                    cv = st32.next()
                    P.dma("sp", [lambda be, cv=cv, b=b: be.dma_start(out=cv[:].rearrange("p (k c) -> p k c", k=8), in_=cav[b].rearrange("(k p) c -> p k c", p=128))], cv, writes=[cv])
                    P.op("pool", lambda be, cv=cv, b=b: be.tensor_copy(out=V1s[b][:, 0:8, :, 0:64], in_=cv[:].rearrange("p (k h d) -> p k h d", k=8, h=2)), reads=[cv], writes=[V1s[b]])
                    ci = st32.next()
                    P.dma("sp", [lambda be, ci=ci, b=b: be.dma_start(out=ci[:, 0:256].rearrange("p (k c) -> p k c", k=8), in_=cik[b].rearrange("(k p) c -> p k c", p=128))], ci, writes=[ci])
                    c4 = stb.next()
                    P.op("dve", lambda be, ci=ci, c4=c4: be.tensor_copy(out=c4[:].rearrange("p (k r c) -> p k r c", k=8, r=4), in_=ci[:, 0:256].rearrange("p (k c) -> p k c", k=8).unsqueeze(2).to_broadcast([128, 8, 4, 32])), reads=[ci], writes=[c4])
                    pr = pR.next()
                    transpose_to(pr, pr, c4, c4, 8, KIs[b], KIs[b][:, 0:1024].rearrange("p (k c) -> p k c", k=8))
                    P.dma("pool", [lambda be, b=b: be.dma_start(out=ubh[b][113:128, :], in_=cpool[b])], ubh[b], writes=[ubh[b]])
                kvl = P.buf("kvl", None, dma=True)
                fl = []
                for q in range((NTP + 15) // 16):
                    t0_ = q * 16; t1_ = min(NTP, t0_ + 16)
                    fl.append(lambda be, t0_=t0_, t1_=t1_: be.dma_start(out=KTp[:, t0_ * 128:t1_ * 128].rearrange("p (t k) -> p t k", k=128), in_=s_KT[t0_:t1_].rearrange("t p k -> p t k")))
                    fl.append(lambda be, t0_=t0_, t1_=t1_: be.dma_start(out=KIp[:, t0_ * 128:t1_ * 128].rearrange("p (t k) -> p t k", k=128), in_=s_KI[t0_:t1_].rearrange("t p k -> p t k")))
                    for hh in range(2):
                        fl.append(lambda be, t0_=t0_, t1_=t1_, hh=hh: be.dma_start(out=V1p[:, t0_:t1_, hh, 0:64], in_=s_V[t0_:t1_, :, hh * 64:(hh + 1) * 64].rearrange("t p d -> p t d")))
                for b in range(2):
                    fl.append(lambda be, b=b: be.dma_start(out=KTs[b][:, 1024:1152], in_=s_KT[NTP + b]))
                    fl.append(lambda be, b=b: be.dma_start(out=KIs[b][:, 1024:1152], in_=s_KI[NTP + b]))
                    fl.append(lambda be, b=b: be.dma_start(out=V1s[b][:, 8, :, 0:64], in_=s_V[NTP + b].rearrange("p (h d) -> p h d", h=2)))
                P.dma("sp", fl, kvl, writes=[KTp, KIp, V1p, KTs[0], KTs[1], KIs[0], KIs[1], V1s[0], V1s[1]])
                P.barrier()
                P.emit_block()
                if KSTOP == 2:
                    return nc

            with ExitStack() as st:
                P.stack = st
                wpl = P.sbuf("wpl", [128, 4, 128], BF16, dma=True)
                P.dma("pool", [lambda be: be.dma_start(out=wpl[:], in_=w_pool.rearrange("g c e -> c g e"))], wpl, writes=[wpl])
                bnd = P.sbuf("bnd", [128, 3, 4, 128], BF16, dma=True)
                P.dma("pool", [lambda be: be.dma_start(out=bnd[:].rearrange("p a g t -> p (a g t)"), in_=bands.rearrange("p a g t -> p (a g t)"))], bnd, writes=[bnd])
                sc = P.sbuf("sc", [128, SCW], F32)
                scC = [P.buf("scC%d" % c, None) for c in range((SCW + 511) // 512)]
                pw = P.sbuf("pw", [128, NBIS + 1], F32); wk = P.sbuf("wk", [128, NBIS + 1], F32)
                for k in range(NBIS + 1):
                    P.op("pool", lambda be, k=k: be.memset(pw[:, k:k + 1], 2.0 ** -(k + 1)), writes=[pw])
                Mq = P.sbuf("Mq", [128, SCW], BF16)
                MT = P.sbuf("MT", [128, SCW // 128, 128], BF16)
                rl = Ring([P.sbuf("rl%d" % i, [128, 512], F32) for i in range(3)])
                er = Ring([P.sbuf("er%d" % i, [128, 4, 128], BF16) for i in range(4)])
                pr_ = Ring([P.sbuf("pp%d" % i, [128, 4, 128], BF16) for i in range(4)])
                ld = {}
                qaL = Ring([P.sbuf("qaL%d" % i, [128, 4, 128], BF16, dma=True) for i in range(3)])
                qiL = Ring([P.sbuf("qiL%d" % i, [128, 3, 128], BF16, dma=True) for i in range(3)])
                qmL = Ring([P.sbuf("qmL%d" % i, [128, 4, 128], BF16, dma=True) for i in range(3)])
                wiL = Ring([P.sbuf("wiL%d" % i, [128, 8], F32, dma=True) for i in range(3)])
                ubL = Ring([P.sbuf("ubL%d" % i, [128, 512], BF16, dma=True) for i in range(4)])
                lo = P.sbuf("lo", [128, 1], F32); w0 = P.sbuf("w0", [128, 1], F32); mx = P.sbuf("mx", [128, 1], F32)
                mid = P.sbuf("mid", [128, 1], F32); cnt = P.sbuf("cnt", [128, 1], F32); tt_ = P.sbuf("tt", [128, 1], F32)
                thr = Ring([P.sbuf("thr%d" % i, [128, 1], F32) for i in range(2)])
                rec = P.sbuf("rec", [128, 8], F32); recm = P.sbuf("recm", [128, 4], F32)
                qa2 = P.sbuf("qa2", [128, 4, 2, 128], BF16)
                a_sb = P.sbuf("a_sb", [128, 512], BF16); m_sb = P.sbuf("m_sb", [128, 512], BF16)
                em = [P.sbuf("em%d" % i, [128, 4, 128], BF16) for i in range(2)]
                pTs = P.sbuf("pTs", [128, 4, 128], BF16)
                br = Ring([P.sbuf("br%d" % i, [128, 3, 4, 128], BF16, dma=True) for i in range(2)])
                pI = Ring([P.psum("pI%d" % i, [128, 512], F32) for i in range(2)])
                pM = Ring([P.psum("pM%d" % i, [128, 8, 128], BF16) for i in range(2)])
                pS = Ring([P.psum("pS%d" % i, [128, 4, 128], F32) for i in range(2)])
                pA = [P.psum("pA%d" % i, [128, 512], F32) for i in range(2)]

                seqs = []
                for t in range(NTP):
                    seqs.append(dict(t=t, KT=KTp, V1=V1p, KI=KIp, nkt=t + 1, samp=False, mi=0))
                for b in range(2):
                    seqs.append(dict(t=NTP + b, KT=KTs[b], V1=V1s[b], KI=KIs[b], nkt=9, samp=True, mi=1 + b, b=b))

                def a2_load(i):
                    s = seqs[i]; t = s["t"]
                    d = dict(qa=qaL.next(), qi=qiL.next(), qm=qmL.next(), wi=wiL.next(), ub=ubL.next())
                    ld[i] = d
                    P.dma("sp", [lambda be: be.dma_start(out=d["qa"][:].rearrange("p a b -> p (a b)"), in_=s_qa[t])], d["qa"], writes=[d["qa"]])
                    P.dma("sp", [lambda be: be.dma_start(out=d["qi"][0:96, 0:2, :].rearrange("p a b -> p (a b)"), in_=s_qi[t, 0:96, 0:256]),
                             lambda be: be.dma_start(out=d["qi"][0:64, 2, :], in_=s_qi[t, 0:64, 256:384])], d["qi"], writes=[d["qi"]])
                    P.dma("sp", [lambda be: be.dma_start(out=d["qm"][:].rearrange("p a b -> p (a b)"), in_=s_qm[t])], d["qm"], writes=[d["qm"]])
                    P.dma("sp", [lambda be: be.dma_start(out=d["wi"][:], in_=s_wi[t])], d["wi"], writes=[d["wi"]])
                    P.dma("sp", [lambda be: be.dma_start(out=d["ub"][:], in_=s_ub[t])], d["ub"], writes=[d["ub"]])

                def stageA(i):
                    s = seqs[i]; d = ld[i]; S = s["nkt"] * 128
                    KI = s["KI"]; qi = d["qi"]; wi = d["wi"]
                    nch = (S + 511) // 512
                    for h in range(8):
                        r0 = (h % 3) * 32
                        for c in range(nch):
                            c0 = c * 512; cw = min(512, S - c0)
                            ps = pI.next()
                            P.op("pe", lambda be, ps=ps, r0=r0, h=h, c0=c0, cw=cw: be.matmul(ps[:, 0:cw], lhsT=qi[r0:r0 + 32, h // 3, :], rhs=KI[r0:r0 + 32, c0:c0 + cw], start=True, stop=True), reads=[qi, KI], writes=[ps])
                            r = rl.next()
                            P.op("act", lambda be, ps=ps, r=r, cw=cw: be.activation(out=r[:, 0:cw], in_=ps[:, 0:cw], func=AF.Relu), reads=[ps], writes=[r])
                            if h == 0:
                                P.op("dve", lambda be, r=r, c0=c0, cw=cw: be.tensor_scalar(out=sc[:, c0:c0 + cw], in0=r[:, 0:cw], scalar1=wi[:, 0:1], scalar2=None, op0=ALU.mult), reads=[r, wi], writes=[scC[c]])
                            else:
                                P.op("dve", lambda be, r=r, h=h, c0=c0, cw=cw: be.scalar_tensor_tensor(out=sc[:, c0:c0 + cw], in0=r[:, 0:cw], scalar=wi[:, h:h + 1], in1=sc[:, c0:c0 + cw], op0=ALU.mult, op1=ALU.add), reads=[r, wi, scC[c]], writes=[scC[c]])
                    if s["samp"]:
                        P.op("dve", lambda be: be.memset(sc[:, S - 64:S], NEG), writes=[scC[nch - 1]])
                    else:
                        P.op("dve", lambda be: be.memset(sc[0:64, S - 64:S], NEG), writes=[scC[nch - 1]])

                def stageC1(i):
                    s = seqs[i]; S = s["nkt"] * 128
                    nch = (S + 511) // 512
                    scs = scC[0:nch]
                    if S <= 256:
                        P.op("dve", lambda be: be.tensor_scalar(out=Mq[:, 0:S], in0=sc[:, 0:S], scalar1=-1.0e29, scalar2=None, op0=ALU.is_ge), reads=scs, writes=[Mq])
                        return
                    P.op("dve", lambda be: be.tensor_reduce(out=lo[:], in_=sc[:, 0:S - 64], axis=AX.X, op=ALU.min), reads=scs, writes=[lo])
                    P.op("dve", lambda be: be.tensor_reduce(out=mx[:], in_=sc[:, 0:S], axis=AX.X, op=ALU.max), reads=scs, writes=[mx])
                    P.op("dve", lambda be: be.tensor_tensor(out=w0[:], in0=mx[:], in1=lo[:], op=ALU.subtract), reads=[mx, lo], writes=[w0])
                    P.op("dve", lambda be: be.tensor_scalar(out=w0[:], in0=w0[:], scalar1=1.0 + 2.0 ** -10, scalar2=1e-12, op0=ALU.mult, op1=ALU.add), reads=[w0], writes=[w0])
                    P.op("dve", lambda be: be.tensor_scalar(out=wk[:], in0=pw[:], scalar1=w0[:, 0:1], scalar2=None, op0=ALU.mult), reads=[pw, w0], writes=[wk])
                    P.op("dve", lambda be: be.tensor_tensor(out=mid[:], in0=lo[:], in1=wk[:, 0:1], op=ALU.add), reads=[lo, wk], writes=[mid])
                    for k in range(NBIS):
                        P.op("dve", lambda be: be.tensor_scalar(out=Mq[:, 0:S], in0=sc[:, 0:S], scalar1=mid[:, 0:1], scalar2=None, op0=ALU.is_ge, op1=ALU.add, accum_out=cnt[:]), reads=scs + [mid], writes=[Mq, cnt])
                        P.op("dve", lambda be: be.tensor_scalar(out=tt_[:], in0=cnt[:], scalar1=255.5, scalar2=0.5, op0=ALU.is_ge, op1=ALU.subtract), reads=[cnt], writes=[tt_])
                        P.op("dve", lambda be, k=k: be.scalar_tensor_tensor(out=mid[:], in0=tt_[:], scalar=wk[:, k:k + 1], in1=mid[:], op0=ALU.mult, op1=ALU.add), reads=[tt_, wk, mid], writes=[mid])
                    P.op("dve", lambda be: be.tensor_tensor(out=lo[:], in0=mid[:], in1=wk[:, NBIS:NBIS + 1], op=ALU.subtract), reads=[mid, wk], writes=[lo])
                    P.op("dve", lambda be: be.tensor_scalar(out=Mq[:, 0:S], in0=sc[:, 0:S], scalar1=lo[:, 0:1], scalar2=None, op0=ALU.is_ge), reads=scs + [lo], writes=[Mq])

                def stageC2(i):
                    s = seqs[i]; nkt = s["nkt"]
                    for j in range((nkt + 7) // 8):
                        n = min(8, nkt - 8 * j)
                        pm = pM.next()
                        transpose_to(pm, pm, Mq, Mq[:, j * 1024:j * 1024 + n * 128], n, MT, MT[:, 8 * j:8 * j + n, :])

                def stageB(i):
                    s = seqs[i]; d = ld.pop(i); nkt = s["nkt"]; t = s["t"]
                    KT = s["KT"]; V1 = s["V1"]; qa = d["qa"]
                    mi = s["mi"]; qm = d["qm"]; ub = d["ub"]
                    bo = br.next()
                    for mt in range(2):
                        ps = pS.next()
                        for h in range(4):
                            P.op("pe", lambda be, ps=ps, h=h, mt=mt: be.matmul(ps[:, h, :], lhsT=mkT[mi][:, h, mt * 128:(mt + 1) * 128], rhs=qm[:, h, :], start=True, stop=True), reads=[mkT[mi], qm], writes=[ps])
                        P.op("act", lambda be, ps=ps, mt=mt: be.activation(out=em[mt][:], in_=ps[:], func=AF.Exp, scale=128.0 ** -0.5), reads=[ps], writes=[em[mt]])
                    if s["samp"]:
                        prev = ubh[s["b"]]; ai = 0
                    elif t == 0:
                        prev = None; ai = 2
                    else:
                        prev = s_prev_ub[0]; ai = 0
                    psp = pI.next()
                    pp3 = psp[:].rearrange("p (g t) -> p g t", g=4)
                    for g in range(4):
                        P.op("pe", lambda be, g=g: be.matmul(pp3[:, g, :], lhsT=ub[:, g * 128:(g + 1) * 128], rhs=bnd[:, ai, g, :], start=True, stop=(prev is None)), reads=[ub, bnd], writes=[psp])
                        if prev is not None:
                            P.op("pe", lambda be, g=g: be.matmul(pp3[:, g, :], lhsT=prev[:, g * 128:(g + 1) * 128], rhs=bnd[:, 1, g, :], start=False, stop=True), reads=[prev, bnd], writes=[psp])
                    P.op("act", lambda be: be.copy(out=pTs[:], in_=pp3), reads=[psp], writes=[pTs])
                    P.op("pool", lambda be: be.memset(qa2[:], 0.0), writes=[qa2])
                    P.op("pool", lambda be: be.tensor_copy(out=qa2[0:64, :, 0, :], in_=qa[0:64, :, :]), reads=[qa], writes=[qa2])
                    P.op("pool", lambda be: be.tensor_copy(out=qa2[64:128, :, 1, :], in_=qa[64:128, :, :]), reads=[qa], writes=[qa2])
                    accs = [pA[kv][:, 0:260].rearrange("p (h d) -> p h d", h=4) for kv in range(2)]
                    nch2 = (nkt + 1) // 2
                    its = [(g, c) for g in range(4) for c in range(nch2)]
                    stq = {}

                    def emit_qk(j):
                        g, c = its[j]
                        n = min(2, nkt - 2 * c)
                        ps = pS.next()
                        ps4 = ps[:].rearrange("p (k h) q -> p k h q", k=2)
                        for k in range(n):
                            kt = 2 * c + k
                            P.op("pe", lambda be, k=k, kt=kt: be.matmul(ps4[:, k, :, :], lhsT=KT[:, kt * 128:(kt + 1) * 128], rhs=qa2[:, g, :, :], start=True, stop=True), reads=[KT, qa2], writes=[ps])
                        e = er.next()
                        e4 = e[:].rearrange("p (k h) q -> p k h q", k=2)
                        P.op("act", lambda be: be.activation(out=e4[:, 0:n, :, :], in_=ps4[:, 0:n, :, :], func=AF.Exp, scale=0.125), reads=[ps], writes=[e])
                        p_ = pr_.next()
                        p4 = p_[:].rearrange("p (k h) q -> p k h q", k=2)
                        P.op("pool", lambda be: be.tensor_tensor(out=p4[:, 0:n, :, :], in0=e4[:, 0:n, :, :], in1=MT[:, 2 * c:2 * c + n, :].unsqueeze(2).to_broadcast([128, n, 2, 128]), op=ALU.mult), reads=[e, MT], writes=[p_])
                        stq[j] = (p_, p4, n)

                    def emit_pv(j):
                        g, c = its[j]
                        p_, p4, n = stq.pop(j)
                        for k in range(n):
                            kt = 2 * c + k
                            for hh in range(2):
                                P.op("pe", lambda be, k=k, kt=kt, hh=hh: be.matmul(accs[hh][:, g, :], lhsT=p4[:, k, hh, :], rhs=V1[:, kt, hh, :], start=(kt == 0), stop=(kt == nkt - 1)), reads=[p_, V1], writes=[pA[hh]])

                    LA = 2
                    for j in range(len(its) + LA):
                        if j < len(its):
                            emit_qk(j)
                        if j - LA >= 0:
                            emit_pv(j - LA)
                    pm1 = pS.next(); pm2 = pS.next()
                    accm = [pm1[:].rearrange("p a q -> p (a q)")[:, 0:258].rearrange("p (h d) -> p h d", h=2),
                            pm2[:].rearrange("p a q -> p (a q)")[:, 0:258].rearrange("p (h d) -> p h d", h=2)]
                    pmb = [pm1, pm2]
                    for h in range(4):
                        for mt in range(2):
                            P.op("pe", lambda be, h=h, mt=mt: be.matmul(accm[h // 2][:, h % 2, :], lhsT=em[mt][:, h, :], rhs=mv1[mi][:, mt, h, :], start=(mt == 0), stop=(mt == 1)), reads=[em[mt], mv1[mi]], writes=[pmb[h // 2]])
                    ps2 = pI.next()
                    py3 = ps2[:].rearrange("p (g t) -> p g t", g=4)
                    for g in range(4):
                        P.op("pe", lambda be, g=g: be.matmul(py3[:, g, :], lhsT=wpl[:, g, :], rhs=pTs[:, g, :], start=True, stop=True), reads=[wpl, pTs], writes=[ps2])
                    for kv in range(2):
                        acc = accs[kv]
                        P.op("dve", lambda be, acc=acc, kv=kv: be.reciprocal(out=rec[:, 4 * kv:4 * kv + 4], in_=acc[:, :, 64]), reads=[pA[kv]], writes=[rec])
                        P.op("dve", lambda be, acc=acc, kv=kv: be.tensor_tensor(out=a_sb[:, 256 * kv:256 * kv + 256].rearrange("p (h d) -> p h d", h=4), in0=acc[:, :, 0:64], in1=bl(rec[:, 4 * kv:4 * kv + 4], [128, 4, 64]), op=ALU.mult), reads=[pA[kv], rec], writes=[a_sb])
                    for hh in range(2):
                        P.op("dve", lambda be, hh=hh: be.reciprocal(out=recm[:, 2 * hh:2 * hh + 2], in_=accm[hh][:, :, 128]), reads=[pmb[hh]], writes=[recm])
                        P.op("dve", lambda be, hh=hh: be.tensor_tensor(out=m_sb[:, 256 * hh:256 * hh + 256].rearrange("p (h d) -> p h d", h=2), in0=accm[hh][:, :, 0:128], in1=bl(recm[:, 2 * hh:2 * hh + 2], [128, 2, 128]), op=ALU.mult), reads=[pmb[hh], recm], writes=[m_sb])
                    P.op("dve", lambda be: be.tensor_tensor(out=bo[:, 1, :, :], in0=py3, in1=bl(spl[:], [128, 4, 128]), op=ALU.mult), reads=[ps2, spl], writes=[bo])
                    pm = pM.next()
                    transpose_to(pm, pm, a_sb, a_sb, 4, bo, bo[:, 0, :, :])
                    pm = pM.next()
                    transpose_to(pm, pm, m_sb, m_sb, 4, bo, bo[:, 2, :, :])
                    P.dma("sp", [lambda be: be.dma_start(out=s_br[t], in_=bo[:].rearrange("p a b c -> p (a b c)"))], bo, reads=[bo])
                    s_prev_ub[0] = ub

                s_prev_ub = [None]
                nseq = len(seqs)
                a2_load(0); a2_load(1)
                stageA(0); stageC1(0); stageC2(0)
                for i in range(nseq):
                    if i + 2 < nseq:
                        a2_load(i + 2)
                    if i + 1 < nseq:
                        stageA(i + 1); stageC1(i + 1)
                    stageB(i)
                    if i + 1 < nseq:
                        stageC2(i + 1)
                P.barrier()
                P.emit_block()
                if KSTOP == 3:
                    return nc

        with ExitStack() as st:
            P.stack = st
            wo = P.sbuf("wo", [128, 3, 4, D], BF16, dma=True)
            fl = []
            for b in range(3):
                for kc in range(4):
                    fl.append(lambda be, b=b, kc=kc: be.dma_start(out=wo[:, b, kc, :], in_=w_o[b][kc * 128:(kc + 1) * 128, :]))
            P.dma("pool", fl, wo, writes=[wo])
            wout = P.sbuf("wout", [128, 8, D], BF16, dma=True)
            load_w_cast(wout, lambda kc, c0, cw: wout[:, kc, c0:c0 + cw], w_out, 1024, D, 8)
            brL = Ring([P.sbuf("brL%d" % i, [128, 3, 4, 128], BF16, dma=True) for i in range(3)])
            gtL = Ring([P.sbuf("gtL%d" % i, [128, 3, D], BF16, dma=True) for i in range(3)])
            xL = Ring([P.sbuf("xL%d" % i, [128, D], F32, dma=True) for i in range(3)])
            mixed = P.sbuf("mixed", [128, D], F32); tmpm = P.sbuf("tmpm", [128, D], F32)
            mxb = P.sbuf("mxb", [128, D], BF16); mxT = P.sbuf("mxT", [128, 8, 128], BF16)
            x1o = Ring([P.sbuf("x1o%d" % i, [128, D], F32, dma=True) for i in range(2)])
            pt = Ring([P.psum("pt3_%d" % i, [128, D], F32) for i in range(2)])
            pT = P.psum("pT3", [128, 8, 128], BF16)
            pO = P.psum("pO3", [128, D], F32)
            l3 = {}

            def a3_load(t):
                d = dict(br=brL.next(), gt=gtL.next(), x=xL.next())
                l3[t] = d
                P.dma("sp", [lambda be: be.dma_start(out=d["br"][:].rearrange("p a b c -> p (a b c)"), in_=s_br[t])], d["br"], writes=[d["br"]])
                P.dma("sp", [lambda be: be.dma_start(out=d["gt"][:].rearrange("p a b -> p (a b)"), in_=s_gate[t])], d["gt"], writes=[d["gt"]])
                P.dma("sp", [lambda be: be.dma_start(out=d["x"][:], in_=xsrc(t))], d["x"], writes=[d["x"]])

            def a3_tile(t):
                d = l3.pop(t)
                brt = d["br"]; gt = d["gt"]; xb = d["x"]
                for b in range(3):
                    ps = pt.next()
                    for half in range(2):
                        for kc in range(4):
                            P.op("pe", lambda be, ps=ps, b=b, half=half, kc=kc: be.matmul(ps[:, half * 512:(half + 1) * 512], lhsT=brt[:, b, kc, :], rhs=wo[:, b, kc, half * 512:(half + 1) * 512], start=(kc == 0), stop=(kc == 3)), reads=[brt, wo], writes=[ps])
                    if b == 0:
                        P.op("dve", lambda be, ps=ps: be.tensor_tensor(out=mixed[:], in0=ps[:], in1=gt[:, 0, :], op=ALU.mult), reads=[ps, gt], writes=[mixed])
                    else:
                        P.op("dve", lambda be, ps=ps, b=b: be.tensor_tensor(out=tmpm[:], in0=ps[:], in1=gt[:, b, :], op=ALU.mult), reads=[ps, gt], writes=[tmpm])
                        if b == 1:
                            P.op("pool", lambda be: be.tensor_tensor(out=mixed[:], in0=mixed[:], in1=tmpm[:], op=ALU.add), reads=[mixed, tmpm], writes=[mixed])
                        else:
                            P.op("pool", lambda be: be.tensor_tensor(out=mxb[:], in0=mixed[:], in1=tmpm[:], op=ALU.add), reads=[mixed, tmpm], writes=[mxb])
                transpose_to(pT, pT, mxb, mxb, 8, mxT, mxT[:])
                for half in range(2):
                    for kc in range(8):
                        P.op("pe", lambda be, half=half, kc=kc: be.matmul(pO[:, half * 512:(half + 1) * 512], lhsT=mxT[:, kc, :], rhs=wout[:, kc, half * 512:(half + 1) * 512], start=(kc == 0), stop=(kc == 7)), reads=[mxT, wout], writes=[pO])
                xo = x1o.next()
                P.op("dve", lambda be, xo=xo: be.tensor_tensor(out=xo[:], in0=pO[:], in1=xb[:], op=ALU.add), reads=[pO, xb], writes=[xo])
                P.dma("sp", [lambda be, xo=xo: be.dma_start(out=s_x1[t], in_=xo[:])], xo, reads=[xo])

            a3_load(0); a3_load(1)
            for t in range(NT):
                if t + 2 < NT:
                    a3_load(t + 2)
                a3_tile(t)
            P.barrier()
            P.emit_block()
            if KSTOP == 4:
                return nc

        with ExitStack() as st:
            P.stack = st
            wg = P.sbuf("wg", [128, 8, DFF], BF16, dma=True)
            load_w_cast(wg, lambda kc, c0, cw: wg[:, kc, c0:c0 + cw], w_gate, 1024, DFF, 8)
            wu = P.sbuf("wu", [128, 8, DFF], BF16, dma=True)
            load_w_cast(wu, lambda kc, c0, cw: wu[:, kc, c0:c0 + cw], w_up, 1024, DFF, 8)
            wd = P.sbuf("wd", [128, 22, D], BF16, dma=True)
            load_w_cast(wd, lambda kc, c0, cw: wd[:, kc, c0:c0 + cw], w_down, DFF, D, 22)
            gffn = P.sbuf("gffn", [128, D], F32, dma=True)
            P.dma("sp", [lambda be: be.dma_start(out=gffn[:], in_=g_ffn.partition_broadcast(128))], gffn, writes=[gffn])
            x1L = Ring([P.sbuf("x1L%d" % i, [128, D], F32, dma=True) for i in range(3)])
            junk = P.sbuf("junkb", [128, D], BF16)
            ss = P.sbuf("ssb", [128, 1], F32); rs = P.sbuf("rsb", [128, 1], F32)
            hb = P.sbuf("hbb", [128, D], BF16); hT = P.sbuf("hTb", [128, 8, 128], BF16)
            sg = Ring([P.sbuf("sg%d" % i, [128, 4, 128], F32) for i in range(2)])
            gT = P.sbuf("gT", [128, 22, 128], BF16)
            yo = Ring([P.sbuf("yo%d" % i, [128, D], F32, dma=True) for i in range(2)])
            pT = P.psum("pTb", [128, 8, 128], BF16)
            pG = Ring([P.psum("pG%d" % i, [128, 4, 128], F32) for i in range(2)])
            pU = Ring([P.psum("pU%d" % i, [128, 4, 128], F32) for i in range(2)])
            pO = P.psum("pOb", [128, D], F32)
            lb = {}

            def b_load(t):
                xb = x1L.next()
                lb[t] = xb
                P.dma("sp", [lambda be: be.dma_start(out=xb[:], in_=s_x1[t])], xb, writes=[xb])

            def b_tile(t):
                xb = lb.pop(t)
                rms_rows(xb, xb[:], gffn, hb, D, junk, ss, rs)
                transpose_to(pT, pT, hb, hb, 8, hT, hT[:])
                for fg in range(6):
                    nf = min(4, 22 - 4 * fg)
                    pg = pG.next(); pu = pU.next()
                    for j in range(nf):
                        fc = 4 * fg + j
                        for kc in range(8):
                            P.op("pe", lambda be, pg=pg, j=j, fc=fc, kc=kc: be.matmul(pg[:, j, :], lhsT=wg[:, kc, fc * 128:(fc + 1) * 128], rhs=hT[:, kc, :], start=(kc == 0), stop=(kc == 7)), reads=[wg, hT], writes=[pg])
                    for j in range(nf):
                        fc = 4 * fg + j
                        for kc in range(8):
                            P.op("pe", lambda be, pu=pu, j=j, fc=fc, kc=kc: be.matmul(pu[:, j, :], lhsT=wu[:, kc, fc * 128:(fc + 1) * 128], rhs=hT[:, kc, :], start=(kc == 0), stop=(kc == 7)), reads=[wu, hT], writes=[pu])
                    s_ = sg.next()
                    P.op("act", lambda be, pg=pg, s_=s_, nf=nf: be.activation(out=s_[:, 0:nf, :], in_=pg[:, 0:nf, :], func=AF.Silu), reads=[pg], writes=[s_])
                    P.op("dve", lambda be, pu=pu, s_=s_, nf=nf, fg=fg: be.tensor_tensor(out=gT[:, 4 * fg:4 * fg + nf, :], in0=pu[:, 0:nf, :], in1=s_[:, 0:nf, :], op=ALU.mult), reads=[pu, s_], writes=[gT])
                for half in range(2):
                    for fc in range(22):
                        P.op("pe", lambda be, half=half, fc=fc: be.matmul(pO[:, half * 512:(half + 1) * 512], lhsT=gT[:, fc, :], rhs=wd[:, fc, half * 512:(half + 1) * 512], start=(fc == 0), stop=(fc == 21)), reads=[gT, wd], writes=[pO])
                yb = yo.next()
                P.op("dve", lambda be, yb=yb: be.tensor_tensor(out=yb[:], in0=pO[:], in1=xb[:], op=ALU.add), reads=[pO, xb], writes=[yb])
                if t < NTP:
                    P.dma("sp", [lambda be, yb=yb: be.dma_start(out=y_p[t * 128:(t + 1) * 128, :], in_=yb[:])], yb, reads=[yb])
                else:
                    P.dma("sp", [lambda be, yb=yb: be.dma_start(out=y_s[t - NTP], in_=yb[0:64, :])], yb, reads=[yb])

            b_load(0); b_load(1)
            for t in range(NT):
                if t + 2 < NT:
                    b_load(t + 2)
                b_tile(t)
            P.barrier()
            P.emit_block()
            if KSTOP == 5:
                return nc
    return nc


def _consts():
    NT = NTP + 2
    theta = np.float32(10000.0)
    tab = np.zeros((NT, 128, 96), np.float32)
    inv64 = (theta ** (-np.arange(32, dtype=np.float32) / np.float32(32))).astype(np.float32)
    inv32 = (theta ** (-np.arange(16, dtype=np.float32) / np.float32(16))).astype(np.float32)
    for t in range(NT):
        pos = (np.arange(128) + (t * 128 if t < NTP else 1024)).astype(np.float32)
        a64 = (pos[:, None] * inv64[None, :]).astype(np.float32)
        a32 = (pos[:, None] * inv32[None, :]).astype(np.float32)
        tab[t, :, 0:32] = np.cos(a64.astype(np.float64)); tab[t, :, 32:64] = np.sin(a64.astype(np.float64))
        tab[t, :, 64:80] = np.cos(a32.astype(np.float64)); tab[t, :, 80:96] = np.sin(a32.astype(np.float64))
    bands = np.zeros((128, 3, 4, 128), np.float32)
    tp = np.arange(128)[:, None]; tq = np.arange(128)[None, :]
    for g, w in enumerate((2, 4, 8, 16)):
        inwin = (tp <= tq) & (tp > tq - w)
        bands[:, 0, g, :] = inwin / w - (tp == tq)
        bands[:, 1, g, :] = ((tp - 128) > (tq - w)) / w
        cntf = np.minimum(w, tq + 1).astype(np.float64)
        bands[:, 2, g, :] = inwin / cntf - (tp == tq)
    return tab, bands.astype(np.float32), np.eye(128, dtype=np.float32)


_CACHE = {}


def kernel(x_prompt, x_sample, mem_prompt, cache_a_k, cache_a_v, cache_idx_k, cache_pool, cache_mem_k,
           cache_mem_v, g_mix, w_in, g_qa, g_ka, g_kidx, g_qm, g_mem, w_mem_kv, g_km, w_pool, s_pool,
           w_oa, w_ob, w_om, w_out, g_ffn, w_gate, w_up, w_down):
    f = lambda a: np.ascontiguousarray(np.asarray(a, dtype=np.float32))
    if "nc" not in _CACHE:
        _CACHE["nc"] = build_program()
        _CACHE["consts"] = _consts()
    nc = _CACHE["nc"]
    tab, bands, ident = _CACHE["consts"]
    xs_pad = np.zeros((16, 128, D), np.float32)
    xs_pad[:, 0:64, :] = f(x_sample)
    shared = {
        "w_in": f(w_in[0]), "w_mem": f(w_mem_kv[0]), "w_pool": f(w_pool[0]), "w_oa": f(w_oa[0]), "w_ob": f(w_ob[0]),
        "w_om": f(w_om[0]), "w_out": f(w_out[0]), "w_gate": f(w_gate[0]), "w_up": f(w_up[0]), "w_down": f(w_down[0]),
        "g_mix": f(g_mix), "g_ffn": f(g_ffn), "g_mem": f(g_mem), "g_qa": f(g_qa), "g_ka": f(g_ka), "g_kidx": f(g_kidx),
        "g_qm": f(g_qm), "g_km": f(g_km), "s_pool": f(np.asarray(s_pool[0]).reshape(4, 128).T),
        "rope": tab, "bands": bands, "ident": ident,
    }
    in_maps = []
    for c in range(8):
        m = dict(shared)
        m["x_p"] = f(x_prompt[c]); m["x_s"] = np.ascontiguousarray(xs_pad[2 * c:2 * c + 2]); m["mem"] = f(mem_prompt[c])
        m["cak"] = f(np.asarray(cache_a_k[0, 2 * c:2 * c + 2]).reshape(2, 1024, 128))
        m["cav"] = f(np.asarray(cache_a_v[0, 2 * c:2 * c + 2]).reshape(2, 1024, 128))
        m["cik"] = f(cache_idx_k[0, 2 * c:2 * c + 2]); m["cpool"] = f(cache_pool[0, 2 * c:2 * c + 2])
        m["cmk"] = f(np.asarray(cache_mem_k[0, 2 * c:2 * c + 2]).reshape(2, 256, 512))
        m["cmv"] = f(np.asarray(cache_mem_v[0, 2 * c:2 * c + 2]).reshape(2, 256, 512))
        in_maps.append(m)
    res = run_bass_kernel_spmd(nc, in_maps, core_ids=list(range(8)))
    R = res.results
    cat = lambda k: np.stack([np.asarray(r[k], dtype=np.float32) for r in R], 0)
    cat2 = lambda k: np.concatenate([np.asarray(r[k], dtype=np.float32) for r in R], 0)
    y_prompt = cat("y_p")
    y_sample = cat2("y_s")
    return (
        y_prompt, y_sample,
        cat("nak_p").reshape(1, 8, 8192, 2, 64), cat("nav_p").reshape(1, 8, 8192, 2, 64), cat("nik_p").reshape(1, 8, 8192, 32),
        cat("npool_p").reshape(1, 8, 15, 512), cat("nmk_p").reshape(1, 8, 256, 4, 128), cat("nmv_p").reshape(1, 8, 256, 4, 128),
        cat2("nak_s").reshape(1, 16, 64, 2, 64), cat2("nav_s").reshape(1, 16, 64, 2, 64), cat2("nik_s").reshape(1, 16, 64, 32),
        cat2("npool_s").reshape(1, 16, 15, 512),
    )
```

```python
from contextlib import ExitStack
import numpy as np
import concourse.bass as bass
import concourse.mybir as mybir
from concourse.bass_utils import run_bass_kernel_spmd

F32 = mybir.dt.float32
BF16 = mybir.dt.bfloat16
AF = mybir.ActivationFunctionType
ALU = mybir.AluOpType
AX = mybir.AxisListType


class Ctr:
    def __init__(self, name, sem):
        self.name = name
        self.sem = sem
        self.count = 0


class Buf:
    def __init__(self, name, ap, ctr=None):
        self.name = name
        self.ap = ap
        self.ctr = ctr
        self.w = None
        self.r = {}

    def __getitem__(self, k):
        return self.ap[k]


class Eng:
    def __init__(self, name, be, ctr):
        self.name = name
        self.be = be
        self.ctr = ctr
        self.ops = []
        self.seen = {}


class Prog:
    def __init__(self, nc, stack):
        self.nc = nc
        self.gstack = stack
        self.stack = stack
        self.engs = {}
        self.nsem = 0
        for nm, be in (("pe", nc.tensor), ("act", nc.scalar), ("dve", nc.vector),
                       ("pool", nc.gpsimd), ("sp", nc.sync)):
            self.engs[nm] = Eng(nm, be, self.new_ctr("e_" + nm))
        self.dma_ctrs = []

    def new_ctr(self, name):
        sem = self.gstack.enter_context(self.nc.semaphore(name))
        self.nsem += 1
        return Ctr(name, sem)

    def sbuf(self, name, shape, dtype, dma=False):
        t = self.stack.enter_context(self.nc.sbuf_tensor(name, list(shape), dtype))
        return self.buf(name, t, dma)

    def psum(self, name, shape, dtype):
        t = self.stack.enter_context(self.nc.psum_tensor(name, list(shape), dtype))
        return Buf(name, t)

    def buf(self, name, ap, dma=False):
        c = None
        if dma:
            c = self.new_ctr("d_" + name)
            self.dma_ctrs.append(c)
        return Buf(name, ap, c)

    def _deps(self, eng, reads, writes, skip_same_pe=False):
        deps = {}

        def add(cv):
            if cv is None:
                return
            c, v = cv
            if skip_same_pe and c is eng.ctr:
                return
            if deps.get(c, 0) < v:
                deps[c] = v

        for b in reads:
            add(b.w)
        for b in writes:
            add(b.w)
            for c, v in b.r.items():
                add((c, v))
        waits = []
        for c, v in deps.items():
            if eng.seen.get(c, 0) < v:
                eng.seen[c] = v
                waits.append((c.sem, v))
        return waits

    def _cut(self):
        import os
        self.nrec = getattr(self, "nrec", 0) + 1
        return self.nrec > int(os.environ.get("OPCUT", "1000000000"))

    def op(self, engname, fn, reads=(), writes=()):
        if self._cut():
            return 0
        eng = self.engs[engname]
        waits = self._deps(eng, reads, writes, skip_same_pe=(engname == "pe"))
        eng.ctr.count += 1
        val = eng.ctr.count
        sem = eng.ctr.sem

        def emit(be, waits=waits, fn=fn, sem=sem):
            for s, v in waits:
                be.wait_ge(s, v)
            fn(be).then_inc(sem, 1)

        eng.ops.append(emit)
        for b in writes:
            b.w = (eng.ctr, val)
            b.r = {}
        for b in reads:
            if b not in writes:
                b.r[eng.ctr] = val
        return val

    def dma(self, engname, fns, ctrbuf, reads=(), writes=()):
        if self._cut():
            return 0
        eng = self.engs[engname]
        ctr = ctrbuf.ctr
        assert ctr is not None, ctrbuf.name
        waits = self._deps(eng, reads, writes)
        ctr.count += 16 * len(fns)
        val = ctr.count
        sem = ctr.sem

        def emit(be, waits=waits, fns=fns, sem=sem):
            for s, v in waits:
                be.wait_ge(s, v)
            for f in fns:
                f(be).then_inc(sem, 16)

        eng.ops.append(emit)
        for b in writes:
            b.w = (ctr, val)
            b.r = {}
        for b in reads:
            if b not in writes:
                b.r[ctr] = val
        return val

    def barrier(self):
        targets = [(e.ctr, e.ctr.count) for e in self.engs.values()]
        targets += [(c, c.count) for c in self.dma_ctrs]
        for eng in self.engs.values():
            waits = []
            for c, v in targets:
                if v > 0 and c is not eng.ctr and eng.seen.get(c, 0) < v:
                    eng.seen[c] = v
                    waits.append((c.sem, v))

            def emit(be, waits=waits):
                for s, v in waits:
                    be.wait_ge(s, v)

            eng.ops.append(emit)

    def emit_block(self):
        nc = self.nc
        ops = {k: e.ops for k, e in self.engs.items()}
        for e in self.engs.values():
            e.ops = []
        with nc.Block() as block:
            @block.tensor
            def _(be):
                for f in ops["pe"]:
                    f(be)

            @block.scalar
            def _(be):
                for f in ops["act"]:
                    f(be)

            @block.vector
            def _(be):
                for f in ops["dve"]:
                    f(be)

            @block.gpsimd
            def _(be):
                for f in ops["pool"]:
                    f(be)

            @block.sync
            def _(be):
                for f in ops["sp"]:
                    f(be)

D = 1024
NTP = 64
DIN = 5160
DFF = 2816
EPS = 1e-6
NEG = -1.0e30
NBIS = 12
IDX_SCALE = 256.0 ** -0.5


class Ring:
    def __init__(self, bufs):
        self.bufs = bufs
        self.i = 0

    def next(self):
        b = self.bufs[self.i % len(self.bufs)]
        self.i += 1
        return b


def build_program(dbg=False):
    import os as _os
    KSTOP = int(_os.environ.get('KSTOP', '99'))
    NT = NTP + 2
    SP = NTP * 128
    SCW = max(SP, 1152)
    nc = bass.Bass("TRN2", target_bir_lowering=False)

    def din(name, shape, dt=F32):
        return nc.dram_tensor(name, list(shape), dt, kind="ExternalInput").ap()

    def dout(name, shape, dt=F32):
        return nc.dram_tensor(name, list(shape), dt, kind="ExternalOutput").ap()

    def dscr(name, shape, dt):
        return nc.dram_tensor(name, list(shape), dt, kind=("ExternalOutput" if dbg else "Internal")).ap()

    x_p = din("x_p", [SP, D]); x_s = din("x_s", [2, 128, D]); mem = din("mem", [256, D])
    cak = din("cak", [2, 1024, 128]); cav = din("cav", [2, 1024, 128]); cik = din("cik", [2, 1024, 32])
    cpool = din("cpool", [2, 15, 512]); cmk = din("cmk", [2, 256, 512]); cmv = din("cmv", [2, 256, 512])
    w_in = din("w_in", [D, DIN]); w_mem = din("w_mem", [D, 1024]); w_pool = din("w_pool", [4, 128, 128])
    w_o = [din("w_oa", [512, D]), din("w_ob", [512, D]), din("w_om", [512, D])]
    w_out = din("w_out", [D, D]); w_gate = din("w_gate", [D, DFF]); w_up = din("w_up", [D, DFF]); w_down = din("w_down", [DFF, D])
    g_mix = din("g_mix", [1, D]); g_ffn = din("g_ffn", [1, D]); g_mem = din("g_mem", [1, D])
    g_qa = din("g_qa", [1, 64]); g_ka = din("g_ka", [1, 64]); g_kidx = din("g_kidx", [1, 32])
    g_qm = din("g_qm", [1, 128]); g_km = din("g_km", [1, 128]); s_pool = din("s_pool", [128, 4])
    rope = din("rope", [NT, 128, 96]); bands = din("bands", [128, 3, 4, 128]); ident = din("ident", [128, 128])

    y_p = dout("y_p", [SP, D]); y_s = dout("y_s", [2, 64, D])
    nak_p = dout("nak_p", [SP, 128]); nav_p = dout("nav_p", [SP, 128]); nik_p = dout("nik_p", [SP, 32])
    npool_p = dout("npool_p", [15, 512]); nmk_p = dout("nmk_p", [256, 512]); nmv_p = dout("nmv_p", [256, 512])
    nak_s = dout("nak_s", [2, 64, 128]); nav_s = dout("nav_s", [2, 64, 128]); nik_s = dout("nik_s", [2, 64, 32])
    npool_s = dout("npool_s", [2, 15, 512])

    s_qa = dscr("s_qa", [NT, 128, 512], BF16); s_qi = dscr("s_qi", [NT, 128, 384], BF16)
    s_qm = dscr("s_qm", [NT, 128, 512], BF16); s_ub = dscr("s_ub", [NT, 128, 512], BF16)
    s_wi = dscr("s_wi", [NT, 128, 8], F32); s_gate = dscr("s_gate", [NT, 128, 3072], BF16)
    s_KT = dscr("s_KT", [NT, 128, 128], BF16); s_V = dscr("s_V", [NT, 128, 128], BF16); s_KI = dscr("s_KI", [NT, 128, 128], BF16)
    s_br = dscr("s_br", [NT, 128, 1536], BF16); s_x1 = dscr("s_x1", [NT, 128, D], F32)

    def xsrc(t):
        return x_p[t * 128:(t + 1) * 128, :] if t < NTP else x_s[t - NTP]

    with ExitStack() as gst:
        P = Prog(nc, gst)

        def bc(ap2, shape):
            return ap2.unsqueeze(1).to_broadcast(shape)

        def bl(ap2, shape):
            return ap2.unsqueeze(2).to_broadcast(shape)

        idb = P.sbuf("idb", [128, 128], BF16)
        gq = P.sbuf("gq", [128, 64 + 64 + 32 + 128 + 128], F32, dma=True)
        spl = P.sbuf("spl", [128, 4], F32, dma=True)
        with ExitStack() as st0:
            P.stack = st0
            idf = P.sbuf("idf", [128, 128], F32, dma=True)
            P.dma("sp", [lambda be: be.dma_start(out=idf[:], in_=ident)], idf, writes=[idf])
            P.op("dve", lambda be: be.tensor_copy(out=idb[:], in_=idf[:]), reads=[idf], writes=[idb])
            offs = {}
            o = 0
            fl = []
            for nm, ap_, n in (("qa", g_qa, 64), ("ka", g_ka, 64), ("kidx", g_kidx, 32), ("qm", g_qm, 128), ("km", g_km, 128)):
                offs[nm] = (o, n)
                fl.append(lambda be, ap_=ap_, o=o, n=n: be.dma_start(out=gq[:, o:o + n], in_=ap_.partition_broadcast(128)))
                o += n
            P.dma("sp", fl, gq, writes=[gq])
            P.dma("sp", [lambda be: be.dma_start(out=spl[:], in_=s_pool)], spl, writes=[spl])
            P.barrier()
            P.emit_block()
            if KSTOP == 0:
                return nc
        P.stack = gst

        def gvec(nm):
            o, n = offs[nm]
            return gq[:, o:o + n]

        def rms_rows(xb, xap, gb, hb, n, junk, ss, rs):
            P.op("act", lambda be: be.activation(out=junk[:, 0:n], in_=xap, func=AF.Square, accum_out=ss[:]), reads=[xb], writes=[junk, ss])
            P.op("dve", lambda be: be.tensor_scalar(out=rs[:], in0=ss[:], scalar1=1.0 / n, scalar2=EPS, op0=ALU.mult, op1=ALU.add), reads=[ss], writes=[rs])
            P.op("act", lambda be: be.activation(out=rs[:], in_=rs[:], func=AF.Sqrt), reads=[rs], writes=[rs])
            P.op("dve", lambda be: be.reciprocal(out=rs[:], in_=rs[:]), reads=[rs], writes=[rs])
            P.op("dve", lambda be: be.scalar_tensor_tensor(out=hb[:], in0=xap, scalar=rs[:, 0:1], in1=gb[:], op0=ALU.mult, op1=ALU.mult), reads=[xb, rs, gb], writes=[hb])

        def head_norm(srcb, src3, gap, outb, out3, H, hd, sq, ssq, eng="dve"):
            sh = [128, H, hd]
            sqv = sq[:, 0:H * hd].rearrange("p (h d) -> p h d", h=H)
            P.op(eng, lambda be: be.tensor_tensor(out=sqv, in0=src3, in1=src3, op=ALU.mult), reads=[srcb], writes=[sq])
            P.op("dve", lambda be: be.tensor_reduce(out=ssq[:, 0:H], in_=sqv, axis=AX.X, op=ALU.add), reads=[sq], writes=[ssq])
            P.op("dve", lambda be: be.tensor_scalar(out=ssq[:, 0:H], in0=ssq[:, 0:H], scalar1=1.0 / hd, scalar2=EPS, op0=ALU.mult, op1=ALU.add), reads=[ssq], writes=[ssq])
            P.op("act", lambda be: be.activation(out=ssq[:, 0:H], in_=ssq[:, 0:H], func=AF.Sqrt), reads=[ssq], writes=[ssq])
            P.op("dve", lambda be: be.reciprocal(out=ssq[:, 0:H], in_=ssq[:, 0:H]), reads=[ssq], writes=[ssq])
            P.op(eng, lambda be: be.tensor_tensor(out=sqv, in0=src3, in1=bl(ssq[:, 0:H], sh), op=ALU.mult), reads=[srcb, ssq], writes=[sq])
            P.op(eng, lambda be: be.tensor_tensor(out=out3, in0=sqv, in1=bc(gap, sh), op=ALU.mult), reads=[sq, gq], writes=[outb])

        def rope_ops(srcb, x1, x2, cs, sn, outb, o1, o2, tb, t1, t2, eng="dve"):
            P.op(eng, lambda be: be.tensor_tensor(out=t1, in0=x1, in1=cs, op=ALU.mult), reads=[srcb, rp_cur[0]], writes=[tb])
            P.op(eng, lambda be: be.tensor_tensor(out=t2, in0=x2, in1=sn, op=ALU.mult), reads=[srcb, rp_cur[0]], writes=[tb])
            P.op(eng, lambda be: be.tensor_tensor(out=o1, in0=t1, in1=t2, op=ALU.subtract), reads=[tb], writes=[outb])
            P.op(eng, lambda be: be.tensor_tensor(out=t1, in0=x2, in1=cs, op=ALU.mult), reads=[srcb, rp_cur[0]], writes=[tb])
            P.op(eng, lambda be: be.tensor_tensor(out=t2, in0=x1, in1=sn, op=ALU.mult), reads=[srcb, rp_cur[0]], writes=[tb])
            P.op(eng, lambda be: be.tensor_tensor(out=o2, in0=t1, in1=t2, op=ALU.add), reads=[tb], writes=[outb])

        rp_cur = [None]

        def load_w_cast(dst, dst_view_fn, src, rows, cols, kcs):
            fl = []
            for kc in range(kcs):
                c0 = 0
                while c0 < cols:
                    cw = min(2048, cols - c0)
                    fl.append(lambda be, kc=kc, c0=c0, cw=cw: be.dma_start(out=dst_view_fn(kc, c0, cw), in_=src[kc * 128:(kc + 1) * 128, c0:c0 + cw]))
                    c0 += cw
            P.dma("pool", fl, dst, writes=[dst])

        def transpose_to(psb, ps3, srcb, src2, n, dstb, dst3, evac="act"):
            for k in range(n):
                P.op("pe", lambda be, k=k: be.transpose(out=ps3[:, k, :], in_=src2[:, k * 128:(k + 1) * 128], identity=idb[:]), reads=[srcb, idb], writes=[psb])
            if evac == "act":
                P.op("act", lambda be: be.copy(out=dst3, in_=ps3[:, 0:n, :]), reads=[psb], writes=[dstb])
            else:
                P.op("dve", lambda be: be.tensor_copy(out=dst3, in_=ps3[:, 0:n, :]), reads=[psb], writes=[dstb])

        def interleave(gens):
            gens = list(gens)
            while gens:
                for g_ in list(gens):
                    try:
                        next(g_)
                    except StopIteration:
                        gens.remove(g_)

        def head_norm_g(srcb, src3, gap, outb, out3, H, hd, sq, ssq):
            sh = [128, H, hd]
            sqv = sq[:, 0:H * hd].rearrange("p (h d) -> p h d", h=H)
            P.op("dve", lambda be: be.tensor_tensor(out=sqv, in0=src3, in1=src3, op=ALU.mult), reads=[srcb], writes=[sq]); yield
            P.op("dve", lambda be: be.tensor_reduce(out=ssq[:, 0:H], in_=sqv, axis=AX.X, op=ALU.add), reads=[sq], writes=[ssq]); yield
            P.op("dve", lambda be: be.tensor_scalar(out=ssq[:, 0:H], in0=ssq[:, 0:H], scalar1=1.0 / hd, scalar2=EPS, op0=ALU.mult, op1=ALU.add), reads=[ssq], writes=[ssq]); yield
            P.op("act", lambda be: be.activation(out=ssq[:, 0:H], in_=ssq[:, 0:H], func=AF.Sqrt), reads=[ssq], writes=[ssq]); yield
            P.op("dve", lambda be: be.reciprocal(out=ssq[:, 0:H], in_=ssq[:, 0:H]), reads=[ssq], writes=[ssq]); yield
            P.op("dve", lambda be: be.tensor_tensor(out=sqv, in0=src3, in1=bl(ssq[:, 0:H], sh), op=ALU.mult), reads=[srcb, ssq], writes=[sq]); yield
            P.op("dve", lambda be: be.tensor_tensor(out=out3, in0=sqv, in1=bc(gap, sh), op=ALU.mult), reads=[sq, gq], writes=[outb]); yield

        def rope_g(srcb, x1, x2, cs, sn, rb, outb, o1, o2, tbs, tv):
            P.op("dve", lambda be: be.tensor_tensor(out=tv[0], in0=x1, in1=cs, op=ALU.mult), reads=[srcb, rb], writes=[tbs[0]]); yield
            P.op("dve", lambda be: be.tensor_tensor(out=tv[1], in0=x2, in1=sn, op=ALU.mult), reads=[srcb, rb], writes=[tbs[1]]); yield
            P.op("dve", lambda be: be.tensor_tensor(out=tv[2], in0=x2, in1=cs, op=ALU.mult), reads=[srcb, rb], writes=[tbs[2]]); yield
            P.op("dve", lambda be: be.tensor_tensor(out=tv[3], in0=x1, in1=sn, op=ALU.mult), reads=[srcb, rb], writes=[tbs[3]]); yield
            P.op("dve", lambda be: be.tensor_tensor(out=o1, in0=tv[0], in1=tv[1], op=ALU.subtract), reads=[tbs[0], tbs[1]], writes=[outb]); yield
            P.op("dve", lambda be: be.tensor_tensor(out=o2, in0=tv[2], in1=tv[3], op=ALU.add), reads=[tbs[2], tbs[3]], writes=[outb]); yield

        with ExitStack() as st:
            P.stack = st
            win = P.sbuf("win", [128, 8, DIN], BF16, dma=True)
            load_w_cast(win, lambda kc, c0, cw: win[:, kc, c0:c0 + cw], w_in, 1024, DIN, 8)
            gmix = P.sbuf("gmix", [128, D], F32, dma=True)
            P.dma("sp", [lambda be: be.dma_start(out=gmix[:], in_=g_mix.partition_broadcast(128))], gmix, writes=[gmix])
            xr = Ring([P.sbuf("xa%d" % i, [128, D], F32, dma=True) for i in range(3)])
            rpr = Ring([P.sbuf("rp%d" % i, [128, 96], F32, dma=True) for i in range(3)])
            junk = P.sbuf("junk1", [128, D], BF16)
            ssL = [P.sbuf("ss1_%d" % i, [128, 1], F32) for i in range(2)]; rsL = [P.sbuf("rs1_%d" % i, [128, 1], F32) for i in range(2)]
            hbL = [P.sbuf("hb1_%d" % i, [128, D], BF16) for i in range(2)]; hTL = [P.sbuf("hT1_%d" % i, [128, 8, 128], BF16) for i in range(2)]
            c0L = [P.sbuf("c0f%d" % i, [128, 512], F32) for i in range(2)]
            c1L = [P.sbuf("c1f%d" % i, [128, 512], F32, dma=True) for i in range(2)]
            c2L = [P.sbuf("c2f%d" % i, [128, 40], F32) for i in range(2)]
            c4L = [P.sbuf("c4f%d" % i, [128, 512], F32) for i in range(2)]
            sq_qa = P.sbuf("sq_qa", [128, 512], F32); ssq_qa = P.sbuf("ssq_qa", [128, 8], F32); qn_qa = P.sbuf("qn_qa", [128, 512], F32)
            tb_qa = [P.sbuf("tb_qa%d" % i, [128, 256], F32) for i in range(4)]
            sq_ka = P.sbuf("sq_ka", [128, 128], F32); ssq_ka = P.sbuf("ssq_ka", [128, 8], F32); qn_ka = P.sbuf("qn_ka", [128, 128], F32)
            tb_ka = [P.sbuf("tb_ka%d" % i, [128, 64], F32) for i in range(4)]
            tb_qi = [P.sbuf("tb_qi%d" % i, [128, 128], F32) for i in range(4)]
            sq_ki = P.sbuf("sq_ki", [128, 32], F32); ssq_ki = P.sbuf("ssq_ki", [128, 8], F32); qn_ki = P.sbuf("qn_ki", [128, 32], F32)
            tb_ki = [P.sbuf("tb_ki%d" % i, [128, 16], F32) for i in range(4)]
            sq_qm = P.sbuf("sq_qm", [128, 512], F32); ssq_qm = P.sbuf("ssq_qm", [128, 8], F32)
            qab = P.sbuf("qab", [128, 512], BF16)
            kof = Ring([P.sbuf("kof%d" % i, [128, 128], F32, dma=True) for i in range(2)])
            kbb = P.sbuf("kbb", [128, 128], BF16)
            qib = P.sbuf("qib", [128, 256], BF16)
            kif = Ring([P.sbuf("kif%d" % i, [128, 32], F32, dma=True) for i in range(2)])
            ki4 = P.sbuf("ki4", [128, 128], BF16)
            wif = Ring([P.sbuf("wif%d" % i, [128, 8], F32, dma=True) for i in range(2)])
            ubb = Ring([P.sbuf("ubb%d" % i, [128, 512], BF16, dma=True) for i in range(2)])
            ubf = P.sbuf("ubf", [128, 512], F32, dma=True)
            qmb = P.sbuf("qmb", [128, 512], BF16)
            qaT = Ring([P.sbuf("qaT%d" % i, [128, 4, 128], BF16, dma=True) for i in range(2)])
            qiT = Ring([P.sbuf("qiT%d" % i, [128, 3, 128], BF16, dma=True) for i in range(2)])
            qmT = Ring([P.sbuf("qmT%d" % i, [128, 4, 128], BF16, dma=True) for i in range(2)])
            gsb = Ring([P.sbuf("gsb%d" % i, [128, 6, 512], BF16, dma=True) for i in range(2)])
            ktT = Ring([P.sbuf("ktT%d" % i, [128, 128], BF16, dma=True) for i in range(2)])
            vbT = Ring([P.sbuf("vbT%d" % i, [128, 128], BF16, dma=True) for i in range(2)])
            kiT = Ring([P.sbuf("kiT%d" % i, [128, 128], BF16, dma=True) for i in range(2)])
            pT = P.psum("pT1", [128, 8, 128], BF16)
            pP = Ring([P.psum("pP1_%d" % i, [128, 512], F32) for i in range(4)])
            pR = Ring([P.psum("pR1_%d" % i, [128, 8, 128], BF16) for i in range(2)])

            xbufs = {}
            rbufs = {}
            tst = {}

            def a1_load(t):
                xb = xr.next(); rb = rpr.next()
                xbufs[t] = xb; rbufs[t] = rb
                P.dma("sp", [lambda be: be.dma_start(out=xb[:], in_=xsrc(t))], xb, writes=[xb])
                P.dma("sp", [lambda be: be.dma_start(out=rb[:], in_=rope[t])], rb, writes=[rb])

            def a1_R(t):
                par = t % 2
                xb = xbufs.pop(t)
                rms_rows(xb, xb[:], gmix, hbL[par], D, junk, ssL[par], rsL[par])
                transpose_to(pT, pT, hbL[par], hbL[par], 8, hTL[par], hTL[par][:])

            def a1_M(t):
                par = t % 2
                hT = hTL[par]
                samp = t >= NTP
                b = t - NTP

                def proj(ps, c0, cw):
                    for kc in range(8):
                        P.op("pe", lambda be, kc=kc: be.matmul(ps[:, 0:cw], lhsT=hT[:, kc, :], rhs=win[:, kc, c0:c0 + cw], start=(kc == 0), stop=(kc == 7)), reads=[hT, win], writes=[ps])

                c0f, c1, c2f, c4f = c0L[par], c1L[par], c2L[par], c4L[par]
                ps0 = pP.next(); proj(ps0, 0, 512)
                P.op("act", lambda be: be.copy(out=c0f[:], in_=ps0[:]), reads=[ps0], writes=[c0f])
                ps1 = pP.next(); proj(ps1, 512, 512)
                P.op("act", lambda be: be.copy(out=c1[:], in_=ps1[:]), reads=[ps1], writes=[c1])
                ps2 = pP.next(); proj(ps2, 1024, 40)
                P.op("act", lambda be: be.copy(out=c2f[:], in_=ps2[:, 0:40]), reads=[ps2], writes=[c2f])
                ps3 = pP.next(); proj(ps3, 1064, 512)
                ub_ = ubb.next()
                P.op("act", lambda be: be.copy(out=ub_[:], in_=ps3[:]), reads=[ps3], writes=[ub_])
                P.dma("sp", [lambda be: be.dma_start(out=s_ub[t], in_=ub_[:])], ub_, reads=[ub_])
                if samp or t == NTP - 1:
                    P.op("act", lambda be: be.copy(out=ubf[:], in_=ps3[:]), reads=[ps3], writes=[ubf])
                    if samp:
                        P.dma("sp", [lambda be: be.dma_start(out=npool_s[b], in_=ubf[49:64, :])], ubf, reads=[ubf])
                    else:
                        P.dma("sp", [lambda be: be.dma_start(out=npool_p, in_=ubf[113:128, :])], ubf, reads=[ubf])
                ps4 = pP.next(); proj(ps4, 1576, 512)
                P.op("act", lambda be: be.copy(out=c4f[:], in_=ps4[:]), reads=[ps4], writes=[c4f])
                gs = gsb.next()
                for j in range(6):
                    ps = pP.next(); proj(ps, 2088 + 512 * j, 512)
                    P.op("act", lambda be, ps=ps, j=j: be.activation(out=gs[:, j, :], in_=ps[:], func=AF.Sigmoid), reads=[ps], writes=[gs])
                P.dma("sp", [lambda be: be.dma_start(out=s_gate[t], in_=gs[:].rearrange("p a b -> p (a b)"))], gs, reads=[gs])

            def a1_C(t):
                par = t % 2
                rb = rbufs.pop(t)
                c0f, c1, c2f, c4f = c0L[par], c1L[par], c2L[par], c4L[par]
                cos64 = rb[:, 0:32]; sin64 = rb[:, 32:64]; cos32 = rb[:, 64:80]; sin32 = rb[:, 80:96]
                ko = kof.next(); kio = kif.next(); wo = wif.next()
                tst[t] = dict(ko=ko, kio=kio, wo=wo, c1=c1)

                def ch_qa():
                    yield from head_norm_g(c0f, c0f[:].rearrange("p (h d) -> p h d", h=8), gvec("qa"), qn_qa, qn_qa[:].rearrange("p (h d) -> p h d", h=8), 8, 64, sq_qa, ssq_qa)
                    qn4 = qn_qa[:].rearrange("p (k g d) -> p k g d", k=2, g=4)
                    qo4 = qab[:].rearrange("p (g k d) -> p k g d", g=4, k=2)
                    sh4 = [128, 2, 4, 32]
                    cs4 = cos64.unsqueeze(1).unsqueeze(1).to_broadcast(sh4)
                    sn4 = sin64.unsqueeze(1).unsqueeze(1).to_broadcast(sh4)
                    tv = [tb_[:].rearrange("p (k g d) -> p k g d", k=2, g=4) for tb_ in tb_qa]
                    yield from rope_g(qn_qa, qn4[:, :, :, 0:32], qn4[:, :, :, 32:64], cs4, sn4, rb, qab, qo4[:, :, :, 0:32], qo4[:, :, :, 32:64], tb_qa, tv)

                def ch_ka():
                    yield from head_norm_g(c1, c1[:, 0:128].rearrange("p (h d) -> p h d", h=2), gvec("ka"), qn_ka, qn_ka[:].rearrange("p (h d) -> p h d", h=2), 2, 64, sq_ka, ssq_ka)
                    k3 = qn_ka[:].rearrange("p (h d) -> p h d", h=2)
                    ko3 = ko[:].rearrange("p (h d) -> p h d", h=2)
                    sh3 = [128, 2, 32]
                    tv = [tb_[:].rearrange("p (h d) -> p h d", h=2) for tb_ in tb_ka]
                    yield from rope_g(qn_ka, k3[:, :, 0:32], k3[:, :, 32:64], bc(cos64, sh3), bc(sin64, sh3), rb, ko, ko3[:, :, 0:32], ko3[:, :, 32:64], tb_ka, tv)
                    P.op("pool", lambda be: be.tensor_copy(out=kbb[:], in_=ko[:]), reads=[ko], writes=[kbb]); yield

                def ch_qi():
                    qi3 = c1[:, 256:512].rearrange("p (h d) -> p h d", h=8)
                    qo3 = qib[:].rearrange("p (h d) -> p h d", h=8)
                    sh3 = [128, 8, 16]
                    tv = [tb_[:].rearrange("p (h d) -> p h d", h=8) for tb_ in tb_qi]
                    yield from rope_g(c1, qi3[:, :, 0:16], qi3[:, :, 16:32], bc(cos32, sh3), bc(sin32, sh3), rb, qib, qo3[:, :, 0:16], qo3[:, :, 16:32], tb_qi, tv)

                def ch_ki():
                    yield from head_norm_g(c2f, c2f[:, 0:32].unsqueeze(1), gvec("kidx"), qn_ki, qn_ki[:, 0:32].unsqueeze(1), 1, 32, sq_ki, ssq_ki)
                    tv = [tb_[:] for tb_ in tb_ki]
                    yield from rope_g(qn_ki, qn_ki[:, 0:16], qn_ki[:, 16:32], cos32, sin32, rb, kio, kio[:, 0:16], kio[:, 16:32], tb_ki, tv)
                    P.op("pool", lambda be: be.tensor_copy(out=ki4[:].rearrange("p (r c) -> p r c", r=4), in_=kio[:].unsqueeze(1).to_broadcast([128, 4, 32])), reads=[kio], writes=[ki4]); yield
                    P.op("dve", lambda be: be.tensor_scalar(out=wo[:], in0=c2f[:, 32:40], scalar1=IDX_SCALE, scalar2=None, op0=ALU.mult), reads=[c2f], writes=[wo]); yield

                def ch_qm():
                    yield from head_norm_g(c4f, c4f[:].rearrange("p (h d) -> p h d", h=4), gvec("qm"), qmb, qmb[:].rearrange("p (h d) -> p h d", h=4), 4, 128, sq_qm, ssq_qm)

                interleave([ch_qa(), ch_ka(), ch_qi(), ch_ki(), ch_qm()])

            def a1_T(t):
                samp = t >= NTP
                b = t - NTP
                d_ = tst.pop(t)
                ko, kio, wo, c1 = d_["ko"], d_["kio"], d_["wo"], d_["c1"]
                qa_t = qaT.next(); pr = pR.next()
                transpose_to(pr, pr, qab, qab, 4, qa_t, qa_t[:])
                P.dma("sp", [lambda be: be.dma_start(out=s_qa[t], in_=qa_t[:].rearrange("p a b -> p (a b)"))], qa_t, reads=[qa_t])
                if samp:
                    P.dma("sp", [lambda be: be.dma_start(out=nak_s[b], in_=ko[0:64, :]),
                                 lambda be: be.dma_start(out=nav_s[b], in_=c1[0:64, 128:256])], ko, reads=[ko, c1])
                else:
                    P.dma("sp", [lambda be: be.dma_start(out=nak_p[t * 128:(t + 1) * 128, :], in_=ko[:]),
                                 lambda be: be.dma_start(out=nav_p[t * 128:(t + 1) * 128, :], in_=c1[:, 128:256])], ko, reads=[ko, c1])
                pr = pR.next(); kt_o = ktT.next()
                transpose_to(pr, pr, kbb, kbb, 1, kt_o, kt_o[:].unsqueeze(1))
                P.dma("sp", [lambda be: be.dma_start(out=s_KT[t], in_=kt_o[:])], kt_o, reads=[kt_o])
                vb_o = vbT.next()
                P.op("pool", lambda be: be.tensor_copy(out=vb_o[:], in_=c1[:, 128:256]), reads=[c1], writes=[vb_o])
                P.dma("sp", [lambda be: be.dma_start(out=s_V[t], in_=vb_o[:])], vb_o, reads=[vb_o])
                qi_t = qiT.next(); pr = pR.next()
                for k in range(3):
                    nr = 96 if k < 2 else 64
                    P.op("pe", lambda be, k=k, nr=nr, pr=pr: be.transpose(out=pr[0:nr, k, :], in_=qib[:, 96 * k:96 * k + nr], identity=idb[:]), reads=[qib, idb], writes=[pr])
                P.op("act", lambda be, pr=pr: be.copy(out=qi_t[0:96, 0:2, :], in_=pr[0:96, 0:2, :]), reads=[pr], writes=[qi_t])
                P.op("act", lambda be, pr=pr: be.copy(out=qi_t[0:64, 2, :], in_=pr[0:64, 2, :]), reads=[pr], writes=[qi_t])
                P.dma("sp", [lambda be: be.dma_start(out=s_qi[t, 0:96, 0:256], in_=qi_t[0:96, 0:2, :].rearrange("p a b -> p (a b)")),
                             lambda be: be.dma_start(out=s_qi[t, 0:64, 256:384], in_=qi_t[0:64, 2, :])], qi_t, reads=[qi_t])
                P.dma("sp", [lambda be: be.dma_start(out=s_wi[t], in_=wo[:])], wo, reads=[wo])
                if samp:
                    P.dma("sp", [lambda be: be.dma_start(out=nik_s[b], in_=kio[0:64, :])], kio, reads=[kio])
                else:
                    P.dma("sp", [lambda be: be.dma_start(out=nik_p[t * 128:(t + 1) * 128, :], in_=kio[:])], kio, reads=[kio])
                pr = pR.next(); ki_o = kiT.next()
                transpose_to(pr, pr, ki4, ki4, 1, ki_o, ki_o[:].unsqueeze(1))
                P.dma("sp", [lambda be: be.dma_start(out=s_KI[t], in_=ki_o[:])], ki_o, reads=[ki_o])
                qm_t = qmT.next(); pr = pR.next()
                transpose_to(pr, pr, qmb, qmb, 4, qm_t, qm_t[:])
                P.dma("sp", [lambda be: be.dma_start(out=s_qm[t], in_=qm_t[:].rearrange("p a b -> p (a b)"))], qm_t, reads=[qm_t])

            a1_load(0); a1_load(1)
            a1_R(0)
            for t in range(NT):
                if t + 2 < NT:
                    a1_load(t + 2)
                a1_M(t)
                if t >= 1:
                    a1_T(t - 1)
                if t + 1 < NT:
                    a1_R(t + 1)
                a1_C(t)
            a1_T(NT - 1)
            P.barrier()
            P.emit_block()
            if KSTOP == 1:
                return nc

        with ExitStack() as kvst:
            P.stack = kvst
            KTp = P.sbuf("KTp", [128, SP], BF16)
            V1p = P.sbuf("V1p", [128, NTP, 2, 65], BF16)
            KIp = P.sbuf("KIp", [128, SP], BF16)
            KTs = [P.sbuf("KTs%d" % b, [128, 1152], BF16) for b in range(2)]
            V1s = [P.sbuf("V1s%d" % b, [128, 9, 2, 65], BF16) for b in range(2)]
            KIs = [P.sbuf("KIs%d" % b, [128, 1152], BF16) for b in range(2)]
            mkT = [P.sbuf("mkT%d" % i, [128, 4, 256], BF16) for i in range(3)]
            mv1 = [P.sbuf("mv1%d" % i, [128, 2, 4, 129], BF16) for i in range(3)]
            ubh = [P.sbuf("ubh%d" % b, [128, 512], BF16, dma=True) for b in range(2)]

            P.op("pool", lambda be: be.memset(V1p[:, :, :, 64:65], 1.0), writes=[V1p])
            for b in range(2):
                P.op("pool", lambda be, b=b: be.memset(V1s[b][:, :, :, 64:65], 1.0), writes=[V1s[b]])
                P.op("pool", lambda be, b=b: be.memset(ubh[b][:], 0.0), writes=[ubh[b]])
            for i in range(3):
                P.op("pool", lambda be, i=i: be.memset(mv1[i][:, :, :, 128:129], 1.0), writes=[mv1[i]])

            with ExitStack() as st:
                P.stack = st
                wm = P.sbuf("wm", [128, 8, 1024], BF16, dma=True)
                load_w_cast(wm, lambda kc, c0, cw: wm[:, kc, c0:c0 + cw], w_mem, 1024, 1024, 8)
                gmem = P.sbuf("gmem", [128, D], F32, dma=True)
                P.dma("sp", [lambda be: be.dma_start(out=gmem[:], in_=g_mem.partition_broadcast(128))], gmem, writes=[gmem])
                xm = Ring([P.sbuf("xm%d" % i, [128, D], F32, dma=True) for i in range(2)])
                st32 = Ring([P.sbuf("st32_%d" % i, [128, 1024], F32, dma=True) for i in range(3)])
                stb = Ring([P.sbuf("stb%d" % i, [128, 1024], BF16) for i in range(2)])
                junk = P.sbuf("junk0", [128, D], BF16)
                ss = P.sbuf("ss0", [128, 1], F32); rs = P.sbuf("rs0", [128, 1], F32)
                hb = P.sbuf("hb0", [128, D], BF16); hT = P.sbuf("hT0", [128, 8, 128], BF16)
                sq = P.sbuf("sq0", [128, 512], F32); ssq = P.sbuf("ssq0", [128, 8], F32)
                kout = Ring([P.sbuf("kout%d" % i, [128, 512], F32, dma=True) for i in range(2)])
                vout = Ring([P.sbuf("vout%d" % i, [128, 512], F32, dma=True) for i in range(2)])
                kb = P.sbuf("kb0", [128, 512], BF16)
                pT = P.psum("pT0", [128, 8, 128], BF16)
                pK = P.psum("pK0", [128, 512], F32); pV = P.psum("pV0", [128, 512], F32)
                pR = Ring([P.psum("pR0_%d" % i, [128, 8, 128], BF16) for i in range(2)])

                for mt in range(2):
                    xb = xm.next()
                    P.dma("sp", [lambda be, xb=xb, mt=mt: be.dma_start(out=xb[:], in_=mem[mt * 128:(mt + 1) * 128, :])], xb, writes=[xb])
                    rms_rows(xb, xb[:], gmem, hb, D, junk, ss, rs)
                    transpose_to(pT, pT, hb, hb, 8, hT, hT[:])
                    for kc in range(8):
                        P.op("pe", lambda be, kc=kc: be.matmul(pK[:], lhsT=hT[:, kc, :], rhs=wm[:, kc, 0:512], start=(kc == 0), stop=(kc == 7)), reads=[hT, wm], writes=[pK])
                    for kc in range(8):
                        P.op("pe", lambda be, kc=kc: be.matmul(pV[:], lhsT=hT[:, kc, :], rhs=wm[:, kc, 512:1024], start=(kc == 0), stop=(kc == 7)), reads=[hT, wm], writes=[pV])
                    kf = st32.next()
                    P.op("act", lambda be, kf=kf: be.copy(out=kf[:, 0:512], in_=pK[:]), reads=[pK], writes=[kf])
                    ko = kout.next()
                    head_norm(kf, kf[:, 0:512].rearrange("p (h d) -> p h d", h=4), gvec("km"), ko, ko[:].rearrange("p (h d) -> p h d", h=4), 4, 128, sq, ssq)
                    P.dma("sp", [lambda be, ko=ko, mt=mt: be.dma_start(out=nmk_p[mt * 128:(mt + 1) * 128, :], in_=ko[:])], ko, reads=[ko])
                    P.op("pool", lambda be, ko=ko: be.tensor_copy(out=kb[:], in_=ko[:]), reads=[ko], writes=[kb])
                    pr = pR.next()
                    transpose_to(pr, pr, kb, kb, 4, mkT[0], mkT[0][:, :, mt * 128:(mt + 1) * 128])
                    vo = vout.next()
                    P.op("act", lambda be, vo=vo: be.copy(out=vo[:], in_=pV[:]), reads=[pV], writes=[vo])
                    P.dma("sp", [lambda be, vo=vo, mt=mt: be.dma_start(out=nmv_p[mt * 128:(mt + 1) * 128, :], in_=vo[:])], vo, reads=[vo])
                    P.op("pool", lambda be, vo=vo, mt=mt: be.tensor_copy(out=mv1[0][:, mt, :, 0:128], in_=vo[:].rearrange("p (h d) -> p h d", h=4)), reads=[vo], writes=[mv1[0]])
                for b in range(2):
                    for mt in range(2):
                        kf = st32.next()
                        P.dma("sp", [lambda be, kf=kf, b=b, mt=mt: be.dma_start(out=kf[:, 0:512], in_=cmk[b, mt * 128:(mt + 1) * 128, :])], kf, writes=[kf])
                        sb_ = stb.next()
                        P.op("dve", lambda be, kf=kf, sb_=sb_: be.tensor_copy(out=sb_[:, 0:512], in_=kf[:, 0:512]), reads=[kf], writes=[sb_])
                        pr = pR.next()
                        transpose_to(pr, pr, sb_, sb_, 4, mkT[1 + b], mkT[1 + b][:, :, mt * 128:(mt + 1) * 128])
                        vf = st32.next()
                        P.dma("sp", [lambda be, vf=vf, b=b, mt=mt: be.dma_start(out=vf[:, 0:512], in_=cmv[b, mt * 128:(mt + 1) * 128, :])], vf, writes=[vf])
                        P.op("pool", lambda be, vf=vf, b=b, mt=mt: be.tensor_copy(out=mv1[1 + b][:, mt, :, 0:128], in_=vf[:, 0:512].rearrange("p (h d) -> p h d", h=4)), reads=[vf], writes=[mv1[1 + b]])
                for b in range(2):
                    ck = st32.next()
                    P.dma("sp", [lambda be, ck=ck, b=b: be.dma_start(out=ck[:].rearrange("p (k c) -> p k c", k=8), in_=cak[b].rearrange("(k p) c -> p k c", p=128))], ck, writes=[ck])
                    cb = stb.next()
                    P.op("dve", lambda be, ck=ck, cb=cb: be.tensor_copy(out=cb[:], in_=ck[:]), reads=[ck], writes=[cb])
                    pr = pR.next()
                    transpose_to(pr, pr, cb, cb, 8, KTs[b], KTs[b][:, 0:1024].rearrange("p (k c) -> p k c", k=8))
                    cv = st32.next()
                    P.dma("sp", [lambda be, cv=cv, b=b: be.dma_start(out=cv[:].rearrange("p (k c) -> p k c", k=8), in_=cav[b].rearrange("(k p) c -> p k c", p=128))], cv, writes=[cv])
                    P.op("pool", lambda be, cv=cv, b=b: be.tensor_copy(out=V1s[b][:, 0:8, :, 0:64], in_=cv[:].rearrange("p (k h d) -> p k h d", k=8, h=2)), reads=[cv], writes=[V1s[b]])
                    ci = st32.next()
                    P.dma("sp", [lambda be, ci=ci, b=b: be.dma_start(out=ci[:, 0:256].rearrange("p (k c) -> p k c", k=8), in_=cik[b].rearrange("(k p) c -> p k c", p=128))], ci, writes=[ci])
                    c4 = stb.next()
                    P.op("dve", lambda be, ci=ci, c4=c4: be.tensor_copy(out=c4[:].rearrange("p (k r c) -> p k r c", k=8, r=4), in_=ci[:, 0:256].rearrange("p (k c) -> p k c", k=8).unsqueeze(2).to_broadcast([128, 8, 4, 32])), reads=[ci], writes=[c4])
                    pr = pR.next()
                    transpose_to(pr, pr, c4, c4, 8, KIs[b], KIs[b][:, 0:1024].rearrange("p (k c) -> p k c", k=8))
                    P.dma("pool", [lambda be, b=b: be.dma_start(out=ubh[b][113:128, :], in_=cpool[b])], ubh[b], writes=[ubh[b]])
                kvl = P.buf("kvl", None, dma=True)
                fl = []
                for q in range((NTP + 15) // 16):
                    t0_ = q * 16; t1_ = min(NTP, t0_ + 16)
                    fl.append(lambda be, t0_=t0_, t1_=t1_: be.dma_start(out=KTp[:, t0_ * 128:t1_ * 128].rearrange("p (t k) -> p t k", k=128), in_=s_KT[t0_:t1_].rearrange("t p k -> p t k")))
                    fl.append(lambda be, t0_=t0_, t1_=t1_: be.dma_start(out=KIp[:, t0_ * 128:t1_ * 128].rearrange("p (t k) -> p t k", k=128), in_=s_KI[t0_:t1_].rearrange("t p k -> p t k")))
                    for hh in range(2):
                        fl.append(lambda be, t0_=t0_, t1_=t1_, hh=hh: be.dma_start(out=V1p[:, t0_:t1_, hh, 0:64], in_=s_V[t0_:t1_, :, hh * 64:(hh + 1) * 64].rearrange("t p d -> p t d")))
                for b in range(2):
                    fl.append(lambda be, b=b: be.dma_start(out=KTs[b][:, 1024:1152], in_=s_KT[NTP + b]))
                    fl.append(lambda be, b=b: be.dma_start(out=KIs[b][:, 1024:1152], in_=s_KI[NTP + b]))
                    fl.append(lambda be, b=b: be.dma_start(out=V1s[b][:, 8, :, 0:64], in_=s_V[NTP + b].rearrange("p (h d) -> p h d", h=2)))
                P.dma("sp", fl, kvl, writes=[KTp, KIp, V1p, KTs[0], KTs[1], KIs[0], KIs[1], V1s[0], V1s[1]])
                P.barrier()
                P.emit_block()
                if KSTOP == 2:
                    return nc

            with ExitStack() as st:
                P.stack = st
                wpl = P.sbuf("wpl", [128, 4, 128], BF16, dma=True)
                P.dma("pool", [lambda be: be.dma_start(out=wpl[:], in_=w_pool.rearrange("g c e -> c g e"))], wpl, writes=[wpl])
                bnd = P.sbuf("bnd", [128, 3, 4, 128], BF16, dma=True)
                P.dma("pool", [lambda be: be.dma_start(out=bnd[:].rearrange("p a g t -> p (a g t)"), in_=bands.rearrange("p a g t -> p (a g t)"))], bnd, writes=[bnd])
                sc = P.sbuf("sc", [128, SCW], F32)
                scC = [P.buf("scC%d" % c, None) for c in range((SCW + 511) // 512)]
                pw = P.sbuf("pw", [128, NBIS + 1], F32); wk = P.sbuf("wk", [128, NBIS + 1], F32)
                for k in range(NBIS + 1):
                    P.op("pool", lambda be, k=k: be.memset(pw[:, k:k + 1], 2.0 ** -(k + 1)), writes=[pw])
                Mq = P.sbuf("Mq", [128, SCW], BF16)
                MT = P.sbuf("MT", [128, SCW // 128, 128], BF16)
                rl = Ring([P.sbuf("rl%d" % i, [128, 512], F32) for i in range(3)])
                er = Ring([P.sbuf("er%d" % i, [128, 4, 128], BF16) for i in range(4)])
                pr_ = Ring([P.sbuf("pp%d" % i, [128, 4, 128], BF16) for i in range(4)])
                ld = {}
                qaL = Ring([P.sbuf("qaL%d" % i, [128, 4, 128], BF16, dma=True) for i in range(3)])
                qiL = Ring([P.sbuf("qiL%d" % i, [128, 3, 128], BF16, dma=True) for i in range(3)])
                qmL = Ring([P.sbuf("qmL%d" % i, [128, 4, 128], BF16, dma=True) for i in range(3)])
                wiL = Ring([P.sbuf("wiL%d" % i, [128, 8], F32, dma=True) for i in range(3)])
                ubL = Ring([P.sbuf("ubL%d" % i, [128, 512], BF16, dma=True) for i in range(4)])
                lo = P.sbuf("lo", [128, 1], F32); w0 = P.sbuf("w0", [128, 1], F32); mx = P.sbuf("mx", [128, 1], F32)
                mid = P.sbuf("mid", [128, 1], F32); cnt = P.sbuf("cnt", [128, 1], F32); tt_ = P.sbuf("tt", [128, 1], F32)
                thr = Ring([P.sbuf("thr%d" % i, [128, 1], F32) for i in range(2)])
                rec = P.sbuf("rec", [128, 8], F32); recm = P.sbuf("recm", [128, 4], F32)
                qa2 = P.sbuf("qa2", [128, 4, 2, 128], BF16)
                a_sb = P.sbuf("a_sb", [128, 512], BF16); m_sb = P.sbuf("m_sb", [128, 512], BF16)
                em = [P.sbuf("em%d" % i, [128, 4, 128], BF16) for i in range(2)]
                pTs = P.sbuf("pTs", [128, 4, 128], BF16)
                br = Ring([P.sbuf("br%d" % i, [128, 3, 4, 128], BF16, dma=True) for i in range(2)])
                pI = Ring([P.psum("pI%d" % i, [128, 512], F32) for i in range(2)])
                pM = Ring([P.psum("pM%d" % i, [128, 8, 128], BF16) for i in range(2)])
                pS = Ring([P.psum("pS%d" % i, [128, 4, 128], F32) for i in range(2)])
                pA = [P.psum("pA%d" % i, [128, 512], F32) for i in range(2)]

                seqs = []
                for t in range(NTP):
                    seqs.append(dict(t=t, KT=KTp, V1=V1p, KI=KIp, nkt=t + 1, samp=False, mi=0))
                for b in range(2):
                    seqs.append(dict(t=NTP + b, KT=KTs[b], V1=V1s[b], KI=KIs[b], nkt=9, samp=True, mi=1 + b, b=b))

                def a2_load(i):
                    s = seqs[i]; t = s["t"]
                    d = dict(qa=qaL.next(), qi=qiL.next(), qm=qmL.next(), wi=wiL.next(), ub=ubL.next())
                    ld[i] = d
                    P.dma("sp", [lambda be: be.dma_start(out=d["qa"][:].rearrange("p a b -> p (a b)"), in_=s_qa[t])], d["qa"], writes=[d["qa"]])
                    P.dma("sp", [lambda be: be.dma_start(out=d["qi"][0:96, 0:2, :].rearrange("p a b -> p (a b)"), in_=s_qi[t, 0:96, 0:256]),
                             lambda be: be.dma_start(out=d["qi"][0:64, 2, :], in_=s_qi[t, 0:64, 256:384])], d["qi"], writes=[d["qi"]])
                    P.dma("sp", [lambda be: be.dma_start(out=d["qm"][:].rearrange("p a b -> p (a b)"), in_=s_qm[t])], d["qm"], writes=[d["qm"]])
                    P.dma("sp", [lambda be: be.dma_start(out=d["wi"][:], in_=s_wi[t])], d["wi"], writes=[d["wi"]])
                    P.dma("sp", [lambda be: be.dma_start(out=d["ub"][:], in_=s_ub[t])], d["ub"], writes=[d["ub"]])

                def stageA(i):
                    s = seqs[i]; d = ld[i]; S = s["nkt"] * 128
                    KI = s["KI"]; qi = d["qi"]; wi = d["wi"]
                    nch = (S + 511) // 512
                    for h in range(8):
                        r0 = (h % 3) * 32
                        for c in range(nch):
                            c0 = c * 512; cw = min(512, S - c0)
                            ps = pI.next()
                            P.op("pe", lambda be, ps=ps, r0=r0, h=h, c0=c0, cw=cw: be.matmul(ps[:, 0:cw], lhsT=qi[r0:r0 + 32, h // 3, :], rhs=KI[r0:r0 + 32, c0:c0 + cw], start=True, stop=True), reads=[qi, KI], writes=[ps])
                            r = rl.next()
                            P.op("act", lambda be, ps=ps, r=r, cw=cw: be.activation(out=r[:, 0:cw], in_=ps[:, 0:cw], func=AF.Relu), reads=[ps], writes=[r])
                            if h == 0:
                                P.op("dve", lambda be, r=r, c0=c0, cw=cw: be.tensor_scalar(out=sc[:, c0:c0 + cw], in0=r[:, 0:cw], scalar1=wi[:, 0:1], scalar2=None, op0=ALU.mult), reads=[r, wi], writes=[scC[c]])
                            else:
                                P.op("dve", lambda be, r=r, h=h, c0=c0, cw=cw: be.scalar_tensor_tensor(out=sc[:, c0:c0 + cw], in0=r[:, 0:cw], scalar=wi[:, h:h + 1], in1=sc[:, c0:c0 + cw], op0=ALU.mult, op1=ALU.add), reads=[r, wi, scC[c]], writes=[scC[c]])
                    if s["samp"]:
                        P.op("dve", lambda be: be.memset(sc[:, S - 64:S], NEG), writes=[scC[nch - 1]])
                    else:
                        P.op("dve", lambda be: be.memset(sc[0:64, S - 64:S], NEG), writes=[scC[nch - 1]])

                def stageC1(i):
                    s = seqs[i]; S = s["nkt"] * 128
                    nch = (S + 511) // 512
                    scs = scC[0:nch]
                    if S <= 256:
                        P.op("dve", lambda be: be.tensor_scalar(out=Mq[:, 0:S], in0=sc[:, 0:S], scalar1=-1.0e29, scalar2=None, op0=ALU.is_ge), reads=scs, writes=[Mq])
                        return
                    nlo = 512 if S - 64 >= 512 else S - 64
                    P.op("dve", lambda be: be.tensor_reduce(out=lo[:], in_=sc[:, 0:nlo], axis=AX.X, op=ALU.min), reads=scs, writes=[lo])
                    P.op("dve", lambda be: be.tensor_reduce(out=mx[:], in_=sc[:, 0:S], axis=AX.X, op=ALU.max), reads=scs, writes=[mx])
                    P.op("dve", lambda be: be.tensor_tensor(out=w0[:], in0=mx[:], in1=lo[:], op=ALU.subtract), reads=[mx, lo], writes=[w0])
                    P.op("dve", lambda be: be.tensor_scalar(out=w0[:], in0=w0[:], scalar1=1.0 + 2.0 ** -10, scalar2=1e-12, op0=ALU.mult, op1=ALU.add), reads=[w0], writes=[w0])
                    P.op("dve", lambda be: be.tensor_scalar(out=wk[:], in0=pw[:], scalar1=w0[:, 0:1], scalar2=None, op0=ALU.mult), reads=[pw, w0], writes=[wk])
                    P.op("dve", lambda be: be.tensor_tensor(out=mid[:], in0=lo[:], in1=wk[:, 0:1], op=ALU.add), reads=[lo, wk], writes=[mid])
                    for k in range(NBIS):
                        P.op("dve", lambda be: be.tensor_scalar(out=Mq[:, 0:S], in0=sc[:, 0:S], scalar1=mid[:, 0:1], scalar2=None, op0=ALU.is_ge, op1=ALU.add, accum_out=cnt[:]), reads=scs + [mid], writes=[Mq, cnt])
                        P.op("dve", lambda be: be.tensor_scalar(out=tt_[:], in0=cnt[:], scalar1=255.5, scalar2=0.5, op0=ALU.is_ge, op1=ALU.subtract), reads=[cnt], writes=[tt_])
                        P.op("dve", lambda be, k=k: be.scalar_tensor_tensor(out=mid[:], in0=tt_[:], scalar=wk[:, k:k + 1], in1=mid[:], op0=ALU.mult, op1=ALU.add), reads=[tt_, wk, mid], writes=[mid])
                    P.op("dve", lambda be: be.tensor_tensor(out=lo[:], in0=mid[:], in1=wk[:, NBIS:NBIS + 1], op=ALU.subtract), reads=[mid, wk], writes=[lo])
                    P.op("dve", lambda be: be.tensor_scalar(out=Mq[:, 0:S], in0=sc[:, 0:S], scalar1=lo[:, 0:1], scalar2=None, op0=ALU.is_ge), reads=scs + [lo], writes=[Mq])

                def stageC2(i):
                    s = seqs[i]; nkt = s["nkt"]
                    for j in range((nkt + 7) // 8):
                        n = min(8, nkt - 8 * j)
                        pm = pM.next()
                        transpose_to(pm, pm, Mq, Mq[:, j * 1024:j * 1024 + n * 128], n, MT, MT[:, 8 * j:8 * j + n, :])

                def stageB(i):
                    s = seqs[i]; d = ld.pop(i); nkt = s["nkt"]; t = s["t"]
                    KT = s["KT"]; V1 = s["V1"]; qa = d["qa"]
                    mi = s["mi"]; qm = d["qm"]; ub = d["ub"]
                    bo = br.next()
                    for mt in range(2):
                        ps = pS.next()
                        for h in range(4):
                            P.op("pe", lambda be, ps=ps, h=h, mt=mt: be.matmul(ps[:, h, :], lhsT=mkT[mi][:, h, mt * 128:(mt + 1) * 128], rhs=qm[:, h, :], start=True, stop=True), reads=[mkT[mi], qm], writes=[ps])
                        P.op("act", lambda be, ps=ps, mt=mt: be.activation(out=em[mt][:], in_=ps[:], func=AF.Exp, scale=128.0 ** -0.5), reads=[ps], writes=[em[mt]])
                    if s["samp"]:
                        prev = ubh[s["b"]]; ai = 0
                    elif t == 0:
                        prev = None; ai = 2
                    else:
                        prev = s_prev_ub[0]; ai = 0
                    psp = pI.next()
                    pp3 = psp[:].rearrange("p (g t) -> p g t", g=4)
                    for g in range(4):
                        P.op("pe", lambda be, g=g: be.matmul(pp3[:, g, :], lhsT=ub[:, g * 128:(g + 1) * 128], rhs=bnd[:, ai, g, :], start=True, stop=(prev is None)), reads=[ub, bnd], writes=[psp])
                        if prev is not None:
                            P.op("pe", lambda be, g=g: be.matmul(pp3[:, g, :], lhsT=prev[:, g * 128:(g + 1) * 128], rhs=bnd[:, 1, g, :], start=False, stop=True), reads=[prev, bnd], writes=[psp])
                    P.op("act", lambda be: be.copy(out=pTs[:], in_=pp3), reads=[psp], writes=[pTs])
                    P.op("pool", lambda be: be.memset(qa2[:], 0.0), writes=[qa2])
                    P.op("pool", lambda be: be.tensor_copy(out=qa2[0:64, :, 0, :], in_=qa[0:64, :, :]), reads=[qa], writes=[qa2])
                    P.op("pool", lambda be: be.tensor_copy(out=qa2[64:128, :, 1, :], in_=qa[64:128, :, :]), reads=[qa], writes=[qa2])
                    accs = [pA[kv][:, 0:260].rearrange("p (h d) -> p h d", h=4) for kv in range(2)]
                    nch2 = (nkt + 1) // 2
                    its = [(g, c) for g in range(4) for c in range(nch2)]
                    stq = {}

                    def emit_qk(j):
                        g, c = its[j]
                        n = min(2, nkt - 2 * c)
                        ps = pS.next()
                        ps4 = ps[:].rearrange("p (k h) q -> p k h q", k=2)
                        for k in range(n):
                            kt = 2 * c + k
                            P.op("pe", lambda be, k=k, kt=kt: be.matmul(ps4[:, k, :, :], lhsT=KT[:, kt * 128:(kt + 1) * 128], rhs=qa2[:, g, :, :], start=True, stop=True), reads=[KT, qa2], writes=[ps])
                        e = er.next()
                        e4 = e[:].rearrange("p (k h) q -> p k h q", k=2)
                        P.op("act", lambda be: be.activation(out=e4[:, 0:n, :, :], in_=ps4[:, 0:n, :, :], func=AF.Exp, scale=0.125), reads=[ps], writes=[e])
                        p_ = pr_.next()
                        p4 = p_[:].rearrange("p (k h) q -> p k h q", k=2)
                        P.op("pool", lambda be: be.tensor_tensor(out=p4[:, 0:n, :, :], in0=e4[:, 0:n, :, :], in1=MT[:, 2 * c:2 * c + n, :].unsqueeze(2).to_broadcast([128, n, 2, 128]), op=ALU.mult), reads=[e, MT], writes=[p_])
                        stq[j] = (p_, p4, n)

                    def emit_pv(j):
                        g, c = its[j]
                        p_, p4, n = stq.pop(j)
                        for k in range(n):
                            kt = 2 * c + k
                            for hh in range(2):
                                P.op("pe", lambda be, k=k, kt=kt, hh=hh: be.matmul(accs[hh][:, g, :], lhsT=p4[:, k, hh, :], rhs=V1[:, kt, hh, :], start=(kt == 0), stop=(kt == nkt - 1)), reads=[p_, V1], writes=[pA[hh]])

                    LA = 2
                    for j in range(len(its) + LA):
                        if j < len(its):
                            emit_qk(j)
                        if j - LA >= 0:
                            emit_pv(j - LA)
                    pm1 = pS.next(); pm2 = pS.next()
                    accm = [pm1[:].rearrange("p a q -> p (a q)")[:, 0:258].rearrange("p (h d) -> p h d", h=2),
                            pm2[:].rearrange("p a q -> p (a q)")[:, 0:258].rearrange("p (h d) -> p h d", h=2)]
                    pmb = [pm1, pm2]
                    for h in range(4):
                        for mt in range(2):
                            P.op("pe", lambda be, h=h, mt=mt: be.matmul(accm[h // 2][:, h % 2, :], lhsT=em[mt][:, h, :], rhs=mv1[mi][:, mt, h, :], start=(mt == 0), stop=(mt == 1)), reads=[em[mt], mv1[mi]], writes=[pmb[h // 2]])
                    ps2 = pI.next()
                    py3 = ps2[:].rearrange("p (g t) -> p g t", g=4)
                    for g in range(4):
                        P.op("pe", lambda be, g=g: be.matmul(py3[:, g, :], lhsT=wpl[:, g, :], rhs=pTs[:, g, :], start=True, stop=True), reads=[wpl, pTs], writes=[ps2])
                    for kv in range(2):
                        acc = accs[kv]
                        P.op("dve", lambda be, acc=acc, kv=kv: be.reciprocal(out=rec[:, 4 * kv:4 * kv + 4], in_=acc[:, :, 64]), reads=[pA[kv]], writes=[rec])
                        P.op("dve", lambda be, acc=acc, kv=kv: be.tensor_tensor(out=a_sb[:, 256 * kv:256 * kv + 256].rearrange("p (h d) -> p h d", h=4), in0=acc[:, :, 0:64], in1=bl(rec[:, 4 * kv:4 * kv + 4], [128, 4, 64]), op=ALU.mult), reads=[pA[kv], rec], writes=[a_sb])
                    for hh in range(2):
                        P.op("dve", lambda be, hh=hh: be.reciprocal(out=recm[:, 2 * hh:2 * hh + 2], in_=accm[hh][:, :, 128]), reads=[pmb[hh]], writes=[recm])
                        P.op("dve", lambda be, hh=hh: be.tensor_tensor(out=m_sb[:, 256 * hh:256 * hh + 256].rearrange("p (h d) -> p h d", h=2), in0=accm[hh][:, :, 0:128], in1=bl(recm[:, 2 * hh:2 * hh + 2], [128, 2, 128]), op=ALU.mult), reads=[pmb[hh], recm], writes=[m_sb])
                    P.op("dve", lambda be: be.tensor_tensor(out=bo[:, 1, :, :], in0=py3, in1=bl(spl[:], [128, 4, 128]), op=ALU.mult), reads=[ps2, spl], writes=[bo])
                    pm = pM.next()
                    transpose_to(pm, pm, a_sb, a_sb, 4, bo, bo[:, 0, :, :])
                    pm = pM.next()
                    transpose_to(pm, pm, m_sb, m_sb, 4, bo, bo[:, 2, :, :])
                    P.dma("sp", [lambda be: be.dma_start(out=s_br[t], in_=bo[:].rearrange("p a b c -> p (a b c)"))], bo, reads=[bo])
                    s_prev_ub[0] = ub

                s_prev_ub = [None]
                nseq = len(seqs)
                a2_load(0); a2_load(1)
                stageA(0); stageC1(0); stageC2(0)
                for i in range(nseq):
                    if i + 2 < nseq:
                        a2_load(i + 2)
                    if i + 1 < nseq:
                        stageA(i + 1); stageC1(i + 1)
                    stageB(i)
                    if i + 1 < nseq:
                        stageC2(i + 1)
                P.barrier()
                P.emit_block()
                if KSTOP == 3:
                    return nc

        with ExitStack() as st:
            P.stack = st
            wo = P.sbuf("wo", [128, 3, 4, D], BF16, dma=True)
            fl = []
            for b in range(3):
                for kc in range(4):
                    fl.append(lambda be, b=b, kc=kc: be.dma_start(out=wo[:, b, kc, :], in_=w_o[b][kc * 128:(kc + 1) * 128, :]))
            P.dma("pool", fl, wo, writes=[wo])
            wout = P.sbuf("wout", [128, 8, D], BF16, dma=True)
            load_w_cast(wout, lambda kc, c0, cw: wout[:, kc, c0:c0 + cw], w_out, 1024, D, 8)
            brL = Ring([P.sbuf("brL%d" % i, [128, 3, 4, 128], BF16, dma=True) for i in range(4)])
            gtL = Ring([P.sbuf("gtL%d" % i, [128, 3, D], BF16, dma=True) for i in range(4)])
            xL = Ring([P.sbuf("xL%d" % i, [128, D], F32, dma=True) for i in range(4)])
            mixedL = [P.sbuf("mixed%d" % i, [128, D], F32) for i in range(2)]; tmpmL = [P.sbuf("tmpm%d" % i, [128, D], F32) for i in range(2)]
            mxbL = [P.sbuf("mxb%d" % i, [128, D], BF16) for i in range(2)]; mxT = P.sbuf("mxT", [128, 8, 128], BF16)
            x1o = Ring([P.sbuf("x1o%d" % i, [128, D], F32, dma=True) for i in range(2)])
            pt = Ring([P.psum("pt3_%d" % i, [128, D], F32) for i in range(2)])
            pT = P.psum("pT3", [128, 8, 128], BF16)
            pO = P.psum("pO3", [128, D], F32)
            l3 = {}

            def a3_load(t):
                d = dict(br=brL.next(), gt=gtL.next(), x=xL.next())
                l3[t] = d
                P.dma("sp", [lambda be: be.dma_start(out=d["br"][:].rearrange("p a b c -> p (a b c)"), in_=s_br[t])], d["br"], writes=[d["br"]])
                P.dma("sp", [lambda be: be.dma_start(out=d["gt"][:].rearrange("p a b -> p (a b)"), in_=s_gate[t])], d["gt"], writes=[d["gt"]])
                P.dma("sp", [lambda be: be.dma_start(out=d["x"][:], in_=xsrc(t))], d["x"], writes=[d["x"]])

            a3st = {}

            def a3_X(t):
                d = l3[t]
                par = t % 2
                brt = d["br"]; gt = d["gt"]
                mixed = mixedL[par]; tmpm = tmpmL[par]; mxb = mxbL[par]
                for b in range(3):
                    ps = pt.next()
                    for half in range(2):
                        for kc in range(4):
                            P.op("pe", lambda be, ps=ps, b=b, half=half, kc=kc: be.matmul(ps[:, half * 512:(half + 1) * 512], lhsT=brt[:, b, kc, :], rhs=wo[:, b, kc, half * 512:(half + 1) * 512], start=(kc == 0), stop=(kc == 3)), reads=[brt, wo], writes=[ps])
                    if b == 0:
                        P.op("dve", lambda be, ps=ps: be.tensor_tensor(out=mixed[:], in0=ps[:], in1=gt[:, 0, :], op=ALU.mult), reads=[ps, gt], writes=[mixed])
                    else:
                        P.op("dve", lambda be, ps=ps, b=b: be.tensor_tensor(out=tmpm[:], in0=ps[:], in1=gt[:, b, :], op=ALU.mult), reads=[ps, gt], writes=[tmpm])
                        if b == 1:
                            P.op("pool", lambda be: be.tensor_tensor(out=mixed[:], in0=mixed[:], in1=tmpm[:], op=ALU.add), reads=[mixed, tmpm], writes=[mixed])
                        else:
                            P.op("pool", lambda be: be.tensor_tensor(out=mxb[:], in0=mixed[:], in1=tmpm[:], op=ALU.add), reads=[mixed, tmpm], writes=[mxb])

            def a3_Y(t):
                d = l3.pop(t)
                par = t % 2
                xb = d["x"]; mxb = mxbL[par]
                transpose_to(pT, pT, mxb, mxb, 8, mxT, mxT[:])
                for half in range(2):
                    for kc in range(8):
                        P.op("pe", lambda be, half=half, kc=kc: be.matmul(pO[:, half * 512:(half + 1) * 512], lhsT=mxT[:, kc, :], rhs=wout[:, kc, half * 512:(half + 1) * 512], start=(kc == 0), stop=(kc == 7)), reads=[mxT, wout], writes=[pO])
                xo = x1o.next()
                P.op("dve", lambda be: be.tensor_tensor(out=xo[:], in0=pO[:], in1=xb[:], op=ALU.add), reads=[pO, xb], writes=[xo])
                P.dma("sp", [lambda be: be.dma_start(out=s_x1[t], in_=xo[:])], xo, reads=[xo])

            a3_load(0); a3_load(1)
            a3_X(0)
            for t in range(NT):
                if t + 2 < NT:
                    a3_load(t + 2)
                if t + 1 < NT:
                    a3_X(t + 1)
                a3_Y(t)
            P.barrier()
            P.emit_block()
            if KSTOP == 4:
                return nc

        with ExitStack() as st:
            P.stack = st
            wg = P.sbuf("wg", [128, 8, DFF], BF16, dma=True)
            load_w_cast(wg, lambda kc, c0, cw: wg[:, kc, c0:c0 + cw], w_gate, 1024, DFF, 8)
            wu = P.sbuf("wu", [128, 8, DFF], BF16, dma=True)
            load_w_cast(wu, lambda kc, c0, cw: wu[:, kc, c0:c0 + cw], w_up, 1024, DFF, 8)
            wd = P.sbuf("wd", [128, 22, D], BF16, dma=True)
            load_w_cast(wd, lambda kc, c0, cw: wd[:, kc, c0:c0 + cw], w_down, DFF, D, 22)
            gffn = P.sbuf("gffn", [128, D], F32, dma=True)
            P.dma("sp", [lambda be: be.dma_start(out=gffn[:], in_=g_ffn.partition_broadcast(128))], gffn, writes=[gffn])
            x1L = Ring([P.sbuf("x1L%d" % i, [128, D], F32, dma=True) for i in range(4)])
            junk = P.sbuf("junkb", [128, D], BF16)
            ssL = [P.sbuf("ssb%d" % i, [128, 1], F32) for i in range(2)]; rsL = [P.sbuf("rsb%d" % i, [128, 1], F32) for i in range(2)]
            hbL = [P.sbuf("hbb%d" % i, [128, D], BF16) for i in range(2)]; hTL = [P.sbuf("hTb%d" % i, [128, 8, 128], BF16) for i in range(2)]
            sg = Ring([P.sbuf("sg%d" % i, [128, 4, 128], F32) for i in range(2)])
            gT = P.sbuf("gT", [128, 22, 128], BF16)
            yo = Ring([P.sbuf("yo%d" % i, [128, D], F32, dma=True) for i in range(2)])
            pT = P.psum("pTb", [128, 8, 128], BF16)
            pG = Ring([P.psum("pG%d" % i, [128, 4, 128], F32) for i in range(2)])
            pU = Ring([P.psum("pU%d" % i, [128, 4, 128], F32) for i in range(2)])
            pO = P.psum("pOb", [128, D], F32)
            lb = {}

            def b_load(t):
                xb = x1L.next()
                lb[t] = xb
                P.dma("sp", [lambda be: be.dma_start(out=xb[:], in_=s_x1[t])], xb, writes=[xb])

            def b_R(t):
                par = t % 2
                xb = lb[t]
                rms_rows(xb, xb[:], gffn, hbL[par], D, junk, ssL[par], rsL[par])
                transpose_to(pT, pT, hbL[par], hbL[par], 8, hTL[par], hTL[par][:])

            def b_G(t):
                hT = hTL[t % 2]
                for fg in range(6):
                    nf = min(4, 22 - 4 * fg)
                    pg = pG.next(); pu = pU.next()
                    for j in range(nf):
                        fc = 4 * fg + j
                        for kc in range(8):
                            P.op("pe", lambda be, pg=pg, j=j, fc=fc, kc=kc: be.matmul(pg[:, j, :], lhsT=wg[:, kc, fc * 128:(fc + 1) * 128], rhs=hT[:, kc, :], start=(kc == 0), stop=(kc == 7)), reads=[wg, hT], writes=[pg])
                    for j in range(nf):
                        fc = 4 * fg + j
                        for kc in range(8):
                            P.op("pe", lambda be, pu=pu, j=j, fc=fc, kc=kc: be.matmul(pu[:, j, :], lhsT=wu[:, kc, fc * 128:(fc + 1) * 128], rhs=hT[:, kc, :], start=(kc == 0), stop=(kc == 7)), reads=[wu, hT], writes=[pu])
                    s_ = sg.next()
                    P.op("act", lambda be, pg=pg, s_=s_, nf=nf: be.activation(out=s_[:, 0:nf, :], in_=pg[:, 0:nf, :], func=AF.Silu), reads=[pg], writes=[s_])
                    P.op("dve", lambda be, pu=pu, s_=s_, nf=nf, fg=fg: be.tensor_tensor(out=gT[:, 4 * fg:4 * fg + nf, :], in0=pu[:, 0:nf, :], in1=s_[:, 0:nf, :], op=ALU.mult), reads=[pu, s_], writes=[gT])

            def b_D(t):
                xb = lb.pop(t)
                for half in range(2):
                    for fc in range(22):
                        P.op("pe", lambda be, half=half, fc=fc: be.matmul(pO[:, half * 512:(half + 1) * 512], lhsT=gT[:, fc, :], rhs=wd[:, fc, half * 512:(half + 1) * 512], start=(fc == 0), stop=(fc == 21)), reads=[gT, wd], writes=[pO])
                yb = yo.next()
                P.op("dve", lambda be: be.tensor_tensor(out=yb[:], in0=pO[:], in1=xb[:], op=ALU.add), reads=[pO, xb], writes=[yb])
                if t < NTP:
                    P.dma("sp", [lambda be: be.dma_start(out=y_p[t * 128:(t + 1) * 128, :], in_=yb[:])], yb, reads=[yb])
                else:
                    P.dma("sp", [lambda be: be.dma_start(out=y_s[t - NTP], in_=yb[0:64, :])], yb, reads=[yb])

            b_load(0); b_load(1)
            b_R(0)
            for t in range(NT):
                if t + 2 < NT:
                    b_load(t + 2)
                b_G(t)
                if t + 1 < NT:
                    b_R(t + 1)
                b_D(t)
            P.barrier()
            P.emit_block()
            if KSTOP == 5:
                return nc
    return nc


def _consts():
    NT = NTP + 2
    theta = np.float32(10000.0)
    tab = np.zeros((NT, 128, 96), np.float32)
    inv64 = (theta ** (-np.arange(32, dtype=np.float32) / np.float32(32))).astype(np.float32)
    inv32 = (theta ** (-np.arange(16, dtype=np.float32) / np.float32(16))).astype(np.float32)
    for t in range(NT):
        pos = (np.arange(128) + (t * 128 if t < NTP else 1024)).astype(np.float32)
        a64 = (pos[:, None] * inv64[None, :]).astype(np.float32)
        a32 = (pos[:, None] * inv32[None, :]).astype(np.float32)
        tab[t, :, 0:32] = np.cos(a64.astype(np.float64)); tab[t, :, 32:64] = np.sin(a64.astype(np.float64))
        tab[t, :, 64:80] = np.cos(a32.astype(np.float64)); tab[t, :, 80:96] = np.sin(a32.astype(np.float64))
    bands = np.zeros((128, 3, 4, 128), np.float32)
    tp = np.arange(128)[:, None]; tq = np.arange(128)[None, :]
    for g, w in enumerate((2, 4, 8, 16)):
        inwin = (tp <= tq) & (tp > tq - w)
        bands[:, 0, g, :] = inwin / w - (tp == tq)
        bands[:, 1, g, :] = ((tp - 128) > (tq - w)) / w
        cntf = np.minimum(w, tq + 1).astype(np.float64)
        bands[:, 2, g, :] = inwin / cntf - (tp == tq)
    return tab, bands.astype(np.float32), np.eye(128, dtype=np.float32)


_CACHE = {}


def kernel(x_prompt, x_sample, mem_prompt, cache_a_k, cache_a_v, cache_idx_k, cache_pool, cache_mem_k,
           cache_mem_v, g_mix, w_in, g_qa, g_ka, g_kidx, g_qm, g_mem, w_mem_kv, g_km, w_pool, s_pool,
           w_oa, w_ob, w_om, w_out, g_ffn, w_gate, w_up, w_down):
    f = lambda a: np.ascontiguousarray(np.asarray(a, dtype=np.float32))
    if "nc" not in _CACHE:
        _CACHE["nc"] = build_program()
        _CACHE["consts"] = _consts()
    nc = _CACHE["nc"]
    tab, bands, ident = _CACHE["consts"]
    xs_pad = np.zeros((16, 128, D), np.float32)
    xs_pad[:, 0:64, :] = f(x_sample)
    shared = {
        "w_in": f(w_in[0]), "w_mem": f(w_mem_kv[0]), "w_pool": f(w_pool[0]), "w_oa": f(w_oa[0]), "w_ob": f(w_ob[0]),
        "w_om": f(w_om[0]), "w_out": f(w_out[0]), "w_gate": f(w_gate[0]), "w_up": f(w_up[0]), "w_down": f(w_down[0]),
        "g_mix": f(g_mix), "g_ffn": f(g_ffn), "g_mem": f(g_mem), "g_qa": f(g_qa), "g_ka": f(g_ka), "g_kidx": f(g_kidx),
        "g_qm": f(g_qm), "g_km": f(g_km), "s_pool": f(np.asarray(s_pool[0]).reshape(4, 128).T),
        "rope": tab, "bands": bands, "ident": ident,
    }
    in_maps = []
    for c in range(8):
        m = dict(shared)
        m["x_p"] = f(x_prompt[c]); m["x_s"] = np.ascontiguousarray(xs_pad[2 * c:2 * c + 2]); m["mem"] = f(mem_prompt[c])
        m["cak"] = f(np.asarray(cache_a_k[0, 2 * c:2 * c + 2]).reshape(2, 1024, 128))
        m["cav"] = f(np.asarray(cache_a_v[0, 2 * c:2 * c + 2]).reshape(2, 1024, 128))
        m["cik"] = f(cache_idx_k[0, 2 * c:2 * c + 2]); m["cpool"] = f(cache_pool[0, 2 * c:2 * c + 2])
        m["cmk"] = f(np.asarray(cache_mem_k[0, 2 * c:2 * c + 2]).reshape(2, 256, 512))
        m["cmv"] = f(np.asarray(cache_mem_v[0, 2 * c:2 * c + 2]).reshape(2, 256, 512))
        in_maps.append(m)
    res = run_bass_kernel_spmd(nc, in_maps, core_ids=list(range(8)))
    R = res.results
    cat = lambda k: np.stack([np.asarray(r[k], dtype=np.float32) for r in R], 0)
    cat2 = lambda k: np.concatenate([np.asarray(r[k], dtype=np.float32) for r in R], 0)
    y_prompt = cat("y_p")
    y_sample = cat2("y_s")
    return (
        y_prompt, y_sample,
        cat("nak_p").reshape(1, 8, 8192, 2, 64), cat("nav_p").reshape(1, 8, 8192, 2, 64), cat("nik_p").reshape(1, 8, 8192, 32),
        cat("npool_p").reshape(1, 8, 15, 512), cat("nmk_p").reshape(1, 8, 256, 4, 128), cat("nmv_p").reshape(1, 8, 256, 4, 128),
        cat2("nak_s").reshape(1, 16, 64, 2, 64), cat2("nav_s").reshape(1, 16, 64, 2, 64), cat2("nik_s").reshape(1, 16, 64, 32),
        cat2("npool_s").reshape(1, 16, 15, 512),
    )
```

```python
from contextlib import ExitStack
import numpy as np
import concourse.bass as bass
import concourse.mybir as mybir
from concourse.bass_utils import run_bass_kernel_spmd

F32 = mybir.dt.float32
BF16 = mybir.dt.bfloat16
AF = mybir.ActivationFunctionType
ALU = mybir.AluOpType
AX = mybir.AxisListType


class Ctr:
    def __init__(self, name, sem):
        self.name = name
        self.sem = sem
        self.count = 0


class Buf:
    def __init__(self, name, ap, ctr=None):
        self.name = name
        self.ap = ap
        self.ctr = ctr
        self.w = None
        self.r = {}

    def __getitem__(self, k):
        return self.ap[k]


class Eng:
    def __init__(self, name, be, ctr):
        self.name = name
        self.be = be
        self.ctr = ctr
        self.ops = []
        self.seen = {}


class Prog:
    def __init__(self, nc, stack):
        self.nc = nc
        self.gstack = stack
        self.stack = stack
        self.engs = {}
        self.nsem = 0
        for nm, be in (("pe", nc.tensor), ("act", nc.scalar), ("dve", nc.vector),
                       ("pool", nc.gpsimd), ("sp", nc.sync)):
            self.engs[nm] = Eng(nm, be, self.new_ctr("e_" + nm))
        self.dma_ctrs = []

    def new_ctr(self, name):
        sem = self.gstack.enter_context(self.nc.semaphore(name))
        self.nsem += 1
        return Ctr(name, sem)

    def sbuf(self, name, shape, dtype, dma=False):
        t = self.stack.enter_context(self.nc.sbuf_tensor(name, list(shape), dtype))
        return self.buf(name, t, dma)

    def psum(self, name, shape, dtype):
        t = self.stack.enter_context(self.nc.psum_tensor(name, list(shape), dtype))
        return Buf(name, t)

    def buf(self, name, ap, dma=False):
        c = None
        if dma:
            c = self.new_ctr("d_" + name)
            self.dma_ctrs.append(c)
        return Buf(name, ap, c)

    def _deps(self, eng, reads, writes, skip_same_pe=False):
        deps = {}

        def add(cv):
            if cv is None:
                return
            c, v = cv
            if skip_same_pe and c is eng.ctr:
                return
            if deps.get(c, 0) < v:
                deps[c] = v

        for b in reads:
            add(b.w)
        for b in writes:
            add(b.w)
            for c, v in b.r.items():
                add((c, v))
        waits = []
        for c, v in deps.items():
            if eng.seen.get(c, 0) < v:
                eng.seen[c] = v
                waits.append((c.sem, v))
        return waits

    def _cut(self):
        import os
        self.nrec = getattr(self, "nrec", 0) + 1
        return self.nrec > int(os.environ.get("OPCUT", "1000000000"))

    def op(self, engname, fn, reads=(), writes=()):
        if self._cut():
            return 0
        eng = self.engs[engname]
        waits = self._deps(eng, reads, writes, skip_same_pe=(engname == "pe"))
        eng.ctr.count += 1
        val = eng.ctr.count
        sem = eng.ctr.sem

        def emit(be, waits=waits, fn=fn, sem=sem):
            for s, v in waits:
                be.wait_ge(s, v)
            fn(be).then_inc(sem, 1)

        eng.ops.append(emit)
        for b in writes:
            b.w = (eng.ctr, val)
            b.r = {}
        for b in reads:
            if b not in writes:
                b.r[eng.ctr] = val
        return val

    def dma(self, engname, fns, ctrbuf, reads=(), writes=()):
        if self._cut():
            return 0
        eng = self.engs[engname]
        ctr = ctrbuf.ctr
        assert ctr is not None, ctrbuf.name
        waits = self._deps(eng, reads, writes)
        ctr.count += 16 * len(fns)
        val = ctr.count
        sem = ctr.sem

        def emit(be, waits=waits, fns=fns, sem=sem):
            for s, v in waits:
                be.wait_ge(s, v)
            for f in fns:
                f(be).then_inc(sem, 16)

        eng.ops.append(emit)
        for b in writes:
            b.w = (ctr, val)
            b.r = {}
        for b in reads:
            if b not in writes:
                b.r[ctr] = val
        return val

    def barrier(self):
        targets = [(e.ctr, e.ctr.count) for e in self.engs.values()]
        targets += [(c, c.count) for c in self.dma_ctrs]
        for eng in self.engs.values():
            waits = []
            for c, v in targets:
                if v > 0 and c is not eng.ctr and eng.seen.get(c, 0) < v:
                    eng.seen[c] = v
                    waits.append((c.sem, v))

            def emit(be, waits=waits):
                for s, v in waits:
                    be.wait_ge(s, v)

            eng.ops.append(emit)

    def emit_block(self):
        nc = self.nc
        ops = {k: e.ops for k, e in self.engs.items()}
        for e in self.engs.values():
            e.ops = []
        with nc.Block() as block:
            @block.tensor
            def _(be):
                for f in ops["pe"]:
                    f(be)

            @block.scalar
            def _(be):
                for f in ops["act"]:
                    f(be)

            @block.vector
            def _(be):
                for f in ops["dve"]:
                    f(be)

            @block.gpsimd
            def _(be):
                for f in ops["pool"]:
                    f(be)

            @block.sync
            def _(be):
                for f in ops["sp"]:
                    f(be)

D = 1024
NTP = 64
DIN = 5160
DFF = 2816
EPS = 1e-6
NEG = -1.0e30
NBIS = 12
IDX_SCALE = 256.0 ** -0.5


class Ring:
    def __init__(self, bufs):
        self.bufs = bufs
        self.i = 0

    def next(self):
        b = self.bufs[self.i % len(self.bufs)]
        self.i += 1
        return b


def build_program(dbg=False):
    import os as _os
    KSTOP = int(_os.environ.get('KSTOP', '99'))
    NT = NTP + 2
    SP = NTP * 128
    SCW = max(SP, 1152)
    nc = bass.Bass("TRN2", target_bir_lowering=False)

    def din(name, shape, dt=F32):
        return nc.dram_tensor(name, list(shape), dt, kind="ExternalInput").ap()

    def dout(name, shape, dt=F32):
        return nc.dram_tensor(name, list(shape), dt, kind="ExternalOutput").ap()

    def dscr(name, shape, dt):
        return nc.dram_tensor(name, list(shape), dt, kind=("ExternalOutput" if dbg else "Internal")).ap()

    x_p = din("x_p", [SP, D]); x_s = din("x_s", [2, 128, D]); mem = din("mem", [256, D])
    cak = din("cak", [2, 1024, 128]); cav = din("cav", [2, 1024, 128]); cik = din("cik", [2, 1024, 32])
    cpool = din("cpool", [2, 15, 512]); cmk = din("cmk", [2, 256, 512]); cmv = din("cmv", [2, 256, 512])
    w_in = din("w_in", [D, DIN]); w_mem = din("w_mem", [D, 1024]); w_pool = din("w_pool", [4, 128, 128])
    w_o = [din("w_oa", [512, D]), din("w_ob", [512, D]), din("w_om", [512, D])]
    w_out = din("w_out", [D, D]); w_gate = din("w_gate", [D, DFF]); w_up = din("w_up", [D, DFF]); w_down = din("w_down", [DFF, D])
    g_mix = din("g_mix", [1, D]); g_ffn = din("g_ffn", [1, D]); g_mem = din("g_mem", [1, D])
    g_qa = din("g_qa", [1, 64]); g_ka = din("g_ka", [1, 64]); g_kidx = din("g_kidx", [1, 32])
    g_qm = din("g_qm", [1, 128]); g_km = din("g_km", [1, 128]); s_pool = din("s_pool", [128, 4])
    rope = din("rope", [NT, 128, 96]); bands = din("bands", [128, 3, 4, 128]); ident = din("ident", [128, 128])

    y_p = dout("y_p", [SP, D]); y_s = dout("y_s", [2, 64, D])
    nak_p = dout("nak_p", [SP, 128]); nav_p = dout("nav_p", [SP, 128]); nik_p = dout("nik_p", [SP, 32])
    npool_p = dout("npool_p", [15, 512]); nmk_p = dout("nmk_p", [256, 512]); nmv_p = dout("nmv_p", [256, 512])
    nak_s = dout("nak_s", [2, 64, 128]); nav_s = dout("nav_s", [2, 64, 128]); nik_s = dout("nik_s", [2, 64, 32])
    npool_s = dout("npool_s", [2, 15, 512])

    s_qa = dscr("s_qa", [NT, 128, 512], BF16); s_qi = dscr("s_qi", [NT, 128, 384], BF16)
    s_qm = dscr("s_qm", [NT, 128, 512], BF16); s_ub = dscr("s_ub", [NT, 128, 512], BF16)
    s_wi = dscr("s_wi", [NT, 128, 8], F32); s_gate = dscr("s_gate", [NT, 128, 3072], BF16)
    s_KT = dscr("s_KT", [NT, 128, 128], BF16); s_V = dscr("s_V", [NT, 128, 128], BF16); s_KI = dscr("s_KI", [NT, 128, 128], BF16)
    s_br = dscr("s_br", [NT, 128, 1536], BF16); s_x1 = dscr("s_x1", [NT, 128, D], F32)

    def xsrc(t):
        return x_p[t * 128:(t + 1) * 128, :] if t < NTP else x_s[t - NTP]

    with ExitStack() as gst:
        P = Prog(nc, gst)

        def bc(ap2, shape):
            return ap2.unsqueeze(1).to_broadcast(shape)

        def bl(ap2, shape):
            return ap2.unsqueeze(2).to_broadcast(shape)

        idb = P.sbuf("idb", [128, 128], BF16)
        gq = P.sbuf("gq", [128, 64 + 64 + 32 + 128 + 128], F32, dma=True)
        spl = P.sbuf("spl", [128, 4], F32, dma=True)
        with ExitStack() as st0:
            P.stack = st0
            idf = P.sbuf("idf", [128, 128], F32, dma=True)
            P.dma("sp", [lambda be: be.dma_start(out=idf[:], in_=ident)], idf, writes=[idf])
            P.op("dve", lambda be: be.tensor_copy(out=idb[:], in_=idf[:]), reads=[idf], writes=[idb])
            offs = {}
            o = 0
            fl = []
            for nm, ap_, n in (("qa", g_qa, 64), ("ka", g_ka, 64), ("kidx", g_kidx, 32), ("qm", g_qm, 128), ("km", g_km, 128)):
                offs[nm] = (o, n)
                fl.append(lambda be, ap_=ap_, o=o, n=n: be.dma_start(out=gq[:, o:o + n], in_=ap_.partition_broadcast(128)))
                o += n
            P.dma("sp", fl, gq, writes=[gq])
            P.dma("sp", [lambda be: be.dma_start(out=spl[:], in_=s_pool)], spl, writes=[spl])
            P.barrier()
            P.emit_block()
            if KSTOP == 0:
                return nc
        P.stack = gst

        def gvec(nm):
            o, n = offs[nm]
            return gq[:, o:o + n]

        def rms_rows(xb, xap, gb, hb, n, junk, ss, rs):
            P.op("act", lambda be: be.activation(out=junk[:, 0:n], in_=xap, func=AF.Square, accum_out=ss[:]), reads=[xb], writes=[junk, ss])
            P.op("dve", lambda be: be.tensor_scalar(out=rs[:], in0=ss[:], scalar1=1.0 / n, scalar2=EPS, op0=ALU.mult, op1=ALU.add), reads=[ss], writes=[rs])
            P.op("act", lambda be: be.activation(out=rs[:], in_=rs[:], func=AF.Sqrt), reads=[rs], writes=[rs])
            P.op("dve", lambda be: be.reciprocal(out=rs[:], in_=rs[:]), reads=[rs], writes=[rs])
            P.op("dve", lambda be: be.scalar_tensor_tensor(out=hb[:], in0=xap, scalar=rs[:, 0:1], in1=gb[:], op0=ALU.mult, op1=ALU.mult), reads=[xb, rs, gb], writes=[hb])

        def head_norm(srcb, src3, gap, outb, out3, H, hd, sq, ssq, eng="dve"):
            sh = [128, H, hd]
            sqv = sq[:, 0:H * hd].rearrange("p (h d) -> p h d", h=H)
            P.op(eng, lambda be: be.tensor_tensor(out=sqv, in0=src3, in1=src3, op=ALU.mult), reads=[srcb], writes=[sq])
            P.op("dve", lambda be: be.tensor_reduce(out=ssq[:, 0:H], in_=sqv, axis=AX.X, op=ALU.add), reads=[sq], writes=[ssq])
            P.op("dve", lambda be: be.tensor_scalar(out=ssq[:, 0:H], in0=ssq[:, 0:H], scalar1=1.0 / hd, scalar2=EPS, op0=ALU.mult, op1=ALU.add), reads=[ssq], writes=[ssq])
            P.op("act", lambda be: be.activation(out=ssq[:, 0:H], in_=ssq[:, 0:H], func=AF.Sqrt), reads=[ssq], writes=[ssq])
            P.op("dve", lambda be: be.reciprocal(out=ssq[:, 0:H], in_=ssq[:, 0:H]), reads=[ssq], writes=[ssq])
            P.op(eng, lambda be: be.tensor_tensor(out=sqv, in0=src3, in1=bl(ssq[:, 0:H], sh), op=ALU.mult), reads=[srcb, ssq], writes=[sq])
            P.op(eng, lambda be: be.tensor_tensor(out=out3, in0=sqv, in1=bc(gap, sh), op=ALU.mult), reads=[sq, gq], writes=[outb])

        def rope_ops(srcb, x1, x2, cs, sn, outb, o1, o2, tb, t1, t2, eng="dve"):
            P.op(eng, lambda be: be.tensor_tensor(out=t1, in0=x1, in1=cs, op=ALU.mult), reads=[srcb, rp_cur[0]], writes=[tb])
            P.op(eng, lambda be: be.tensor_tensor(out=t2, in0=x2, in1=sn, op=ALU.mult), reads=[srcb, rp_cur[0]], writes=[tb])
            P.op(eng, lambda be: be.tensor_tensor(out=o1, in0=t1, in1=t2, op=ALU.subtract), reads=[tb], writes=[outb])
            P.op(eng, lambda be: be.tensor_tensor(out=t1, in0=x2, in1=cs, op=ALU.mult), reads=[srcb, rp_cur[0]], writes=[tb])
            P.op(eng, lambda be: be.tensor_tensor(out=t2, in0=x1, in1=sn, op=ALU.mult), reads=[srcb, rp_cur[0]], writes=[tb])
            P.op(eng, lambda be: be.tensor_tensor(out=o2, in0=t1, in1=t2, op=ALU.add), reads=[tb], writes=[outb])

        rp_cur = [None]

        def load_w_cast(dst, dst_view_fn, src, rows, cols, kcs):
            fl = []
            for kc in range(kcs):
                c0 = 0
                while c0 < cols:
                    cw = min(2048, cols - c0)
                    fl.append(lambda be, kc=kc, c0=c0, cw=cw: be.dma_start(out=dst_view_fn(kc, c0, cw), in_=src[kc * 128:(kc + 1) * 128, c0:c0 + cw]))
                    c0 += cw
            P.dma("pool", fl, dst, writes=[dst])

        def transpose_to(psb, ps3, srcb, src2, n, dstb, dst3, evac="act"):
            for k in range(n):
                P.op("pe", lambda be, k=k: be.transpose(out=ps3[:, k, :], in_=src2[:, k * 128:(k + 1) * 128], identity=idb[:]), reads=[srcb, idb], writes=[psb])
            if evac == "act":
                P.op("act", lambda be: be.copy(out=dst3, in_=ps3[:, 0:n, :]), reads=[psb], writes=[dstb])
            else:
                P.op("dve", lambda be: be.tensor_copy(out=dst3, in_=ps3[:, 0:n, :]), reads=[psb], writes=[dstb])

        def interleave(gens):
            gens = list(gens)
            while gens:
                for g_ in list(gens):
                    try:
                        next(g_)
                    except StopIteration:
                        gens.remove(g_)

        def head_norm_g(srcb, src3, gap, outb, out3, H, hd, sq, ssq):
            sh = [128, H, hd]
            sqv = sq[:, 0:H * hd].rearrange("p (h d) -> p h d", h=H)
            P.op("dve", lambda be: be.tensor_tensor(out=sqv, in0=src3, in1=src3, op=ALU.mult), reads=[srcb], writes=[sq]); yield
            P.op("dve", lambda be: be.tensor_reduce(out=ssq[:, 0:H], in_=sqv, axis=AX.X, op=ALU.add), reads=[sq], writes=[ssq]); yield
            P.op("dve", lambda be: be.tensor_scalar(out=ssq[:, 0:H], in0=ssq[:, 0:H], scalar1=1.0 / hd, scalar2=EPS, op0=ALU.mult, op1=ALU.add), reads=[ssq], writes=[ssq]); yield
            P.op("act", lambda be: be.activation(out=ssq[:, 0:H], in_=ssq[:, 0:H], func=AF.Sqrt), reads=[ssq], writes=[ssq]); yield
            P.op("dve", lambda be: be.reciprocal(out=ssq[:, 0:H], in_=ssq[:, 0:H]), reads=[ssq], writes=[ssq]); yield
            P.op("dve", lambda be: be.tensor_tensor(out=sqv, in0=src3, in1=bl(ssq[:, 0:H], sh), op=ALU.mult), reads=[srcb, ssq], writes=[sq]); yield
            P.op("dve", lambda be: be.tensor_tensor(out=out3, in0=sqv, in1=bc(gap, sh), op=ALU.mult), reads=[sq, gq], writes=[outb]); yield

        def rope_g(srcb, x1, x2, cs, sn, rb, outb, o1, o2, tbs, tv):
            P.op("dve", lambda be: be.tensor_tensor(out=tv[0], in0=x1, in1=cs, op=ALU.mult), reads=[srcb, rb], writes=[tbs[0]]); yield
            P.op("dve", lambda be: be.tensor_tensor(out=tv[1], in0=x2, in1=sn, op=ALU.mult), reads=[srcb, rb], writes=[tbs[1]]); yield
            P.op("dve", lambda be: be.tensor_tensor(out=tv[2], in0=x2, in1=cs, op=ALU.mult), reads=[srcb, rb], writes=[tbs[2]]); yield
            P.op("dve", lambda be: be.tensor_tensor(out=tv[3], in0=x1, in1=sn, op=ALU.mult), reads=[srcb, rb], writes=[tbs[3]]); yield
            P.op("dve", lambda be: be.tensor_tensor(out=o1, in0=tv[0], in1=tv[1], op=ALU.subtract), reads=[tbs[0], tbs[1]], writes=[outb]); yield
            P.op("dve", lambda be: be.tensor_tensor(out=o2, in0=tv[2], in1=tv[3], op=ALU.add), reads=[tbs[2], tbs[3]], writes=[outb]); yield

        with ExitStack() as st:
            P.stack = st
            win = P.sbuf("win", [128, 8, DIN], BF16, dma=True)
            load_w_cast(win, lambda kc, c0, cw: win[:, kc, c0:c0 + cw], w_in, 1024, DIN, 8)
            gmix = P.sbuf("gmix", [128, D], F32, dma=True)
            P.dma("sp", [lambda be: be.dma_start(out=gmix[:], in_=g_mix.partition_broadcast(128))], gmix, writes=[gmix])
            xr = Ring([P.sbuf("xa%d" % i, [128, D], F32, dma=True) for i in range(3)])
            rpr = Ring([P.sbuf("rp%d" % i, [128, 96], F32, dma=True) for i in range(3)])
            junk = P.sbuf("junk1", [128, D], BF16)
            ssL = [P.sbuf("ss1_%d" % i, [128, 1], F32) for i in range(2)]; rsL = [P.sbuf("rs1_%d" % i, [128, 1], F32) for i in range(2)]
            hbL = [P.sbuf("hb1_%d" % i, [128, D], BF16) for i in range(2)]; hTL = [P.sbuf("hT1_%d" % i, [128, 8, 128], BF16) for i in range(2)]
            c0L = [P.sbuf("c0f%d" % i, [128, 512], F32) for i in range(2)]
            c1L = [P.sbuf("c1f%d" % i, [128, 512], F32, dma=True) for i in range(2)]
            c2L = [P.sbuf("c2f%d" % i, [128, 40], F32) for i in range(2)]
            c4L = [P.sbuf("c4f%d" % i, [128, 512], F32) for i in range(2)]
            sq_qa = P.sbuf("sq_qa", [128, 512], F32); ssq_qa = P.sbuf("ssq_qa", [128, 8], F32); qn_qa = P.sbuf("qn_qa", [128, 512], F32)
            tb_qa = [P.sbuf("tb_qa%d" % i, [128, 256], F32) for i in range(4)]
            sq_ka = P.sbuf("sq_ka", [128, 128], F32); ssq_ka = P.sbuf("ssq_ka", [128, 8], F32); qn_ka = P.sbuf("qn_ka", [128, 128], F32)
            tb_ka = [P.sbuf("tb_ka%d" % i, [128, 64], F32) for i in range(4)]
            tb_qi = [P.sbuf("tb_qi%d" % i, [128, 128], F32) for i in range(4)]
            sq_ki = P.sbuf("sq_ki", [128, 32], F32); ssq_ki = P.sbuf("ssq_ki", [128, 8], F32); qn_ki = P.sbuf("qn_ki", [128, 32], F32)
            tb_ki = [P.sbuf("tb_ki%d" % i, [128, 16], F32) for i in range(4)]
            sq_qm = P.sbuf("sq_qm", [128, 512], F32); ssq_qm = P.sbuf("ssq_qm", [128, 8], F32)
            qab = P.sbuf("qab", [128, 512], BF16)
            kof = Ring([P.sbuf("kof%d" % i, [128, 128], F32, dma=True) for i in range(2)])
            kbb = P.sbuf("kbb", [128, 128], BF16)
            qib = P.sbuf("qib", [128, 256], BF16)
            kif = Ring([P.sbuf("kif%d" % i, [128, 32], F32, dma=True) for i in range(2)])
            ki4 = P.sbuf("ki4", [128, 128], BF16)
            wif = Ring([P.sbuf("wif%d" % i, [128, 8], F32, dma=True) for i in range(2)])
            ubb = Ring([P.sbuf("ubb%d" % i, [128, 512], BF16, dma=True) for i in range(2)])
            ubf = P.sbuf("ubf", [128, 512], F32, dma=True)
            qmb = P.sbuf("qmb", [128, 512], BF16)
            qaT = Ring([P.sbuf("qaT%d" % i, [128, 4, 128], BF16, dma=True) for i in range(2)])
            qiT = Ring([P.sbuf("qiT%d" % i, [128, 3, 128], BF16, dma=True) for i in range(2)])
            qmT = Ring([P.sbuf("qmT%d" % i, [128, 4, 128], BF16, dma=True) for i in range(2)])
            gsb = Ring([P.sbuf("gsb%d" % i, [128, 6, 512], BF16, dma=True) for i in range(2)])
            ktT = Ring([P.sbuf("ktT%d" % i, [128, 128], BF16, dma=True) for i in range(2)])
            vbT = Ring([P.sbuf("vbT%d" % i, [128, 128], BF16, dma=True) for i in range(2)])
            kiT = Ring([P.sbuf("kiT%d" % i, [128, 128], BF16, dma=True) for i in range(2)])
            pT = P.psum("pT1", [128, 8, 128], BF16)
            pP = Ring([P.psum("pP1_%d" % i, [128, 512], F32) for i in range(4)])
            pR = Ring([P.psum("pR1_%d" % i, [128, 8, 128], BF16) for i in range(2)])

            xbufs = {}
            rbufs = {}
            tst = {}

            def a1_load(t):
                xb = xr.next(); rb = rpr.next()
                xbufs[t] = xb; rbufs[t] = rb
                P.dma("sp", [lambda be: be.dma_start(out=xb[:], in_=xsrc(t))], xb, writes=[xb])
                P.dma("sp", [lambda be: be.dma_start(out=rb[:], in_=rope[t])], rb, writes=[rb])

            def a1_R(t):
                par = t % 2
                xb = xbufs.pop(t)
                rms_rows(xb, xb[:], gmix, hbL[par], D, junk, ssL[par], rsL[par])
                transpose_to(pT, pT, hbL[par], hbL[par], 8, hTL[par], hTL[par][:])

            def a1_M(t):
                par = t % 2
                hT = hTL[par]
                samp = t >= NTP
                b = t - NTP

                def proj(ps, c0, cw):
                    for kc in range(8):
                        P.op("pe", lambda be, kc=kc: be.matmul(ps[:, 0:cw], lhsT=hT[:, kc, :], rhs=win[:, kc, c0:c0 + cw], start=(kc == 0), stop=(kc == 7)), reads=[hT, win], writes=[ps])

                c0f, c1, c2f, c4f = c0L[par], c1L[par], c2L[par], c4L[par]
                ps0 = pP.next(); proj(ps0, 0, 512)
                P.op("act", lambda be: be.copy(out=c0f[:], in_=ps0[:]), reads=[ps0], writes=[c0f])
                ps1 = pP.next(); proj(ps1, 512, 512)
                P.op("act", lambda be: be.copy(out=c1[:], in_=ps1[:]), reads=[ps1], writes=[c1])
                ps2 = pP.next(); proj(ps2, 1024, 40)
                P.op("act", lambda be: be.copy(out=c2f[:], in_=ps2[:, 0:40]), reads=[ps2], writes=[c2f])
                ps3 = pP.next(); proj(ps3, 1064, 512)
                ub_ = ubb.next()
                P.op("act", lambda be: be.copy(out=ub_[:], in_=ps3[:]), reads=[ps3], writes=[ub_])
                P.dma("sp", [lambda be: be.dma_start(out=s_ub[t], in_=ub_[:])], ub_, reads=[ub_])
                if samp or t == NTP - 1:
                    P.op("act", lambda be: be.copy(out=ubf[:], in_=ps3[:]), reads=[ps3], writes=[ubf])
                    if samp:
                        P.dma("sp", [lambda be: be.dma_start(out=npool_s[b], in_=ubf[49:64, :])], ubf, reads=[ubf])
                    else:
                        P.dma("sp", [lambda be: be.dma_start(out=npool_p, in_=ubf[113:128, :])], ubf, reads=[ubf])
                ps4 = pP.next(); proj(ps4, 1576, 512)
                P.op("act", lambda be: be.copy(out=c4f[:], in_=ps4[:]), reads=[ps4], writes=[c4f])
                gs = gsb.next()
                for j in range(6):
                    ps = pP.next(); proj(ps, 2088 + 512 * j, 512)
                    P.op("act", lambda be, ps=ps, j=j: be.activation(out=gs[:, j, :], in_=ps[:], func=AF.Sigmoid), reads=[ps], writes=[gs])
                P.dma("sp", [lambda be: be.dma_start(out=s_gate[t], in_=gs[:].rearrange("p a b -> p (a b)"))], gs, reads=[gs])

            def a1_C(t):
                par = t % 2
                rb = rbufs.pop(t)
                c0f, c1, c2f, c4f = c0L[par], c1L[par], c2L[par], c4L[par]
                cos64 = rb[:, 0:32]; sin64 = rb[:, 32:64]; cos32 = rb[:, 64:80]; sin32 = rb[:, 80:96]
                ko = kof.next(); kio = kif.next(); wo = wif.next()
                tst[t] = dict(ko=ko, kio=kio, wo=wo, c1=c1)

                def ch_qa():
                    yield from head_norm_g(c0f, c0f[:].rearrange("p (h d) -> p h d", h=8), gvec("qa"), qn_qa, qn_qa[:].rearrange("p (h d) -> p h d", h=8), 8, 64, sq_qa, ssq_qa)
                    qn4 = qn_qa[:].rearrange("p (k g d) -> p k g d", k=2, g=4)
                    qo4 = qab[:].rearrange("p (g k d) -> p k g d", g=4, k=2)
                    sh4 = [128, 2, 4, 32]
                    cs4 = cos64.unsqueeze(1).unsqueeze(1).to_broadcast(sh4)
                    sn4 = sin64.unsqueeze(1).unsqueeze(1).to_broadcast(sh4)
                    tv = [tb_[:].rearrange("p (k g d) -> p k g d", k=2, g=4) for tb_ in tb_qa]
                    yield from rope_g(qn_qa, qn4[:, :, :, 0:32], qn4[:, :, :, 32:64], cs4, sn4, rb, qab, qo4[:, :, :, 0:32], qo4[:, :, :, 32:64], tb_qa, tv)

                def ch_ka():
                    yield from head_norm_g(c1, c1[:, 0:128].rearrange("p (h d) -> p h d", h=2), gvec("ka"), qn_ka, qn_ka[:].rearrange("p (h d) -> p h d", h=2), 2, 64, sq_ka, ssq_ka)
                    k3 = qn_ka[:].rearrange("p (h d) -> p h d", h=2)
                    ko3 = ko[:].rearrange("p (h d) -> p h d", h=2)
                    sh3 = [128, 2, 32]
                    tv = [tb_[:].rearrange("p (h d) -> p h d", h=2) for tb_ in tb_ka]
                    yield from rope_g(qn_ka, k3[:, :, 0:32], k3[:, :, 32:64], bc(cos64, sh3), bc(sin64, sh3), rb, ko, ko3[:, :, 0:32], ko3[:, :, 32:64], tb_ka, tv)
                    P.op("pool", lambda be: be.tensor_copy(out=kbb[:], in_=ko[:]), reads=[ko], writes=[kbb]); yield

                def ch_qi():
                    qi3 = c1[:, 256:512].rearrange("p (h d) -> p h d", h=8)
                    qo3 = qib[:].rearrange("p (h d) -> p h d", h=8)
                    sh3 = [128, 8, 16]
                    tv = [tb_[:].rearrange("p (h d) -> p h d", h=8) for tb_ in tb_qi]
                    yield from rope_g(c1, qi3[:, :, 0:16], qi3[:, :, 16:32], bc(cos32, sh3), bc(sin32, sh3), rb, qib, qo3[:, :, 0:16], qo3[:, :, 16:32], tb_qi, tv)

                def ch_ki():
                    yield from head_norm_g(c2f, c2f[:, 0:32].unsqueeze(1), gvec("kidx"), qn_ki, qn_ki[:, 0:32].unsqueeze(1), 1, 32, sq_ki, ssq_ki)
                    tv = [tb_[:] for tb_ in tb_ki]
                    yield from rope_g(qn_ki, qn_ki[:, 0:16], qn_ki[:, 16:32], cos32, sin32, rb, kio, kio[:, 0:16], kio[:, 16:32], tb_ki, tv)
                    P.op("pool", lambda be: be.tensor_copy(out=ki4[:].rearrange("p (r c) -> p r c", r=4), in_=kio[:].unsqueeze(1).to_broadcast([128, 4, 32])), reads=[kio], writes=[ki4]); yield
                    P.op("dve", lambda be: be.tensor_scalar(out=wo[:], in0=c2f[:, 32:40], scalar1=IDX_SCALE, scalar2=None, op0=ALU.mult), reads=[c2f], writes=[wo]); yield

                def ch_qm():
                    yield from head_norm_g(c4f, c4f[:].rearrange("p (h d) -> p h d", h=4), gvec("qm"), qmb, qmb[:].rearrange("p (h d) -> p h d", h=4), 4, 128, sq_qm, ssq_qm)

                interleave([ch_qa(), ch_ka(), ch_qi(), ch_ki(), ch_qm()])

            def a1_T(t):
                samp = t >= NTP
                b = t - NTP
                d_ = tst.pop(t)
                ko, kio, wo, c1 = d_["ko"], d_["kio"], d_["wo"], d_["c1"]
                qa_t = qaT.next(); pr = pR.next()
                transpose_to(pr, pr, qab, qab, 4, qa_t, qa_t[:])
                P.dma("sp", [lambda be: be.dma_start(out=s_qa[t], in_=qa_t[:].rearrange("p a b -> p (a b)"))], qa_t, reads=[qa_t])
                if samp:
                    P.dma("sp", [lambda be: be.dma_start(out=nak_s[b], in_=ko[0:64, :]),
                                 lambda be: be.dma_start(out=nav_s[b], in_=c1[0:64, 128:256])], ko, reads=[ko, c1])
                else:
                    P.dma("sp", [lambda be: be.dma_start(out=nak_p[t * 128:(t + 1) * 128, :], in_=ko[:]),
                                 lambda be: be.dma_start(out=nav_p[t * 128:(t + 1) * 128, :], in_=c1[:, 128:256])], ko, reads=[ko, c1])
                pr = pR.next(); kt_o = ktT.next()
                transpose_to(pr, pr, kbb, kbb, 1, kt_o, kt_o[:].unsqueeze(1))
                P.dma("sp", [lambda be: be.dma_start(out=s_KT[t], in_=kt_o[:])], kt_o, reads=[kt_o])
                vb_o = vbT.next()
                P.op("pool", lambda be: be.tensor_copy(out=vb_o[:], in_=c1[:, 128:256]), reads=[c1], writes=[vb_o])
                P.dma("sp", [lambda be: be.dma_start(out=s_V[t], in_=vb_o[:])], vb_o, reads=[vb_o])
                qi_t = qiT.next(); pr = pR.next()
                for k in range(3):
                    nr = 96 if k < 2 else 64
                    P.op("pe", lambda be, k=k, nr=nr, pr=pr: be.transpose(out=pr[0:nr, k, :], in_=qib[:, 96 * k:96 * k + nr], identity=idb[:]), reads=[qib, idb], writes=[pr])
                P.op("act", lambda be, pr=pr: be.copy(out=qi_t[0:96, 0:2, :], in_=pr[0:96, 0:2, :]), reads=[pr], writes=[qi_t])
                P.op("act", lambda be, pr=pr: be.copy(out=qi_t[0:64, 2, :], in_=pr[0:64, 2, :]), reads=[pr], writes=[qi_t])
                P.dma("sp", [lambda be: be.dma_start(out=s_qi[t, 0:96, 0:256], in_=qi_t[0:96, 0:2, :].rearrange("p a b -> p (a b)")),
                             lambda be: be.dma_start(out=s_qi[t, 0:64, 256:384], in_=qi_t[0:64, 2, :])], qi_t, reads=[qi_t])
                P.dma("sp", [lambda be: be.dma_start(out=s_wi[t], in_=wo[:])], wo, reads=[wo])
                if samp:
                    P.dma("sp", [lambda be: be.dma_start(out=nik_s[b], in_=kio[0:64, :])], kio, reads=[kio])
                else:
                    P.dma("sp", [lambda be: be.dma_start(out=nik_p[t * 128:(t + 1) * 128, :], in_=kio[:])], kio, reads=[kio])
                pr = pR.next(); ki_o = kiT.next()
                transpose_to(pr, pr, ki4, ki4, 1, ki_o, ki_o[:].unsqueeze(1))
                P.dma("sp", [lambda be: be.dma_start(out=s_KI[t], in_=ki_o[:])], ki_o, reads=[ki_o])
                qm_t = qmT.next(); pr = pR.next()
                transpose_to(pr, pr, qmb, qmb, 4, qm_t, qm_t[:])
                P.dma("sp", [lambda be: be.dma_start(out=s_qm[t], in_=qm_t[:].rearrange("p a b -> p (a b)"))], qm_t, reads=[qm_t])

            a1_load(0); a1_load(1)
            a1_R(0)
            for t in range(NT):
                if t + 2 < NT:
                    a1_load(t + 2)
                a1_M(t)
                if t >= 1:
                    a1_T(t - 1)
                if t + 1 < NT:
                    a1_R(t + 1)
                a1_C(t)
            a1_T(NT - 1)
            P.barrier()
            P.emit_block()
            if KSTOP == 1:
                return nc

        with ExitStack() as kvst:
            P.stack = kvst
            KTp = P.sbuf("KTp", [128, SP], BF16)
            V1p = P.sbuf("V1p", [128, NTP, 2, 65], BF16)
            KIp = P.sbuf("KIp", [128, SP], BF16)
            KTs = [P.sbuf("KTs%d" % b, [128, 1152], BF16) for b in range(2)]
            V1s = [P.sbuf("V1s%d" % b, [128, 9, 2, 65], BF16) for b in range(2)]
            KIs = [P.sbuf("KIs%d" % b, [128, 1152], BF16) for b in range(2)]
            mkT = [P.sbuf("mkT%d" % i, [128, 4, 256], BF16) for i in range(3)]
            mv1 = [P.sbuf("mv1%d" % i, [128, 2, 4, 129], BF16) for i in range(3)]
            ubh = [P.sbuf("ubh%d" % b, [128, 512], BF16, dma=True) for b in range(2)]

            P.op("pool", lambda be: be.memset(V1p[:, :, :, 64:65], 1.0), writes=[V1p])
            for b in range(2):
                P.op("pool", lambda be, b=b: be.memset(V1s[b][:, :, :, 64:65], 1.0), writes=[V1s[b]])
                P.op("pool", lambda be, b=b: be.memset(ubh[b][:], 0.0), writes=[ubh[b]])
            for i in range(3):
                P.op("pool", lambda be, i=i: be.memset(mv1[i][:, :, :, 128:129], 1.0), writes=[mv1[i]])

            with ExitStack() as st:
                P.stack = st
                wm = P.sbuf("wm", [128, 8, 1024], BF16, dma=True)
                load_w_cast(wm, lambda kc, c0, cw: wm[:, kc, c0:c0 + cw], w_mem, 1024, 1024, 8)
                gmem = P.sbuf("gmem", [128, D], F32, dma=True)
                P.dma("sp", [lambda be: be.dma_start(out=gmem[:], in_=g_mem.partition_broadcast(128))], gmem, writes=[gmem])
                xm = Ring([P.sbuf("xm%d" % i, [128, D], F32, dma=True) for i in range(2)])
                st32 = Ring([P.sbuf("st32_%d" % i, [128, 1024], F32, dma=True) for i in range(3)])
                stb = Ring([P.sbuf("stb%d" % i, [128, 1024], BF16) for i in range(2)])
                junk = P.sbuf("junk0", [128, D], BF16)
                ss = P.sbuf("ss0", [128, 1], F32); rs = P.sbuf("rs0", [128, 1], F32)
                hb = P.sbuf("hb0", [128, D], BF16); hT = P.sbuf("hT0", [128, 8, 128], BF16)
                sq = P.sbuf("sq0", [128, 512], F32); ssq = P.sbuf("ssq0", [128, 8], F32)
                kout = Ring([P.sbuf("kout%d" % i, [128, 512], F32, dma=True) for i in range(2)])
                vout = Ring([P.sbuf("vout%d" % i, [128, 512], F32, dma=True) for i in range(2)])
                kb = P.sbuf("kb0", [128, 512], BF16)
                pT = P.psum("pT0", [128, 8, 128], BF16)
                pK = P.psum("pK0", [128, 512], F32); pV = P.psum("pV0", [128, 512], F32)
                pR = Ring([P.psum("pR0_%d" % i, [128, 8, 128], BF16) for i in range(2)])

                for mt in range(2):
                    xb = xm.next()
                    P.dma("sp", [lambda be, xb=xb, mt=mt: be.dma_start(out=xb[:], in_=mem[mt * 128:(mt + 1) * 128, :])], xb, writes=[xb])
                    rms_rows(xb, xb[:], gmem, hb, D, junk, ss, rs)
                    transpose_to(pT, pT, hb, hb, 8, hT, hT[:])
                    for kc in range(8):
                        P.op("pe", lambda be, kc=kc: be.matmul(pK[:], lhsT=hT[:, kc, :], rhs=wm[:, kc, 0:512], start=(kc == 0), stop=(kc == 7)), reads=[hT, wm], writes=[pK])
                    for kc in range(8):
                        P.op("pe", lambda be, kc=kc: be.matmul(pV[:], lhsT=hT[:, kc, :], rhs=wm[:, kc, 512:1024], start=(kc == 0), stop=(kc == 7)), reads=[hT, wm], writes=[pV])
                    kf = st32.next()
                    P.op("act", lambda be, kf=kf: be.copy(out=kf[:, 0:512], in_=pK[:]), reads=[pK], writes=[kf])
                    ko = kout.next()
                    head_norm(kf, kf[:, 0:512].rearrange("p (h d) -> p h d", h=4), gvec("km"), ko, ko[:].rearrange("p (h d) -> p h d", h=4), 4, 128, sq, ssq)
                    P.dma("sp", [lambda be, ko=ko, mt=mt: be.dma_start(out=nmk_p[mt * 128:(mt + 1) * 128, :], in_=ko[:])], ko, reads=[ko])
                    P.op("pool", lambda be, ko=ko: be.tensor_copy(out=kb[:], in_=ko[:]), reads=[ko], writes=[kb])
                    pr = pR.next()
                    transpose_to(pr, pr, kb, kb, 4, mkT[0], mkT[0][:, :, mt * 128:(mt + 1) * 128])
                    vo = vout.next()
                    P.op("act", lambda be, vo=vo: be.copy(out=vo[:], in_=pV[:]), reads=[pV], writes=[vo])
                    P.dma("sp", [lambda be, vo=vo, mt=mt: be.dma_start(out=nmv_p[mt * 128:(mt + 1) * 128, :], in_=vo[:])], vo, reads=[vo])
                    P.op("pool", lambda be, vo=vo, mt=mt: be.tensor_copy(out=mv1[0][:, mt, :, 0:128], in_=vo[:].rearrange("p (h d) -> p h d", h=4)), reads=[vo], writes=[mv1[0]])
                for b in range(2):
                    for mt in range(2):
                        kf = st32.next()
                        P.dma("sp", [lambda be, kf=kf, b=b, mt=mt: be.dma_start(out=kf[:, 0:512], in_=cmk[b, mt * 128:(mt + 1) * 128, :])], kf, writes=[kf])
                        sb_ = stb.next()
                        P.op("dve", lambda be, kf=kf, sb_=sb_: be.tensor_copy(out=sb_[:, 0:512], in_=kf[:, 0:512]), reads=[kf], writes=[sb_])
                        pr = pR.next()
                        transpose_to(pr, pr, sb_, sb_, 4, mkT[1 + b], mkT[1 + b][:, :, mt * 128:(mt + 1) * 128])
                        vf = st32.next()
                        P.dma("sp", [lambda be, vf=vf, b=b, mt=mt: be.dma_start(out=vf[:, 0:512], in_=cmv[b, mt * 128:(mt + 1) * 128, :])], vf, writes=[vf])
                        P.op("pool", lambda be, vf=vf, b=b, mt=mt: be.tensor_copy(out=mv1[1 + b][:, mt, :, 0:128], in_=vf[:, 0:512].rearrange("p (h d) -> p h d", h=4)), reads=[vf], writes=[mv1[1 + b]])
                for b in range(2):
                    ck = st32.next()
                    P.dma("sp", [lambda be, ck=ck, b=b: be.dma_start(out=ck[:].rearrange("p (k c) -> p k c", k=8), in_=cak[b].rearrange("(k p) c -> p k c", p=128))], ck, writes=[ck])
                    cb = stb.next()
                    P.op("dve", lambda be, ck=ck, cb=cb: be.tensor_copy(out=cb[:], in_=ck[:]), reads=[ck], writes=[cb])
                    pr = pR.next()
                    transpose_to(pr, pr, cb, cb, 8, KTs[b], KTs[b][:, 0:1024].rearrange("p (k c) -> p k c", k=8))
                    cv = st32.next()
                    P.dma("sp", [lambda be, cv=cv, b=b: be.dma_start(out=cv[:].rearrange("p (k c) -> p k c", k=8), in_=cav[b].rearrange("(k p) c -> p k c", p=128))], cv, writes=[cv])
                    P.op("pool", lambda be, cv=cv, b=b: be.tensor_copy(out=V1s[b][:, 0:8, :, 0:64], in_=cv[:].rearrange("p (k h d) -> p k h d", k=8, h=2)), reads=[cv], writes=[V1s[b]])
                    ci = st32.next()
                    P.dma("sp", [lambda be, ci=ci, b=b: be.dma_start(out=ci[:, 0:256].rearrange("p (k c) -> p k c", k=8), in_=cik[b].rearrange("(k p) c -> p k c", p=128))], ci, writes=[ci])
                    c4 = stb.next()
                    P.op("dve", lambda be, ci=ci, c4=c4: be.tensor_copy(out=c4[:].rearrange("p (k r c) -> p k r c", k=8, r=4), in_=ci[:, 0:256].rearrange("p (k c) -> p k c", k=8).unsqueeze(2).to_broadcast([128, 8, 4, 32])), reads=[ci], writes=[c4])
                    pr = pR.next()
                    transpose_to(pr, pr, c4, c4, 8, KIs[b], KIs[b][:, 0:1024].rearrange("p (k c) -> p k c", k=8))
                    P.dma("pool", [lambda be, b=b: be.dma_start(out=ubh[b][113:128, :], in_=cpool[b])], ubh[b], writes=[ubh[b]])
                kvl = P.buf("kvl", None, dma=True)
                fl = []
                for q in range((NTP + 15) // 16):
                    t0_ = q * 16; t1_ = min(NTP, t0_ + 16)
                    fl.append(lambda be, t0_=t0_, t1_=t1_: be.dma_start(out=KTp[:, t0_ * 128:t1_ * 128].rearrange("p (t k) -> p t k", k=128), in_=s_KT[t0_:t1_].rearrange("t p k -> p t k")))
                    fl.append(lambda be, t0_=t0_, t1_=t1_: be.dma_start(out=KIp[:, t0_ * 128:t1_ * 128].rearrange("p (t k) -> p t k", k=128), in_=s_KI[t0_:t1_].rearrange("t p k -> p t k")))
                    for hh in range(2):
                        fl.append(lambda be, t0_=t0_, t1_=t1_, hh=hh: be.dma_start(out=V1p[:, t0_:t1_, hh, 0:64], in_=s_V[t0_:t1_, :, hh * 64:(hh + 1) * 64].rearrange("t p d -> p t d")))
                for b in range(2):
                    fl.append(lambda be, b=b: be.dma_start(out=KTs[b][:, 1024:1152], in_=s_KT[NTP + b]))
                    fl.append(lambda be, b=b: be.dma_start(out=KIs[b][:, 1024:1152], in_=s_KI[NTP + b]))
                    fl.append(lambda be, b=b: be.dma_start(out=V1s[b][:, 8, :, 0:64], in_=s_V[NTP + b].rearrange("p (h d) -> p h d", h=2)))
                P.dma("sp", fl, kvl, writes=[KTp, KIp, V1p, KTs[0], KTs[1], KIs[0], KIs[1], V1s[0], V1s[1]])
                P.barrier()
                P.emit_block()
                if KSTOP == 2:
                    return nc

            with ExitStack() as st:
                P.stack = st
                wpl = P.sbuf("wpl", [128, 4, 128], BF16, dma=True)
                P.dma("pool", [lambda be: be.dma_start(out=wpl[:], in_=w_pool.rearrange("g c e -> c g e"))], wpl, writes=[wpl])
                bnd = P.sbuf("bnd", [128, 3, 4, 128], BF16, dma=True)
                P.dma("pool", [lambda be: be.dma_start(out=bnd[:].rearrange("p a g t -> p (a g t)"), in_=bands.rearrange("p a g t -> p (a g t)"))], bnd, writes=[bnd])
                sc = P.sbuf("sc", [128, SCW], F32)
                scC = [P.buf("scC%d" % c, None) for c in range((SCW + 511) // 512)]
                pw = P.sbuf("pw", [128, NBIS + 1], F32); wk = P.sbuf("wk", [128, NBIS + 1], F32)
                for k in range(NBIS + 1):
                    P.op("pool", lambda be, k=k: be.memset(pw[:, k:k + 1], 2.0 ** -(k + 1)), writes=[pw])
                Mq = P.sbuf("Mq", [128, SCW], BF16)
                MT = P.sbuf("MT", [128, SCW // 128, 128], BF16)
                rl = Ring([P.sbuf("rl%d" % i, [128, 512], F32) for i in range(3)])
                er = Ring([P.sbuf("er%d" % i, [128, 4, 128], BF16) for i in range(4)])
                pr_ = Ring([P.sbuf("pp%d" % i, [128, 4, 128], BF16) for i in range(4)])
                ld = {}
                qaL = Ring([P.sbuf("qaL%d" % i, [128, 4, 128], BF16, dma=True) for i in range(3)])
                qiL = Ring([P.sbuf("qiL%d" % i, [128, 3, 128], BF16, dma=True) for i in range(3)])
                qmL = Ring([P.sbuf("qmL%d" % i, [128, 4, 128], BF16, dma=True) for i in range(3)])
                wiL = Ring([P.sbuf("wiL%d" % i, [128, 8], F32, dma=True) for i in range(3)])
                ubL = Ring([P.sbuf("ubL%d" % i, [128, 512], BF16, dma=True) for i in range(4)])
                lo = P.sbuf("lo", [128, 1], F32); w0 = P.sbuf("w0", [128, 1], F32); mx = P.sbuf("mx", [128, 1], F32)
                mid = P.sbuf("mid", [128, 1], F32); cnt = P.sbuf("cnt", [128, 1], F32); tt_ = P.sbuf("tt", [128, 1], F32)
                thr = Ring([P.sbuf("thr%d" % i, [128, 1], F32) for i in range(2)])
                rec = P.sbuf("rec", [128, 8], F32); recm = P.sbuf("recm", [128, 4], F32)
                qa2 = P.sbuf("qa2", [128, 4, 2, 128], BF16)
                a_sb = P.sbuf("a_sb", [128, 512], BF16); m_sb = P.sbuf("m_sb", [128, 512], BF16)
                em = [P.sbuf("em%d" % i, [128, 4, 128], BF16) for i in range(2)]
                pTs = P.sbuf("pTs", [128, 4, 128], BF16)
                br = Ring([P.sbuf("br%d" % i, [128, 3, 4, 128], BF16, dma=True) for i in range(2)])
                pI = Ring([P.psum("pI%d" % i, [128, 512], F32) for i in range(2)])
                pM = Ring([P.psum("pM%d" % i, [128, 8, 128], BF16) for i in range(2)])
                pS = Ring([P.psum("pS%d" % i, [128, 4, 128], F32) for i in range(2)])
                pA = [P.psum("pA%d" % i, [128, 512], F32) for i in range(2)]

                seqs = []
                for t in range(NTP):
                    seqs.append(dict(t=t, KT=KTp, V1=V1p, KI=KIp, nkt=t + 1, samp=False, mi=0))
                for b in range(2):
                    seqs.append(dict(t=NTP + b, KT=KTs[b], V1=V1s[b], KI=KIs[b], nkt=9, samp=True, mi=1 + b, b=b))

                def a2_load(i):
                    s = seqs[i]; t = s["t"]
                    d = dict(qa=qaL.next(), qi=qiL.next(), qm=qmL.next(), wi=wiL.next(), ub=ubL.next())
                    ld[i] = d
                    P.dma("sp", [lambda be: be.dma_start(out=d["qa"][:].rearrange("p a b -> p (a b)"), in_=s_qa[t])], d["qa"], writes=[d["qa"]])
                    P.dma("sp", [lambda be: be.dma_start(out=d["qi"][0:96, 0:2, :].rearrange("p a b -> p (a b)"), in_=s_qi[t, 0:96, 0:256]),
                             lambda be: be.dma_start(out=d["qi"][0:64, 2, :], in_=s_qi[t, 0:64, 256:384])], d["qi"], writes=[d["qi"]])
                    P.dma("sp", [lambda be: be.dma_start(out=d["qm"][:].rearrange("p a b -> p (a b)"), in_=s_qm[t])], d["qm"], writes=[d["qm"]])
                    P.dma("sp", [lambda be: be.dma_start(out=d["wi"][:], in_=s_wi[t])], d["wi"], writes=[d["wi"]])
                    P.dma("sp", [lambda be: be.dma_start(out=d["ub"][:], in_=s_ub[t])], d["ub"], writes=[d["ub"]])

                def run(g_):
                    for _ in g_:
                        pass

                def mix(gmain, gside, k):
                    ma = gmain is not None; sa = gside is not None
                    while ma or sa:
                        if ma:
                            try:
                                next(gmain)
                            except StopIteration:
                                ma = False
                        if sa:
                            for _ in range(k if ma else 1000000):
                                try:
                                    next(gside)
                                except StopIteration:
                                    sa = False
                                    break

                def stageA(i):
                    s = seqs[i]; d = ld[i]; S = s["nkt"] * 128
                    KI = s["KI"]; qi = d["qi"]; wi = d["wi"]
                    nch = (S + 511) // 512
                    for h in range(8):
                        r0 = (h % 3) * 32
                        for c in range(nch):
                            c0 = c * 512; cw = min(512, S - c0)
                            ps = pI.next()
                            P.op("pe", lambda be, ps=ps, r0=r0, h=h, c0=c0, cw=cw: be.matmul(ps[:, 0:cw], lhsT=qi[r0:r0 + 32, h // 3, :], rhs=KI[r0:r0 + 32, c0:c0 + cw], start=True, stop=True), reads=[qi, KI], writes=[ps])
                            r = rl.next()
                            P.op("act", lambda be, ps=ps, r=r, cw=cw: be.activation(out=r[:, 0:cw], in_=ps[:, 0:cw], func=AF.Relu), reads=[ps], writes=[r])
                            if h == 0:
                                P.op("dve", lambda be, r=r, c0=c0, cw=cw: be.tensor_scalar(out=sc[:, c0:c0 + cw], in0=r[:, 0:cw], scalar1=wi[:, 0:1], scalar2=None, op0=ALU.mult), reads=[r, wi], writes=[scC[c]])
                            else:
                                P.op("dve", lambda be, r=r, h=h, c0=c0, cw=cw: be.scalar_tensor_tensor(out=sc[:, c0:c0 + cw], in0=r[:, 0:cw], scalar=wi[:, h:h + 1], in1=sc[:, c0:c0 + cw], op0=ALU.mult, op1=ALU.add), reads=[r, wi, scC[c]], writes=[scC[c]])
                            yield
                    if s["samp"]:
                        P.op("dve", lambda be: be.memset(sc[:, S - 64:S], NEG), writes=[scC[nch - 1]])
                    else:
                        P.op("dve", lambda be: be.memset(sc[0:64, S - 64:S], NEG), writes=[scC[nch - 1]])

                def stageC1(i):
                    s = seqs[i]; S = s["nkt"] * 128
                    nch = (S + 511) // 512
                    scs = scC[0:nch]
                    if S <= 256:
                        P.op("dve", lambda be: be.tensor_scalar(out=Mq[:, 0:S], in0=sc[:, 0:S], scalar1=-1.0e29, scalar2=None, op0=ALU.is_ge), reads=scs, writes=[Mq])
                        return
                    nlo = 512 if S - 64 >= 512 else S - 64
                    P.op("dve", lambda be: be.tensor_reduce(out=lo[:], in_=sc[:, 0:nlo], axis=AX.X, op=ALU.min), reads=scs, writes=[lo])
                    P.op("dve", lambda be: be.tensor_reduce(out=mx[:], in_=sc[:, 0:S], axis=AX.X, op=ALU.max), reads=scs, writes=[mx])
                    P.op("dve", lambda be: be.tensor_tensor(out=w0[:], in0=mx[:], in1=lo[:], op=ALU.subtract), reads=[mx, lo], writes=[w0])
                    P.op("dve", lambda be: be.tensor_scalar(out=w0[:], in0=w0[:], scalar1=1.0 + 2.0 ** -10, scalar2=1e-12, op0=ALU.mult, op1=ALU.add), reads=[w0], writes=[w0])
                    P.op("dve", lambda be: be.tensor_scalar(out=wk[:], in0=pw[:], scalar1=w0[:, 0:1], scalar2=None, op0=ALU.mult), reads=[pw, w0], writes=[wk])
                    P.op("dve", lambda be: be.tensor_tensor(out=mid[:], in0=lo[:], in1=wk[:, 0:1], op=ALU.add), reads=[lo, wk], writes=[mid])
                    for k in range(NBIS):
                        P.op("dve", lambda be: be.tensor_scalar(out=Mq[:, 0:S], in0=sc[:, 0:S], scalar1=mid[:, 0:1], scalar2=None, op0=ALU.is_ge, op1=ALU.add, accum_out=cnt[:]), reads=scs + [mid], writes=[Mq, cnt])
                        P.op("dve", lambda be: be.tensor_scalar(out=tt_[:], in0=cnt[:], scalar1=255.5, scalar2=0.5, op0=ALU.is_ge, op1=ALU.subtract), reads=[cnt], writes=[tt_])
                        P.op("dve", lambda be, k=k: be.scalar_tensor_tensor(out=mid[:], in0=tt_[:], scalar=wk[:, k:k + 1], in1=mid[:], op0=ALU.mult, op1=ALU.add), reads=[tt_, wk, mid], writes=[mid])
                    P.op("dve", lambda be: be.tensor_tensor(out=lo[:], in0=mid[:], in1=wk[:, NBIS:NBIS + 1], op=ALU.subtract), reads=[mid, wk], writes=[lo])
                    P.op("dve", lambda be: be.tensor_scalar(out=Mq[:, 0:S], in0=sc[:, 0:S], scalar1=lo[:, 0:1], scalar2=None, op0=ALU.is_ge), reads=scs + [lo], writes=[Mq])

                def stageC2(i):
                    s = seqs[i]; nkt = s["nkt"]
                    for j in range((nkt + 7) // 8):
                        n = min(8, nkt - 8 * j)
                        pm = pM.next()
                        for k in range(n):
                            P.op("pe", lambda be, pm=pm, k=k, j=j: be.transpose(out=pm[:, k, :], in_=Mq[:, j * 1024 + k * 128:j * 1024 + (k + 1) * 128], identity=idb[:]), reads=[Mq, idb], writes=[pm])
                            yield
                        P.op("act", lambda be, pm=pm, n=n, j=j: be.copy(out=MT[:, 8 * j:8 * j + n, :], in_=pm[:, 0:n, :]), reads=[pm], writes=[MT])
                        yield

                def stageB(i):
                    s = seqs[i]; d = ld.pop(i); nkt = s["nkt"]; t = s["t"]
                    KT = s["KT"]; V1 = s["V1"]; qa = d["qa"]
                    mi = s["mi"]; qm = d["qm"]; ub = d["ub"]
                    bo = br.next()
                    for mt in range(2):
                        ps = pS.next()
                        for h in range(4):
                            P.op("pe", lambda be, ps=ps, h=h, mt=mt: be.matmul(ps[:, h, :], lhsT=mkT[mi][:, h, mt * 128:(mt + 1) * 128], rhs=qm[:, h, :], start=True, stop=True), reads=[mkT[mi], qm], writes=[ps])
                        P.op("act", lambda be, ps=ps, mt=mt: be.activation(out=em[mt][:], in_=ps[:], func=AF.Exp, scale=128.0 ** -0.5), reads=[ps], writes=[em[mt]])
                    if s["samp"]:
                        prev = ubh[s["b"]]; ai = 0
                    elif t == 0:
                        prev = None; ai = 2
                    else:
                        prev = s_prev_ub[0]; ai = 0
                    psp = pI.next()
                    pp3 = psp[:].rearrange("p (g t) -> p g t", g=4)
                    for g in range(4):
                        P.op("pe", lambda be, g=g: be.matmul(pp3[:, g, :], lhsT=ub[:, g * 128:(g + 1) * 128], rhs=bnd[:, ai, g, :], start=True, stop=(prev is None)), reads=[ub, bnd], writes=[psp])
                        if prev is not None:
                            P.op("pe", lambda be, g=g: be.matmul(pp3[:, g, :], lhsT=prev[:, g * 128:(g + 1) * 128], rhs=bnd[:, 1, g, :], start=False, stop=True), reads=[prev, bnd], writes=[psp])
                    P.op("act", lambda be: be.copy(out=pTs[:], in_=pp3), reads=[psp], writes=[pTs])
                    P.op("pool", lambda be: be.memset(qa2[:], 0.0), writes=[qa2])
                    P.op("pool", lambda be: be.tensor_copy(out=qa2[0:64, :, 0, :], in_=qa[0:64, :, :]), reads=[qa], writes=[qa2])
                    P.op("pool", lambda be: be.tensor_copy(out=qa2[64:128, :, 1, :], in_=qa[64:128, :, :]), reads=[qa], writes=[qa2])
                    accs = [pA[kv][:, 0:260].rearrange("p (h d) -> p h d", h=4) for kv in range(2)]
                    nch2 = (nkt + 1) // 2
                    its = [(g, c) for g in range(4) for c in range(nch2)]
                    stq = {}

                    def emit_qk(j):
                        g, c = its[j]
                        n = min(2, nkt - 2 * c)
                        ps = pS.next()
                        ps4 = ps[:].rearrange("p (k h) q -> p k h q", k=2)
                        for k in range(n):
                            kt = 2 * c + k
                            P.op("pe", lambda be, k=k, kt=kt: be.matmul(ps4[:, k, :, :], lhsT=KT[:, kt * 128:(kt + 1) * 128], rhs=qa2[:, g, :, :], start=True, stop=True), reads=[KT, qa2], writes=[ps])
                        e = er.next()
                        e4 = e[:].rearrange("p (k h) q -> p k h q", k=2)
                        P.op("act", lambda be: be.activation(out=e4[:, 0:n, :, :], in_=ps4[:, 0:n, :, :], func=AF.Exp, scale=0.125), reads=[ps], writes=[e])
                        p_ = pr_.next()
                        p4 = p_[:].rearrange("p (k h) q -> p k h q", k=2)
                        P.op("pool", lambda be: be.tensor_tensor(out=p4[:, 0:n, :, :], in0=e4[:, 0:n, :, :], in1=MT[:, 2 * c:2 * c + n, :].unsqueeze(2).to_broadcast([128, n, 2, 128]), op=ALU.mult), reads=[e, MT], writes=[p_])
                        stq[j] = (p_, p4, n)

                    def emit_pv(j):
                        g, c = its[j]
                        p_, p4, n = stq.pop(j)
                        for k in range(n):
                            kt = 2 * c + k
                            for hh in range(2):
                                P.op("pe", lambda be, k=k, kt=kt, hh=hh: be.matmul(accs[hh][:, g, :], lhsT=p4[:, k, hh, :], rhs=V1[:, kt, hh, :], start=(kt == 0), stop=(kt == nkt - 1)), reads=[p_, V1], writes=[pA[hh]])

                    LA = 2
                    for j in range(len(its) + LA):
                        if j < len(its):
                            emit_qk(j)
                        if j - LA >= 0:
                            emit_pv(j - LA)
                    pm1 = pS.next(); pm2 = pS.next()
                    accm = [pm1[:].rearrange("p a q -> p (a q)")[:, 0:258].rearrange("p (h d) -> p h d", h=2),
                            pm2[:].rearrange("p a q -> p (a q)")[:, 0:258].rearrange("p (h d) -> p h d", h=2)]
                    pmb = [pm1, pm2]
                    for h in range(4):
                        for mt in range(2):
                            P.op("pe", lambda be, h=h, mt=mt: be.matmul(accm[h // 2][:, h % 2, :], lhsT=em[mt][:, h, :], rhs=mv1[mi][:, mt, h, :], start=(mt == 0), stop=(mt == 1)), reads=[em[mt], mv1[mi]], writes=[pmb[h // 2]])
                    ps2 = pI.next()
                    py3 = ps2[:].rearrange("p (g t) -> p g t", g=4)
                    for g in range(4):
                        P.op("pe", lambda be, g=g: be.matmul(py3[:, g, :], lhsT=wpl[:, g, :], rhs=pTs[:, g, :], start=True, stop=True), reads=[wpl, pTs], writes=[ps2])
                    for kv in range(2):
                        acc = accs[kv]
                        P.op("dve", lambda be, acc=acc, kv=kv: be.reciprocal(out=rec[:, 4 * kv:4 * kv + 4], in_=acc[:, :, 64]), reads=[pA[kv]], writes=[rec])
                        P.op("dve", lambda be, acc=acc, kv=kv: be.tensor_tensor(out=a_sb[:, 256 * kv:256 * kv + 256].rearrange("p (h d) -> p h d", h=4), in0=acc[:, :, 0:64], in1=bl(rec[:, 4 * kv:4 * kv + 4], [128, 4, 64]), op=ALU.mult), reads=[pA[kv], rec], writes=[a_sb])
                    for hh in range(2):
                        P.op("dve", lambda be, hh=hh: be.reciprocal(out=recm[:, 2 * hh:2 * hh + 2], in_=accm[hh][:, :, 128]), reads=[pmb[hh]], writes=[recm])
                        P.op("dve", lambda be, hh=hh: be.tensor_tensor(out=m_sb[:, 256 * hh:256 * hh + 256].rearrange("p (h d) -> p h d", h=2), in0=accm[hh][:, :, 0:128], in1=bl(recm[:, 2 * hh:2 * hh + 2], [128, 2, 128]), op=ALU.mult), reads=[pmb[hh], recm], writes=[m_sb])
                    P.op("dve", lambda be: be.tensor_tensor(out=bo[:, 1, :, :], in0=py3, in1=bl(spl[:], [128, 4, 128]), op=ALU.mult), reads=[ps2, spl], writes=[bo])
                    pm = pM.next()
                    transpose_to(pm, pm, a_sb, a_sb, 4, bo, bo[:, 0, :, :])
                    pm = pM.next()
                    transpose_to(pm, pm, m_sb, m_sb, 4, bo, bo[:, 2, :, :])
                    P.dma("sp", [lambda be: be.dma_start(out=s_br[t], in_=bo[:].rearrange("p a b c -> p (a b c)"))], bo, reads=[bo])
                    s_prev_ub[0] = ub

                s_prev_ub = [None]
                nseq = len(seqs)
                for i0 in range(min(3, nseq)):
                    a2_load(i0)
                run(stageA(0)); stageC1(0); run(stageC2(0))
                run(stageA(1)); stageC1(1)
                for i in range(nseq):
                    stageB(i)
                    if i + 3 < nseq:
                        a2_load(i + 3)
                    mix(stageA(i + 2) if i + 2 < nseq else None, stageC2(i + 1) if i + 1 < nseq else None, 1)
                    if i + 2 < nseq:
                        stageC1(i + 2)
                P.barrier()
                P.emit_block()
                if KSTOP == 3:
                    return nc

        with ExitStack() as st:
            P.stack = st
            wo = P.sbuf("wo", [128, 3, 4, D], BF16, dma=True)
            fl = []
            for b in range(3):
                for kc in range(4):
                    fl.append(lambda be, b=b, kc=kc: be.dma_start(out=wo[:, b, kc, :], in_=w_o[b][kc * 128:(kc + 1) * 128, :]))
            P.dma("pool", fl, wo, writes=[wo])
            wout = P.sbuf("wout", [128, 8, D], BF16, dma=True)
            load_w_cast(wout, lambda kc, c0, cw: wout[:, kc, c0:c0 + cw], w_out, 1024, D, 8)
            brL = Ring([P.sbuf("brL%d" % i, [128, 3, 4, 128], BF16, dma=True) for i in range(4)])
            gtL = Ring([P.sbuf("gtL%d" % i, [128, 3, D], BF16, dma=True) for i in range(4)])
            xL = Ring([P.sbuf("xL%d" % i, [128, D], F32, dma=True) for i in range(4)])
            mixedL = [P.sbuf("mixed%d" % i, [128, D], F32) for i in range(2)]; tmpmL = [P.sbuf("tmpm%d" % i, [128, D], F32) for i in range(2)]
            mxbL = [P.sbuf("mxb%d" % i, [128, D], BF16) for i in range(2)]; mxT = P.sbuf("mxT", [128, 8, 128], BF16)
            x1o = Ring([P.sbuf("x1o%d" % i, [128, D], F32, dma=True) for i in range(2)])
            pt = Ring([P.psum("pt3_%d" % i, [128, D], F32) for i in range(2)])
            pT = P.psum("pT3", [128, 8, 128], BF16)
            pO = P.psum("pO3", [128, D], F32)
            l3 = {}

            def a3_load(t):
                d = dict(br=brL.next(), gt=gtL.next(), x=xL.next())
                l3[t] = d
                P.dma("sp", [lambda be: be.dma_start(out=d["br"][:].rearrange("p a b c -> p (a b c)"), in_=s_br[t])], d["br"], writes=[d["br"]])
                P.dma("sp", [lambda be: be.dma_start(out=d["gt"][:].rearrange("p a b -> p (a b)"), in_=s_gate[t])], d["gt"], writes=[d["gt"]])
                P.dma("sp", [lambda be: be.dma_start(out=d["x"][:], in_=xsrc(t))], d["x"], writes=[d["x"]])

            a3st = {}

            def a3_X(t):
                d = l3[t]
                par = t % 2
                brt = d["br"]; gt = d["gt"]
                mixed = mixedL[par]; tmpm = tmpmL[par]; mxb = mxbL[par]
                for b in range(3):
                    ps = pt.next()
                    for half in range(2):
                        for kc in range(4):
                            P.op("pe", lambda be, ps=ps, b=b, half=half, kc=kc: be.matmul(ps[:, half * 512:(half + 1) * 512], lhsT=brt[:, b, kc, :], rhs=wo[:, b, kc, half * 512:(half + 1) * 512], start=(kc == 0), stop=(kc == 3)), reads=[brt, wo], writes=[ps])
                    if b == 0:
                        P.op("dve", lambda be, ps=ps: be.tensor_tensor(out=mixed[:], in0=ps[:], in1=gt[:, 0, :], op=ALU.mult), reads=[ps, gt], writes=[mixed])
                    else:
                        P.op("dve", lambda be, ps=ps, b=b: be.tensor_tensor(out=tmpm[:], in0=ps[:], in1=gt[:, b, :], op=ALU.mult), reads=[ps, gt], writes=[tmpm])
                        if b == 1:
                            P.op("pool", lambda be: be.tensor_tensor(out=mixed[:], in0=mixed[:], in1=tmpm[:], op=ALU.add), reads=[mixed, tmpm], writes=[mixed])
                        else:
                            P.op("pool", lambda be: be.tensor_tensor(out=mxb[:], in0=mixed[:], in1=tmpm[:], op=ALU.add), reads=[mixed, tmpm], writes=[mxb])

            def a3_Y(t):
                d = l3.pop(t)
                par = t % 2
                xb = d["x"]; mxb = mxbL[par]
                transpose_to(pT, pT, mxb, mxb, 8, mxT, mxT[:])
                for half in range(2):
                    for kc in range(8):
                        P.op("pe", lambda be, half=half, kc=kc: be.matmul(pO[:, half * 512:(half + 1) * 512], lhsT=mxT[:, kc, :], rhs=wout[:, kc, half * 512:(half + 1) * 512], start=(kc == 0), stop=(kc == 7)), reads=[mxT, wout], writes=[pO])
                xo = x1o.next()
                P.op("dve", lambda be: be.tensor_tensor(out=xo[:], in0=pO[:], in1=xb[:], op=ALU.add), reads=[pO, xb], writes=[xo])
                P.dma("sp", [lambda be: be.dma_start(out=s_x1[t], in_=xo[:])], xo, reads=[xo])

            a3_load(0); a3_load(1)
            a3_X(0)
            for t in range(NT):
                if t + 2 < NT:
                    a3_load(t + 2)
                if t + 1 < NT:
                    a3_X(t + 1)
                a3_Y(t)
            P.barrier()
            P.emit_block()
            if KSTOP == 4:
                return nc

        with ExitStack() as st:
            P.stack = st
            wg = P.sbuf("wg", [128, 8, DFF], BF16, dma=True)
            load_w_cast(wg, lambda kc, c0, cw: wg[:, kc, c0:c0 + cw], w_gate, 1024, DFF, 8)
            wu = P.sbuf("wu", [128, 8, DFF], BF16, dma=True)
            load_w_cast(wu, lambda kc, c0, cw: wu[:, kc, c0:c0 + cw], w_up, 1024, DFF, 8)
            wd = P.sbuf("wd", [128, 22, D], BF16, dma=True)
            load_w_cast(wd, lambda kc, c0, cw: wd[:, kc, c0:c0 + cw], w_down, DFF, D, 22)
            gffn = P.sbuf("gffn", [128, D], F32, dma=True)
            P.dma("sp", [lambda be: be.dma_start(out=gffn[:], in_=g_ffn.partition_broadcast(128))], gffn, writes=[gffn])
            x1L = Ring([P.sbuf("x1L%d" % i, [128, D], F32, dma=True) for i in range(4)])
            junk = P.sbuf("junkb", [128, D], BF16)
            ssL = [P.sbuf("ssb%d" % i, [128, 1], F32) for i in range(2)]; rsL = [P.sbuf("rsb%d" % i, [128, 1], F32) for i in range(2)]
            hbL = [P.sbuf("hbb%d" % i, [128, D], BF16) for i in range(2)]; hTL = [P.sbuf("hTb%d" % i, [128, 8, 128], BF16) for i in range(2)]
            sg = Ring([P.sbuf("sg%d" % i, [128, 4, 128], F32) for i in range(2)])
            gT = P.sbuf("gT", [128, 22, 128], BF16)
            yo = Ring([P.sbuf("yo%d" % i, [128, D], F32, dma=True) for i in range(2)])
            pT = P.psum("pTb", [128, 8, 128], BF16)
            pG = Ring([P.psum("pG%d" % i, [128, 4, 128], F32) for i in range(2)])
            pU = Ring([P.psum("pU%d" % i, [128, 4, 128], F32) for i in range(2)])
            pO = P.psum("pOb", [128, D], F32)
            lb = {}

            def b_load(t):
                xb = x1L.next()
                lb[t] = xb
                P.dma("sp", [lambda be: be.dma_start(out=xb[:], in_=s_x1[t])], xb, writes=[xb])

            def b_R(t):
                par = t % 2
                xb = lb[t]
                rms_rows(xb, xb[:], gffn, hbL[par], D, junk, ssL[par], rsL[par])
                transpose_to(pT, pT, hbL[par], hbL[par], 8, hTL[par], hTL[par][:])

            def b_G(t):
                hT = hTL[t % 2]
                for fg in range(6):
                    nf = min(4, 22 - 4 * fg)
                    pg = pG.next(); pu = pU.next()
                    for j in range(nf):
                        fc = 4 * fg + j
                        for kc in range(8):
                            P.op("pe", lambda be, pg=pg, j=j, fc=fc, kc=kc: be.matmul(pg[:, j, :], lhsT=wg[:, kc, fc * 128:(fc + 1) * 128], rhs=hT[:, kc, :], start=(kc == 0), stop=(kc == 7)), reads=[wg, hT], writes=[pg])
                    for j in range(nf):
                        fc = 4 * fg + j
                        for kc in range(8):
                            P.op("pe", lambda be, pu=pu, j=j, fc=fc, kc=kc: be.matmul(pu[:, j, :], lhsT=wu[:, kc, fc * 128:(fc + 1) * 128], rhs=hT[:, kc, :], start=(kc == 0), stop=(kc == 7)), reads=[wu, hT], writes=[pu])
                    s_ = sg.next()
                    P.op("act", lambda be, pg=pg, s_=s_, nf=nf: be.activation(out=s_[:, 0:nf, :], in_=pg[:, 0:nf, :], func=AF.Silu), reads=[pg], writes=[s_])
                    P.op("dve", lambda be, pu=pu, s_=s_, nf=nf, fg=fg: be.tensor_tensor(out=gT[:, 4 * fg:4 * fg + nf, :], in0=pu[:, 0:nf, :], in1=s_[:, 0:nf, :], op=ALU.mult), reads=[pu, s_], writes=[gT])

            def b_D(t):
                xb = lb.pop(t)
                for half in range(2):
                    for fc in range(22):
                        P.op("pe", lambda be, half=half, fc=fc: be.matmul(pO[:, half * 512:(half + 1) * 512], lhsT=gT[:, fc, :], rhs=wd[:, fc, half * 512:(half + 1) * 512], start=(fc == 0), stop=(fc == 21)), reads=[gT, wd], writes=[pO])
                yb = yo.next()
                P.op("dve", lambda be: be.tensor_tensor(out=yb[:], in0=pO[:], in1=xb[:], op=ALU.add), reads=[pO, xb], writes=[yb])
                if t < NTP:
                    P.dma("sp", [lambda be: be.dma_start(out=y_p[t * 128:(t + 1) * 128, :], in_=yb[:])], yb, reads=[yb])
                else:
                    P.dma("sp", [lambda be: be.dma_start(out=y_s[t - NTP], in_=yb[0:64, :])], yb, reads=[yb])

            b_load(0); b_load(1)
            b_R(0)
            for t in range(NT):
                if t + 2 < NT:
                    b_load(t + 2)
                b_G(t)
                if t + 1 < NT:
                    b_R(t + 1)
                b_D(t)
            P.barrier()
            P.emit_block()
            if KSTOP == 5:
                return nc
    return nc


def _consts():
    NT = NTP + 2
    theta = np.float32(10000.0)
    tab = np.zeros((NT, 128, 96), np.float32)
    inv64 = (theta ** (-np.arange(32, dtype=np.float32) / np.float32(32))).astype(np.float32)
    inv32 = (theta ** (-np.arange(16, dtype=np.float32) / np.float32(16))).astype(np.float32)
    for t in range(NT):
        pos = (np.arange(128) + (t * 128 if t < NTP else 1024)).astype(np.float32)
        a64 = (pos[:, None] * inv64[None, :]).astype(np.float32)
        a32 = (pos[:, None] * inv32[None, :]).astype(np.float32)
        tab[t, :, 0:32] = np.cos(a64.astype(np.float64)); tab[t, :, 32:64] = np.sin(a64.astype(np.float64))
        tab[t, :, 64:80] = np.cos(a32.astype(np.float64)); tab[t, :, 80:96] = np.sin(a32.astype(np.float64))
    bands = np.zeros((128, 3, 4, 128), np.float32)
    tp = np.arange(128)[:, None]; tq = np.arange(128)[None, :]
    for g, w in enumerate((2, 4, 8, 16)):
        inwin = (tp <= tq) & (tp > tq - w)
        bands[:, 0, g, :] = inwin / w - (tp == tq)
        bands[:, 1, g, :] = ((tp - 128) > (tq - w)) / w
        cntf = np.minimum(w, tq + 1).astype(np.float64)
        bands[:, 2, g, :] = inwin / cntf - (tp == tq)
    return tab, bands.astype(np.float32), np.eye(128, dtype=np.float32)


_CACHE = {}


def kernel(x_prompt, x_sample, mem_prompt, cache_a_k, cache_a_v, cache_idx_k, cache_pool, cache_mem_k,
           cache_mem_v, g_mix, w_in, g_qa, g_ka, g_kidx, g_qm, g_mem, w_mem_kv, g_km, w_pool, s_pool,
           w_oa, w_ob, w_om, w_out, g_ffn, w_gate, w_up, w_down):
    f = lambda a: np.ascontiguousarray(np.asarray(a, dtype=np.float32))
    if "nc" not in _CACHE:
        _CACHE["nc"] = build_program()
        _CACHE["consts"] = _consts()
    nc = _CACHE["nc"]
    tab, bands, ident = _CACHE["consts"]
    xs_pad = np.zeros((16, 128, D), np.float32)
    xs_pad[:, 0:64, :] = f(x_sample)
    shared = {
        "w_in": f(w_in[0]), "w_mem": f(w_mem_kv[0]), "w_pool": f(w_pool[0]), "w_oa": f(w_oa[0]), "w_ob": f(w_ob[0]),
        "w_om": f(w_om[0]), "w_out": f(w_out[0]), "w_gate": f(w_gate[0]), "w_up": f(w_up[0]), "w_down": f(w_down[0]),
        "g_mix": f(g_mix), "g_ffn": f(g_ffn), "g_mem": f(g_mem), "g_qa": f(g_qa), "g_ka": f(g_ka), "g_kidx": f(g_kidx),
        "g_qm": f(g_qm), "g_km": f(g_km), "s_pool": f(np.asarray(s_pool[0]).reshape(4, 128).T),
        "rope": tab, "bands": bands, "ident": ident,
    }
    in_maps = []
    for c in range(8):
        m = dict(shared)
        m["x_p"] = f(x_prompt[c]); m["x_s"] = np.ascontiguousarray(xs_pad[2 * c:2 * c + 2]); m["mem"] = f(mem_prompt[c])
        m["cak"] = f(np.asarray(cache_a_k[0, 2 * c:2 * c + 2]).reshape(2, 1024, 128))
        m["cav"] = f(np.asarray(cache_a_v[0, 2 * c:2 * c + 2]).reshape(2, 1024, 128))
        m["cik"] = f(cache_idx_k[0, 2 * c:2 * c + 2]); m["cpool"] = f(cache_pool[0, 2 * c:2 * c + 2])
        m["cmk"] = f(np.asarray(cache_mem_k[0, 2 * c:2 * c + 2]).reshape(2, 256, 512))
        m["cmv"] = f(np.asarray(cache_mem_v[0, 2 * c:2 * c + 2]).reshape(2, 256, 512))
        in_maps.append(m)
    res = run_bass_kernel_spmd(nc, in_maps, core_ids=list(range(8)))
    R = res.results
    cat = lambda k: np.stack([np.asarray(r[k], dtype=np.float32) for r in R], 0)
    cat2 = lambda k: np.concatenate([np.asarray(r[k], dtype=np.float32) for r in R], 0)
    y_prompt = cat("y_p")
    y_sample = cat2("y_s")
    return (
        y_prompt, y_sample,
        cat("nak_p").reshape(1, 8, 8192, 2, 64), cat("nav_p").reshape(1, 8, 8192, 2, 64), cat("nik_p").reshape(1, 8, 8192, 32),
        cat("npool_p").reshape(1, 8, 15, 512), cat("nmk_p").reshape(1, 8, 256, 4, 128), cat("nmv_p").reshape(1, 8, 256, 4, 128),
        cat2("nak_s").reshape(1, 16, 64, 2, 64), cat2("nav_s").reshape(1, 16, 64, 2, 64), cat2("nik_s").reshape(1, 16, 64, 32),
        cat2("npool_s").reshape(1, 16, 15, 512),
    )
```
